# Optimizing a Trainium2 kernel written in Bass

```python
import jax, jax.numpy as jnp
from jax import lax
import numpy as np

D_MODEL = 1024
BATCH = 16
SEQ = 4096
DEPTH = 2

N_MEM = 256
EPS = 1e-6
F32 = jnp.float32
CHUNK = 64
CONV_WIDTH = 3
M_HEADS = 4
M_HEAD_DIM = D_MODEL // 8
M_WIDTH = M_HEADS * M_HEAD_DIM
G_HEADS = 4
G_KEY_DIM = D_MODEL // 16
G_VAL_DIM = D_MODEL // 8
G_KEY_WIDTH = G_HEADS * G_KEY_DIM
G_VAL_WIDTH = G_HEADS * G_VAL_DIM
G_DECAY_RANK = 16
G_DECAY_TAU = 16.0
EVEN_SPLITS = (M_WIDTH, M_WIDTH, M_WIDTH, M_WIDTH, 2 * M_HEADS, 2 * M_HEADS, G_KEY_WIDTH, G_KEY_WIDTH, G_VAL_WIDTH, G_VAL_WIDTH, 2 * G_DECAY_RANK)
EVEN_IN_WIDTH = sum(EVEN_SPLITS)
EVEN_OUT_WIDTH = M_WIDTH + G_VAL_WIDTH
R_HEAD_DIM = 64
R_HEADS = D_MODEL // R_HEAD_DIM
R_DECAY_RANK = max(32, int(round(1.8 * D_MODEL ** 0.5 / 32)) * 32)
R_A_RANK = max(32, int(round(1.8 * D_MODEL ** 0.5 / 32)) * 32)
R_GATE_RANK = max(32, int(round(0.6 * D_MODEL ** 0.8 / 32)) * 32)
R_GN_EPS = 64e-5
X_HEADS = 4
X_HEAD_DIM = D_MODEL // X_HEADS
D_FF = -(-8 * D_MODEL // (3 * 256)) * 256

kernel_name = 'hybrid_mlstm_gla_rwkv7_encoder'


def rms_norm(x, g):
    xf = x.astype(F32)
    y = xf * lax.rsqrt(jnp.mean(xf * xf, axis=-1, keepdims=True) + EPS)
    return (y * g.astype(F32)).astype(x.dtype)


def head_rms(x, g):
    B, T, H, d = x.shape
    xf = x.astype(F32)
    y = xf * lax.rsqrt(jnp.mean(xf * xf, axis=-1, keepdims=True) + EPS)
    return y.reshape(B, T, H * d) * g.astype(F32)


def centred_depthwise_conv(x, w):
    K, C = w.shape
    return lax.conv_general_dilated(x, w[:, None, :].astype(x.dtype), window_strides=(1,), padding=[(K // 2, K // 2)], dimension_numbers=('NWC', 'WIO', 'NWC'), feature_group_count=C)


def _both(fwd, bwd):
    return jnp.stack([fwd, jnp.flip(bwd, axis=1)], axis=0)


def _to_chunks(a):
    n, b, t, h = a.shape[:4]
    a = a.reshape((n, b, t // CHUNK, CHUNK, h) + a.shape[4:])
    return jnp.moveaxis(a, (2, 4), (0, 3))


def _from_chunks(a):
    a = jnp.moveaxis(a, (0, 3), (2, 4))
    n, b, nc, l, h = a.shape[:5]
    return a.reshape((n, b, nc * l, h) + a.shape[5:])


def mlstm_bidirectional(q, k, v, ig, lf):
    _, B, _, H, d = q.shape
    mask = jnp.tril(jnp.ones((CHUNK, CHUNK), dtype=bool))

    def body(carry, xs):
        C, n, m = carry
        qc, kc, vc, igc, lfc = xs
        b = jnp.cumsum(lfc, axis=-1)
        g = b[..., -1]
        D = jnp.where(mask, b[..., :, None] - b[..., None, :] + igc[..., None, :], -jnp.inf)
        inter_log = b + m[..., None]
        m_t = jnp.maximum(inter_log, jnp.max(D, axis=-1))
        sc = jnp.einsum('nbhtk,nbhsk->nbhts', qc, kc) * jnp.exp(D - m_t[..., None])
        w_inter = jnp.exp(inter_log - m_t)
        num = jnp.einsum('nbhts,nbhsv->nbhtv', sc, vc) + w_inter[..., None] * jnp.einsum('nbhtk,nbhkv->nbhtv', qc, C)
        den = jnp.sum(sc, axis=-1) + w_inter * jnp.einsum('nbhtk,nbhk->nbht', qc, n)
        h = num / jnp.maximum(jnp.abs(den), jnp.exp(-m_t))[..., None]
        kw_log = g[..., None] - b + igc
        m_new = jnp.maximum(g + m, jnp.max(kw_log, axis=-1))
        kw = jnp.exp(kw_log - m_new[..., None])
        decay = jnp.exp(g + m - m_new)
        C = decay[..., None, None] * C + jnp.einsum('nbhsk,nbhsv->nbhkv', kc * kw[..., None], vc)
        n = decay[..., None] * n + jnp.einsum('nbhsk,nbhs->nbhk', kc, kw)
        return (C, n, m_new), h

    init = (jnp.zeros((2, B, H, d, d), F32), jnp.zeros((2, B, H, d), F32), jnp.zeros((2, B, H), F32))
    _, h = lax.scan(body, init, (_to_chunks(q), _to_chunks(k), _to_chunks(v), _to_chunks(ig), _to_chunks(lf)))
    return _from_chunks(h)


def gla_bidirectional(q, k, v, la):
    _, B, _, H, dk = q.shape
    dv = v.shape[-1]
    mask = jnp.tril(jnp.ones((CHUNK, CHUNK), dtype=bool))

    def body(S, xs):
        qc, kc, vc, lac = xs
        b = jnp.cumsum(lac, axis=-2)
        g = b[..., -1, :]
        rel = jnp.where(mask[:, :, None], b[..., :, None, :] - b[..., None, :, :], -jnp.inf)
        A = jnp.einsum('nbhtk,nbhsk,nbhtsk->nbhts', qc, kc, jnp.exp(rel))
        o = jnp.einsum('nbhts,nbhsv->nbhtv', A, vc) + jnp.einsum('nbhtk,nbhkv->nbhtv', qc * jnp.exp(b), S)
        S = jnp.exp(g)[..., None] * S + jnp.einsum('nbhsk,nbhsv->nbhkv', kc * jnp.exp(g[..., None, :] - b), vc)
        return S, o

    _, o = lax.scan(body, jnp.zeros((2, B, H, dk, dv), F32), (_to_chunks(q), _to_chunks(k), _to_chunks(v), _to_chunks(la)))
    return _from_chunks(o)


def even_mixer(h, w_in, conv_qk, m_ig_bias, m_fg_bias, m_norm, g_decay_w2, g_decay_b, g_norm, w_out):
    B, T, _ = h.shape
    split_at = np.cumsum(EVEN_SPLITS)[:-1].tolist()
    mq, mk, mv, mo, mi, mf, gq, gk, gv, gg, glr = jnp.split(h @ w_in, split_at, axis=-1)
    qk = jax.nn.silu(centred_depthwise_conv(jnp.concatenate([mq, mk], axis=-1), conv_qk))
    mq, mk = jnp.split(qk, 2, axis=-1)
    q = mq.astype(F32).reshape(B, T, M_HEADS, M_HEAD_DIM) * M_HEAD_DIM ** -0.5
    k = mk.astype(F32).reshape(B, T, M_HEADS, M_HEAD_DIM)
    v = mv.astype(F32).reshape(B, T, M_HEADS, M_HEAD_DIM)
    ig = mi.astype(F32).reshape(B, T, 2, M_HEADS) + m_ig_bias.astype(F32)
    lf = jax.nn.log_sigmoid(mf.astype(F32).reshape(B, T, 2, M_HEADS) + m_fg_bias.astype(F32))
    hm = mlstm_bidirectional(_both(q, q), _both(k, k), _both(v, v), _both(ig[:, :, 0], ig[:, :, 1]), _both(lf[:, :, 0], lf[:, :, 1]))
    hm = hm[0] + jnp.flip(hm[1], axis=1)
    hm = head_rms(hm, m_norm) * jax.nn.sigmoid(mo.astype(F32))
    gqh = gq.astype(F32).reshape(B, T, G_HEADS, G_KEY_DIM) * G_KEY_DIM ** -0.5
    gkh = gk.astype(F32).reshape(B, T, G_HEADS, G_KEY_DIM)
    gvh = gv.astype(F32).reshape(B, T, G_HEADS, G_VAL_DIM)
    z = jnp.einsum('btnr,nrk->btnk', glr.astype(F32).reshape(B, T, 2, G_DECAY_RANK), g_decay_w2.astype(F32)) + g_decay_b.astype(F32)
    la = (jax.nn.log_sigmoid(z) / G_DECAY_TAU).reshape(B, T, 2, G_HEADS, G_KEY_DIM)
    og = gla_bidirectional(_both(gqh, gqh), _both(gkh, gkh), _both(gvh, gvh), _both(la[:, :, 0], la[:, :, 1]))
    og = og[0] + jnp.flip(og[1], axis=1)
    og = head_rms(og, g_norm) * jax.nn.silu(gg.astype(F32))
    merged = jnp.concatenate([hm, og], axis=-1).astype(h.dtype)
    return merged @ w_out


def rwkv7_scan_bidirectional(r, w_f, w_b, k, v, a, b):
    B, T, H, N = r.shape
    tm = lambda t: jnp.moveaxis(t, 1, 0)

    def step(S, inp):
        r_t, w_t, k_t, v_t, a_t, b_t = inp
        sa = jnp.einsum('bhvk,bhk->bhv', S, a_t)
        S = S * w_t[..., None, :] + sa[..., :, None] * b_t[..., None, :] + v_t[..., :, None] * k_t[..., None, :]
        return S, jnp.einsum('bhvk,bhk->bhv', S, r_t)

    S0 = jnp.zeros((B, H, N, N), F32)
    rt, kt, vt, at, bt = tm(r), tm(k), tm(v), tm(a), tm(b)
    _, y_f = lax.scan(step, S0, (rt, tm(w_f), kt, vt, at, bt))
    _, y_b = lax.scan(step, S0, (rt, tm(w_b), kt, vt, at, bt), reverse=True)
    return jnp.moveaxis(y_f + y_b, 0, 1)


def rwkv7_mixer(h, mu, w_rkv, w0, w1, w2, a0, a1, a2, g1, g2, k_k, k_a, r_k, ln_w, ln_b, w_o):
    B, T, C = h.shape
    zero = jnp.zeros_like(h[:, :1])
    h_prev = jnp.concatenate([zero, h[:, :-1]], axis=1)
    h_next = jnp.concatenate([h[:, 1:], zero], axis=1)
    hh = 0.5 * (h_prev + h_next) - h
    xr, xw, xk, xv, xa, xg = (h + hh * mu[i] for i in range(6))
    r = xr @ w_rkv[0]
    k = xk @ w_rkv[1]
    v = xv @ w_rkv[2]
    a = jax.nn.sigmoid(a0 + (xa @ a1) @ a2)
    g = jax.nn.sigmoid(xg @ g1) @ g2

    def decay(d):
        z = (w0[d] + jnp.tanh(xw @ w1[d]) @ w2[d]).astype(F32)
        return jnp.exp(-jnp.exp(-jax.nn.softplus(-z) - 0.5))

    def heads(t):
        return t.astype(F32).reshape(B, T, R_HEADS, R_HEAD_DIM)

    kk = heads(k * k_k)
    kk = kk / jnp.maximum(jnp.sqrt(jnp.sum(kk * kk, axis=-1, keepdims=True)), 1e-12)
    k = k * (1 + (a - 1) * k_a)
    rh, kh, vh, ah = heads(r), heads(k), heads(v), heads(a)
    y = rwkv7_scan_bidirectional(rh, heads(decay(0)), heads(decay(1)), kh, vh, -kk, kk * ah)
    mean = jnp.mean(y, axis=-1, keepdims=True)
    var = jnp.mean(jnp.square(y - mean), axis=-1, keepdims=True)
    y = ((y - mean) * lax.rsqrt(var + R_GN_EPS)).reshape(B, T, C) * ln_w.astype(F32) + ln_b.astype(F32)
    bonus = jnp.sum(rh * kh * r_k.astype(F32), axis=-1, keepdims=True) * vh
    y = (y + bonus.reshape(B, T, C)) * g.astype(F32)
    return y.astype(h.dtype) @ w_o


def memory_cross_attention(h, mem_n, wq, wkv, wo):
    B, T, _ = h.shape
    M = mem_n.shape[1]
    q = (h @ wq).reshape(B, T, X_HEADS, X_HEAD_DIM)
    k, v = jnp.split(mem_n @ wkv, 2, axis=-1)
    k = k.reshape(B, M, X_HEADS, X_HEAD_DIM)
    v = v.reshape(B, M, X_HEADS, X_HEAD_DIM)
    s = jnp.einsum('bthd,bmhd->bhtm', q, k).astype(F32) * X_HEAD_DIM ** -0.5
    p = jax.nn.softmax(s, axis=-1).astype(v.dtype)
    o = jnp.einsum('bhtm,bmhd->bthd', p, v).reshape(B, T, D_MODEL)
    return o @ wo


def swiglu_ffn(h, w_gate, w_up, w_down):
    return (jax.nn.silu(h @ w_gate) * (h @ w_up)) @ w_down


def setup_inputs(seed: int = 0) -> dict:
    key = jax.random.key(seed)
    keys = iter(jax.random.split(key, 64))

    def nrm(shape, scale):
        return jax.random.normal(next(keys), shape, jnp.float32) * scale

    def gain(shape):
        return 1.0 + nrm(shape, 0.02)

    D = D_MODEL
    NE = (DEPTH + 1) // 2
    NO = DEPTH // 2
    return {
        'x': nrm((BATCH, SEQ, D), 1.0),
        'mem': nrm((BATCH, N_MEM, D), 1.0),
        'norm_mix': gain((DEPTH, D)),
        'norm_xattn': gain((DEPTH, D)),
        'norm_mem': gain((DEPTH, D)),
        'norm_ffn': gain((DEPTH, D)),
        'norm_final': gain((D,)),
        'xa_wq': nrm((DEPTH, D, D), D ** -0.5),
        'xa_wkv': nrm((DEPTH, D, 2 * D), D ** -0.5),
        'xa_wo': nrm((DEPTH, D, D), D ** -0.5),
        'ffn_w_gate': nrm((DEPTH, D, D_FF), D ** -0.5),
        'ffn_w_up': nrm((DEPTH, D, D_FF), D ** -0.5),
        'ffn_w_down': nrm((DEPTH, D_FF, D), D_FF ** -0.5),
        'ev_w_in': nrm((NE, D, EVEN_IN_WIDTH), D ** -0.5),
        'ev_conv_qk': nrm((NE, CONV_WIDTH, 2 * M_WIDTH), CONV_WIDTH ** -0.5),
        'ev_m_ig_bias': nrm((NE, 2, M_HEADS), 0.1),
        'ev_m_fg_bias': jnp.linspace(3.0, 6.0, M_HEADS, dtype=jnp.float32) + nrm((NE, 2, M_HEADS), 0.1),
        'ev_m_norm': gain((NE, M_WIDTH)),
        'ev_g_decay_w2': nrm((NE, 2, G_DECAY_RANK, G_KEY_WIDTH), G_DECAY_RANK ** -0.5),
        'ev_g_decay_b': nrm((NE, 2, G_KEY_WIDTH), 0.1),
        'ev_g_norm': gain((NE, G_VAL_WIDTH)),
        'ev_w_out': nrm((NE, EVEN_OUT_WIDTH, D), EVEN_OUT_WIDTH ** -0.5),
        'od_mu': jax.random.uniform(next(keys), (NO, 6, D), jnp.float32),
        'od_w_rkv': nrm((NO, 3, D, D), D ** -0.5),
        'od_w0': jnp.linspace(-6.0, -1.0, D, dtype=jnp.float32) + nrm((NO, 2, D), 0.1),
        'od_w1': nrm((NO, 2, D, R_DECAY_RANK), D ** -0.5),
        'od_w2': nrm((NO, 2, R_DECAY_RANK, D), 0.5 * R_DECAY_RANK ** -0.5),
        'od_a0': nrm((NO, D), 0.1),
        'od_a1': nrm((NO, D, R_A_RANK), D ** -0.5),
        'od_a2': nrm((NO, R_A_RANK, D), R_A_RANK ** -0.5),
        'od_g1': nrm((NO, D, R_GATE_RANK), D ** -0.5),
        'od_g2': nrm((NO, R_GATE_RANK, D), R_GATE_RANK ** -0.5),
        'od_k_k': 0.85 + nrm((NO, D), 0.02),
        'od_k_a': 1.0 + nrm((NO, D), 0.02),
        'od_r_k': -0.04 + nrm((NO, R_HEADS, R_HEAD_DIM), 0.02),
        'od_ln_w': gain((NO, D)),
        'od_ln_b': nrm((NO, D), 0.02),
        'od_w_o': nrm((NO, D, D), D ** -0.5),
    }


def reference(x, mem, norm_mix, norm_xattn, norm_mem, norm_ffn, norm_final,
              xa_wq, xa_wkv, xa_wo, ffn_w_gate, ffn_w_up, ffn_w_down,
              ev_w_in, ev_conv_qk, ev_m_ig_bias, ev_m_fg_bias, ev_m_norm,
              ev_g_decay_w2, ev_g_decay_b, ev_g_norm, ev_w_out,
              od_mu, od_w_rkv, od_w0, od_w1, od_w2, od_a0, od_a1, od_a2,
              od_g1, od_g2, od_k_k, od_k_a, od_r_k, od_ln_w, od_ln_b, od_w_o):
    for layer in range(DEPTH):
        h = rms_norm(x, norm_mix[layer])
        if layer % 2 == 0:
            e = layer // 2
            mix = even_mixer(h, ev_w_in[e], ev_conv_qk[e], ev_m_ig_bias[e], ev_m_fg_bias[e], ev_m_norm[e],
                             ev_g_decay_w2[e], ev_g_decay_b[e], ev_g_norm[e], ev_w_out[e])
        else:
            o = layer // 2
            mix = rwkv7_mixer(h, od_mu[o], od_w_rkv[o], od_w0[o], od_w1[o], od_w2[o], od_a0[o], od_a1[o], od_a2[o],
                              od_g1[o], od_g2[o], od_k_k[o], od_k_a[o], od_r_k[o], od_ln_w[o], od_ln_b[o], od_w_o[o])
        x = x + mix
        x = x + memory_cross_attention(rms_norm(x, norm_xattn[layer]), rms_norm(mem, norm_mem[layer]),
                                       xa_wq[layer], xa_wkv[layer], xa_wo[layer])
        x = x + swiglu_ffn(rms_norm(x, norm_ffn[layer]), ffn_w_gate[layer], ffn_w_up[layer], ffn_w_down[layer])
    return rms_norm(x, norm_final)
```

```python
import numpy as np
import concourse.bass as bass
import concourse.mybir as mybir
from concourse.bass_utils import run_bass_kernel_spmd

F32 = mybir.dt.float32
BF16 = mybir.dt.bfloat16
AF = mybir.ActivationFunctionType
ALU = mybir.AluOpType
AX = mybir.AxisListType

ENGS = ("pe", "act", "dve", "pool", "sp")
SEM_WRAP = 30000


class Buf:
    __slots__ = ("name", "ap", "w", "r")

    def __init__(self, name, ap=None):
        self.name = name
        self.ap = ap
        self.w = None
        self.r = {}


class Ins:
    __slots__ = ("eng", "fn", "deps", "sig", "sigval", "dma", "dsem", "dval", "prev_dma", "idx")

    def __init__(self, eng, fn, deps, dma=False):
        self.eng = eng
        self.fn = fn
        self.deps = deps
        self.sig = False
        self.sigval = None
        self.dma = dma
        self.dsem = None
        self.dval = None
        self.prev_dma = None


class Ring:
    def __init__(self, P, n, shape, dt, psum=False):
        self.slots = []
        for _ in range(n):
            t = P.ps(shape, dt) if psum else P.sb(shape, dt)
            self.slots.append((t, Buf("ring")))
        self.i = 0

    def next(self):
        s = self.slots[self.i % len(self.slots)]
        self.i += 1
        return s


class Prog:
    def __init__(self, nc, n_dma_sems=16, same_engine_sync=True):
        self.nc = nc
        self.streams = {e: [] for e in ENGS}
        self.n_dma_sems = n_dma_sems
        self.same_engine_sync = same_engine_sync
        self.dma_rr = {e: 0 for e in ENGS}
        self.dma_last = {}
        self.stack = []
        self.nbuf = 0
        self.extra = {e: [] for e in ENGS}

    def mark(self):
        return len(self.stack)

    def release(self, mark):
        deps = []
        for e in ENGS:
            for ins in reversed(self.streams[e]):
                if not ins.dma:
                    deps.append(ins)
                    break
        deps.extend(self.dma_last.values())
        for e in ENGS:
            self.extra[e] = list(deps)
        while len(self.stack) > mark:
            self.stack.pop().__exit__(None, None, None)

    def sb(self, shape, dt, name=None):
        self.nbuf += 1
        g = self.nc.sbuf_tensor(name or f"sb{self.nbuf}", list(shape), dt)
        t = g.__enter__()
        self.stack.append(g)
        return t

    def ps(self, shape, dt=F32, name=None):
        self.nbuf += 1
        g = self.nc.psum_tensor(name or f"ps{self.nbuf}", list(shape), dt)
        t = g.__enter__()
        self.stack.append(g)
        return t

    def buf(self, name="b"):
        return Buf(name)

    def ring(self, n, shape, dt, psum=False):
        return Ring(self, n, shape, dt, psum)

    def _deps(self, eng, reads, writes):
        deps = []
        for b in reads:
            if b.w is not None:
                deps.append(b.w)
        for b in writes:
            if b.w is not None:
                deps.append(b.w)
            deps.extend(b.r.values())
        if self.extra[eng]:
            deps.extend(self.extra[eng])
            self.extra[eng] = []
        return deps

    def op(self, eng, fn, reads=(), writes=()):
        deps = self._deps(eng, reads, writes)
        ins = Ins(eng, fn, deps)
        for b in writes:
            b.w = ins
            b.r = {}
        for b in reads:
            b.r[eng] = ins
        self.streams[eng].append(ins)
        return ins

    def dma(self, eng, out, in_, reads=(), writes=(), **kw):
        deps = self._deps(eng, reads, writes)

        def fn(e, out=out, in_=in_, kw=kw):
            return e.dma_start(out=out, in_=in_, **kw)

        ins = Ins(eng, fn, deps, dma=True)
        slot = (eng, self.dma_rr[eng] % self.n_dma_sems)
        self.dma_rr[eng] += 1
        ins.dsem = slot
        prev = self.dma_last.get(slot)
        ins.prev_dma = prev
        ins.dval = (prev.dval if prev is not None else 0) + 16
        self.dma_last[slot] = ins
        for b in writes:
            b.w = ins
            b.r = {}
        for b in reads:
            b.r[("dma", id(ins))] = ins
        self.streams[eng].append(ins)
        return ins

    def emit(self, final_waits=()):
        nc = self.nc
        for e in ENGS:
            for ins in self.streams[e]:
                for d in ins.deps:
                    if d.dma:
                        continue
                    if d.eng == "pe" and ins.eng == "pe":
                        continue
                    if d.eng == ins.eng and not self.same_engine_sync:
                        continue
                    d.sig = True
        for d in final_waits:
            if not d.dma:
                d.sig = True
        nsig = {}
        for e in ENGS:
            n = 0
            for ins in self.streams[e]:
                if ins.sig:
                    ins.sigval = (n // SEM_WRAP, n % SEM_WRAP + 1)
                    n += 1
            nsig[e] = n
        sems = {}
        guards = []

        def getsem(key):
            if key not in sems:
                g = nc.semaphore("s_" + "_".join(str(k) for k in key))
                sems[key] = g.__enter__()
                guards.append(g)
            return sems[key]

        for e in ENGS:
            for k in range((nsig[e] + SEM_WRAP - 1) // SEM_WRAP):
                getsem(("c", e, k))
        for slot in self.dma_last:
            getsem(("d",) + slot)

        engobj = {"pe": "tensor", "act": "scalar", "dve": "vector", "pool": "gpsimd", "sp": "sync"}
        streams = self.streams

        def run(e, eng):
            waited = {}
            for ins in streams[e]:
                need = {}
                for d in ins.deps:
                    if d.dma:
                        key = ("d",) + d.dsem
                        val = d.dval
                    else:
                        if d.eng == "pe" and e == "pe":
                            continue
                        if d.eng == e and not self.same_engine_sync:
                            continue
                        key = ("c", d.eng, d.sigval[0])
                        val = d.sigval[1]
                    if need.get(key, 0) < val:
                        need[key] = val
                if ins.dma and ins.prev_dma is not None:
                    key = ("d",) + ins.dsem
                    if need.get(key, 0) < ins.prev_dma.dval:
                        need[key] = ins.prev_dma.dval
                for key, val in need.items():
                    if waited.get(key, 0) < val:
                        eng.wait_ge(sems[key], val)
                        waited[key] = val
                bi = ins.fn(eng)
                if ins.dma:
                    bi.then_inc(sems[("d",) + ins.dsem], 16)
                elif ins.sig:
                    bi.then_inc(sems[("c", e, ins.sigval[0])], 1)
            if e == "sp":
                for d in final_waits:
                    if d.dma:
                        eng.wait_ge(sems[("d",) + d.dsem], d.dval)
                    else:
                        eng.wait_ge(sems[("c", d.eng, d.sigval[0])], d.sigval[1])

        with nc.Block() as block:
            @block.tensor
            def _(eng):
                run("pe", eng)

            @block.scalar
            def _(eng):
                run("act", eng)

            @block.vector
            def _(eng):
                run("dve", eng)

            @block.gpsimd
            def _(eng):
                run("pool", eng)

            @block.sync
            def _(eng):
                run("sp", eng)
        for g in reversed(guards):
            g.__exit__(None, None, None)

    def close(self):
        for g in reversed(self.stack):
            g.__exit__(None, None, None)
        self.stack = []


class Tile:
    __slots__ = ("t", "b")

    def __init__(self, t):
        self.t = t
        self.b = Buf("t")

    def __getitem__(self, k):
        return self.t[k]


def _tok(xs):
    out = []
    for x in xs:
        if x is None:
            continue
        out.append(x.b if isinstance(x, Tile) else x)
    return out


class K:
    def __init__(self, P):
        self.P = P

    def tile(self, shape, dt, psum=False):
        return Tile(self.P.ps(shape, dt) if psum else self.P.sb(shape, dt))

    def ring(self, n, shape, dt, psum=False):
        return TRing([self.tile(shape, dt, psum) for _ in range(n)])

    def dma(self, q, out, in_, r=(), w=(), **kw):
        return self.P.dma(q, out, in_, reads=_tok(r), writes=_tok(w), **kw)

    def act(self, out, in_, func, r=(), w=(), eng="act", **kw):
        return self.P.op(eng, lambda e: e.activation(out=out, in_=in_, func=func, **kw), _tok(r), _tok(w))

    def ts(self, out, in0, s1, s2, op0, op1=None, r=(), w=(), eng="dve", **kw):
        if op1 is None:
            return self.P.op(eng, lambda e: e.tensor_scalar(out=out, in0=in0, scalar1=s1, scalar2=None, op0=op0, **kw), _tok(r), _tok(w))
        return self.P.op(eng, lambda e: e.tensor_scalar(out=out, in0=in0, scalar1=s1, scalar2=s2, op0=op0, op1=op1, **kw), _tok(r), _tok(w))

    def tt(self, out, in0, in1, op, r=(), w=(), eng="dve"):
        return self.P.op(eng, lambda e: e.tensor_tensor(out=out, in0=in0, in1=in1, op=op), _tok(r), _tok(w))

    def stt(self, out, in0, scalar, in1, op0, op1, r=(), w=()):
        return self.P.op("dve", lambda e: e.scalar_tensor_tensor(out=out, in0=in0, scalar=scalar, in1=in1, op0=op0, op1=op1), _tok(r), _tok(w))

    def copy(self, out, in_, r=(), w=(), eng="dve"):
        if eng == "act":
            return self.P.op("act", lambda e: e.copy(out=out, in_=in_), _tok(r), _tok(w))
        return self.P.op(eng, lambda e: e.tensor_copy(out=out, in_=in_), _tok(r), _tok(w))

    def recip(self, out, in_, r=(), w=()):
        return self.P.op("dve", lambda e: e.reciprocal(out=out, in_=in_), _tok(r), _tok(w))

    def memset(self, out, val, w=(), eng="pool"):
        return self.P.op(eng, lambda e: e.memset(out, val), (), _tok(w))

    def reduce(self, out, in_, op, r=(), w=()):
        return self.P.op("dve", lambda e: e.tensor_reduce(out=out, in_=in_, axis=AX.X, op=op), _tok(r), _tok(w))

    def mm(self, out, lhsT, rhs, start=True, stop=True, r=(), w=()):
        return self.P.op("pe", lambda e: e.matmul(out, lhsT=lhsT, rhs=rhs, start=start, stop=stop), _tok(r), _tok(w))

    def tr(self, out, in_, ident, r=(), w=()):
        return self.P.op("pe", lambda e: e.transpose(out=out, in_=in_, identity=ident), _tok(r), _tok(w))


class TRing:
    def __init__(self, tiles):
        self.tiles = tiles
        self.i = 0

    def next(self):
        t = self.tiles[self.i % len(self.tiles)]
        self.i += 1
        return t


D = 1024
NMEM = 256
DFF = 2816
EPS = 1e-6


def build_consts():
    i = np.arange(128)
    s, t = i[:, None], i[None, :]
    c = {}
    f = lambda m: np.asarray(m, np.float32)
    c["ident"] = f(s == t)
    c["MU"] = f(s <= t)
    c["ML"] = f(s >= t)
    c["ONES"] = np.ones((128, 128), np.float32)
    c["NU"] = -f(s <= t)
    c["NL"] = -f(s >= t)
    c["NONES"] = -np.ones((128, 128), np.float32)
    c["BLK"] = f((s // 64) == (t // 64))
    c["BLKM"] = f((s // 64) == (t // 64)) / 64.0
    s6, t6 = s % 64, t % 64
    c["SF"] = f(s6 < t6)
    c["SB"] = f(s6 > t6)
    c["IF"] = f(s6 <= t6)
    c["IB"] = f(s6 >= t6)
    c["UN"] = -f(s <= t) / 16.0
    c["LN"] = -f(s >= t) / 16.0
    c["UC"] = -f(s > t) / 16.0
    c["LC"] = -f(s < t) / 16.0
    names = list(c)
    arr = np.concatenate([np.asarray(c[k], np.float32) for k in names], axis=1)
    offs = {k: j * 128 for j, k in enumerate(names)}
    return arr, offs


class Ctx:
    def __init__(self, nc, P, T, nseq, consts_ap):
        self.nc = nc
        self.P = P
        self.k = K(P)
        self.T = T
        self.nseq = nseq
        self.n = T * nseq
        k = self.k
        arr, offs = build_consts()
        self.coffs = offs
        self.C = k.tile([128, arr.shape[1]], F32)
        k.dma("sp", self.C[:], consts_ap, w=[self.C])
        self.identb = k.tile([128, 128], BF16)
        k.copy(self.identb[:], self.cst("ident"), r=[self.C], w=[self.identb])
        self.banks = k.ring(8, [128, 512], F32, psum=True)
        self.dram_tok = {}

    def cst(self, name):
        o = self.coffs[name]
        return self.C[:, o:o + 128]

    def bank(self):
        return self.banks.next()

    def dtok(self, key):
        if key not in self.dram_tok:
            self.dram_tok[key] = Buf(str(key))
        return self.dram_tok[key]


def load_gain_fm(ctx, g_ap, nchunk=8):
    k = ctx.k
    g = k.tile([128, nchunk], F32)
    k.dma("sp", g[:], g_ap.rearrange("(c p) -> p c", p=128), w=[g], allow_slow_non_contiguous=True)
    return g


def load_w(ctx, w_ap, gain_fm=None, dst=None, col0=0):
    k = ctx.k
    Kd, F = w_ap.shape
    nch = Kd // 128
    if dst is None:
        dst = k.tile([128, nch, F], BF16)
    for c in range(nch):
        k.dma("pool", dst[:, c, col0:col0 + F], w_ap[c * 128:(c + 1) * 128, :], w=[dst])
    if gain_fm is not None:
        for c in range(nch):
            k.ts(dst[:, c, col0:col0 + F], dst[:, c, col0:col0 + F], gain_fm[:, c:c + 1], None, ALU.mult,
                 r=[dst, gain_fm], w=[dst])
    return dst


class NormT:
    def __init__(self, ctx, with_xn=True):
        k = ctx.k
        self.ctx = ctx
        self.junk = k.ring(2, [128, D], BF16)
        self.ss = k.ring(4, [128, 1], F32)
        self.rs = k.ring(4, [128, 1], F32)
        if with_xn:
            self.xn = k.ring(3, [128, D], BF16)
        self.flip = 0

    def rstd(self, xt_ap, xt_tile, width=D, rows=128):
        k = self.ctx.k
        junk = self.junk.next()
        ss = self.ss.next()
        rs = self.rs.next()
        R = slice(0, rows)
        k.act(junk[R, 0:width], xt_ap, AF.Square, r=[xt_tile], w=[junk, ss], accum_out=ss[R, :])
        k.ts(rs[R, :], ss[R, :], 1.0 / width, EPS, ALU.mult, ALU.add, r=[ss], w=[rs])
        k.act(rs[R, :], rs[R, :], AF.Ln, r=[rs], w=[rs])
        k.act(rs[R, :], rs[R, :], AF.Exp, r=[rs], w=[rs], scale=-0.5)
        return rs

    def norm(self, xt):
        k = self.ctx.k
        rs = self.rstd(xt[:], xt)
        xn = self.xn.next()
        k.ts(xn[:], xt[:], rs[:], None, ALU.mult, r=[xt, rs], w=[xn])
        return xn

    def to_fm(self, xn, dst_ap, dst_tok, nchunk=8):
        ctx = self.ctx
        k = ctx.k
        bank = ctx.bank()
        pb = bank.t[:].bitcast(BF16)
        for c in range(nchunk):
            k.tr(pb[:, c * 128:(c + 1) * 128], xn[:, c * 128:(c + 1) * 128], ctx.identb[:], r=[xn, ctx.identb], w=[bank])
        src = pb[:, 0:nchunk * 128].rearrange("p (c t) -> p c t", c=nchunk)
        self.flip ^= 1
        k.copy(dst_ap, src, r=[bank], w=[dst_tok], eng="act" if self.flip else "dve")


def phase_ffn(ctx, x_in, x_out, wg, wu, wd, gain_ap, tok_in, tok_out, final_gain_ap=None):
    k = ctx.k
    n = ctx.n
    TB = 256
    NF = DFF // 128
    mark = ctx.P.mark()
    g_fm = load_gain_fm(ctx, gain_ap)
    Wgu = k.tile([128, 8, 2 * DFF], BF16)
    load_w(ctx, wg, None, dst=Wgu, col0=0)
    load_w(ctx, wu, None, dst=Wgu, col0=DFF)
    for c in range(8):
        k.ts(Wgu[:, c, :], Wgu[:, c, :], g_fm[:, c:c + 1], None, ALU.mult, r=[Wgu, g_fm], w=[Wgu])
    Wd = load_w(ctx, wd)
    nt = NormT(ctx)
    xts = k.ring(3, [128, D], F32)
    hTs = k.ring(2, [128, 8, TB], BF16)
    actT = k.tile([128, NF, TB], BF16)
    act_parts = [Buf("a") for _ in range(NF)]
    sgs = k.ring(2, [128, TB], F32)
    xrs = k.ring(2, [128, D], F32)
    xos = k.ring(2, [128, D], F32)
    if final_gain_ap is not None:
        gbc = k.tile([128, D], F32)
        k.dma("sp", gbc[:], final_gain_ap.partition_broadcast(128), w=[gbc])
    outs = []
    for blk in range(n // TB):
        hT = hTs.next()
        for j in range(TB // 128):
            t0 = blk * TB + j * 128
            xt = xts.next()
            k.dma("sp", xt[:], x_in[t0:t0 + 128, :], r=[tok_in], w=[xt])
            xn = nt.norm(xt)
            nt.to_fm(xn, hT[:, :, j * 128:(j + 1) * 128], hT)
        for f in range(NF):
            pg = ctx.bank()
            pu = ctx.bank()
            for c in range(8):
                k.mm(pg[:, 0:TB], Wgu[:, c, f * 128:(f + 1) * 128], hT[:, c, :], c == 0, c == 7, r=[Wgu, hT], w=[pg])
            for c in range(8):
                k.mm(pu[:, 0:TB], Wgu[:, c, DFF + f * 128:DFF + (f + 1) * 128], hT[:, c, :], c == 0, c == 7, r=[Wgu, hT], w=[pu])
            sg = sgs.next()
            k.act(sg[:], pg[:, 0:TB], AF.Silu, r=[pg], w=[sg])
            k.tt(actT[:, f, :], sg[:], pu[:, 0:TB], ALU.mult, r=[sg, pu], w=[act_parts[f]])
        for j in range(TB // 128):
            t0 = blk * TB + j * 128
            xr = xrs.next()
            k.dma("sp", xr[:], x_in[t0:t0 + 128, :], r=[tok_in], w=[xr])
            xo = xos.next()
            for half in range(2):
                po = ctx.bank()
                for f in range(NF):
                    k.mm(po[:], actT[:, f, j * 128:(j + 1) * 128], Wd[:, f, half * 512:(half + 1) * 512], f == 0, f == NF - 1,
                         r=[act_parts[f], Wd], w=[po])
                k.tt(xo[:, half * 512:(half + 1) * 512], po[:], xr[:, half * 512:(half + 1) * 512], ALU.add, r=[po, xr], w=[xo])
            if final_gain_ap is not None:
                rs = nt.rstd(xo[:], xo)
                k.stt(xo[:], xo[:], rs[:], gbc[:], ALU.mult, ALU.mult, r=[xo, rs, gbc], w=[xo])
            outs.append(k.dma("sp", x_out[t0:t0 + 128, :], xo[:], r=[xo], w=[tok_out]))
    ctx.P.release(mark)
    return outs


def phase_xattn(ctx, x_in, x_out, mem, wq, wkv, wo, g_x_ap, g_mem_ap, tok_in, tok_out):
    k = ctx.k
    n, T = ctx.n, ctx.T
    TB = 512
    mark = ctx.P.mark()
    gx = load_gain_fm(ctx, g_x_ap)
    gm = load_gain_fm(ctx, g_mem_ap)
    Wkv = load_w(ctx, wkv, gm)
    Wq = load_w(ctx, wq, gx)
    Wo = load_w(ctx, wo)
    nt = NormT(ctx)
    xts = k.ring(3, [128, D], F32)
    KT = k.tile([128, ctx.nseq, 8, NMEM], BF16)
    V = k.tile([128, ctx.nseq, 2, D], BF16)
    memT = k.tile([128, 8, NMEM], BF16)
    flip = 0
    for s in range(ctx.nseq):
        for j in range(2):
            xt = xts.next()
            k.dma("sp", xt[:], mem[s * NMEM + j * 128:s * NMEM + (j + 1) * 128, :], w=[xt])
            xn = nt.norm(xt)
            nt.to_fm(xn, memT[:, :, j * 128:(j + 1) * 128], memT)
        for f in range(8):
            b = ctx.bank()
            for c in range(8):
                k.mm(b[:, 0:NMEM], Wkv[:, c, f * 128:(f + 1) * 128], memT[:, c, :], c == 0, c == 7, r=[Wkv, memT], w=[b])
            flip ^= 1
            k.copy(KT[:, s, f, :], b[:, 0:NMEM], r=[b], w=[KT], eng="act" if flip else "dve")
        for j in range(2):
            for half in range(2):
                b = ctx.bank()
                for c in range(8):
                    k.mm(b[:], memT[:, c, j * 128:(j + 1) * 128], Wkv[:, c, D + half * 512:D + (half + 1) * 512], c == 0, c == 7,
                         r=[Wkv, memT], w=[b])
                flip ^= 1
                k.copy(V[:, s, j, half * 512:(half + 1) * 512], b[:], r=[b], w=[V], eng="act" if flip else "dve")
    hTs = k.ring(2, [128, 8, TB], BF16)
    qT = k.tile([128, 8, TB], BF16)
    qparts = [Buf("q") for _ in range(8)]
    pT = k.tile([128, 8, TB], BF16)
    pparts = [Buf("p") for _ in range(TB // 128)]
    oT = k.tile([128, 8, TB], BF16)
    oparts = [Buf("o") for _ in range(8)]
    mxs = k.ring(2, [128, 4], F32)
    nmxs = k.ring(2, [128, 4], F32)
    rsums = k.ring(2, [128, 4], F32)
    rinvs = k.ring(2, [128, 4], F32)
    ps_ = k.ring(2, [128, 4, NMEM], BF16)
    pns = k.ring(2, [128, 4, NMEM], BF16)
    xrs = k.ring(2, [128, D], F32)
    xos = k.ring(2, [128, D], F32)
    outs = []
    for blk in range(n // TB):
        s = (blk * TB) // T
        hT = hTs.next()
        for j in range(TB // 128):
            t0 = blk * TB + j * 128
            xt = xts.next()
            k.dma("sp", xt[:], x_in[t0:t0 + 128, :], r=[tok_in], w=[xt])
            xn = nt.norm(xt)
            nt.to_fm(xn, hT[:, :, j * 128:(j + 1) * 128], hT)
        for f in range(8):
            b = ctx.bank()
            for c in range(8):
                k.mm(b[:], Wq[:, c, f * 128:(f + 1) * 128], hT[:, c, :], c == 0, c == 7, r=[Wq, hT], w=[b])
            flip ^= 1
            k.copy(qT[:, f, :], b[:], r=[b], w=[qparts[f]], eng="act" if flip else "dve")
        for j in range(TB // 128):
            cols = slice(j * 128, (j + 1) * 128)
            b2 = [ctx.bank(), ctx.bank()]
            for h in range(4):
                b = b2[h // 2]
                for e in range(2):
                    k.mm(b[:, (h % 2) * 256:(h % 2 + 1) * 256], qT[:, 2 * h + e, cols], KT[:, s, 2 * h + e, :], e == 0, e == 1,
                         r=[qparts[2 * h + e], KT], w=[b])
            mx = mxs.next()
            for i in range(2):
                k.reduce(mx[:, 2 * i:2 * i + 2], b2[i][:].rearrange("p (h m) -> p h m", h=2), ALU.max, r=[b2[i]], w=[mx])
            nmx = nmxs.next()
            k.ts(nmx[:], mx[:], -1.0 / 16.0, None, ALU.mult, r=[mx], w=[nmx])
            p = ps_.next()
            rsum = rsums.next()
            for h in range(4):
                k.act(p[:, h, :], b2[h // 2][:, (h % 2) * 256:(h % 2 + 1) * 256], AF.Exp, r=[b2[h // 2], nmx], w=[p, rsum],
                      bias=nmx[:, h:h + 1], scale=1.0 / 16.0, accum_out=rsum[:, h:h + 1])
            rinv = rinvs.next()
            k.recip(rinv[:], rsum[:], r=[rsum], w=[rinv])
            pn = pns.next()
            k.tt(pn[:], p[:], rinv[:].unsqueeze(2).to_broadcast([128, 4, NMEM]), ALU.mult, r=[p, rinv], w=[pn])
            bank = ctx.bank()
            pb = bank.t[:].bitcast(BF16)
            for h in range(4):
                for e in range(2):
                    i = 2 * h + e
                    k.tr(pb[:, i * 128:(i + 1) * 128], pn[:, h, e * 128:(e + 1) * 128], ctx.identb[:], r=[pn, ctx.identb], w=[bank])
            flip ^= 1
            k.copy(pT[:, :, cols], pb.rearrange("p (c t) -> p c t", c=8), r=[bank], w=[pparts[j]], eng="act" if flip else "dve")
        for f in range(8):
            h = f // 2
            b = ctx.bank()
            for jm in range(2):
                k.mm(b[:], V[:, s, jm, f * 128:(f + 1) * 128], pT[:, 2 * h + jm, :], jm == 0, jm == 1, r=[V] + pparts, w=[b])
            flip ^= 1
            k.copy(oT[:, f, :], b[:], r=[b], w=[oparts[f]], eng="act" if flip else "dve")
        for j in range(TB // 128):
            t0 = blk * TB + j * 128
            cols = slice(j * 128, (j + 1) * 128)
            xr = xrs.next()
            k.dma("sp", xr[:], x_in[t0:t0 + 128, :], r=[tok_in], w=[xr])
            xo = xos.next()
            for half in range(2):
                po = ctx.bank()
                for c in range(8):
                    k.mm(po[:], oT[:, c, cols], Wo[:, c, half * 512:(half + 1) * 512], c == 0, c == 7, r=[oparts[c], Wo], w=[po])
                k.tt(xo[:, half * 512:(half + 1) * 512], po[:], xr[:, half * 512:(half + 1) * 512], ALU.add, r=[po, xr], w=[xo])
            outs.append(k.dma("sp", x_out[t0:t0 + 128, :], xo[:], r=[xo], w=[tok_out]))
    ctx.P.release(mark)
    return outs


FM_ROWS = 1568
TOKW = 2320


def phase_a0(ctx, x_in, w_in, gain_ap, S_fm, S_tok, tok_in, tok_fm, tok_tok):
    k = ctx.k
    n = ctx.n
    TB = 512
    mark = ctx.P.mark()
    g_fm = load_gain_fm(ctx, gain_ap)
    Win = load_w(ctx, w_in, g_fm)
    nt = NormT(ctx)
    xts = k.ring(3, [128, D], F32)
    hTs = k.ring(2, [128, 8, TB], BF16)
    fos = k.ring(3, [128, TB], F32)
    tks = k.ring(2, [128, TOKW], F32)
    fm_cols = [(j * 128, 128) for j in range(8)] + [(2064 + j * 128, 128) for j in range(4)] + [(3600, 32)]
    tok_groups = [(1024, 512, 0), (1536, 512, 512), (2048, 16, 1024), (2320, 256, 1040), (2576, 512, 1296), (3088, 512, 1808)]
    flip = 0
    for blk in range(n // TB):
        hT = hTs.next()
        for j in range(TB // 128):
            t0 = blk * TB + j * 128
            xt = xts.next()
            k.dma("sp", xt[:], x_in[t0:t0 + 128, :], r=[tok_in], w=[xt])
            xn = nt.norm(xt)
            nt.to_fm(xn, hT[:, :, j * 128:(j + 1) * 128], hT)
        for i, (c0, m) in enumerate(fm_cols):
            b = ctx.bank()
            for c in range(8):
                k.mm(b[0:m, :], Win[:, c, c0:c0 + m], hT[:, c, :], c == 0, c == 7, r=[Win, hT], w=[b])
            fo = fos.next()
            flip ^= 1
            k.copy(fo[0:m, :], b[0:m, :], r=[b], w=[fo], eng="act" if flip else "dve")
            r0 = i * 128
            k.dma("sp", S_fm[r0:r0 + m, blk * TB:(blk + 1) * TB], fo[0:m, :], r=[fo], w=[tok_fm])
        for j in range(TB // 128):
            t0 = blk * TB + j * 128
            tk = tks.next()
            for (c0, w, o0) in tok_groups:
                b = ctx.bank()
                for c in range(8):
                    k.mm(b[:, 0:w], hT[:, c, j * 128:(j + 1) * 128], Win[:, c, c0:c0 + w], c == 0, c == 7, r=[Win, hT], w=[b])
                flip ^= 1
                k.copy(tk[:, o0:o0 + w], b[:, 0:w], r=[b], w=[tk], eng="act" if flip else "dve")
            k.dma("sp", S_tok[t0:t0 + 128, :], tk[:], r=[tk], w=[tok_tok])
    ctx.P.release(mark)


class MixL0:
    def __init__(self, ctx, S_fm, S_tok, S_cb, S_sb, conv_ap, igb_ap, fgb_ap, mnorm_ap, w2_ap, db_ap, gnorm_ap, wout_ap,
                 tok_fm, tok_tok):
        self.ctx = ctx
        k = self.k = ctx.k
        self.S_fm, self.S_tok, self.S_cb, self.S_sb = S_fm, S_tok, S_cb, S_sb
        self.tok_fm, self.tok_tok = tok_fm, tok_tok
        self.tok_cb = Buf("cb")
        self.cw = k.tile([128, 3, 8], F32)
        for j in range(3):
            k.dma("sp", self.cw[:, j, :], conv_ap[j].rearrange("(c p) -> p c", p=128), w=[self.cw], allow_slow_non_contiguous=True)
        k.ts(self.cw[:], self.cw[:], 0.5, None, ALU.mult, r=[self.cw], w=[self.cw])
        self.gb = k.tile([128, 16], F32)
        k.dma("sp", self.gb[:, 0:8], igb_ap.rearrange("a b -> (a b)").partition_broadcast(128), w=[self.gb])
        k.dma("sp", self.gb[:, 8:16], fgb_ap.rearrange("a b -> (a b)").partition_broadcast(128), w=[self.gb])
        self.mnorm = k.tile([128, 512], F32)
        k.dma("sp", self.mnorm[:], mnorm_ap.partition_broadcast(128), w=[self.mnorm])
        self.gnorm = k.tile([128, 512], F32)
        k.dma("sp", self.gnorm[:], gnorm_ap.partition_broadcast(128), w=[self.gnorm])
        k.ts(self.mnorm[:], self.mnorm[:], 0.5, None, ALU.mult, r=[self.mnorm], w=[self.mnorm])
        k.ts(self.gnorm[:], self.gnorm[:], 0.5, None, ALU.mult, r=[self.gnorm], w=[self.gnorm])
        self.dbias = k.tile([128, 512], F32)
        k.dma("sp", self.dbias[:], db_ap.rearrange("a b -> (a b)").partition_broadcast(128), w=[self.dbias])
        self.w2p = k.tile([32, 2, 256], F32)
        k.memset(self.w2p[:], 0.0, w=[self.w2p])
        k.dma("sp", self.w2p[0:16, 0, :], w2_ap[0], w=[self.w2p])
        k.dma("sp", self.w2p[16:32, 1, :], w2_ap[1], w=[self.w2p])
        self.Wout = load_w(ctx, wout_ap)
        r = k.ring
        self.Xs = r(2, [128, 8, 130], F32)
        self.z1s = r(2, [128, 8, 128], F32)
        self.z2s = r(2, [128, 8, 128], F32)
        self.QKs = r(2, [128, 8, 128], BF16)
        self.TKs = r(2, [128, TOKW], F32)
        self.vps = r(2, [128, 4, 129], BF16)
        for t in self.vps.tiles:
            k.memset(t[:, :, 128:129], 1.0, w=[t])
        self.g8 = [r(2, [128, 8], F32) for _ in range(8)]
        self.glrs = r(2, [32, 128], F32)
        self.gqks = r(2, [128, 4, 128], F32)
        self.w512 = [r(2, [128, 512], F32) for _ in range(8)]
        self.khats = r(2, [128, 2, 256], BF16)
        self.ktz = r(2, [128, 8, 128], BF16)
        self.qtz = r(2, [128, 8, 128], BF16)
        self.thBs = r(2, [128, 512], F32)
        self.qts = r(2, [128, 512], BF16)
        self.kts = r(2, [128, 512], BF16)
        self.gvbs = r(2, [128, 512], BF16)
        self.Cfb = r(2, [128, 4, 129], BF16)
        self.Cbb = r(2, [128, 4, 129], BF16)
        self.Sfb = r(2, [128, 2, 128], BF16)
        self.Sbb = r(2, [128, 2, 128], BF16)
        for rr_ in (self.ktz, self.qtz):
            for t in rr_.tiles:
                k.memset(t[:], 0.0, w=[t])
        self.kToks = r(2, [128, 4, 128], BF16)
        self.CF = r(2, [128, 4, 129], F32)
        self.CB = r(3, [128, 4, 129], F32)
        self.SF = r(2, [128, 2, 128], F32)
        self.SB = r(3, [128, 2, 128], F32)
        self.pFB = r(2, [128, 8, 128], BF16)
        self.pA = r(2, [128, 8, 128], BF16)
        self.hms = r(2, [128, 4, 128], F32)
        self.small = [r(2, [128, 8], F32) for _ in range(8)]
        self.junks = r(2, [128, 128], F32)
        self.merged = r(2, [128, D], BF16)
        self.mTs = r(2, [128, 8, 128], BF16)
        self.xos = r(2, [128, D], F32)
        self.nt = NormT(ctx, with_xn=False)

    def prep(self, s, c, d_state):
        ctx, k = self.ctx, self.k
        T = ctx.T
        nch = T // 128
        t0 = s * T + c * 128
        o = {}
        X = self.Xs.next()
        lo = 1 if c == 0 else 0
        hi = 129 if c == nch - 1 else 130
        if c == 0:
            k.memset(X[:, :, 0:1], 0.0, w=[X])
        if c == nch - 1:
            k.memset(X[:, :, 129:130], 0.0, w=[X])
        k.dma("sp", X[:, :, lo:hi], self.S_fm[0:1024, t0 - 1 + lo:t0 - 1 + hi].rearrange("(c p) t -> p c t", p=128),
              r=[self.tok_fm], w=[X])
        yield
        z1 = self.z1s.next()
        z2 = self.z2s.next()
        cwb = lambda j: self.cw[:, j, :].unsqueeze(2).to_broadcast([128, 8, 128])
        k.tt(z1[:], X[:, :, 0:128], cwb(0), ALU.mult, r=[X, self.cw], w=[z1])
        yield
        k.tt(z2[:], X[:, :, 1:129], cwb(1), ALU.mult, r=[X, self.cw], w=[z2])
        yield
        k.tt(z1[:], z1[:], z2[:], ALU.add, r=[z1, z2], w=[z1])
        yield
        k.tt(z2[:], X[:, :, 2:130], cwb(2), ALU.mult, r=[X, self.cw], w=[z2])
        yield
        k.tt(z1[:], z1[:], z2[:], ALU.add, r=[z1, z2], w=[z1])
        yield
        k.act(z2[:], z1[:], AF.Tanh, r=[z1], w=[z2])
        yield
        QK = self.QKs.next()
        k.stt(QK[:], z2[:], 1.0, z1[:], ALU.add, ALU.mult, r=[z1, z2], w=[QK])
        yield
        o["QK"] = QK
        TK = self.TKs.next()
        k.dma("sp", TK[:], self.S_tok[t0:t0 + 128, :], r=[self.tok_tok], w=[TK])
        o["TK"] = TK
        vp = self.vps.next()
        k.copy(vp[:, :, 0:128], TK[:, 0:512].rearrange("p (h d) -> p h d", h=4), r=[TK], w=[vp], eng="pool")
        o["vp"] = vp
        thA = self.w512[7].next()
        thB = self.thBs.next()
        k.act(thA[:], TK[:, 512:1024], AF.Tanh, r=[TK], w=[thA], scale=0.5)
        yield
        k.act(thB[:], TK[:, 1808:2320], AF.Tanh, r=[TK], w=[thB], scale=0.5)
        yield
        o["thA"], o["thB"] = thA, thB
        gvb = self.gvbs.next()
        k.copy(gvb[:], TK[:, 1296:1808], r=[TK], w=[gvb], eng="act")
        yield
        o["gvb"] = gvb
        g = [rr.next() for rr in self.g8]
        ig, zf, l1f, t1, sw, qe, eg, kw = g
        k.tt(ig[:], TK[:, 1024:1032], self.gb[:, 0:8], ALU.add, r=[TK, self.gb], w=[ig])
        k.tt(zf[:], TK[:, 1032:1040], self.gb[:, 8:16], ALU.add, r=[TK, self.gb], w=[zf])
        yield
        k.act(zf[:], zf[:], AF.Exp, r=[zf], w=[zf], scale=-1.0)
        yield
        k.act(l1f[:], zf[:], AF.Ln, r=[zf], w=[l1f], bias=1.0)
        yield
        Gb = ctx.bank()
        k.mm(Gb[:, 0:4], ctx.cst("NU"), l1f[:, 0:4], r=[ctx.C, l1f], w=[Gb])
        k.mm(Gb[:, 4:8], ctx.cst("NL"), l1f[:, 4:8], r=[ctx.C, l1f], w=[Gb])
        k.mm(Gb[:, 8:16], ctx.cst("NONES"), l1f[:, 0:8], r=[ctx.C, l1f], w=[Gb])
        k.tt(t1[:], ig[:], Gb[:, 0:8], ALU.subtract, r=[ig, Gb], w=[t1])
        k.act(qe[:], Gb[:, 0:8], AF.Exp, r=[Gb], w=[qe])
        k.act(eg[:], Gb[:, 8:16], AF.Exp, r=[Gb], w=[eg])
        yield
        k.act(sw[:], t1[:], AF.Exp, r=[t1], w=[sw])
        yield
        k.ts(qe[:], qe[:], 128.0 ** -0.5, None, ALU.mult, r=[qe], w=[qe])
        yield
        k.tt(kw[:], sw[:], eg[:], ALU.mult, r=[sw, eg], w=[kw])
        yield
        o.update(sw=sw, qe=qe, eg=eg, kw=kw)
        kTb = ctx.bank()
        kTpb = kTb.t[:].bitcast(BF16)
        for h in range(4):
            k.tr(kTpb[:, h * 128:(h + 1) * 128], QK[:, 4 + h, :], ctx.identb[:], r=[QK, ctx.identb], w=[kTb])
        kTok = self.kToks.next()
        k.tt(kTok[:], kTpb[:, 0:512].rearrange("p (h t) -> p h t", h=4),
             kw[:, d_state * 4:(d_state + 1) * 4].unsqueeze(2).to_broadcast([128, 4, 128]), ALU.mult, r=[kTb, kw], w=[kTok])
        yield
        o["kTok"] = kTok
        glr = self.glrs.next()
        k.dma("sp", glr[:], self.S_fm[1536:1568, t0:t0 + 128], r=[self.tok_fm], w=[glr])
        gqk = self.gqks.next()
        k.dma("sp", gqk[:], self.S_fm[1024:1536, t0:t0 + 128].rearrange("(c p) t -> p c t", p=128), r=[self.tok_fm], w=[gqk])
        w = [rr.next() for rr in self.w512[0:7]]
        zb, l1, eT, emT, _q, _k, egmb = w
        qtT, ktT = self.qts.next(), self.kts.next()
        zbk = ctx.bank()
        for d in range(2):
            k.mm(zbk[:, d * 256:(d + 1) * 256], glr[:], self.w2p[:, d, :], r=[glr, self.w2p], w=[zbk])
        k.tt(zb[:], zbk[:], self.dbias[:], ALU.add, r=[zbk, self.dbias], w=[zb])
        yield
        k.act(zb[:], zb[:], AF.Exp, r=[zb], w=[zb], scale=-1.0)
        yield
        k.act(l1[:], zb[:], AF.Ln, r=[zb], w=[l1], bias=1.0)
        yield
        bTb = ctx.bank()
        for d in range(2):
            for j in range(2):
                i = d * 2 + j
                k.mm(bTb[:, i * 128:(i + 1) * 128], l1[:, d * 256 + j * 128:d * 256 + (j + 1) * 128],
                     ctx.cst("UN" if d == 0 else "LN"), r=[l1, ctx.C], w=[bTb])
        k.act(eT[:], bTb[:], AF.Exp, r=[bTb], w=[eT])
        k.act(emT[:], bTb[:], AF.Exp, r=[bTb], w=[emT], scale=-1.0)
        yield
        v4 = lambda t: t[:].rearrange("p (a b) -> p a b", a=4)
        for d in range(2):
            k.stt(v4(qtT)[:, d * 2:(d + 1) * 2, :], gqk[:, 0:2, :], 0.125, v4(eT)[:, d * 2:(d + 1) * 2, :], ALU.mult, ALU.mult,
                  r=[gqk, eT], w=[qtT])
            yield
            k.tt(v4(ktT)[:, d * 2:(d + 1) * 2, :], gqk[:, 2:4, :], v4(emT)[:, d * 2:(d + 1) * 2, :], ALU.mult, r=[gqk, emT], w=[ktT])
            yield
        gmb = ctx.bank()
        k.mm(gmb[:, 0:256], ctx.cst("UC"), l1[:, 0:256], r=[ctx.C, l1], w=[gmb])
        k.mm(gmb[:, 256:512], ctx.cst("LC"), l1[:, 256:512], r=[ctx.C, l1], w=[gmb])
        k.act(egmb[:], gmb[:], AF.Exp, r=[gmb], w=[egmb])
        yield
        khat = self.khats.next()
        for d in range(2):
            k.tt(khat[:, d, :], TK[:, 1040:1296], egmb[:, d * 256:(d + 1) * 256], ALU.mult, r=[TK, egmb], w=[khat])
            yield
        o.update(eT=eT, qtT=qtT, ktT=ktT, khat=khat)
        return o

    def state_update(self, o, d, Cold, Sold, Cring, Sring):
        ctx, k = self.ctx, self.k
        TK, vp = o["TK"], o["vp"]
        kTok = o["kTok"]
        Cn = Cring.next()
        for p in range(2):
            b = ctx.bank()
            bv = b[:, 0:258].rearrange("p (h e) -> p h e", h=2)
            for hh in range(2):
                h = 2 * p + hh
                k.mm(bv[:, hh, :], kTok[:, h, :], vp[:, h, :], r=[kTok, vp], w=[b])
            for hh in range(2):
                h = 2 * p + hh
                k.stt(Cn[:, h, :], Cold[:, h, :], o["eg"][:, d * 4 + h:d * 4 + h + 1], bv[:, hh, :], ALU.mult, ALU.add,
                      r=[Cold, o["eg"], b], w=[Cn])
            yield
        Sn = Sring.next()
        eT4 = o["eT"][:].rearrange("p (a b) -> p a b", a=4)
        col = 127 if d == 0 else 0
        for j in range(2):
            b = ctx.bank()
            k.mm(b[:, 0:256], o["khat"][:, d, j * 128:(j + 1) * 128], o["gvb"][:, j * 256:(j + 1) * 256], r=[o["khat"], o["gvb"]], w=[b])
            for e in range(2):
                rows = slice(e * 64, (e + 1) * 64)
                k.stt(Sn[rows, j, :], Sold[rows, j, :], eT4[rows, d * 2 + j, col:col + 1], b[rows, e * 128:(e + 1) * 128],
                      ALU.mult, ALU.add, r=[Sold, o["eT"], b], w=[Sn])
            yield
        return Cn, Sn

    def pass1(self, s):
        ctx, k = self.ctx, self.k
        nch = ctx.T // 128
        Cb = self.CB.next()
        Sb = self.SB.next()
        k.memset(Cb[:], 0.0, w=[Cb])
        k.memset(Sb[:], 0.0, w=[Sb])
        for c in range(nch - 1, -1, -1):
            idx = s * nch + c
            k.dma("sp", self.S_cb[idx], Cb[:].rearrange("p h e -> p (h e)"), r=[Cb], w=[self.tok_cb])
            k.dma("sp", self.S_sb[idx], Sb[:].rearrange("p j v -> p (j v)"), r=[Sb], w=[self.tok_cb])
            yield
            if c == 0:
                break
            o = yield from self.prep(s, c, 1)
            Cb, Sb = yield from self.state_update(o, 1, Cb, Sb, self.CB, self.SB)

    def pass2(self, s, x_in, x_out, tok_in, tok_out, outs):
        ctx, k = self.ctx, self.k
        T = ctx.T
        nch = T // 128
        Cf = self.CF.next()
        Sf = self.SF.next()
        k.memset(Cf[:], 0.0, w=[Cf])
        k.memset(Sf[:], 0.0, w=[Sf])
        Cfb, Sfb = self.Cfb.next(), self.Sfb.next()
        k.memset(Cfb[:], 0.0, w=[Cfb])
        k.memset(Sfb[:], 0.0, w=[Sfb])
        MU, ML = ctx.cst("MU"), ctx.cst("ML")
        import os
        kp2 = int(os.environ.get("KP2", "9"))
        for c in range(nch):
            t0 = s * T + c * 128
            idx = s * nch + c
            o = yield from self.prep(s, c, 0)
            QK, TK, vp = o["QK"], o["TK"], o["vp"]
            Cb = self.CB.next()
            Sb = self.SB.next()
            k.dma("sp", Cb[:].rearrange("p h e -> p (h e)"), self.S_cb[idx], r=[self.tok_cb], w=[Cb])
            k.dma("sp", Sb[:].rearrange("p j v -> p (j v)"), self.S_sb[idx], r=[self.tok_cb], w=[Sb])
            Cbb, Sbb = self.Cbb.next(), self.Sbb.next()
            k.copy(Cbb[:], Cb[:], r=[Cb], w=[Cbb], eng="act")
            k.copy(Sbb[:], Sb[:], r=[Sb], w=[Sbb], eng="pool")
            yield
            sb_ = ctx.bank()
            for h in range(4):
                k.mm(sb_[:, h * 128:(h + 1) * 128], QK[:, 4 + h, :], QK[:, h, :], r=[QK], w=[sb_])
            pFB = self.pFB.next()
            for d in range(2):
                for h in range(4):
                    k.stt(pFB[:, d * 4 + h, :], sb_[:, h * 128:(h + 1) * 128], o["sw"][:, d * 4 + h:d * 4 + h + 1], MU if d == 0 else ML,
                          ALU.mult, ALU.mult, r=[sb_, o["sw"], ctx.C], w=[pFB])
            if kp2 <= 1:
                continue
            yield
            sm = [rr.next() for rr in self.small]
            d1, nd, d2, rr_, ss, rs, ss2, rs2 = sm
            nb = {}
            for p in range(2):
                for d in range(2):
                    b = ctx.bank()
                    bv = b[:, 0:258].rearrange("p (h e) -> p h e", h=2)
                    Cst = Cfb if d == 0 else Cbb
                    for hh in range(2):
                        h = 2 * p + hh
                        k.mm(bv[:, hh, :], pFB[:, d * 4 + h, :], vp[:, h, :], True, False, r=[pFB, vp], w=[b])
                        k.mm(bv[:, hh, :], QK[:, h, :], Cst[:, h, :], False, True, r=[QK, Cst], w=[b])
                    nb[(p, d)] = (b, bv)
                    k.tt(d1[:, d * 4 + 2 * p:d * 4 + 2 * p + 2], bv[:, :, 128], o["qe"][:, d * 4 + 2 * p:d * 4 + 2 * p + 2], ALU.mult,
                         r=[b, o["qe"]], w=[d1])
            k.ts(nd[:], d1[:], -1.0, None, ALU.mult, r=[d1], w=[nd])
            k.tt(d2[:], d1[:], nd[:], ALU.max, r=[d1, nd], w=[d2])
            k.ts(d2[:], d2[:], 1.0, None, ALU.max, r=[d2], w=[d2])
            k.recip(d2[:], d2[:], r=[d2], w=[d2])
            k.tt(rr_[:], d2[:], o["qe"][:], ALU.mult, r=[d2, o["qe"]], w=[rr_])
            hm = self.hms.next()
            for h in range(4):
                p, hh = h // 2, h % 2
                bF, bvF = nb[(p, 0)]
                bB, bvB = nb[(p, 1)]
                k.ts(hm[:, h, :], bvF[:, hh, 0:128], rr_[:, h:h + 1], None, ALU.mult, r=[bF, rr_], w=[hm])
                k.stt(hm[:, h, :], bvB[:, hh, 0:128], rr_[:, 4 + h:5 + h], hm[:, h, :], ALU.mult, ALU.add, r=[bB, rr_, hm], w=[hm])
            yield
            for h in range(4):
                junk = self.junks.next()
                k.act(junk[:], hm[:, h, :], AF.Square, r=[hm], w=[junk, ss], accum_out=ss[:, h:h + 1])
            yield
            k.ts(rs[:, 0:4], ss[:, 0:4], 1.0 / 128, EPS, ALU.mult, ALU.add, r=[ss], w=[rs])
            yield
            k.act(rs[:, 0:4], rs[:, 0:4], AF.Ln, r=[rs], w=[rs])
            yield
            k.act(rs[:, 0:4], rs[:, 0:4], AF.Exp, r=[rs], w=[rs], scale=-0.5)
            yield
            wA = o["thA"]
            k.stt(wA[:], wA[:], 1.0, self.mnorm[:], ALU.add, ALU.mult, r=[wA, self.mnorm], w=[wA])
            yield
            mg = self.merged.next()
            for h in range(4):
                k.stt(mg[:, h * 128:(h + 1) * 128], hm[:, h, :], rs[:, h:h + 1], wA[:, h * 128:(h + 1) * 128], ALU.mult, ALU.mult,
                      r=[hm, rs, wA], w=[mg])
            if kp2 <= 2:
                continue
            yield
            qt4 = o["qtT"][:].rearrange("p (a b) -> p a b", a=4)
            kt4 = o["ktT"][:].rearrange("p (a b) -> p a b", a=4)
            pA = self.pA.next()
            ktz, qtz = self.ktz.next(), self.qtz.next()
            for d in range(2):
                for h in range(4):
                    j, e = h // 2, h % 2
                    rows = slice(e * 64, (e + 1) * 64)
                    k.copy(ktz[rows, d * 4 + h, :], kt4[rows, d * 2 + j, :], r=[o["ktT"]], w=[ktz], eng="pool")
                    k.copy(qtz[rows, d * 4 + h, :], qt4[rows, d * 2 + j, :], r=[o["qtT"]], w=[qtz])
            yield
            for d in range(2):
                b = ctx.bank()
                for h in range(4):
                    j, e = h // 2, h % 2
                    k.mm(b[:, h * 128:(h + 1) * 128], ktz[:, d * 4 + h, :], qt4[:, d * 2 + j, :], r=[ktz, o["qtT"]], w=[b])
                k.tt(pA[:, d * 4:(d + 1) * 4, :], b[:].rearrange("p (h t) -> p h t", h=4),
                     (MU if d == 0 else ML).unsqueeze(1).to_broadcast([128, 4, 128]), ALU.mult, r=[b, ctx.C], w=[pA])
                yield
            if kp2 <= 3:
                continue
            ob = ctx.bank()
            for h in range(4):
                j, e = h // 2, h % 2
                rows = slice(e * 64, (e + 1) * 64)
                gv = o["gvb"][:, h * 128:(h + 1) * 128]
                dst = ob[:, h * 128:(h + 1) * 128]
                k.mm(dst, pA[:, h, :], gv, True, False, r=[pA, o["gvb"]], w=[ob])
                k.mm(dst, pA[:, 4 + h, :], gv, False, False, r=[pA, o["gvb"]], w=[ob])
                k.mm(dst, qtz[:, h, :], Sfb[:, j, :], False, False, r=[qtz, Sfb], w=[ob])
                k.mm(dst, qtz[:, 4 + h, :], Sbb[:, j, :], False, True, r=[qtz, Sbb], w=[ob])
            for h in range(4):
                junk = self.junks.next()
                k.act(junk[:], ob[:, h * 128:(h + 1) * 128], AF.Square, r=[ob], w=[junk, ss2], accum_out=ss2[:, h:h + 1])
            k.ts(rs2[:, 0:4], ss2[:, 0:4], 1.0 / 128, EPS, ALU.mult, ALU.add, r=[ss2], w=[rs2])
            k.act(rs2[:, 0:4], rs2[:, 0:4], AF.Ln, r=[rs2], w=[rs2])
            k.act(rs2[:, 0:4], rs2[:, 0:4], AF.Exp, r=[rs2], w=[rs2], scale=-0.5)
            wB = o["thB"]
            k.stt(wB[:], wB[:], 1.0, TK[:, 1808:2320], ALU.add, ALU.mult, r=[wB, TK], w=[wB])
            k.tt(wB[:], wB[:], self.gnorm[:], ALU.mult, r=[wB, self.gnorm], w=[wB], eng="pool")
            for h in range(4):
                k.stt(mg[:, 512 + h * 128:512 + (h + 1) * 128], ob[:, h * 128:(h + 1) * 128], rs2[:, h:h + 1],
                      wB[:, h * 128:(h + 1) * 128], ALU.mult, ALU.mult, r=[ob, rs2, wB], w=[mg])
            if kp2 <= 4:
                continue
            yield
            if c < nch - 1:
                Cf, Sf = yield from self.state_update(o, 0, Cf, Sf, self.CF, self.SF)
                Cfb, Sfb = self.Cfb.next(), self.Sfb.next()
                k.copy(Cfb[:], Cf[:], r=[Cf], w=[Cfb], eng="act")
                k.copy(Sfb[:], Sf[:], r=[Sf], w=[Sfb], eng="pool")
                yield
            mT = self.mTs.next()
            self.nt.to_fm(mg, mT[:], mT)
            yield
            xo = self.xos.next()
            k.dma("sp", xo[:], x_in[t0:t0 + 128, :], r=[tok_in], w=[xo])
            for half in range(2):
                po = ctx.bank()
                for cc in range(8):
                    k.mm(po[:], mT[:, cc, :], self.Wout[:, cc, half * 512:(half + 1) * 512], cc == 0, cc == 7, r=[mT, self.Wout], w=[po])
                k.tt(xo[:, half * 512:(half + 1) * 512], po[:], xo[:, half * 512:(half + 1) * 512], ALU.add, r=[po, xo], w=[xo])
                yield
            outs.append(k.dma("sp", x_out[t0:t0 + 128, :], xo[:], r=[xo], w=[tok_out]))
            yield


def phase_b0(ctx, x_in, x_out, S_fm, S_tok, S_cb, S_sb, prm, tok_in, tok_fm, tok_tok, tok_out):
    mark = ctx.P.mark()
    mx = MixL0(ctx, S_fm, S_tok, S_cb, S_sb, prm["conv"], prm["igb"], prm["fgb"], prm["mnorm"], prm["w2"], prm["db"],
               prm["gnorm"], prm["wout"], tok_fm, tok_tok)
    outs = []
    import os
    kb0 = int(os.environ.get("KB0", "9"))
    if kb0 == 0:
        return list(ctx.P.dma_last.values())
    def run_all(gens):
        alive = list(gens)
        while alive:
            nxt = []
            for g_ in alive:
                try:
                    next(g_)
                    nxt.append(g_)
                except StopIteration:
                    pass
            alive = nxt
    for s0 in range(0, ctx.nseq, 2):
        ss_ = list(range(s0, min(s0 + 2, ctx.nseq)))
        run_all([mx.pass1(s) for s in ss_])
        run_all([mx.pass2(s, x_in, x_out, tok_in, tok_out, outs) for s in ss_])
    ctx.P.release(mark)
    return outs


C0 = float(np.exp(-0.5))
LCH = 64


def phase_a1(ctx, x_in, prm, S1, S_wt, S_vtok, S_bonus, S_g, tok_in, tok_s1):
    k = ctx.k
    n, T = ctx.n, ctx.T
    TB = 256
    NQ = TB // LCH
    mark = ctx.P.mark()
    g_fm = load_gain_fm(ctx, prm["gain"])
    Wr = load_w(ctx, prm["w_rkv"][0], g_fm)
    Wk = load_w(ctx, prm["w_rkv"][1], g_fm)
    Wv = load_w(ctx, prm["w_rkv"][2], g_fm)
    W1 = k.tile([128, 8, 352], BF16)
    load_w(ctx, prm["w1"][0], None, dst=W1, col0=0)
    load_w(ctx, prm["w1"][1], None, dst=W1, col0=64)
    load_w(ctx, prm["a1"], None, dst=W1, col0=128)
    load_w(ctx, prm["g1"], None, dst=W1, col0=192)
    for c in range(8):
        k.ts(W1[:, c, :], W1[:, c, :], g_fm[:, c:c + 1], None, ALU.mult, r=[W1, g_fm], w=[W1])
    w2t = k.tile([64, 3, D], BF16)
    k.dma("pool", w2t[:, 0, :], prm["w2"][0], w=[w2t])
    k.dma("pool", w2t[:, 1, :], prm["w2"][1], w=[w2t])
    k.dma("pool", w2t[:, 2, :], prm["a2"], w=[w2t])
    g2t = k.tile([128, 2, D], BF16)
    k.dma("pool", g2t[:, 0, :], prm["g2"][0:128, :], w=[g2t])
    k.dma("pool", g2t[0:32, 1, :], prm["g2"][128:160, :], w=[g2t])
    pc = k.tile([128, 7, 8], F32)
    srcs = [prm["w0"][0], prm["w0"][1], prm["a0"], prm["k_k"], prm["k_a"], prm["k_a"], prm["r_k"].rearrange("h d -> (h d)")]
    for i, sap in enumerate(srcs):
        k.dma("sp", pc[:, i, :], sap.rearrange("(c p) -> p c", p=128), w=[pc], allow_slow_non_contiguous=True)
    k.ts(pc[:, 5, :], pc[:, 5, :], -1.0, 1.0, ALU.mult, ALU.add, r=[pc], w=[pc])
    mu = k.tile([128, 6, 8], F32)
    for i in range(6):
        k.dma("sp", mu[:, i, :], prm["mu"][i].rearrange("(c p) -> p c", p=128), w=[mu], allow_slow_non_contiguous=True)
    nt = NormT(ctx, with_xn=False)
    xts = k.ring(2, [128, D], F32)
    xnf = k.ring(1, [128, D], F32)
    xh = k.ring(1, [2, D], F32)
    hTs = k.ring(1, [128, 8, TB + 2], F32)
    hhs = k.ring(1, [128, 8, TB], F32)
    mix_sets = [[k.tile([128, 8, TB], BF16) for _ in range(6)] for _ in range(2)]
    lows = k.ring(2, [128, 5, TB], BF16)
    NWAY = 2
    fsets = [[k.tile([128, TB], F32) for _ in range(13)] for _ in range(NWAY)]
    vts = k.ring(2, [128, D], BF16)
    ob4s = k.ring(5, [128, 4, TB], BF16)
    bg16 = k.ring(4, [128, TB], BF16)
    wtall = k.ring(2, [128, 2, 8, NQ], F32)
    ident = ctx.cst("ident")
    flipb = [0]

    def prologue(blk):
        mixes = mix_sets[blk % 2]
        b0 = blk * TB
        tpos = b0 % T
        hT = hTs.next()
        xhh = xh.next()
        k.memset(xhh[:], 0.0, w=[xhh])
        if tpos > 0:
            k.dma("sp", xhh[0:1, :], x_in[b0 - 1:b0, :], r=[tok_in], w=[xhh])
        if tpos + TB < T:
            k.dma("sp", xhh[1:2, :], x_in[b0 + TB:b0 + TB + 1, :], r=[tok_in], w=[xhh])
        rs = nt.rstd(xhh[:], xhh, rows=2)
        xn2 = xhh
        k.ts(xn2[:], xhh[:], rs[0:2, :], None, ALU.mult, r=[xhh, rs], w=[xn2])
        bk = ctx.bank()
        for c in range(8):
            k.tr(bk[:, c * 2:(c + 1) * 2], xn2[0:2, c * 128:(c + 1) * 128], ident[0:2, 0:2], r=[xn2, ctx.C], w=[bk])
        bkv = bk[:, 0:16].rearrange("p (c e) -> p c e", e=2)
        k.copy(hT[:, :, 0], bkv[:, :, 0], r=[bk], w=[hT])
        k.copy(hT[:, :, TB + 1], bkv[:, :, 1], r=[bk], w=[hT])
        yield
        for j in range(TB // 128):
            t0 = b0 + j * 128
            xt = xts.next()
            k.dma("sp", xt[:], x_in[t0:t0 + 128, :], r=[tok_in], w=[xt])
            rs = nt.rstd(xt[:], xt)
            xn = xnf.next()
            k.ts(xn[:], xt[:], rs[:], None, ALU.mult, r=[xt, rs], w=[xn])
            yield
            for half in range(2):
                bk = ctx.bank()
                for c4 in range(4):
                    c = half * 4 + c4
                    k.tr(bk[:, c4 * 128:(c4 + 1) * 128], xn[:, c * 128:(c + 1) * 128], ident, r=[xn, ctx.C], w=[bk])
                flipb[0] ^= 1
                k.copy(hT[:, half * 4:(half + 1) * 4, 1 + j * 128:1 + (j + 1) * 128], bk[:].rearrange("p (c t) -> p c t", c=4),
                       r=[bk], w=[hT], eng="act" if flipb[0] else "dve")
                yield
        hh = hhs.next()
        k.tt(hh[:], hT[:, :, 0:TB], hT[:, :, 2:TB + 2], ALU.add, r=[hT], w=[hh], eng="pool")
        yield
        k.stt(hh[:], hh[:], 0.5, hT[:, :, 1:TB + 1], ALU.mult, ALU.subtract, r=[hh, hT], w=[hh])
        yield
        for i in range(6):
            for c in range(8):
                k.stt(mixes[i][:, c, :], hh[:, c, :], mu[:, i, c:c + 1], hT[:, c, 1:TB + 1], ALU.mult, ALU.add, r=[hh, mu, hT], w=[mixes[i]])
                yield
        xr, xw, xk, xv, xa, xg = mixes
        low = lows.next()
        specs = [(xw, 0, 64, 0, AF.Tanh), (xw, 64, 64, 1, AF.Tanh), (xa, 128, 64, 2, AF.Copy), (xg, 192, 128, 3, AF.Sigmoid), (xg, 320, 32, 4, AF.Sigmoid)]
        for (src, c0, m, slot, fn) in specs:
            bk = ctx.bank()
            for c in range(8):
                k.mm(bk[0:m, 0:TB], W1[:, c, c0:c0 + m], src[:, c, :], c == 0, c == 7, r=[W1, src], w=[bk])
            k.act(low[0:m, slot, :], bk[0:m, 0:TB], fn, r=[bk], w=[low])
            yield
        for j in range(TB // 128):
            t0 = b0 + j * 128
            vt = vts.next()
            for half in range(2):
                bk = ctx.bank()
                for c in range(8):
                    k.mm(bk[:], xv[:, c, j * 128:(j + 1) * 128], Wv[:, c, half * 512:(half + 1) * 512], c == 0, c == 7, r=[xv, Wv], w=[bk])
                flipb[0] ^= 1
                k.copy(vt[:, half * 512:(half + 1) * 512], bk[:], r=[bk], w=[vt], eng="act" if flipb[0] else "dve")
                yield
            k.dma("sp", S_vtok[t0:t0 + 128, :], vt[:], r=[vt], w=[tok_s1])
            yield
        wta = wtall.next()
        wta_parts = [Buf("wt") for _ in range(16)]
        return (xr, xk, xv, low, b0, blk, wta, wta_parts)

    def fc_chain(fc, F, B):
        xr, xk, xv, low, b0, blk, wta, wta_parts = B
        fs = slice(fc * 128, (fc + 1) * 128)
        r_, k_, v_, a_, kk, k2, t1, t2, sg, G, cI, cE, W = F
        col = lambda i: pc[:, i, fc:fc + 1]

        def proj(Wt, src):
            bk = ctx.bank()
            for c in range(8):
                k.mm(bk[:, 0:TB], Wt[:, c, fs], src[:, c, :], c == 0, c == 7, r=[Wt, src], w=[bk])
            return bk
        bk = proj(Wr, xr)
        k.copy(r_[:], bk[:, 0:TB], r=[bk], w=[r_], eng="act")
        yield
        bk = proj(Wk, xk)
        k.copy(k_[:], bk[:, 0:TB], r=[bk], w=[k_])
        yield
        bk = proj(Wv, xv)
        k.copy(v_[:], bk[:, 0:TB], r=[bk], w=[v_], eng="act")
        yield
        bk = ctx.bank()
        k.mm(bk[:, 0:TB], w2t[:, 2, fs], low[0:64, 2, :], r=[w2t, low], w=[bk])
        k.act(a_[:], bk[:, 0:TB], AF.Sigmoid, r=[bk, pc], w=[a_], bias=col(2))
        yield
        bk = ctx.bank()
        k.mm(bk[:, 0:TB], g2t[:, 0, fs], low[:, 3, :], True, False, r=[g2t, low], w=[bk])
        k.mm(bk[:, 0:TB], g2t[0:32, 1, fs], low[0:32, 4, :], False, True, r=[g2t, low], w=[bk])
        gb16 = bg16.next()
        k.copy(gb16[:], bk[:, 0:TB], r=[bk], w=[gb16])
        k.dma("sp", S_g[blk, :, fc, :], gb16[:], r=[gb16], w=[tok_s1])
        yield
        k.ts(kk[:], k_[:], col(3), None, ALU.mult, r=[k_, pc], w=[kk])
        yield
        k.act(t2[:], kk[:], AF.Square, r=[kk], w=[t2])
        yield
        bk = ctx.bank()
        k.mm(bk[:, 0:TB], ctx.cst("BLK"), t2[:], r=[ctx.C, t2], w=[bk])
        k.ts(t2[:], bk[:, 0:TB], 1e-24, None, ALU.max, r=[bk], w=[t2])
        yield
        k.act(t2[:], t2[:], AF.Ln, r=[t2], w=[t2])
        yield
        k.act(t2[:], t2[:], AF.Exp, r=[t2], w=[t2], scale=-0.5)
        yield
        k.tt(kk[:], kk[:], t2[:], ALU.mult, r=[kk, t2], w=[kk])
        k.ts(k2[:], a_[:], col(4), col(5), ALU.mult, ALU.add, r=[a_, pc], w=[k2])
        yield
        k.tt(k2[:], k2[:], k_[:], ALU.mult, r=[k2, k_], w=[k2])
        yield
        k.stt(t2[:], r_[:], col(6), k2[:], ALU.mult, ALU.mult, r=[r_, pc, k2], w=[t2])
        yield
        bk = ctx.bank()
        k.mm(bk[:, 0:TB], ctx.cst("BLK"), t2[:], r=[ctx.C, t2], w=[bk])
        bb16 = bg16.next()
        k.tt(bb16[:], bk[:, 0:TB], v_[:], ALU.mult, r=[bk, v_], w=[bb16])
        k.dma("sp", S_bonus[blk, :, fc, :], bb16[:], r=[bb16], w=[tok_s1])
        k.tt(a_[:], a_[:], kk[:], ALU.mult, r=[a_, kk], w=[a_], eng="pool")
        yield
        for d in range(2):
            bk = ctx.bank()
            k.mm(bk[:, 0:TB], w2t[:, d, fs], low[0:64, d, :], r=[w2t, low], w=[bk])
            k.act(sg[:], bk[:, 0:TB], AF.Sigmoid, r=[bk, pc], w=[sg], bias=col(d))
            yield
            ctx.P.op("dve", lambda e, G=G, sg=sg: e.tensor_tensor_scan(out=G[:], data0=sg[:], data1=sg[:], initial=0.0, op0=ALU.add, op1=ALU.bypass),
                     _tok([sg]), _tok([G]))
            yield
            G3 = G[:].rearrange("p (q t) -> p q t", t=LCH)
            c3 = cI[:].rearrange("p (q t) -> p q t", t=LCH)
            e3 = cE[:].rearrange("p (q t) -> p q t", t=LCH)
            k.copy(c3[:, 0, :], G3[:, 0, :], r=[G], w=[cI], eng="pool")
            k.tt(c3[:, 1:NQ, :], G3[:, 1:NQ, :], G3[:, 0:NQ - 1, LCH - 1:LCH].to_broadcast([128, NQ - 1, LCH]), ALU.subtract, r=[G], w=[cI])
            yield
            tot = c3[:, :, LCH - 1:LCH]
            k.act(wta[:, d, fc, :], c3[:, :, LCH - 1], AF.Exp, r=[cI], w=[wta_parts[d * 8 + fc]], scale=-C0)
            if d == 0:
                k.tt(cE[:], cI[:], sg[:], ALU.subtract, r=[cI, sg], w=[cE], eng="pool")
                inc, exc = cI, cE
                yield
            else:
                k.tt(e3, tot.to_broadcast([128, NQ, LCH]), c3, ALU.subtract, r=[cI], w=[cE])
                yield
                k.tt(G[:], cE[:], sg[:], ALU.add, r=[cE, sg], w=[G], eng="pool")
                inc, exc = G, cE
                yield
            base = d * 4
            k.act(W[:], inc[:], AF.Exp, r=[inc], w=[W], scale=-C0)
            yield
            ob = ob4s.next()
            k.tt(ob[:, 3, :], r_[:], W[:], ALU.mult, r=[r_, W], w=[ob])
            yield
            k.act(W[:], inc[:], AF.Exp, r=[inc], w=[W], scale=C0)
            yield
            k.tt(ob[:, 2, :], k2[:], W[:], ALU.mult, r=[k2, W], w=[ob])
            k.tt(ob[:, 1, :], a_[:], W[:], ALU.mult, r=[a_, W], w=[ob], eng="pool")
            yield
            k.act(W[:], exc[:], AF.Exp, r=[exc], w=[W], scale=-C0)
            yield
            k.stt(ob[:, 0, :], kk[:], -1.0, W[:], ALU.mult, ALU.mult, r=[kk, W], w=[ob])
            k.dma("sp", S1[base:base + 4, fs, b0:b0 + TB].rearrange("a q t -> q a t"), ob[:], r=[ob], w=[tok_s1])
            yield

    def step(g_):
        try:
            next(g_)
            return True, None
        except StopIteration as e_:
            return False, e_.value

    nblk = n // TB
    pro = prologue(0)
    while True:
        ok, val = step(pro)
        if not ok:
            Bcur = val
            break
    for blk in range(nblk):
        pro = prologue(blk + 1) if blk + 1 < nblk else None
        Bnext = None
        for g0 in range(0, 8, NWAY):
            alive = [fc_chain(g0 + i_, fsets[i_], Bcur) for i_ in range(NWAY)]
            while alive:
                if pro is not None:
                    ok, val = step(pro)
                    if not ok:
                        Bnext, pro = val, None
                alive = [g_ for g_ in alive if step(g_)[0]]
        while pro is not None:
            ok, val = step(pro)
            if not ok:
                Bnext, pro = val, None
        wta, wta_parts = Bcur[6], Bcur[7]
        for d in range(2):
            k.dma("sp", S_wt[d].rearrange("(p q) c -> q p c", q=128)[:, :, blk * NQ:(blk + 1) * NQ], wta[:, d, :, :],
                  r=wta_parts[d * 8:(d + 1) * 8], w=[tok_s1], allow_slow_non_contiguous=True)
        Bcur = Bnext
    ctx.P.release(mark)


def phase_b1(ctx, x_in, x_out, prm, S1, S_wt, S_vtok, S_bonus, S_g, S_yb, tok_in, tok_s1, tok_out):
    k = ctx.k
    n, T = ctx.n, ctx.T
    NCH = T // LCH
    mark = ctx.P.mark()
    Wo = load_w(ctx, prm["w_o"])
    lnw = k.tile([128, 8], F32)
    lnb = k.tile([128, 8], F32)
    k.dma("sp", lnw[:], prm["ln_w"].rearrange("(c p) -> p c", p=128), w=[lnw], allow_slow_non_contiguous=True)
    k.dma("sp", lnb[:], prm["ln_b"].rearrange("(c p) -> p c", p=128), w=[lnb], allow_slow_non_contiguous=True)
    tok_yb = Buf("yb")

    def bdring(nslots):
        rr = k.ring(nslots, [128, 8, 128], F32)
        for t in rr.tiles:
            k.memset(t[:], 0.0, w=[t])
        return rr
    ATs, BTs, KTs, Vbs = bdring(2), bdring(2), bdring(2), bdring(2)
    RTs = k.ring(2, [128, 8, LCH], F32)
    wts = k.ring(2, [128, 8], F32)
    big = lambda nslots: k.ring(nslots, [128, 8, 128], F32)
    Ns, NTs, Ps = big(2), big(2), big(2)
    AKs, Xs, Us, Bts, Kts = big(1), big(1), big(1), big(1), big(1)
    Ms = big(2)
    tmpM = big(1)
    RBs = k.ring(1, [128, 8, LCH], F32)
    RKs = k.ring(1, [128, 8, LCH], F32)
    ysb = k.ring(2, [128, 8, LCH], F32)
    ybl = k.ring(2, [128, 8, LCH], F32)
    o512 = [k.ring(2, [128, 8, LCH], F32) for _ in range(5)]
    zTs = k.ring(2, [128, 8, LCH], BF16)
    xrs = k.ring(2, [LCH, D], F32)
    xos = k.ring(2, [LCH, D], F32)
    ident = ctx.cst("ident")
    outs = []
    flip = [0]

    def bd_src(ap2d, t0):
        v = ap2d.rearrange("(p e k) t -> e k p t", e=2, k=64)
        return [v[e][:, :, t0:t0 + LCH] for e in range(2)]

    def evac(dst_ap, src_ap, r, w):
        flip[0] ^= 1
        k.copy(dst_ap, src_ap, r=r, w=w, eng="act" if flip[0] else "dve")

    def pairs_mm(lhs_tile, rhs_tile, width=128, lhs2=None, rhs2=None):
        per_bank = 512 // width
        res = []
        for b0 in range(0, 8, per_bank):
            bk = ctx.bank()
            for p in range(b0, b0 + per_bank):
                o = bk[:, (p - b0) * width:(p - b0 + 1) * width]
                k.mm(o, lhs_tile[:, p, :], rhs_tile[:, p, :], True, lhs2 is None, r=[lhs_tile, rhs_tile], w=[bk])
                if lhs2 is not None:
                    k.mm(o, lhs2[:, p, :], rhs2[:, p, :], False, True, r=[lhs2, rhs2], w=[bk])
            res.append((bk, bk[:].rearrange("p (a b) -> p a b", b=width), b0, per_bank))
        return res

    for s in range(ctx.nseq):
        for d in (1, 0):
            base = d * 4
            Mst = Ms.next()
            k.memset(Mst[:], 0.0, w=[Mst])
            strict = ctx.cst("SF" if d == 0 else "SB")
            strictT = ctx.cst("SB" if d == 0 else "SF")
            incl = ctx.cst("IF" if d == 0 else "IB")[:, 0:LCH]
            order = range(NCH) if d == 0 else range(NCH - 1, -1, -1)
            for c in order:
                t0 = s * T + c * LCH
                cg = t0 // LCH
                AT, BT, KT, Vb = ATs.next(), BTs.next(), KTs.next(), Vbs.next()
                for e in range(2):
                    rows = slice(e * 64, (e + 1) * 64)
                    k.dma("sp", AT[rows, :, rows], bd_src(S1[base + 0], t0)[e], r=[tok_s1], w=[AT])
                    k.dma("sp", BT[rows, :, rows], bd_src(S1[base + 1], t0)[e], r=[tok_s1], w=[BT])
                    k.dma("sp", KT[rows, :, rows], bd_src(S1[base + 2], t0)[e], r=[tok_s1], w=[KT])
                    k.dma("sp", Vb[rows, :, rows], S_vtok[t0:t0 + LCH, :].rearrange("t (p e v) -> e t p v", e=2, v=64)[e],
                          r=[tok_s1], w=[Vb])
                RT = RTs.next()
                k.dma("sp", RT[:], S1[base + 3].rearrange("(p q) t -> q p t", q=128)[:, :, t0:t0 + LCH], r=[tok_s1], w=[RT])
                wt = wts.next()
                k.dma("sp", wt[:], S_wt[d].rearrange("(p q) c -> q p c", q=128)[:, :, cg], r=[tok_s1], w=[wt], allow_slow_non_contiguous=True)
                N, NT, P_ = Ns.next(), NTs.next(), Ps.next()
                AK = AKs.next()
                for (bk, v, b0, nb) in pairs_mm(BT, AT):
                    k.tt(N[:, b0:b0 + nb, :], v, strict.unsqueeze(1).to_broadcast([128, nb, 128]), ALU.mult, r=[bk, ctx.C], w=[N])
                for (bk, v, b0, nb) in pairs_mm(AT, BT):
                    k.tt(NT[:, b0:b0 + nb, :], v, strictT.unsqueeze(1).to_broadcast([128, nb, 128]), ALU.mult, r=[bk, ctx.C], w=[NT])
                for (bk, v, b0, nb) in pairs_mm(KT, AT):
                    k.tt(AK[:, b0:b0 + nb, :], v, strict.unsqueeze(1).to_broadcast([128, nb, 128]), ALU.mult, r=[bk, ctx.C], w=[AK])
                RB, RK = RBs.next(), RKs.next()
                for (bk, v, b0, nb) in pairs_mm(BT, RT, width=LCH):
                    k.tt(RB[:, b0:b0 + nb, :], v, incl.unsqueeze(1).to_broadcast([128, nb, LCH]), ALU.mult, r=[bk, ctx.C], w=[RB])
                for (bk, v, b0, nb) in pairs_mm(KT, RT, width=LCH):
                    k.tt(RK[:, b0:b0 + nb, :], v, incl.unsqueeze(1).to_broadcast([128, nb, LCH]), ALU.mult, r=[bk, ctx.C], w=[RK])
                k.tt(P_[:], N[:], ident.unsqueeze(1).to_broadcast([128, 8, 128]), ALU.add, r=[N, ctx.C], w=[P_], eng="pool")
                for lvl in range(5):
                    last = lvl == 4
                    N2 = None if last else Ns.next()
                    NT2 = NTs.next()
                    if not last:
                        for (bk, v, b0, nb) in pairs_mm(NT, N):
                            evac(N2[:, b0:b0 + nb, :], v, [bk], [N2])
                    for (bk, v, b0, nb) in pairs_mm(N, NT):
                        evac(NT2[:, b0:b0 + nb, :], v, [bk], [NT2])
                    P2 = Ps.next()
                    for (bk, v, b0, nb) in pairs_mm(NT2, P_):
                        k.tt(P2[:, b0:b0 + nb, :], v, P_[:, b0:b0 + nb, :], ALU.add, r=[bk, P_], w=[P2])
                    N, NT, P_ = N2, NT2, P2
                X = Xs.next()
                for (bk, v, b0, nb) in pairs_mm(AT, Mst, lhs2=AK, rhs2=Vb):
                    evac(X[:, b0:b0 + nb, :], v, [bk], [X])
                U = Us.next()
                for (bk, v, b0, nb) in pairs_mm(P_, X):
                    evac(U[:, b0:b0 + nb, :], v, [bk], [U])
                yb = ctx.bank()
                for p in range(8):
                    o = yb[:, p * LCH:(p + 1) * LCH]
                    k.mm(o, Mst[:, p, :], RT[:, p, :], True, False, r=[Mst, RT], w=[yb])
                    k.mm(o, U[:, p, :], RB[:, p, :], False, False, r=[U, RB], w=[yb])
                    k.mm(o, Vb[:, p, :], RK[:, p, :], False, True, r=[Vb, RK], w=[yb])
                yv = yb[:].rearrange("p (a b) -> p a b", b=LCH)
                if d == 1:
                    ys = ysb.next()
                    k.copy(ys[:], yv, r=[yb], w=[ys])
                    k.dma("sp", S_yb.rearrange("(p q) t -> q p t", q=128)[:, :, t0:t0 + LCH], ys[:], r=[ys], w=[tok_yb])
                if c != order[-1]:
                    Bt, Kt = Bts.next(), Kts.next()
                    for (src, dst) in ((BT, Bt), (KT, Kt)):
                        for b0 in (0, 4):
                            bk = ctx.bank()
                            for p in range(b0, b0 + 4):
                                k.tr(bk[:, (p - b0) * 128:(p - b0 + 1) * 128], src[:, p, :], ident, r=[src, ctx.C], w=[bk])
                            evac(dst[:, b0:b0 + 4, :], bk[:].rearrange("p (a b) -> p a b", b=128), [bk], [dst])
                    Mn = Ms.next()
                    tm = tmpM.next()
                    for (bk, v, b0, nb) in pairs_mm(Bt, U, lhs2=Kt, rhs2=Vb):
                        k.tt(tm[:, b0:b0 + nb, :], v, Mst[:, b0:b0 + nb, :], ALU.add, r=[bk, Mst], w=[tm])
                        k.tt(Mn[:, b0:b0 + nb, :], tm[:, b0:b0 + nb, :], wt[:, b0:b0 + nb].unsqueeze(2).to_broadcast([128, nb, 128]), ALU.mult,
                             r=[tm, wt], w=[Mn], eng="pool")
                    Mst = Mn
                if d == 0:
                    ybt = ybl.next()
                    k.dma("sp", ybt[:], S_yb.rearrange("(p q) t -> q p t", q=128)[:, :, t0:t0 + LCH], r=[tok_yb], w=[ybt])
                    bon = o512[0].next()
                    gt = o512[1].next()
                    k.dma("sp", bon[:], S_bonus.rearrange("(p q) t -> q p t", q=128)[:, :, t0:t0 + LCH], r=[tok_s1], w=[bon])
                    k.dma("sp", gt[:], S_g.rearrange("(p q) t -> q p t", q=128)[:, :, t0:t0 + LCH], r=[tok_s1], w=[gt])
                    ysum, ysq, t3 = o512[2].next(), o512[3].next(), o512[4].next()
                    k.tt(ysum[:], yv, ybt[:], ALU.add, r=[yb, ybt], w=[ysum])
                    k.act(ysq[:], ysum[:], AF.Square, r=[ysum], w=[ysq])
                    f2 = lambda t: t[:].rearrange("p a b -> p (a b)")
                    mb, qb = ctx.bank(), ctx.bank()
                    k.mm(mb[:], ctx.cst("BLKM"), f2(ysum), r=[ctx.C, ysum], w=[mb])
                    k.mm(qb[:], ctx.cst("BLKM"), f2(ysq), r=[ctx.C, ysq], w=[qb])
                    k.act(f2(ysq), mb[:], AF.Square, r=[mb], w=[ysq])
                    k.tt(f2(ysq), qb[:], f2(ysq), ALU.subtract, r=[qb, ysq], w=[ysq])
                    k.ts(f2(ysq), f2(ysq), 64e-5, None, ALU.add, r=[ysq], w=[ysq], eng="pool")
                    k.act(f2(ysq), f2(ysq), AF.Ln, r=[ysq], w=[ysq])
                    k.act(f2(ysq), f2(ysq), AF.Exp, r=[ysq], w=[ysq], scale=-0.5)
                    k.tt(f2(t3), f2(ysum), mb[:], ALU.subtract, r=[ysum, mb], w=[t3])
                    k.tt(t3[:], t3[:], ysq[:], ALU.mult, r=[t3, ysq], w=[t3])
                    k.tt(t3[:], t3[:], lnw[:].unsqueeze(2).to_broadcast([128, 8, LCH]), ALU.mult, r=[t3, lnw], w=[t3], eng="pool")
                    k.tt(t3[:], t3[:], lnb[:].unsqueeze(2).to_broadcast([128, 8, LCH]), ALU.add, r=[t3, lnb], w=[t3], eng="pool")
                    k.tt(t3[:], t3[:], bon[:], ALU.add, r=[t3, bon], w=[t3])
                    zT = zTs.next()
                    k.tt(zT[:], t3[:], gt[:], ALU.mult, r=[t3, gt], w=[zT])
                    xr = xrs.next()
                    k.dma("sp", xr[:], x_in[t0:t0 + LCH, :], r=[tok_in], w=[xr])
                    xo = xos.next()
                    for half in range(2):
                        po = ctx.bank()
                        for cc in range(8):
                            k.mm(po[0:LCH, :], zT[:, cc, :], Wo[:, cc, half * 512:(half + 1) * 512], cc == 0, cc == 7, r=[zT, Wo], w=[po])
                        k.tt(xo[:, half * 512:(half + 1) * 512], po[0:LCH, :], xr[:, half * 512:(half + 1) * 512], ALU.add, r=[po, xr], w=[xo])
                    outs.append(k.dma("sp", x_out[t0:t0 + LCH, :], xo[:], r=[xo], w=[tok_out]))
    ctx.P.release(mark)
    return outs


PARAM_SHAPES = None


def build_program(T, nseq, shapes):
    import os
    nc = bass.Bass("TRN2", target_bir_lowering=False)
    n = T * nseq
    carr, _ = build_consts()

    def din(name, shape):
        return nc.dram_tensor(name, list(shape), F32, kind="ExternalInput").ap()

    def dint(name, shape, dt=F32):
        return nc.dram_tensor(name, list(shape), dt, kind=os.environ.get("KSCR", "Internal")).ap()

    class _Sl:
        def __init__(self, lst):
            self.lst = lst

        def __getitem__(self, i):
            return self.lst[i]
    A = {}
    for name, shp in shapes.items():
        if name in ("x", "mem", "norm_final"):
            A[name] = din(name, shp)
        else:
            A[name] = _Sl([din(f"{name}_{i}", shp[1:]) for i in range(shp[0])])
    cst = din("consts", carr.shape)
    out = nc.dram_tensor("out", [n, D], F32, kind="ExternalOutput").ap()
    xa, xb = dint("xa", (n, D)), dint("xb", (n, D))
    S_fm, S_tok = dint("S_fm", (FM_ROWS, n)), dint("S_tok", (n, TOKW))
    nch = n // 128
    S_cb, S_sb = dint("S_cb", (nch, 128, 516)), dint("S_sb", (nch, 128, 256))
    S1, S_wt = dint("S1", (8, D, n), BF16), dint("S_wt", (2, D, n // LCH))
    S_vtok, S_bonus, S_g, S_yb = (dint("S_vtok", (n, D), BF16), dint("S_bonus", (n // 256, 128, 8, 256), BF16),
                                    dint("S_g", (n // 256, 128, 8, 256), BF16), dint("S_yb", (2, n // 256, 128, 8, 256), BF16))
    P = Prog(nc)
    ctx = Ctx(nc, P, T, nseq, cst)
    tx = Buf("x")
    ta, tb_ = Buf("xa"), Buf("xb")
    import os
    nph = int(os.environ.get("KPH", "99"))

    def finish(outs_):
        P.emit(final_waits=outs_)
        P.close()
        return nc, carr
    tfm, ttok = Buf("fm"), Buf("tok")
    phase_a0(ctx, A["x"], A["ev_w_in"][0], A["norm_mix"][0], S_fm, S_tok, tx, tfm, ttok)
    if nph <= 0:
        return finish(list(P.dma_last.values()))
    prm0 = dict(conv=A["ev_conv_qk"][0], igb=A["ev_m_ig_bias"][0], fgb=A["ev_m_fg_bias"][0], mnorm=A["ev_m_norm"][0],
                w2=A["ev_g_decay_w2"][0], db=A["ev_g_decay_b"][0], gnorm=A["ev_g_norm"][0], wout=A["ev_w_out"][0])
    o_ = phase_b0(ctx, A["x"], out if nph <= 1 else xa, S_fm, S_tok, S_cb, S_sb, prm0, tx, tfm, ttok, ta)
    if nph <= 1:
        return finish(o_)
    mem2 = A["mem"]
    o_ = phase_xattn(ctx, xa, out if nph <= 2 else xb, mem2, A["xa_wq"][0], A["xa_wkv"][0], A["xa_wo"][0], A["norm_xattn"][0], A["norm_mem"][0], ta, tb_)
    if nph <= 2:
        return finish(o_)
    o_ = phase_ffn(ctx, xb, out if nph <= 3 else xa, A["ffn_w_gate"][0], A["ffn_w_up"][0], A["ffn_w_down"][0], A["norm_ffn"][0], tb_, ta)
    if nph <= 3:
        return finish(o_)
    prm1 = dict(gain=A["norm_mix"][1], w_rkv=A["od_w_rkv"][0], w0=A["od_w0"][0], w1=A["od_w1"][0], w2=A["od_w2"][0], a0=A["od_a0"][0],
                a1=A["od_a1"][0], a2=A["od_a2"][0], g1=A["od_g1"][0], g2=A["od_g2"][0], k_k=A["od_k_k"][0], k_a=A["od_k_a"][0],
                r_k=A["od_r_k"][0], mu=A["od_mu"][0], ln_w=A["od_ln_w"][0], ln_b=A["od_ln_b"][0], w_o=A["od_w_o"][0])
    ts1 = Buf("s1")
    phase_a1(ctx, xa, prm1, S1, S_wt, S_vtok, S_bonus, S_g, ta, ts1)
    ty = Buf("y")
    phase_b1v2(ctx, prm1, S1, S_wt, S_vtok, S_yb, ts1, ty)
    o_ = phase_c1(ctx, xa, out if nph <= 4 else xb, prm1, S_yb, S_bonus, S_g, ta, ts1, ty, tb_)
    if nph <= 4:
        return finish(o_)
    phase_xattn(ctx, xb, xa, mem2, A["xa_wq"][1], A["xa_wkv"][1], A["xa_wo"][1], A["norm_xattn"][1], A["norm_mem"][1], tb_, ta)
    tout = Buf("out")
    outs = phase_ffn(ctx, xa, out, A["ffn_w_gate"][1], A["ffn_w_up"][1], A["ffn_w_down"][1], A["norm_ffn"][1], ta, tout,
                     final_gain_ap=A["norm_final"])
    P.emit(final_waits=outs)
    P.close()
    return nc, carr


_CACHE = {}


def kernel(**inputs):
    import os
    ncores = int(os.environ.get("KNC", "8"))
    x = np.asarray(inputs["x"], np.float32)
    B, T, _ = x.shape
    nseq = B // ncores
    shapes = {}
    per_core = []
    for name, v in inputs.items():
        v = np.ascontiguousarray(np.asarray(v, np.float32))
        if name == "x":
            shapes[name] = (nseq * T, D)
        elif name == "mem":
            shapes[name] = (nseq * NMEM, D)
        else:
            shapes[name] = v.shape
    key = (T, nseq)
    if key not in _CACHE:
        _CACHE[key] = build_program(T, nseq, shapes)
    nc, carr = _CACHE[key]
    in_maps = []
    for c in range(ncores):
        m = {"consts": carr}
        for name, v in inputs.items():
            v = np.ascontiguousarray(np.asarray(v, np.float32))
            if name == "x":
                m[name] = np.ascontiguousarray(v[c * nseq:(c + 1) * nseq].reshape(nseq * T, D))
            elif name == "mem":
                m[name] = np.ascontiguousarray(v[c * nseq:(c + 1) * nseq].reshape(nseq * NMEM, D))
            elif name == "norm_final":
                m[name] = v
            else:
                for i in range(v.shape[0]):
                    m[f"{name}_{i}"] = np.ascontiguousarray(v[i])
        in_maps.append(m)
    res = run_bass_kernel_spmd(nc, in_maps, core_ids=list(range(ncores)))
    outs = [np.asarray(r["out"], np.float32).reshape(nseq, T, D) for r in res.results]
    return np.concatenate(outs, axis=0)


def phase_b1v2(ctx, prm, S1, S_wt, S_vtok, S_y, tok_s1, tok_y, nchains=2):
    k = ctx.k
    n, T = ctx.n, ctx.T
    NCH = T // LCH
    mark = ctx.P.mark()
    ident = ctx.cst("ident")
    flip = [0]

    def evac(dst_ap, src_ap, r, w):
        flip[0] = (flip[0] + 1) % 4
        k.copy(dst_ap, src_ap, r=r, w=w, eng="dve" if flip[0] == 0 else "act")

    def pairs_mm(lhs_tile, rhs_tile, width=128, lhs2=None, rhs2=None):
        per_bank = 512 // width
        res = []
        for b0 in range(0, 8, per_bank):
            bk = ctx.bank()
            for p in range(b0, b0 + per_bank):
                o = bk[:, (p - b0) * width:(p - b0 + 1) * width]
                k.mm(o, lhs_tile[:, p, :], rhs_tile[:, p, :], True, lhs2 is None, r=[lhs_tile, rhs_tile], w=[bk])
                if lhs2 is not None:
                    k.mm(o, lhs2[:, p, :], rhs2[:, p, :], False, True, r=[lhs2, rhs2], w=[bk])
            res.append((bk, bk[:].rearrange("p (a b) -> p a b", b=width), b0, per_bank))
        return res

    def bd_src(ap2d, t0):
        v = ap2d.rearrange("(p e k) t -> e k p t", e=2, k=64)
        return [v[e][:, :, t0:t0 + LCH] for e in range(2)]

    class Work:
        def __init__(self):
            big = lambda: k.tile([128, 8, 128], BF16)
            big32 = lambda: k.tile([128, 8, 128], F32)
            self.AT, self.BT, self.KT, self.Vb = big(), big(), big(), big()
            for t in (self.AT, self.BT, self.KT, self.Vb):
                k.memset(t[:], 0.0, w=[t])
            self.RT = k.tile([128, 8, LCH], BF16)
            self.wt = k.tile([128, 8, NCH], F32)
            self.St = [k.tile([128, 32, 256], BF16), k.tile([128, 32, 256], BF16)]
            self.Vt = k.tile([128, D], BF16)
            self.N = [big(), big()]
            self.NT = [big(), big()]
            self.P = [big(), big()]
            self.AK, self.X, self.U, self.Bt, self.Kt = big(), big(), big(), big(), big()
            self.M = [big32(), big32()]
            self.Mb = [big(), big()]
            self.RB = k.tile([128, 8, LCH], BF16)
            self.RK = k.tile([128, 8, LCH], BF16)
            self.ys = k.tile([128, 8, LCH], BF16)

    works = [Work() for _ in range(nchains)]

    def chain(W, s, d):
        base = d * 4
        mi = 0
        Mst = W.M[mi]
        Mb = W.Mb[mi]
        k.memset(Mst[:], 0.0, w=[Mst])
        k.memset(Mb[:], 0.0, w=[Mb])
        strict = ctx.cst("SF" if d == 0 else "SB")
        strictT = ctx.cst("SB" if d == 0 else "SF")
        incl = ctx.cst("IF" if d == 0 else "IB")[:, 0:LCH]
        order = list(range(NCH)) if d == 0 else list(range(NCH - 1, -1, -1))
        S1flat = S1[base:base + 4].rearrange("a r t -> (a r) t")
        cg0 = (s * T) // LCH
        k.dma("sp", W.wt[:], S_wt[d].rearrange("(p q) c -> q p c", q=128)[:, :, cg0:cg0 + NCH], r=[tok_s1], w=[W.wt],
              allow_slow_non_contiguous=True)
        groups = []
        for c in order:
            if not groups or groups[-1] != c // 4:
                groups.append(c // 4)

        def load_group(gi):
            g = groups[gi]
            St = W.St[gi % 2]
            tg = s * T + g * 256
            k.dma("sp", St[:], S1flat[:, tg:tg + 256].rearrange("(ap q) t -> q ap t", q=128), r=[tok_s1], w=[St])
        load_group(0)
        for c in order:
            t0 = s * T + c * LCH
            gi = groups.index(c // 4)
            if c // 4 != (order[order.index(c) - 1] // 4 if order.index(c) > 0 else -1) and gi + 1 < len(groups):
                load_group(gi + 1)
            St = W.St[gi % 2]
            off = (c % 4) * LCH
            AT, BT, KT, Vb, RT = W.AT, W.BT, W.KT, W.Vb, W.RT
            wt = W.wt
            for e in range(2):
                rows = slice(e * 64, (e + 1) * 64)
                k.dma("sp", W.Vt[rows, :], S_vtok[t0:t0 + LCH, :], r=[tok_s1], w=[W.Vt])
            for ai, dstt in ((0, AT), (1, BT), (2, KT)):
                for e in range(2):
                    rows = slice(e * 64, (e + 1) * 64)
                    k.copy(dstt[rows, :, rows], St[rows, ai * 8:(ai + 1) * 8, off:off + LCH], r=[St], w=[dstt], eng="pool" if e == 0 else "act")
            k.copy(RT[:], St[:, 24:32, off:off + LCH], r=[St], w=[RT], eng="pool")
            for e in range(2):
                rows = slice(e * 64, (e + 1) * 64)
                k.copy(Vb[rows, :, rows], W.Vt[rows, :].rearrange("t (p e v) -> t p e v", e=2, v=64)[:, :, e, :], r=[W.Vt], w=[Vb],
                       eng="pool" if e == 0 else "act")
            yield
            ni = 0
            N, NT, P_ = W.N[0], W.NT[0], W.P[0]
            AK, RB, RK = W.AK, W.RB, W.RK
            for (bk, v, b0, nb) in pairs_mm(BT, AT):
                k.tt(N[:, b0:b0 + nb, :], v, strict.unsqueeze(1).to_broadcast([128, nb, 128]), ALU.mult, r=[bk, ctx.C], w=[N])
            for (bk, v, b0, nb) in pairs_mm(AT, BT):
                k.tt(NT[:, b0:b0 + nb, :], v, strictT.unsqueeze(1).to_broadcast([128, nb, 128]), ALU.mult, r=[bk, ctx.C], w=[NT])
            k.tt(P_[:], N[:], ident.unsqueeze(1).to_broadcast([128, 8, 128]), ALU.add, r=[N, ctx.C], w=[P_], eng="pool")
            yield
            for (bk, v, b0, nb) in pairs_mm(KT, AT):
                k.tt(AK[:, b0:b0 + nb, :], v, strict.unsqueeze(1).to_broadcast([128, nb, 128]), ALU.mult, r=[bk, ctx.C], w=[AK])
            for (bk, v, b0, nb) in pairs_mm(BT, RT, width=LCH):
                k.tt(RB[:, b0:b0 + nb, :], v, incl.unsqueeze(1).to_broadcast([128, nb, LCH]), ALU.mult, r=[bk, ctx.C], w=[RB])
            for (bk, v, b0, nb) in pairs_mm(KT, RT, width=LCH):
                k.tt(RK[:, b0:b0 + nb, :], v, incl.unsqueeze(1).to_broadcast([128, nb, LCH]), ALU.mult, r=[bk, ctx.C], w=[RK])
            yield
            last_c = c == order[-1]
            if not last_c:
                for (src, dst) in ((BT, W.Bt), (KT, W.Kt)):
                    bk = ctx.bank()
                    pb = bk.t[:].bitcast(BF16)
                    for p in range(8):
                        k.tr(pb[:, p * 128:(p + 1) * 128], src[:, p, :], ctx.identb[:], r=[src, ctx.identb], w=[bk])
                    evac(dst[:], pb.rearrange("p (a b) -> p a b", b=128), [bk], [dst])
                yield
            for lvl in range(5):
                last = lvl == 4
                N2 = None if last else W.N[1 - ni]
                NT2 = W.NT[1 - ni]
                if not last:
                    for (bk, v, b0, nb) in pairs_mm(NT, N):
                        evac(N2[:, b0:b0 + nb, :], v, [bk], [N2])
                for (bk, v, b0, nb) in pairs_mm(N, NT):
                    evac(NT2[:, b0:b0 + nb, :], v, [bk], [NT2])
                yield
                P2 = W.P[1 - ni]
                for (bk, v, b0, nb) in pairs_mm(NT2, P_):
                    k.tt(P2[:, b0:b0 + nb, :], v, P_[:, b0:b0 + nb, :], ALU.add, r=[bk, P_], w=[P2])
                N, NT, P_ = N2, NT2, P2
                ni = 1 - ni
                yield
            X, U = W.X, W.U
            for (bk, v, b0, nb) in pairs_mm(AT, Mb, lhs2=AK, rhs2=Vb):
                evac(X[:, b0:b0 + nb, :], v, [bk], [X])
            yield
            for (bk, v, b0, nb) in pairs_mm(P_, X):
                evac(U[:, b0:b0 + nb, :], v, [bk], [U])
            yield
            yb = ctx.bank()
            for p in range(8):
                o = yb[:, p * LCH:(p + 1) * LCH]
                k.mm(o, Mb[:, p, :], RT[:, p, :], True, False, r=[Mb, RT], w=[yb])
                k.mm(o, U[:, p, :], RB[:, p, :], False, False, r=[U, RB], w=[yb])
                k.mm(o, Vb[:, p, :], RK[:, p, :], False, True, r=[Vb, RK], w=[yb])
            evac(W.ys[:], yb[:].rearrange("p (a b) -> p a b", b=LCH), [yb], [W.ys])
            k.dma("sp", S_y[d, t0 // 256, :, :, t0 % 256:t0 % 256 + LCH], W.ys[:], r=[W.ys], w=[tok_y])
            if not last_c:
                Mn = W.M[1 - mi]
                for (bk, v, b0, nb) in pairs_mm(W.Bt, U, lhs2=W.Kt, rhs2=Vb):
                    k.tt(Mn[:, b0:b0 + nb, :], v, Mst[:, b0:b0 + nb, :], ALU.add, r=[bk, Mst], w=[Mn])
                    k.tt(Mn[:, b0:b0 + nb, :], Mn[:, b0:b0 + nb, :], wt[:, b0:b0 + nb, c:c + 1].to_broadcast([128, nb, 128]), ALU.mult,
                         r=[Mn, wt], w=[Mn], eng="pool")
                Mb = W.Mb[1 - mi]
                k.copy(Mb[:], Mn[:], r=[Mn], w=[Mb], eng="act")
                Mst = Mn
                mi = 1 - mi
            yield

    jobs = [(s, d) for s in range(ctx.nseq) for d in (0, 1)]
    for g0 in range(0, len(jobs), nchains):
        gens = [chain(works[i], *jobs[g0 + i]) for i in range(min(nchains, len(jobs) - g0))]
        alive = list(gens)
        while alive:
            nxt = []
            for g in alive:
                try:
                    next(g)
                    nxt.append(g)
                except StopIteration:
                    pass
            alive = nxt
    ctx.P.release(mark)


def phase_c1(ctx, x_in, x_out, prm, S_y, S_bonus, S_g, tok_in, tok_s1, tok_y, tok_out):
    k = ctx.k
    n = ctx.n
    TT = 256
    mark = ctx.P.mark()
    Wo = load_w(ctx, prm["w_o"])
    lnw = k.tile([128, 8], F32)
    lnb = k.tile([128, 8], F32)
    k.dma("sp", lnw[:], prm["ln_w"].rearrange("(c p) -> p c", p=128), w=[lnw], allow_slow_non_contiguous=True)
    k.dma("sp", lnb[:], prm["ln_b"].rearrange("(c p) -> p c", p=128), w=[lnb], allow_slow_non_contiguous=True)
    rings = [k.ring(2, [128, 8, TT], BF16) for _ in range(4)] + [k.ring(2, [128, 8, TT], F32) for _ in range(3)]
    zTs = k.ring(2, [128, 8, TT], BF16)
    xrs = k.ring(2, [128, D], F32)
    xos = k.ring(2, [128, D], F32)
    outs = []
    fm = lambda ap2d, t0: ap2d.rearrange("(p q) t -> q p t", q=128)[:, :, t0:t0 + TT]
    f2 = lambda t: t[:].rearrange("p a b -> p (a b)")
    def step_gen(st):
        t0 = st * TT
        yf, yb, bon, gt, ysum, ysq, t3 = [r.next() for r in rings]
        k.dma("sp", yf[:], S_y[0, st], r=[tok_y], w=[yf])
        k.dma("sp", yb[:], S_y[1, st], r=[tok_y], w=[yb])
        k.dma("sp", bon[:], S_bonus[st], r=[tok_s1], w=[bon])
        k.dma("sp", gt[:], S_g[st], r=[tok_s1], w=[gt])
        yield
        k.tt(ysum[:], yf[:], yb[:], ALU.add, r=[yf, yb], w=[ysum])
        yield
        k.act(ysq[:], ysum[:], AF.Square, r=[ysum], w=[ysq])
        yield
        NH = (8 * TT) // 512
        for hf in range(NH):
            cs = slice(hf * 512, (hf + 1) * 512)
            mbk, qbk = ctx.bank(), ctx.bank()
            k.mm(mbk[:], ctx.cst("BLKM"), f2(ysum)[:, cs], r=[ctx.C, ysum], w=[mbk])
            k.mm(qbk[:], ctx.cst("BLKM"), f2(ysq)[:, cs], r=[ctx.C, ysq], w=[qbk])
            k.act(f2(ysq)[:, cs], mbk[:], AF.Square, r=[mbk], w=[ysq])
            k.tt(f2(ysq)[:, cs], qbk[:], f2(ysq)[:, cs], ALU.subtract, r=[qbk, ysq], w=[ysq])
            k.tt(f2(t3)[:, cs], f2(ysum)[:, cs], mbk[:], ALU.subtract, r=[ysum, mbk], w=[t3])
            yield
        k.ts(f2(ysq), f2(ysq), 64e-5, None, ALU.add, r=[ysq], w=[ysq], eng="pool")
        yield
        k.act(f2(ysq), f2(ysq), AF.Ln, r=[ysq], w=[ysq])
        yield
        k.act(f2(ysq), f2(ysq), AF.Exp, r=[ysq], w=[ysq], scale=-0.5)
        yield
        k.tt(t3[:], t3[:], ysq[:], ALU.mult, r=[t3, ysq], w=[t3])
        yield
        k.tt(t3[:], t3[:], lnw[:].unsqueeze(2).to_broadcast([128, 8, TT]), ALU.mult, r=[t3, lnw], w=[t3], eng="pool")
        yield
        k.tt(t3[:], t3[:], lnb[:].unsqueeze(2).to_broadcast([128, 8, TT]), ALU.add, r=[t3, lnb], w=[t3])
        yield
        k.tt(t3[:], t3[:], bon[:], ALU.add, r=[t3, bon], w=[t3])
        yield
        zT = zTs.next()
        k.tt(zT[:], t3[:], gt[:], ALU.mult, r=[t3, gt], w=[zT])
        yield
        for sub in range(TT // 128):
            ts0 = t0 + sub * 128
            xr = xrs.next()
            k.dma("sp", xr[:], x_in[ts0:ts0 + 128, :], r=[tok_in], w=[xr])
            xo = xos.next()
            for half in range(2):
                po = ctx.bank()
                for cc in range(8):
                    k.mm(po[:], zT[:, cc, sub * 128:(sub + 1) * 128], Wo[:, cc, half * 512:(half + 1) * 512], cc == 0, cc == 7, r=[zT, Wo], w=[po])
                k.tt(xo[:, half * 512:(half + 1) * 512], po[:], xr[:, half * 512:(half + 1) * 512], ALU.add, r=[po, xr], w=[xo])
            outs.append(k.dma("sp", x_out[ts0:ts0 + 128, :], xo[:], r=[xo], w=[tok_out]))
            yield

    nsteps = n // TT
    for s0 in range(0, nsteps, 2):
        alive = [step_gen(s0 + i) for i in range(min(2, nsteps - s0))]
        while alive:
            nxt = []
            for g_ in alive:
                try:
                    next(g_)
                    nxt.append(g_)
                except StopIteration:
                    pass
            alive = nxt
    ctx.P.release(mark)
    return outs
```

```python
import numpy as np
import concourse.bass as bass
import concourse.mybir as mybir
from concourse.bass_utils import run_bass_kernel_spmd

F32 = mybir.dt.float32
BF16 = mybir.dt.bfloat16
AF = mybir.ActivationFunctionType
ALU = mybir.AluOpType
AX = mybir.AxisListType

ENGS = ("pe", "act", "dve", "pool", "sp")
SEM_WRAP = 30000


class Buf:
    __slots__ = ("name", "ap", "w", "r")

    def __init__(self, name, ap=None):
        self.name = name
        self.ap = ap
        self.w = None
        self.r = {}


class Ins:
    __slots__ = ("eng", "fn", "deps", "sig", "sigval", "dma", "dsem", "dval", "prev_dma", "idx")

    def __init__(self, eng, fn, deps, dma=False):
        self.eng = eng
        self.fn = fn
        self.deps = deps
        self.sig = False
        self.sigval = None
        self.dma = dma
        self.dsem = None
        self.dval = None
        self.prev_dma = None


class Ring:
    def __init__(self, P, n, shape, dt, psum=False):
        self.slots = []
        for _ in range(n):
            t = P.ps(shape, dt) if psum else P.sb(shape, dt)
            self.slots.append((t, Buf("ring")))
        self.i = 0

    def next(self):
        s = self.slots[self.i % len(self.slots)]
        self.i += 1
        return s


class Prog:
    def __init__(self, nc, n_dma_sems=16, same_engine_sync=True):
        self.nc = nc
        self.streams = {e: [] for e in ENGS}
        self.n_dma_sems = n_dma_sems
        self.same_engine_sync = same_engine_sync
        self.dma_rr = {e: 0 for e in ENGS}
        self.dma_last = {}
        self.stack = []
        self.nbuf = 0
        self.extra = {e: [] for e in ENGS}

    def mark(self):
        return len(self.stack)

    def release(self, mark):
        deps = []
        for e in ENGS:
            for ins in reversed(self.streams[e]):
                if not ins.dma:
                    deps.append(ins)
                    break
        deps.extend(self.dma_last.values())
        for e in ENGS:
            self.extra[e] = list(deps)
        while len(self.stack) > mark:
            self.stack.pop().__exit__(None, None, None)

    def sb(self, shape, dt, name=None):
        self.nbuf += 1
        g = self.nc.sbuf_tensor(name or f"sb{self.nbuf}", list(shape), dt)
        t = g.__enter__()
        self.stack.append(g)
        return t

    def ps(self, shape, dt=F32, name=None):
        self.nbuf += 1
        g = self.nc.psum_tensor(name or f"ps{self.nbuf}", list(shape), dt)
        t = g.__enter__()
        self.stack.append(g)
        return t

    def buf(self, name="b"):
        return Buf(name)

    def ring(self, n, shape, dt, psum=False):
        return Ring(self, n, shape, dt, psum)

    def _deps(self, eng, reads, writes):
        deps = []
        for b in reads:
            if b.w is not None:
                deps.append(b.w)
        for b in writes:
            if b.w is not None:
                deps.append(b.w)
            deps.extend(b.r.values())
        if self.extra[eng]:
            deps.extend(self.extra[eng])
            self.extra[eng] = []
        return deps

    def op(self, eng, fn, reads=(), writes=()):
        deps = self._deps(eng, reads, writes)
        ins = Ins(eng, fn, deps)
        for b in writes:
            b.w = ins
            b.r = {}
        for b in reads:
            b.r[eng] = ins
        self.streams[eng].append(ins)
        return ins

    def dma(self, eng, out, in_, reads=(), writes=(), **kw):
        deps = self._deps(eng, reads, writes)

        def fn(e, out=out, in_=in_, kw=kw):
            return e.dma_start(out=out, in_=in_, **kw)

        ins = Ins(eng, fn, deps, dma=True)
        slot = (eng, self.dma_rr[eng] % self.n_dma_sems)
        self.dma_rr[eng] += 1
        ins.dsem = slot
        prev = self.dma_last.get(slot)
        ins.prev_dma = prev
        ins.dval = (prev.dval if prev is not None else 0) + 16
        self.dma_last[slot] = ins
        for b in writes:
            b.w = ins
            b.r = {}
        for b in reads:
            b.r[("dma", id(ins))] = ins
        self.streams[eng].append(ins)
        return ins

    def emit(self, final_waits=()):
        nc = self.nc
        for e in ENGS:
            for ins in self.streams[e]:
                for d in ins.deps:
                    if d.dma:
                        continue
                    if d.eng == "pe" and ins.eng == "pe":
                        continue
                    if d.eng == ins.eng and not self.same_engine_sync:
                        continue
                    d.sig = True
        for d in final_waits:
            if not d.dma:
                d.sig = True
        nsig = {}
        for e in ENGS:
            n = 0
            for ins in self.streams[e]:
                if ins.sig:
                    ins.sigval = (n // SEM_WRAP, n % SEM_WRAP + 1)
                    n += 1
            nsig[e] = n
        sems = {}
        guards = []

        def getsem(key):
            if key not in sems:
                g = nc.semaphore("s_" + "_".join(str(k) for k in key))
                sems[key] = g.__enter__()
                guards.append(g)
            return sems[key]

        for e in ENGS:
            for k in range((nsig[e] + SEM_WRAP - 1) // SEM_WRAP):
                getsem(("c", e, k))
        for slot in self.dma_last:
            getsem(("d",) + slot)

        engobj = {"pe": "tensor", "act": "scalar", "dve": "vector", "pool": "gpsimd", "sp": "sync"}
        streams = self.streams

        def run(e, eng):
            waited = {}
            for ins in streams[e]:
                need = {}
                for d in ins.deps:
                    if d.dma:
                        key = ("d",) + d.dsem
                        val = d.dval
                    else:
                        if d.eng == "pe" and e == "pe":
                            continue
                        if d.eng == e and not self.same_engine_sync:
                            continue
                        key = ("c", d.eng, d.sigval[0])
                        val = d.sigval[1]
                    if need.get(key, 0) < val:
                        need[key] = val
                if ins.dma and ins.prev_dma is not None:
                    key = ("d",) + ins.dsem
                    if need.get(key, 0) < ins.prev_dma.dval:
                        need[key] = ins.prev_dma.dval
                for key, val in need.items():
                    if waited.get(key, 0) < val:
                        eng.wait_ge(sems[key], val)
                        waited[key] = val
                bi = ins.fn(eng)
                if ins.dma:
                    bi.then_inc(sems[("d",) + ins.dsem], 16)
                elif ins.sig:
                    bi.then_inc(sems[("c", e, ins.sigval[0])], 1)
            if e == "sp":
                for d in final_waits:
                    if d.dma:
                        eng.wait_ge(sems[("d",) + d.dsem], d.dval)
                    else:
                        eng.wait_ge(sems[("c", d.eng, d.sigval[0])], d.sigval[1])

        with nc.Block() as block:
            @block.tensor
            def _(eng):
                run("pe", eng)

            @block.scalar
            def _(eng):
                run("act", eng)

            @block.vector
            def _(eng):
                run("dve", eng)

            @block.gpsimd
            def _(eng):
                run("pool", eng)

            @block.sync
            def _(eng):
                run("sp", eng)
        for g in reversed(guards):
            g.__exit__(None, None, None)

    def close(self):
        for g in reversed(self.stack):
            g.__exit__(None, None, None)
        self.stack = []


class Tile:
    __slots__ = ("t", "b")

    def __init__(self, t):
        self.t = t
        self.b = Buf("t")

    def __getitem__(self, k):
        return self.t[k]


def _tok(xs):
    out = []
    for x in xs:
        if x is None:
            continue
        out.append(x.b if isinstance(x, Tile) else x)
    return out


class K:
    def __init__(self, P):
        self.P = P

    def tile(self, shape, dt, psum=False):
        return Tile(self.P.ps(shape, dt) if psum else self.P.sb(shape, dt))

    def ring(self, n, shape, dt, psum=False):
        return TRing([self.tile(shape, dt, psum) for _ in range(n)])

    def dma(self, q, out, in_, r=(), w=(), **kw):
        return self.P.dma(q, out, in_, reads=_tok(r), writes=_tok(w), **kw)

    def act(self, out, in_, func, r=(), w=(), eng="act", **kw):
        return self.P.op(eng, lambda e: e.activation(out=out, in_=in_, func=func, **kw), _tok(r), _tok(w))

    def ts(self, out, in0, s1, s2, op0, op1=None, r=(), w=(), eng="dve", **kw):
        if op1 is None:
            return self.P.op(eng, lambda e: e.tensor_scalar(out=out, in0=in0, scalar1=s1, scalar2=None, op0=op0, **kw), _tok(r), _tok(w))
        return self.P.op(eng, lambda e: e.tensor_scalar(out=out, in0=in0, scalar1=s1, scalar2=s2, op0=op0, op1=op1, **kw), _tok(r), _tok(w))

    def tt(self, out, in0, in1, op, r=(), w=(), eng="dve"):
        return self.P.op(eng, lambda e: e.tensor_tensor(out=out, in0=in0, in1=in1, op=op), _tok(r), _tok(w))

    def stt(self, out, in0, scalar, in1, op0, op1, r=(), w=()):
        return self.P.op("dve", lambda e: e.scalar_tensor_tensor(out=out, in0=in0, scalar=scalar, in1=in1, op0=op0, op1=op1), _tok(r), _tok(w))

    def copy(self, out, in_, r=(), w=(), eng="dve"):
        if eng == "act":
            return self.P.op("act", lambda e: e.copy(out=out, in_=in_), _tok(r), _tok(w))
        return self.P.op(eng, lambda e: e.tensor_copy(out=out, in_=in_), _tok(r), _tok(w))

    def recip(self, out, in_, r=(), w=()):
        return self.P.op("dve", lambda e: e.reciprocal(out=out, in_=in_), _tok(r), _tok(w))

    def memset(self, out, val, w=(), eng="pool"):
        return self.P.op(eng, lambda e: e.memset(out, val), (), _tok(w))

    def reduce(self, out, in_, op, r=(), w=()):
        return self.P.op("dve", lambda e: e.tensor_reduce(out=out, in_=in_, axis=AX.X, op=op), _tok(r), _tok(w))

    def mm(self, out, lhsT, rhs, start=True, stop=True, r=(), w=()):
        return self.P.op("pe", lambda e: e.matmul(out, lhsT=lhsT, rhs=rhs, start=start, stop=stop), _tok(r), _tok(w))

    def tr(self, out, in_, ident, r=(), w=()):
        return self.P.op("pe", lambda e: e.transpose(out=out, in_=in_, identity=ident), _tok(r), _tok(w))


class TRing:
    def __init__(self, tiles):
        self.tiles = tiles
        self.i = 0

    def next(self):
        t = self.tiles[self.i % len(self.tiles)]
        self.i += 1
        return t


D = 1024
NMEM = 256
DFF = 2816
EPS = 1e-6


def build_consts():
    i = np.arange(128)
    s, t = i[:, None], i[None, :]
    c = {}
    f = lambda m: np.asarray(m, np.float32)
    c["ident"] = f(s == t)
    c["MU"] = f(s <= t)
    c["ML"] = f(s >= t)
    c["ONES"] = np.ones((128, 128), np.float32)
    c["NU"] = -f(s <= t)
    c["NL"] = -f(s >= t)
    c["NONES"] = -np.ones((128, 128), np.float32)
    c["BLK"] = f((s // 64) == (t // 64))
    c["BLKM"] = f((s // 64) == (t // 64)) / 64.0
    s6, t6 = s % 64, t % 64
    c["SF"] = f(s6 < t6)
    c["SB"] = f(s6 > t6)
    c["IF"] = f(s6 <= t6)
    c["IB"] = f(s6 >= t6)
    c["UN"] = -f(s <= t) / 16.0
    c["LN"] = -f(s >= t) / 16.0
    c["UC"] = -f(s > t) / 16.0
    c["LC"] = -f(s < t) / 16.0
    names = list(c)
    arr = np.concatenate([np.asarray(c[k], np.float32) for k in names], axis=1)
    offs = {k: j * 128 for j, k in enumerate(names)}
    return arr, offs


class Ctx:
    def __init__(self, nc, P, T, nseq, consts_ap):
        self.nc = nc
        self.P = P
        self.k = K(P)
        self.T = T
        self.nseq = nseq
        self.n = T * nseq
        k = self.k
        arr, offs = build_consts()
        self.coffs = offs
        self.C = k.tile([128, arr.shape[1]], F32)
        k.dma("sp", self.C[:], consts_ap, w=[self.C])
        self.identb = k.tile([128, 128], BF16)
        k.copy(self.identb[:], self.cst("ident"), r=[self.C], w=[self.identb])
        self.banks = k.ring(8, [128, 512], F32, psum=True)
        self.dram_tok = {}

    def cst(self, name):
        o = self.coffs[name]
        return self.C[:, o:o + 128]

    def bank(self):
        return self.banks.next()

    def dtok(self, key):
        if key not in self.dram_tok:
            self.dram_tok[key] = Buf(str(key))
        return self.dram_tok[key]


def load_gain_fm(ctx, g_ap, nchunk=8):
    k = ctx.k
    g = k.tile([128, nchunk], F32)
    k.dma("sp", g[:], g_ap.rearrange("(c p) -> p c", p=128), w=[g], allow_slow_non_contiguous=True)
    return g


def load_w(ctx, w_ap, gain_fm=None, dst=None, col0=0):
    k = ctx.k
    Kd, F = w_ap.shape
    nch = Kd // 128
    if dst is None:
        dst = k.tile([128, nch, F], BF16)
    for c in range(nch):
        k.dma("pool", dst[:, c, col0:col0 + F], w_ap[c * 128:(c + 1) * 128, :], w=[dst])
    if gain_fm is not None:
        for c in range(nch):
            k.ts(dst[:, c, col0:col0 + F], dst[:, c, col0:col0 + F], gain_fm[:, c:c + 1], None, ALU.mult,
                 r=[dst, gain_fm], w=[dst])
    return dst


class NormT:
    def __init__(self, ctx, with_xn=True, nxn=3, njunk=2):
        k = ctx.k
        self.ctx = ctx
        self.junk = k.ring(njunk, [128, D], BF16)
        self.ss = k.ring(4, [128, 1], F32)
        self.rs = k.ring(4, [128, 1], F32)
        if with_xn:
            self.xn = k.ring(nxn, [128, D], BF16)
        self.flip = 0

    def rstd(self, xt_ap, xt_tile, width=D, rows=128):
        k = self.ctx.k
        junk = self.junk.next()
        ss = self.ss.next()
        rs = self.rs.next()
        R = slice(0, rows)
        k.act(junk[R, 0:width], xt_ap, AF.Square, r=[xt_tile], w=[junk, ss], accum_out=ss[R, :])
        k.ts(rs[R, :], ss[R, :], 1.0 / width, EPS, ALU.mult, ALU.add, r=[ss], w=[rs])
        k.act(rs[R, :], rs[R, :], AF.Ln, r=[rs], w=[rs])
        k.act(rs[R, :], rs[R, :], AF.Exp, r=[rs], w=[rs], scale=-0.5)
        return rs

    def norm(self, xt):
        k = self.ctx.k
        rs = self.rstd(xt[:], xt)
        xn = self.xn.next()
        k.ts(xn[:], xt[:], rs[:], None, ALU.mult, r=[xt, rs], w=[xn])
        return xn

    def to_fm(self, xn, dst_ap, dst_tok, nchunk=8):
        ctx = self.ctx
        k = ctx.k
        bank = ctx.bank()
        pb = bank.t[:].bitcast(BF16)
        for c in range(nchunk):
            k.tr(pb[:, c * 128:(c + 1) * 128], xn[:, c * 128:(c + 1) * 128], ctx.identb[:], r=[xn, ctx.identb], w=[bank])
        src = pb[:, 0:nchunk * 128].rearrange("p (c t) -> p c t", c=nchunk)
        self.flip ^= 1
        k.copy(dst_ap, src, r=[bank], w=[dst_tok], eng="act" if self.flip else "dve")


def phase_ffn(ctx, x_in, x_out, wg, wu, wd, gain_ap, tok_in, tok_out, final_gain_ap=None):
    k = ctx.k
    n = ctx.n
    TB = 512
    NF = DFF // 128
    mark = ctx.P.mark()
    g_fm = load_gain_fm(ctx, gain_ap)
    Wgu = k.tile([128, 8, 2 * DFF], BF16)
    load_w(ctx, wg, None, dst=Wgu, col0=0)
    load_w(ctx, wu, None, dst=Wgu, col0=DFF)
    for c in range(8):
        k.ts(Wgu[:, c, :], Wgu[:, c, :], g_fm[:, c:c + 1], None, ALU.mult, r=[Wgu, g_fm], w=[Wgu])
    Wd = load_w(ctx, wd)
    nt = NormT(ctx, nxn=2, njunk=1)
    xts = k.ring(2, [128, D], F32)
    hTs = k.ring(1, [128, 8, TB], BF16)
    actT = k.tile([128, NF, TB], BF16)
    act_parts = [Buf("a") for _ in range(NF)]
    sgs = k.ring(2, [128, TB], F32)
    xos = k.ring(2, [128, D], F32)
    if final_gain_ap is not None:
        gbc = k.tile([128, D], F32)
        k.dma("sp", gbc[:], final_gain_ap.partition_broadcast(128), w=[gbc])
    outs = []
    for blk in range(n // TB):
        hT = hTs.next()
        for j in range(TB // 128):
            t0 = blk * TB + j * 128
            xt = xts.next()
            k.dma("sp", xt[:], x_in[t0:t0 + 128, :], r=[tok_in], w=[xt])
            xn = nt.norm(xt)
            nt.to_fm(xn, hT[:, :, j * 128:(j + 1) * 128], hT)
        for f in range(NF):
            pg = ctx.bank()
            pu = ctx.bank()
            for c in range(8):
                k.mm(pg[:, 0:TB], Wgu[:, c, f * 128:(f + 1) * 128], hT[:, c, :], c == 0, c == 7, r=[Wgu, hT], w=[pg])
            for c in range(8):
                k.mm(pu[:, 0:TB], Wgu[:, c, DFF + f * 128:DFF + (f + 1) * 128], hT[:, c, :], c == 0, c == 7, r=[Wgu, hT], w=[pu])
            sg = sgs.next()
            k.act(sg[:], pg[:, 0:TB], AF.Silu, r=[pg], w=[sg])
            k.tt(actT[:, f, :], sg[:], pu[:, 0:TB], ALU.mult, r=[sg, pu], w=[act_parts[f]])
        for j in range(TB // 128):
            t0 = blk * TB + j * 128
            xo = xos.next()
            k.dma("sp", xo[:], x_in[t0:t0 + 128, :], r=[tok_in], w=[xo])
            for half in range(2):
                po = ctx.bank()
                for f in range(NF):
                    k.mm(po[:], actT[:, f, j * 128:(j + 1) * 128], Wd[:, f, half * 512:(half + 1) * 512], f == 0, f == NF - 1,
                         r=[act_parts[f], Wd], w=[po])
                k.tt(xo[:, half * 512:(half + 1) * 512], po[:], xo[:, half * 512:(half + 1) * 512], ALU.add, r=[po, xo], w=[xo])
            if final_gain_ap is not None:
                rs = nt.rstd(xo[:], xo)
                k.stt(xo[:], xo[:], rs[:], gbc[:], ALU.mult, ALU.mult, r=[xo, rs, gbc], w=[xo])
            outs.append(k.dma("sp", x_out[t0:t0 + 128, :], xo[:], r=[xo], w=[tok_out]))
    ctx.P.release(mark)
    return outs


def phase_xattn(ctx, x_in, x_out, mem, wq, wkv, wo, g_x_ap, g_mem_ap, tok_in, tok_out):
    k = ctx.k
    n, T = ctx.n, ctx.T
    TB = 512
    mark = ctx.P.mark()
    gx = load_gain_fm(ctx, g_x_ap)
    gm = load_gain_fm(ctx, g_mem_ap)
    Wkv = load_w(ctx, wkv, gm)
    Wq = load_w(ctx, wq, gx)
    Wo = load_w(ctx, wo)
    nt = NormT(ctx)
    xts = k.ring(3, [128, D], F32)
    KT = k.tile([128, ctx.nseq, 8, NMEM], BF16)
    V = k.tile([128, ctx.nseq, 2, D], BF16)
    memT = k.tile([128, 8, NMEM], BF16)
    flip = 0
    for s in range(ctx.nseq):
        for j in range(2):
            xt = xts.next()
            k.dma("sp", xt[:], mem[s * NMEM + j * 128:s * NMEM + (j + 1) * 128, :], w=[xt])
            xn = nt.norm(xt)
            nt.to_fm(xn, memT[:, :, j * 128:(j + 1) * 128], memT)
        for f in range(8):
            b = ctx.bank()
            for c in range(8):
                k.mm(b[:, 0:NMEM], Wkv[:, c, f * 128:(f + 1) * 128], memT[:, c, :], c == 0, c == 7, r=[Wkv, memT], w=[b])
            flip ^= 1
            k.copy(KT[:, s, f, :], b[:, 0:NMEM], r=[b], w=[KT], eng="act" if flip else "dve")
        for j in range(2):
            for half in range(2):
                b = ctx.bank()
                for c in range(8):
                    k.mm(b[:], memT[:, c, j * 128:(j + 1) * 128], Wkv[:, c, D + half * 512:D + (half + 1) * 512], c == 0, c == 7,
                         r=[Wkv, memT], w=[b])
                flip ^= 1
                k.copy(V[:, s, j, half * 512:(half + 1) * 512], b[:], r=[b], w=[V], eng="act" if flip else "dve")
    hTs = k.ring(2, [128, 8, TB], BF16)
    qT = k.tile([128, 8, TB], BF16)
    qparts = [Buf("q") for _ in range(8)]
    pT = k.tile([128, 8, TB], BF16)
    pparts = [Buf("p") for _ in range(TB // 128)]
    oT = k.tile([128, 8, TB], BF16)
    oparts = [Buf("o") for _ in range(8)]
    mxs = k.ring(2, [128, 4], F32)
    nmxs = k.ring(2, [128, 4], F32)
    rsums = k.ring(2, [128, 4], F32)
    rinvs = k.ring(2, [128, 4], F32)
    ps_ = k.ring(2, [128, 4, NMEM], BF16)
    pns = k.ring(2, [128, 4, NMEM], BF16)
    xrs = k.ring(2, [128, D], F32)
    xos = k.ring(2, [128, D], F32)
    outs = []
    for blk in range(n // TB):
        s = (blk * TB) // T
        hT = hTs.next()
        for j in range(TB // 128):
            t0 = blk * TB + j * 128
            xt = xts.next()
            k.dma("sp", xt[:], x_in[t0:t0 + 128, :], r=[tok_in], w=[xt])
            xn = nt.norm(xt)
            nt.to_fm(xn, hT[:, :, j * 128:(j + 1) * 128], hT)
        for f in range(8):
            b = ctx.bank()
            for c in range(8):
                k.mm(b[:], Wq[:, c, f * 128:(f + 1) * 128], hT[:, c, :], c == 0, c == 7, r=[Wq, hT], w=[b])
            flip ^= 1
            k.copy(qT[:, f, :], b[:], r=[b], w=[qparts[f]], eng="act" if flip else "dve")
        for j in range(TB // 128):
            cols = slice(j * 128, (j + 1) * 128)
            b2 = [ctx.bank(), ctx.bank()]
            for h in range(4):
                b = b2[h // 2]
                for e in range(2):
                    k.mm(b[:, (h % 2) * 256:(h % 2 + 1) * 256], qT[:, 2 * h + e, cols], KT[:, s, 2 * h + e, :], e == 0, e == 1,
                         r=[qparts[2 * h + e], KT], w=[b])
            mx = mxs.next()
            for i in range(2):
                k.reduce(mx[:, 2 * i:2 * i + 2], b2[i][:].rearrange("p (h m) -> p h m", h=2), ALU.max, r=[b2[i]], w=[mx])
            nmx = nmxs.next()
            k.ts(nmx[:], mx[:], -1.0 / 16.0, None, ALU.mult, r=[mx], w=[nmx])
            p = ps_.next()
            rsum = rsums.next()
            for h in range(4):
                k.act(p[:, h, :], b2[h // 2][:, (h % 2) * 256:(h % 2 + 1) * 256], AF.Exp, r=[b2[h // 2], nmx], w=[p, rsum],
                      bias=nmx[:, h:h + 1], scale=1.0 / 16.0, accum_out=rsum[:, h:h + 1])
            rinv = rinvs.next()
            k.recip(rinv[:], rsum[:], r=[rsum], w=[rinv])
            pn = pns.next()
            k.tt(pn[:], p[:], rinv[:].unsqueeze(2).to_broadcast([128, 4, NMEM]), ALU.mult, r=[p, rinv], w=[pn])
            bank = ctx.bank()
            pb = bank.t[:].bitcast(BF16)
            for h in range(4):
                for e in range(2):
                    i = 2 * h + e
                    k.tr(pb[:, i * 128:(i + 1) * 128], pn[:, h, e * 128:(e + 1) * 128], ctx.identb[:], r=[pn, ctx.identb], w=[bank])
            flip ^= 1
            k.copy(pT[:, :, cols], pb.rearrange("p (c t) -> p c t", c=8), r=[bank], w=[pparts[j]], eng="act" if flip else "dve")
        for f in range(8):
            h = f // 2
            b = ctx.bank()
            for jm in range(2):
                k.mm(b[:], V[:, s, jm, f * 128:(f + 1) * 128], pT[:, 2 * h + jm, :], jm == 0, jm == 1, r=[V] + pparts, w=[b])
            flip ^= 1
            k.copy(oT[:, f, :], b[:], r=[b], w=[oparts[f]], eng="act" if flip else "dve")
        for j in range(TB // 128):
            t0 = blk * TB + j * 128
            cols = slice(j * 128, (j + 1) * 128)
            xr = xrs.next()
            k.dma("sp", xr[:], x_in[t0:t0 + 128, :], r=[tok_in], w=[xr])
            xo = xos.next()
            for half in range(2):
                po = ctx.bank()
                for c in range(8):
                    k.mm(po[:], oT[:, c, cols], Wo[:, c, half * 512:(half + 1) * 512], c == 0, c == 7, r=[oparts[c], Wo], w=[po])
                k.tt(xo[:, half * 512:(half + 1) * 512], po[:], xr[:, half * 512:(half + 1) * 512], ALU.add, r=[po, xr], w=[xo])
            outs.append(k.dma("sp", x_out[t0:t0 + 128, :], xo[:], r=[xo], w=[tok_out]))
    ctx.P.release(mark)
    return outs


FM_ROWS = 1568
TOKW = 2320


def phase_a0(ctx, x_in, w_in, gain_ap, S_fm, S_tok, tok_in, tok_fm, tok_tok):
    k = ctx.k
    n = ctx.n
    TB = 512
    mark = ctx.P.mark()
    g_fm = load_gain_fm(ctx, gain_ap)
    Win = load_w(ctx, w_in, g_fm)
    nt = NormT(ctx)
    xts = k.ring(3, [128, D], F32)
    hTs = k.ring(2, [128, 8, TB], BF16)
    fos = k.ring(3, [128, TB], F32)
    tks = k.ring(2, [128, TOKW], F32)
    fm_cols = [(j * 128, 128) for j in range(8)] + [(2064 + j * 128, 128) for j in range(4)] + [(3600, 32)]
    tok_groups = [(1024, 512, 0), (1536, 512, 512), (2048, 16, 1024), (2320, 256, 1040), (2576, 512, 1296), (3088, 512, 1808)]
    flip = 0
    for blk in range(n // TB):
        hT = hTs.next()
        for j in range(TB // 128):
            t0 = blk * TB + j * 128
            xt = xts.next()
            k.dma("sp", xt[:], x_in[t0:t0 + 128, :], r=[tok_in], w=[xt])
            xn = nt.norm(xt)
            nt.to_fm(xn, hT[:, :, j * 128:(j + 1) * 128], hT)
        for i, (c0, m) in enumerate(fm_cols):
            b = ctx.bank()
            for c in range(8):
                k.mm(b[0:m, :], Win[:, c, c0:c0 + m], hT[:, c, :], c == 0, c == 7, r=[Win, hT], w=[b])
            fo = fos.next()
            flip ^= 1
            k.copy(fo[0:m, :], b[0:m, :], r=[b], w=[fo], eng="act" if flip else "dve")
            r0 = i * 128
            k.dma("sp", S_fm[r0:r0 + m, blk * TB:(blk + 1) * TB], fo[0:m, :], r=[fo], w=[tok_fm])
        for j in range(TB // 128):
            t0 = blk * TB + j * 128
            tk = tks.next()
            for (c0, w, o0) in tok_groups:
                b = ctx.bank()
                for c in range(8):
                    k.mm(b[:, 0:w], hT[:, c, j * 128:(j + 1) * 128], Win[:, c, c0:c0 + w], c == 0, c == 7, r=[Win, hT], w=[b])
                flip ^= 1
                k.copy(tk[:, o0:o0 + w], b[:, 0:w], r=[b], w=[tk], eng="act" if flip else "dve")
            k.dma("sp", S_tok[t0:t0 + 128, :], tk[:], r=[tk], w=[tok_tok])
    ctx.P.release(mark)


class MixL0:
    def __init__(self, ctx, S_fm, S_tok, S_cb, S_sb, conv_ap, igb_ap, fgb_ap, mnorm_ap, w2_ap, db_ap, gnorm_ap, wout_ap,
                 tok_fm, tok_tok):
        self.ctx = ctx
        k = self.k = ctx.k
        self.S_fm, self.S_tok, self.S_cb, self.S_sb = S_fm, S_tok, S_cb, S_sb
        self.tok_fm, self.tok_tok = tok_fm, tok_tok
        self.tok_cb = Buf("cb")
        self.cw = k.tile([128, 3, 8], F32)
        for j in range(3):
            k.dma("sp", self.cw[:, j, :], conv_ap[j].rearrange("(c p) -> p c", p=128), w=[self.cw], allow_slow_non_contiguous=True)
        k.ts(self.cw[:], self.cw[:], 0.5, None, ALU.mult, r=[self.cw], w=[self.cw])
        self.gb = k.tile([128, 16], F32)
        k.dma("sp", self.gb[:, 0:8], igb_ap.rearrange("a b -> (a b)").partition_broadcast(128), w=[self.gb])
        k.dma("sp", self.gb[:, 8:16], fgb_ap.rearrange("a b -> (a b)").partition_broadcast(128), w=[self.gb])
        self.mnorm = k.tile([128, 512], F32)
        k.dma("sp", self.mnorm[:], mnorm_ap.partition_broadcast(128), w=[self.mnorm])
        self.gnorm = k.tile([128, 512], F32)
        k.dma("sp", self.gnorm[:], gnorm_ap.partition_broadcast(128), w=[self.gnorm])
        k.ts(self.mnorm[:], self.mnorm[:], 0.5, None, ALU.mult, r=[self.mnorm], w=[self.mnorm])
        k.ts(self.gnorm[:], self.gnorm[:], 0.5, None, ALU.mult, r=[self.gnorm], w=[self.gnorm])
        self.dbias = k.tile([128, 512], F32)
        k.dma("sp", self.dbias[:], db_ap.rearrange("a b -> (a b)").partition_broadcast(128), w=[self.dbias])
        self.w2p = k.tile([32, 2, 256], F32)
        k.memset(self.w2p[:], 0.0, w=[self.w2p])
        k.dma("sp", self.w2p[0:16, 0, :], w2_ap[0], w=[self.w2p])
        k.dma("sp", self.w2p[16:32, 1, :], w2_ap[1], w=[self.w2p])
        self.Wout = load_w(ctx, wout_ap)
        r = k.ring
        self.Xs = r(2, [128, 8, 130], F32)
        self.z1s = r(2, [128, 8, 128], F32)
        self.z2s = r(2, [128, 8, 128], F32)
        self.QKs = r(2, [128, 8, 128], BF16)
        self.TKs = r(2, [128, TOKW], F32)
        self.vps = r(2, [128, 4, 129], BF16)
        for t in self.vps.tiles:
            k.memset(t[:, :, 128:129], 1.0, w=[t])
        self.g8 = [r(2, [128, 8], F32) for _ in range(8)]
        self.glrs = r(2, [32, 128], F32)
        self.gqks = r(2, [128, 4, 128], F32)
        self.w512 = [r(2, [128, 512], F32) for _ in range(8)]
        self.khats = r(2, [128, 2, 256], BF16)
        self.ktz = r(2, [128, 8, 128], BF16)
        self.qtz = r(2, [128, 8, 128], BF16)
        self.thBs = r(2, [128, 512], F32)
        self.qts = r(2, [128, 512], BF16)
        self.kts = r(2, [128, 512], BF16)
        self.gvbs = r(2, [128, 512], BF16)
        self.Cfb = r(2, [128, 4, 129], BF16)
        self.Cbb = r(2, [128, 4, 129], BF16)
        self.Sfb = r(2, [128, 2, 128], BF16)
        self.Sbb = r(2, [128, 2, 128], BF16)
        for rr_ in (self.ktz, self.qtz):
            for t in rr_.tiles:
                k.memset(t[:], 0.0, w=[t])
        self.kToks = r(2, [128, 4, 128], BF16)
        self.CF = r(2, [128, 4, 129], F32)
        self.CB = r(3, [128, 4, 129], F32)
        self.SF = r(2, [128, 2, 128], F32)
        self.SB = r(3, [128, 2, 128], F32)
        self.pFB = r(2, [128, 8, 128], BF16)
        self.pA = r(2, [128, 8, 128], BF16)
        self.hms = r(2, [128, 4, 128], F32)
        self.small = [r(2, [128, 8], F32) for _ in range(8)]
        self.junks = r(2, [128, 128], F32)
        self.merged = r(2, [128, D], BF16)
        self.mTs = r(2, [128, 8, 128], BF16)
        self.xos = r(2, [128, D], F32)
        self.nt = NormT(ctx, with_xn=False)

    def prep(self, s, c, d_state):
        ctx, k = self.ctx, self.k
        T = ctx.T
        nch = T // 128
        t0 = s * T + c * 128
        o = {}
        X = self.Xs.next()
        lo = 1 if c == 0 else 0
        hi = 129 if c == nch - 1 else 130
        if c == 0:
            k.memset(X[:, :, 0:1], 0.0, w=[X])
        if c == nch - 1:
            k.memset(X[:, :, 129:130], 0.0, w=[X])
        k.dma("sp", X[:, :, lo:hi], self.S_fm[0:1024, t0 - 1 + lo:t0 - 1 + hi].rearrange("(c p) t -> p c t", p=128),
              r=[self.tok_fm], w=[X])
        yield
        z1 = self.z1s.next()
        z2 = self.z2s.next()
        cwb = lambda j: self.cw[:, j, :].unsqueeze(2).to_broadcast([128, 8, 128])
        k.tt(z1[:], X[:, :, 0:128], cwb(0), ALU.mult, r=[X, self.cw], w=[z1])
        yield
        k.tt(z2[:], X[:, :, 1:129], cwb(1), ALU.mult, r=[X, self.cw], w=[z2])
        yield
        k.tt(z1[:], z1[:], z2[:], ALU.add, r=[z1, z2], w=[z1])
        yield
        k.tt(z2[:], X[:, :, 2:130], cwb(2), ALU.mult, r=[X, self.cw], w=[z2])
        yield
        k.tt(z1[:], z1[:], z2[:], ALU.add, r=[z1, z2], w=[z1])
        yield
        k.act(z2[:], z1[:], AF.Tanh, r=[z1], w=[z2])
        yield
        QK = self.QKs.next()
        k.stt(QK[:], z2[:], 1.0, z1[:], ALU.add, ALU.mult, r=[z1, z2], w=[QK])
        yield
        o["QK"] = QK
        TK = self.TKs.next()
        k.dma("sp", TK[:], self.S_tok[t0:t0 + 128, :], r=[self.tok_tok], w=[TK])
        o["TK"] = TK
        vp = self.vps.next()
        k.copy(vp[:, :, 0:128], TK[:, 0:512].rearrange("p (h d) -> p h d", h=4), r=[TK], w=[vp], eng="pool")
        o["vp"] = vp
        thA = self.w512[7].next()
        thB = self.thBs.next()
        k.act(thA[:], TK[:, 512:1024], AF.Tanh, r=[TK], w=[thA], scale=0.5)
        yield
        k.act(thB[:], TK[:, 1808:2320], AF.Tanh, r=[TK], w=[thB], scale=0.5)
        yield
        o["thA"], o["thB"] = thA, thB
        gvb = self.gvbs.next()
        k.copy(gvb[:], TK[:, 1296:1808], r=[TK], w=[gvb], eng="act")
        yield
        o["gvb"] = gvb
        g = [rr.next() for rr in self.g8]
        ig, zf, l1f, t1, sw, qe, eg, kw = g
        k.tt(ig[:], TK[:, 1024:1032], self.gb[:, 0:8], ALU.add, r=[TK, self.gb], w=[ig])
        k.tt(zf[:], TK[:, 1032:1040], self.gb[:, 8:16], ALU.add, r=[TK, self.gb], w=[zf])
        yield
        k.act(zf[:], zf[:], AF.Exp, r=[zf], w=[zf], scale=-1.0)
        yield
        k.act(l1f[:], zf[:], AF.Ln, r=[zf], w=[l1f], bias=1.0)
        yield
        Gb = ctx.bank()
        k.mm(Gb[:, 0:4], ctx.cst("NU"), l1f[:, 0:4], r=[ctx.C, l1f], w=[Gb])
        k.mm(Gb[:, 4:8], ctx.cst("NL"), l1f[:, 4:8], r=[ctx.C, l1f], w=[Gb])
        k.mm(Gb[:, 8:16], ctx.cst("NONES"), l1f[:, 0:8], r=[ctx.C, l1f], w=[Gb])
        k.tt(t1[:], ig[:], Gb[:, 0:8], ALU.subtract, r=[ig, Gb], w=[t1])
        k.act(qe[:], Gb[:, 0:8], AF.Exp, r=[Gb], w=[qe])
        k.act(eg[:], Gb[:, 8:16], AF.Exp, r=[Gb], w=[eg])
        yield
        k.act(sw[:], t1[:], AF.Exp, r=[t1], w=[sw])
        yield
        k.ts(qe[:], qe[:], 128.0 ** -0.5, None, ALU.mult, r=[qe], w=[qe])
        yield
        k.tt(kw[:], sw[:], eg[:], ALU.mult, r=[sw, eg], w=[kw])
        yield
        o.update(sw=sw, qe=qe, eg=eg, kw=kw)
        kTb = ctx.bank()
        kTpb = kTb.t[:].bitcast(BF16)
        for h in range(4):
            k.tr(kTpb[:, h * 128:(h + 1) * 128], QK[:, 4 + h, :], ctx.identb[:], r=[QK, ctx.identb], w=[kTb])
        kTok = self.kToks.next()
        k.tt(kTok[:], kTpb[:, 0:512].rearrange("p (h t) -> p h t", h=4),
             kw[:, d_state * 4:(d_state + 1) * 4].unsqueeze(2).to_broadcast([128, 4, 128]), ALU.mult, r=[kTb, kw], w=[kTok])
        yield
        o["kTok"] = kTok
        glr = self.glrs.next()
        k.dma("sp", glr[:], self.S_fm[1536:1568, t0:t0 + 128], r=[self.tok_fm], w=[glr])
        gqk = self.gqks.next()
        k.dma("sp", gqk[:], self.S_fm[1024:1536, t0:t0 + 128].rearrange("(c p) t -> p c t", p=128), r=[self.tok_fm], w=[gqk])
        w = [rr.next() for rr in self.w512[0:7]]
        zb, l1, eT, emT, _q, _k, egmb = w
        qtT, ktT = self.qts.next(), self.kts.next()
        zbk = ctx.bank()
        for d in range(2):
            k.mm(zbk[:, d * 256:(d + 1) * 256], glr[:], self.w2p[:, d, :], r=[glr, self.w2p], w=[zbk])
        k.tt(zb[:], zbk[:], self.dbias[:], ALU.add, r=[zbk, self.dbias], w=[zb])
        yield
        k.act(zb[:], zb[:], AF.Exp, r=[zb], w=[zb], scale=-1.0)
        yield
        k.act(l1[:], zb[:], AF.Ln, r=[zb], w=[l1], bias=1.0)
        yield
        bTb = ctx.bank()
        for d in range(2):
            for j in range(2):
                i = d * 2 + j
                k.mm(bTb[:, i * 128:(i + 1) * 128], l1[:, d * 256 + j * 128:d * 256 + (j + 1) * 128],
                     ctx.cst("UN" if d == 0 else "LN"), r=[l1, ctx.C], w=[bTb])
        k.act(eT[:], bTb[:], AF.Exp, r=[bTb], w=[eT])
        k.act(emT[:], bTb[:], AF.Exp, r=[bTb], w=[emT], scale=-1.0)
        yield
        v4 = lambda t: t[:].rearrange("p (a b) -> p a b", a=4)
        for d in range(2):
            k.stt(v4(qtT)[:, d * 2:(d + 1) * 2, :], gqk[:, 0:2, :], 0.125, v4(eT)[:, d * 2:(d + 1) * 2, :], ALU.mult, ALU.mult,
                  r=[gqk, eT], w=[qtT])
            yield
            k.tt(v4(ktT)[:, d * 2:(d + 1) * 2, :], gqk[:, 2:4, :], v4(emT)[:, d * 2:(d + 1) * 2, :], ALU.mult, r=[gqk, emT], w=[ktT])
            yield
        gmb = ctx.bank()
        k.mm(gmb[:, 0:256], ctx.cst("UC"), l1[:, 0:256], r=[ctx.C, l1], w=[gmb])
        k.mm(gmb[:, 256:512], ctx.cst("LC"), l1[:, 256:512], r=[ctx.C, l1], w=[gmb])
        k.act(egmb[:], gmb[:], AF.Exp, r=[gmb], w=[egmb])
        yield
        khat = self.khats.next()
        for d in range(2):
            k.tt(khat[:, d, :], TK[:, 1040:1296], egmb[:, d * 256:(d + 1) * 256], ALU.mult, r=[TK, egmb], w=[khat])
            yield
        o.update(eT=eT, qtT=qtT, ktT=ktT, khat=khat)
        return o

    def state_update(self, o, d, Cold, Sold, Cring, Sring):
        ctx, k = self.ctx, self.k
        TK, vp = o["TK"], o["vp"]
        kTok = o["kTok"]
        Cn = Cring.next()
        for p in range(2):
            b = ctx.bank()
            bv = b[:, 0:258].rearrange("p (h e) -> p h e", h=2)
            for hh in range(2):
                h = 2 * p + hh
                k.mm(bv[:, hh, :], kTok[:, h, :], vp[:, h, :], r=[kTok, vp], w=[b])
            for hh in range(2):
                h = 2 * p + hh
                k.stt(Cn[:, h, :], Cold[:, h, :], o["eg"][:, d * 4 + h:d * 4 + h + 1], bv[:, hh, :], ALU.mult, ALU.add,
                      r=[Cold, o["eg"], b], w=[Cn])
            yield
        Sn = Sring.next()
        eT4 = o["eT"][:].rearrange("p (a b) -> p a b", a=4)
        col = 127 if d == 0 else 0
        for j in range(2):
            b = ctx.bank()
            k.mm(b[:, 0:256], o["khat"][:, d, j * 128:(j + 1) * 128], o["gvb"][:, j * 256:(j + 1) * 256], r=[o["khat"], o["gvb"]], w=[b])
            for e in range(2):
                rows = slice(e * 64, (e + 1) * 64)
                k.stt(Sn[rows, j, :], Sold[rows, j, :], eT4[rows, d * 2 + j, col:col + 1], b[rows, e * 128:(e + 1) * 128],
                      ALU.mult, ALU.add, r=[Sold, o["eT"], b], w=[Sn])
            yield
        return Cn, Sn

    def pass1(self, s):
        ctx, k = self.ctx, self.k
        nch = ctx.T // 128
        Cb = self.CB.next()
        Sb = self.SB.next()
        k.memset(Cb[:], 0.0, w=[Cb])
        k.memset(Sb[:], 0.0, w=[Sb])
        for c in range(nch - 1, -1, -1):
            idx = s * nch + c
            k.dma("sp", self.S_cb[idx], Cb[:].rearrange("p h e -> p (h e)"), r=[Cb], w=[self.tok_cb])
            k.dma("sp", self.S_sb[idx], Sb[:].rearrange("p j v -> p (j v)"), r=[Sb], w=[self.tok_cb])
            yield
            if c == 0:
                break
            o = yield from self.prep(s, c, 1)
            Cb, Sb = yield from self.state_update(o, 1, Cb, Sb, self.CB, self.SB)

    def pass2(self, s, x_in, x_out, tok_in, tok_out, outs):
        ctx, k = self.ctx, self.k
        T = ctx.T
        nch = T // 128
        Cf = self.CF.next()
        Sf = self.SF.next()
        k.memset(Cf[:], 0.0, w=[Cf])
        k.memset(Sf[:], 0.0, w=[Sf])
        Cfb, Sfb = self.Cfb.next(), self.Sfb.next()
        k.memset(Cfb[:], 0.0, w=[Cfb])
        k.memset(Sfb[:], 0.0, w=[Sfb])
        MU, ML = ctx.cst("MU"), ctx.cst("ML")
        import os
        kp2 = int(os.environ.get("KP2", "9"))
        for c in range(nch):
            t0 = s * T + c * 128
            idx = s * nch + c
            o = yield from self.prep(s, c, 0)
            QK, TK, vp = o["QK"], o["TK"], o["vp"]
            Cb = self.CB.next()
            Sb = self.SB.next()
            k.dma("sp", Cb[:].rearrange("p h e -> p (h e)"), self.S_cb[idx], r=[self.tok_cb], w=[Cb])
            k.dma("sp", Sb[:].rearrange("p j v -> p (j v)"), self.S_sb[idx], r=[self.tok_cb], w=[Sb])
            Cbb, Sbb = self.Cbb.next(), self.Sbb.next()
            k.copy(Cbb[:], Cb[:], r=[Cb], w=[Cbb], eng="act")
            k.copy(Sbb[:], Sb[:], r=[Sb], w=[Sbb], eng="pool")
            yield
            sb_ = ctx.bank()
            for h in range(4):
                k.mm(sb_[:, h * 128:(h + 1) * 128], QK[:, 4 + h, :], QK[:, h, :], r=[QK], w=[sb_])
            pFB = self.pFB.next()
            for d in range(2):
                for h in range(4):
                    k.stt(pFB[:, d * 4 + h, :], sb_[:, h * 128:(h + 1) * 128], o["sw"][:, d * 4 + h:d * 4 + h + 1], MU if d == 0 else ML,
                          ALU.mult, ALU.mult, r=[sb_, o["sw"], ctx.C], w=[pFB])
            if kp2 <= 1:
                continue
            yield
            sm = [rr.next() for rr in self.small]
            d1, nd, d2, rr_, ss, rs, ss2, rs2 = sm
            nb = {}
            for p in range(2):
                for d in range(2):
                    b = ctx.bank()
                    bv = b[:, 0:258].rearrange("p (h e) -> p h e", h=2)
                    Cst = Cfb if d == 0 else Cbb
                    for hh in range(2):
                        h = 2 * p + hh
                        k.mm(bv[:, hh, :], pFB[:, d * 4 + h, :], vp[:, h, :], True, False, r=[pFB, vp], w=[b])
                        k.mm(bv[:, hh, :], QK[:, h, :], Cst[:, h, :], False, True, r=[QK, Cst], w=[b])
                    nb[(p, d)] = (b, bv)
                    k.tt(d1[:, d * 4 + 2 * p:d * 4 + 2 * p + 2], bv[:, :, 128], o["qe"][:, d * 4 + 2 * p:d * 4 + 2 * p + 2], ALU.mult,
                         r=[b, o["qe"]], w=[d1])
            k.ts(nd[:], d1[:], -1.0, None, ALU.mult, r=[d1], w=[nd])
            k.tt(d2[:], d1[:], nd[:], ALU.max, r=[d1, nd], w=[d2])
            k.ts(d2[:], d2[:], 1.0, None, ALU.max, r=[d2], w=[d2])
            k.recip(d2[:], d2[:], r=[d2], w=[d2])
            k.tt(rr_[:], d2[:], o["qe"][:], ALU.mult, r=[d2, o["qe"]], w=[rr_])
            hm = self.hms.next()
            for h in range(4):
                p, hh = h // 2, h % 2
                bF, bvF = nb[(p, 0)]
                bB, bvB = nb[(p, 1)]
                k.ts(hm[:, h, :], bvF[:, hh, 0:128], rr_[:, h:h + 1], None, ALU.mult, r=[bF, rr_], w=[hm])
                k.stt(hm[:, h, :], bvB[:, hh, 0:128], rr_[:, 4 + h:5 + h], hm[:, h, :], ALU.mult, ALU.add, r=[bB, rr_, hm], w=[hm])
            yield
            for h in range(4):
                junk = self.junks.next()
                k.act(junk[:], hm[:, h, :], AF.Square, r=[hm], w=[junk, ss], accum_out=ss[:, h:h + 1])
            yield
            k.ts(rs[:, 0:4], ss[:, 0:4], 1.0 / 128, EPS, ALU.mult, ALU.add, r=[ss], w=[rs])
            yield
            k.act(rs[:, 0:4], rs[:, 0:4], AF.Ln, r=[rs], w=[rs])
            yield
            k.act(rs[:, 0:4], rs[:, 0:4], AF.Exp, r=[rs], w=[rs], scale=-0.5)
            yield
            wA = o["thA"]
            k.stt(wA[:], wA[:], 1.0, self.mnorm[:], ALU.add, ALU.mult, r=[wA, self.mnorm], w=[wA])
            yield
            mg = self.merged.next()
            for h in range(4):
                k.stt(mg[:, h * 128:(h + 1) * 128], hm[:, h, :], rs[:, h:h + 1], wA[:, h * 128:(h + 1) * 128], ALU.mult, ALU.mult,
                      r=[hm, rs, wA], w=[mg])
            if kp2 <= 2:
                continue
            yield
            qt4 = o["qtT"][:].rearrange("p (a b) -> p a b", a=4)
            kt4 = o["ktT"][:].rearrange("p (a b) -> p a b", a=4)
            pA = self.pA.next()
            ktz, qtz = self.ktz.next(), self.qtz.next()
            for d in range(2):
                for h in range(4):
                    j, e = h // 2, h % 2
                    rows = slice(e * 64, (e + 1) * 64)
                    k.copy(ktz[rows, d * 4 + h, :], kt4[rows, d * 2 + j, :], r=[o["ktT"]], w=[ktz], eng="pool")
                    k.copy(qtz[rows, d * 4 + h, :], qt4[rows, d * 2 + j, :], r=[o["qtT"]], w=[qtz])
            yield
            for d in range(2):
                b = ctx.bank()
                for h in range(4):
                    j, e = h // 2, h % 2
                    k.mm(b[:, h * 128:(h + 1) * 128], ktz[:, d * 4 + h, :], qt4[:, d * 2 + j, :], r=[ktz, o["qtT"]], w=[b])
                k.tt(pA[:, d * 4:(d + 1) * 4, :], b[:].rearrange("p (h t) -> p h t", h=4),
                     (MU if d == 0 else ML).unsqueeze(1).to_broadcast([128, 4, 128]), ALU.mult, r=[b, ctx.C], w=[pA])
                yield
            if kp2 <= 3:
                continue
            ob = ctx.bank()
            for h in range(4):
                j, e = h // 2, h % 2
                rows = slice(e * 64, (e + 1) * 64)
                gv = o["gvb"][:, h * 128:(h + 1) * 128]
                dst = ob[:, h * 128:(h + 1) * 128]
                k.mm(dst, pA[:, h, :], gv, True, False, r=[pA, o["gvb"]], w=[ob])
                k.mm(dst, pA[:, 4 + h, :], gv, False, False, r=[pA, o["gvb"]], w=[ob])
                k.mm(dst, qtz[:, h, :], Sfb[:, j, :], False, False, r=[qtz, Sfb], w=[ob])
                k.mm(dst, qtz[:, 4 + h, :], Sbb[:, j, :], False, True, r=[qtz, Sbb], w=[ob])
            for h in range(4):
                junk = self.junks.next()
                k.act(junk[:], ob[:, h * 128:(h + 1) * 128], AF.Square, r=[ob], w=[junk, ss2], accum_out=ss2[:, h:h + 1])
            k.ts(rs2[:, 0:4], ss2[:, 0:4], 1.0 / 128, EPS, ALU.mult, ALU.add, r=[ss2], w=[rs2])
            k.act(rs2[:, 0:4], rs2[:, 0:4], AF.Ln, r=[rs2], w=[rs2])
            k.act(rs2[:, 0:4], rs2[:, 0:4], AF.Exp, r=[rs2], w=[rs2], scale=-0.5)
            wB = o["thB"]
            k.stt(wB[:], wB[:], 1.0, TK[:, 1808:2320], ALU.add, ALU.mult, r=[wB, TK], w=[wB])
            k.tt(wB[:], wB[:], self.gnorm[:], ALU.mult, r=[wB, self.gnorm], w=[wB], eng="pool")
            for h in range(4):
                k.stt(mg[:, 512 + h * 128:512 + (h + 1) * 128], ob[:, h * 128:(h + 1) * 128], rs2[:, h:h + 1],
                      wB[:, h * 128:(h + 1) * 128], ALU.mult, ALU.mult, r=[ob, rs2, wB], w=[mg])
            if kp2 <= 4:
                continue
            yield
            if c < nch - 1:
                Cf, Sf = yield from self.state_update(o, 0, Cf, Sf, self.CF, self.SF)
                Cfb, Sfb = self.Cfb.next(), self.Sfb.next()
                k.copy(Cfb[:], Cf[:], r=[Cf], w=[Cfb], eng="act")
                k.copy(Sfb[:], Sf[:], r=[Sf], w=[Sfb], eng="pool")
                yield
            mT = self.mTs.next()
            self.nt.to_fm(mg, mT[:], mT)
            yield
            xo = self.xos.next()
            k.dma("sp", xo[:], x_in[t0:t0 + 128, :], r=[tok_in], w=[xo])
            for half in range(2):
                po = ctx.bank()
                for cc in range(8):
                    k.mm(po[:], mT[:, cc, :], self.Wout[:, cc, half * 512:(half + 1) * 512], cc == 0, cc == 7, r=[mT, self.Wout], w=[po])
                k.tt(xo[:, half * 512:(half + 1) * 512], po[:], xo[:, half * 512:(half + 1) * 512], ALU.add, r=[po, xo], w=[xo])
                yield
            outs.append(k.dma("sp", x_out[t0:t0 + 128, :], xo[:], r=[xo], w=[tok_out]))
            yield


def phase_b0(ctx, x_in, x_out, S_fm, S_tok, S_cb, S_sb, prm, tok_in, tok_fm, tok_tok, tok_out):
    mark = ctx.P.mark()
    mx = MixL0(ctx, S_fm, S_tok, S_cb, S_sb, prm["conv"], prm["igb"], prm["fgb"], prm["mnorm"], prm["w2"], prm["db"],
               prm["gnorm"], prm["wout"], tok_fm, tok_tok)
    outs = []
    import os
    kb0 = int(os.environ.get("KB0", "9"))
    if kb0 == 0:
        return list(ctx.P.dma_last.values())
    def run_all(gens):
        alive = list(gens)
        while alive:
            nxt = []
            for g_ in alive:
                try:
                    next(g_)
                    nxt.append(g_)
                except StopIteration:
                    pass
            alive = nxt
    for s0 in range(0, ctx.nseq, 2):
        ss_ = list(range(s0, min(s0 + 2, ctx.nseq)))
        run_all([mx.pass1(s) for s in ss_])
        run_all([mx.pass2(s, x_in, x_out, tok_in, tok_out, outs) for s in ss_])
    ctx.P.release(mark)
    return outs


C0 = float(np.exp(-0.5))
LCH = 64


def phase_a1(ctx, x_in, prm, S1, S_wt, S_vtok, S_bonus, S_g, tok_in, tok_s1):
    k = ctx.k
    n, T = ctx.n, ctx.T
    TB = 256
    NQ = TB // LCH
    mark = ctx.P.mark()
    g_fm = load_gain_fm(ctx, prm["gain"])
    Wr = load_w(ctx, prm["w_rkv"][0], g_fm)
    Wk = load_w(ctx, prm["w_rkv"][1], g_fm)
    Wv = load_w(ctx, prm["w_rkv"][2], g_fm)
    W1 = k.tile([128, 8, 352], BF16)
    load_w(ctx, prm["w1"][0], None, dst=W1, col0=0)
    load_w(ctx, prm["w1"][1], None, dst=W1, col0=64)
    load_w(ctx, prm["a1"], None, dst=W1, col0=128)
    load_w(ctx, prm["g1"], None, dst=W1, col0=192)
    for c in range(8):
        k.ts(W1[:, c, :], W1[:, c, :], g_fm[:, c:c + 1], None, ALU.mult, r=[W1, g_fm], w=[W1])
    w2t = k.tile([64, 3, D], BF16)
    k.dma("pool", w2t[:, 0, :], prm["w2"][0], w=[w2t])
    k.dma("pool", w2t[:, 1, :], prm["w2"][1], w=[w2t])
    k.dma("pool", w2t[:, 2, :], prm["a2"], w=[w2t])
    g2t = k.tile([128, 2, D], BF16)
    k.dma("pool", g2t[:, 0, :], prm["g2"][0:128, :], w=[g2t])
    k.dma("pool", g2t[0:32, 1, :], prm["g2"][128:160, :], w=[g2t])
    pc = k.tile([128, 7, 8], F32)
    srcs = [prm["w0"][0], prm["w0"][1], prm["a0"], prm["k_k"], prm["k_a"], prm["k_a"], prm["r_k"].rearrange("h d -> (h d)")]
    for i, sap in enumerate(srcs):
        k.dma("sp", pc[:, i, :], sap.rearrange("(c p) -> p c", p=128), w=[pc], allow_slow_non_contiguous=True)
    k.ts(pc[:, 5, :], pc[:, 5, :], -1.0, 1.0, ALU.mult, ALU.add, r=[pc], w=[pc])
    mu = k.tile([128, 6, 8], F32)
    for i in range(6):
        k.dma("sp", mu[:, i, :], prm["mu"][i].rearrange("(c p) -> p c", p=128), w=[mu], allow_slow_non_contiguous=True)
    nt = NormT(ctx, with_xn=False)
    xts = k.ring(2, [128, D], F32)
    xnf = k.ring(1, [128, D], F32)
    xh = k.ring(1, [2, D], F32)
    hTs = k.ring(1, [128, 8, TB + 2], F32)
    hhs = k.ring(1, [128, 8, TB], F32)
    mix_sets = [[k.tile([128, 8, TB], BF16) for _ in range(6)] for _ in range(2)]
    lows = k.ring(2, [128, 5, TB], BF16)
    NWAY = 2
    fsets = [[k.tile([128, TB], F32) for _ in range(13)] for _ in range(NWAY)]
    vts = k.ring(2, [128, D], BF16)
    ob4s = k.ring(5, [128, 4, TB], BF16)
    bg16 = k.ring(4, [128, TB], BF16)
    wtall = k.ring(2, [128, 2, 8, NQ], F32)
    ident = ctx.cst("ident")
    flipb = [0]

    def prologue(blk):
        mixes = mix_sets[blk % 2]
        b0 = blk * TB
        tpos = b0 % T
        hT = hTs.next()
        xhh = xh.next()
        k.memset(xhh[:], 0.0, w=[xhh])
        if tpos > 0:
            k.dma("sp", xhh[0:1, :], x_in[b0 - 1:b0, :], r=[tok_in], w=[xhh])
        if tpos + TB < T:
            k.dma("sp", xhh[1:2, :], x_in[b0 + TB:b0 + TB + 1, :], r=[tok_in], w=[xhh])
        rs = nt.rstd(xhh[:], xhh, rows=2)
        xn2 = xhh
        k.ts(xn2[:], xhh[:], rs[0:2, :], None, ALU.mult, r=[xhh, rs], w=[xn2])
        bk = ctx.bank()
        for c in range(8):
            k.tr(bk[:, c * 2:(c + 1) * 2], xn2[0:2, c * 128:(c + 1) * 128], ident[0:2, 0:2], r=[xn2, ctx.C], w=[bk])
        bkv = bk[:, 0:16].rearrange("p (c e) -> p c e", e=2)
        k.copy(hT[:, :, 0], bkv[:, :, 0], r=[bk], w=[hT])
        k.copy(hT[:, :, TB + 1], bkv[:, :, 1], r=[bk], w=[hT])
        yield
        for j in range(TB // 128):
            t0 = b0 + j * 128
            xt = xts.next()
            k.dma("sp", xt[:], x_in[t0:t0 + 128, :], r=[tok_in], w=[xt])
            rs = nt.rstd(xt[:], xt)
            xn = xnf.next()
            k.ts(xn[:], xt[:], rs[:], None, ALU.mult, r=[xt, rs], w=[xn])
            yield
            for half in range(2):
                bk = ctx.bank()
                for c4 in range(4):
                    c = half * 4 + c4
                    k.tr(bk[:, c4 * 128:(c4 + 1) * 128], xn[:, c * 128:(c + 1) * 128], ident, r=[xn, ctx.C], w=[bk])
                flipb[0] ^= 1
                k.copy(hT[:, half * 4:(half + 1) * 4, 1 + j * 128:1 + (j + 1) * 128], bk[:].rearrange("p (c t) -> p c t", c=4),
                       r=[bk], w=[hT], eng="act" if flipb[0] else "dve")
                yield
        hh = hhs.next()
        k.tt(hh[:], hT[:, :, 0:TB], hT[:, :, 2:TB + 2], ALU.add, r=[hT], w=[hh], eng="pool")
        yield
        k.stt(hh[:], hh[:], 0.5, hT[:, :, 1:TB + 1], ALU.mult, ALU.subtract, r=[hh, hT], w=[hh])
        yield
        for i in range(6):
            for c in range(8):
                k.stt(mixes[i][:, c, :], hh[:, c, :], mu[:, i, c:c + 1], hT[:, c, 1:TB + 1], ALU.mult, ALU.add, r=[hh, mu, hT], w=[mixes[i]])
                yield
        xr, xw, xk, xv, xa, xg = mixes
        low = lows.next()
        specs = [(xw, 0, 64, 0, AF.Tanh), (xw, 64, 64, 1, AF.Tanh), (xa, 128, 64, 2, AF.Copy), (xg, 192, 128, 3, AF.Sigmoid), (xg, 320, 32, 4, AF.Sigmoid)]
        for (src, c0, m, slot, fn) in specs:
            bk = ctx.bank()
            for c in range(8):
                k.mm(bk[0:m, 0:TB], W1[:, c, c0:c0 + m], src[:, c, :], c == 0, c == 7, r=[W1, src], w=[bk])
            k.act(low[0:m, slot, :], bk[0:m, 0:TB], fn, r=[bk], w=[low])
            yield
        for j in range(TB // 128):
            t0 = b0 + j * 128
            vt = vts.next()
            for half in range(2):
                bk = ctx.bank()
                for c in range(8):
                    k.mm(bk[:], xv[:, c, j * 128:(j + 1) * 128], Wv[:, c, half * 512:(half + 1) * 512], c == 0, c == 7, r=[xv, Wv], w=[bk])
                flipb[0] ^= 1
                k.copy(vt[:, half * 512:(half + 1) * 512], bk[:], r=[bk], w=[vt], eng="act" if flipb[0] else "dve")
                yield
            k.dma("sp", S_vtok[t0:t0 + 128, :], vt[:], r=[vt], w=[tok_s1])
            yield
        wta = wtall.next()
        wta_parts = [Buf("wt") for _ in range(16)]
        return (xr, xk, xv, low, b0, blk, wta, wta_parts)

    def fc_chain(fc, F, B):
        xr, xk, xv, low, b0, blk, wta, wta_parts = B
        fs = slice(fc * 128, (fc + 1) * 128)
        r_, k_, v_, a_, kk, k2, t1, t2, sg, G, cI, cE, W = F
        col = lambda i: pc[:, i, fc:fc + 1]

        def proj(Wt, src):
            bk = ctx.bank()
            for c in range(8):
                k.mm(bk[:, 0:TB], Wt[:, c, fs], src[:, c, :], c == 0, c == 7, r=[Wt, src], w=[bk])
            return bk
        bk = proj(Wr, xr)
        k.copy(r_[:], bk[:, 0:TB], r=[bk], w=[r_], eng="act")
        yield
        bk = proj(Wk, xk)
        k.copy(k_[:], bk[:, 0:TB], r=[bk], w=[k_])
        yield
        bk = proj(Wv, xv)
        k.copy(v_[:], bk[:, 0:TB], r=[bk], w=[v_], eng="act")
        yield
        bk = ctx.bank()
        k.mm(bk[:, 0:TB], w2t[:, 2, fs], low[0:64, 2, :], r=[w2t, low], w=[bk])
        k.act(a_[:], bk[:, 0:TB], AF.Sigmoid, r=[bk, pc], w=[a_], bias=col(2))
        yield
        bk = ctx.bank()
        k.mm(bk[:, 0:TB], g2t[:, 0, fs], low[:, 3, :], True, False, r=[g2t, low], w=[bk])
        k.mm(bk[:, 0:TB], g2t[0:32, 1, fs], low[0:32, 4, :], False, True, r=[g2t, low], w=[bk])
        gb16 = bg16.next()
        k.copy(gb16[:], bk[:, 0:TB], r=[bk], w=[gb16])
        k.dma("sp", S_g[blk, :, fc, :], gb16[:], r=[gb16], w=[tok_s1])
        yield
        k.ts(kk[:], k_[:], col(3), None, ALU.mult, r=[k_, pc], w=[kk])
        yield
        k.act(t2[:], kk[:], AF.Square, r=[kk], w=[t2])
        yield
        bk = ctx.bank()
        k.mm(bk[:, 0:TB], ctx.cst("BLK"), t2[:], r=[ctx.C, t2], w=[bk])
        k.ts(t2[:], bk[:, 0:TB], 1e-24, None, ALU.max, r=[bk], w=[t2])
        yield
        k.act(t2[:], t2[:], AF.Ln, r=[t2], w=[t2])
        yield
        k.act(t2[:], t2[:], AF.Exp, r=[t2], w=[t2], scale=-0.5)
        yield
        k.tt(kk[:], kk[:], t2[:], ALU.mult, r=[kk, t2], w=[kk])
        k.ts(k2[:], a_[:], col(4), col(5), ALU.mult, ALU.add, r=[a_, pc], w=[k2])
        yield
        k.tt(k2[:], k2[:], k_[:], ALU.mult, r=[k2, k_], w=[k2])
        yield
        k.stt(t2[:], r_[:], col(6), k2[:], ALU.mult, ALU.mult, r=[r_, pc, k2], w=[t2])
        yield
        bk = ctx.bank()
        k.mm(bk[:, 0:TB], ctx.cst("BLK"), t2[:], r=[ctx.C, t2], w=[bk])
        bb16 = bg16.next()
        k.tt(bb16[:], bk[:, 0:TB], v_[:], ALU.mult, r=[bk, v_], w=[bb16])
        k.dma("sp", S_bonus[blk, :, fc, :], bb16[:], r=[bb16], w=[tok_s1])
        k.tt(a_[:], a_[:], kk[:], ALU.mult, r=[a_, kk], w=[a_], eng="pool")
        yield
        for d in range(2):
            bk = ctx.bank()
            k.mm(bk[:, 0:TB], w2t[:, d, fs], low[0:64, d, :], r=[w2t, low], w=[bk])
            k.act(sg[:], bk[:, 0:TB], AF.Sigmoid, r=[bk, pc], w=[sg], bias=col(d))
            yield
            ctx.P.op("dve", lambda e, G=G, sg=sg: e.tensor_tensor_scan(out=G[:], data0=sg[:], data1=sg[:], initial=0.0, op0=ALU.add, op1=ALU.bypass),
                     _tok([sg]), _tok([G]))
            yield
            G3 = G[:].rearrange("p (q t) -> p q t", t=LCH)
            c3 = cI[:].rearrange("p (q t) -> p q t", t=LCH)
            e3 = cE[:].rearrange("p (q t) -> p q t", t=LCH)
            k.copy(c3[:, 0, :], G3[:, 0, :], r=[G], w=[cI], eng="pool")
            k.tt(c3[:, 1:NQ, :], G3[:, 1:NQ, :], G3[:, 0:NQ - 1, LCH - 1:LCH].to_broadcast([128, NQ - 1, LCH]), ALU.subtract, r=[G], w=[cI])
            yield
            tot = c3[:, :, LCH - 1:LCH]
            k.act(wta[:, d, fc, :], c3[:, :, LCH - 1], AF.Exp, r=[cI], w=[wta_parts[d * 8 + fc]], scale=-C0)
            if d == 0:
                k.tt(cE[:], cI[:], sg[:], ALU.subtract, r=[cI, sg], w=[cE], eng="pool")
                inc, exc = cI, cE
                yield
            else:
                k.tt(e3, tot.to_broadcast([128, NQ, LCH]), c3, ALU.subtract, r=[cI], w=[cE])
                yield
                k.tt(G[:], cE[:], sg[:], ALU.add, r=[cE, sg], w=[G], eng="pool")
                inc, exc = G, cE
                yield
            base = d * 4
            k.act(W[:], inc[:], AF.Exp, r=[inc], w=[W], scale=-C0)
            yield
            ob = ob4s.next()
            k.tt(ob[:, 3, :], r_[:], W[:], ALU.mult, r=[r_, W], w=[ob])
            yield
            k.act(W[:], inc[:], AF.Exp, r=[inc], w=[W], scale=C0)
            yield
            k.tt(ob[:, 2, :], k2[:], W[:], ALU.mult, r=[k2, W], w=[ob])
            k.tt(ob[:, 1, :], a_[:], W[:], ALU.mult, r=[a_, W], w=[ob], eng="pool")
            yield
            k.act(W[:], exc[:], AF.Exp, r=[exc], w=[W], scale=-C0)
            yield
            k.stt(ob[:, 0, :], kk[:], -1.0, W[:], ALU.mult, ALU.mult, r=[kk, W], w=[ob])
            k.dma("sp", S1[base:base + 4, fs, b0:b0 + TB].rearrange("a q t -> q a t"), ob[:], r=[ob], w=[tok_s1])
            yield

    def step(g_):
        try:
            next(g_)
            return True, None
        except StopIteration as e_:
            return False, e_.value

    nblk = n // TB
    pro = prologue(0)
    while True:
        ok, val = step(pro)
        if not ok:
            Bcur = val
            break
    for blk in range(nblk):
        pro = prologue(blk + 1) if blk + 1 < nblk else None
        Bnext = None
        for g0 in range(0, 8, NWAY):
            alive = [fc_chain(g0 + i_, fsets[i_], Bcur) for i_ in range(NWAY)]
            while alive:
                if pro is not None:
                    ok, val = step(pro)
                    if not ok:
                        Bnext, pro = val, None
                alive = [g_ for g_ in alive if step(g_)[0]]
        while pro is not None:
            ok, val = step(pro)
            if not ok:
                Bnext, pro = val, None
        wta, wta_parts = Bcur[6], Bcur[7]
        for d in range(2):
            k.dma("sp", S_wt[d].rearrange("(p q) c -> q p c", q=128)[:, :, blk * NQ:(blk + 1) * NQ], wta[:, d, :, :],
                  r=wta_parts[d * 8:(d + 1) * 8], w=[tok_s1], allow_slow_non_contiguous=True)
        Bcur = Bnext
    ctx.P.release(mark)


def phase_b1(ctx, x_in, x_out, prm, S1, S_wt, S_vtok, S_bonus, S_g, S_yb, tok_in, tok_s1, tok_out):
    k = ctx.k
    n, T = ctx.n, ctx.T
    NCH = T // LCH
    mark = ctx.P.mark()
    Wo = load_w(ctx, prm["w_o"])
    lnw = k.tile([128, 8], F32)
    lnb = k.tile([128, 8], F32)
    k.dma("sp", lnw[:], prm["ln_w"].rearrange("(c p) -> p c", p=128), w=[lnw], allow_slow_non_contiguous=True)
    k.dma("sp", lnb[:], prm["ln_b"].rearrange("(c p) -> p c", p=128), w=[lnb], allow_slow_non_contiguous=True)
    tok_yb = Buf("yb")

    def bdring(nslots):
        rr = k.ring(nslots, [128, 8, 128], F32)
        for t in rr.tiles:
            k.memset(t[:], 0.0, w=[t])
        return rr
    ATs, BTs, KTs, Vbs = bdring(2), bdring(2), bdring(2), bdring(2)
    RTs = k.ring(2, [128, 8, LCH], F32)
    wts = k.ring(2, [128, 8], F32)
    big = lambda nslots: k.ring(nslots, [128, 8, 128], F32)
    Ns, NTs, Ps = big(2), big(2), big(2)
    AKs, Xs, Us, Bts, Kts = big(1), big(1), big(1), big(1), big(1)
    Ms = big(2)
    tmpM = big(1)
    RBs = k.ring(1, [128, 8, LCH], F32)
    RKs = k.ring(1, [128, 8, LCH], F32)
    ysb = k.ring(2, [128, 8, LCH], F32)
    ybl = k.ring(2, [128, 8, LCH], F32)
    o512 = [k.ring(2, [128, 8, LCH], F32) for _ in range(5)]
    zTs = k.ring(2, [128, 8, LCH], BF16)
    xrs = k.ring(2, [LCH, D], F32)
    xos = k.ring(2, [LCH, D], F32)
    ident = ctx.cst("ident")
    outs = []
    flip = [0]

    def bd_src(ap2d, t0):
        v = ap2d.rearrange("(p e k) t -> e k p t", e=2, k=64)
        return [v[e][:, :, t0:t0 + LCH] for e in range(2)]

    def evac(dst_ap, src_ap, r, w):
        flip[0] ^= 1
        k.copy(dst_ap, src_ap, r=r, w=w, eng="act" if flip[0] else "dve")

    def pairs_mm(lhs_tile, rhs_tile, width=128, lhs2=None, rhs2=None):
        per_bank = 512 // width
        res = []
        for b0 in range(0, 8, per_bank):
            bk = ctx.bank()
            for p in range(b0, b0 + per_bank):
                o = bk[:, (p - b0) * width:(p - b0 + 1) * width]
                k.mm(o, lhs_tile[:, p, :], rhs_tile[:, p, :], True, lhs2 is None, r=[lhs_tile, rhs_tile], w=[bk])
                if lhs2 is not None:
                    k.mm(o, lhs2[:, p, :], rhs2[:, p, :], False, True, r=[lhs2, rhs2], w=[bk])
            res.append((bk, bk[:].rearrange("p (a b) -> p a b", b=width), b0, per_bank))
        return res

    for s in range(ctx.nseq):
        for d in (1, 0):
            base = d * 4
            Mst = Ms.next()
            k.memset(Mst[:], 0.0, w=[Mst])
            strict = ctx.cst("SF" if d == 0 else "SB")
            strictT = ctx.cst("SB" if d == 0 else "SF")
            incl = ctx.cst("IF" if d == 0 else "IB")[:, 0:LCH]
            order = range(NCH) if d == 0 else range(NCH - 1, -1, -1)
            for c in order:
                t0 = s * T + c * LCH
                cg = t0 // LCH
                AT, BT, KT, Vb = ATs.next(), BTs.next(), KTs.next(), Vbs.next()
                for e in range(2):
                    rows = slice(e * 64, (e + 1) * 64)
                    k.dma("sp", AT[rows, :, rows], bd_src(S1[base + 0], t0)[e], r=[tok_s1], w=[AT])
                    k.dma("sp", BT[rows, :, rows], bd_src(S1[base + 1], t0)[e], r=[tok_s1], w=[BT])
                    k.dma("sp", KT[rows, :, rows], bd_src(S1[base + 2], t0)[e], r=[tok_s1], w=[KT])
                    k.dma("sp", Vb[rows, :, rows], S_vtok[t0:t0 + LCH, :].rearrange("t (p e v) -> e t p v", e=2, v=64)[e],
                          r=[tok_s1], w=[Vb])
                RT = RTs.next()
                k.dma("sp", RT[:], S1[base + 3].rearrange("(p q) t -> q p t", q=128)[:, :, t0:t0 + LCH], r=[tok_s1], w=[RT])
                wt = wts.next()
                k.dma("sp", wt[:], S_wt[d].rearrange("(p q) c -> q p c", q=128)[:, :, cg], r=[tok_s1], w=[wt], allow_slow_non_contiguous=True)
                N, NT, P_ = Ns.next(), NTs.next(), Ps.next()
                AK = AKs.next()
                for (bk, v, b0, nb) in pairs_mm(BT, AT):
                    k.tt(N[:, b0:b0 + nb, :], v, strict.unsqueeze(1).to_broadcast([128, nb, 128]), ALU.mult, r=[bk, ctx.C], w=[N])
                for (bk, v, b0, nb) in pairs_mm(AT, BT):
                    k.tt(NT[:, b0:b0 + nb, :], v, strictT.unsqueeze(1).to_broadcast([128, nb, 128]), ALU.mult, r=[bk, ctx.C], w=[NT])
                for (bk, v, b0, nb) in pairs_mm(KT, AT):
                    k.tt(AK[:, b0:b0 + nb, :], v, strict.unsqueeze(1).to_broadcast([128, nb, 128]), ALU.mult, r=[bk, ctx.C], w=[AK])
                RB, RK = RBs.next(), RKs.next()
                for (bk, v, b0, nb) in pairs_mm(BT, RT, width=LCH):
                    k.tt(RB[:, b0:b0 + nb, :], v, incl.unsqueeze(1).to_broadcast([128, nb, LCH]), ALU.mult, r=[bk, ctx.C], w=[RB])
                for (bk, v, b0, nb) in pairs_mm(KT, RT, width=LCH):
                    k.tt(RK[:, b0:b0 + nb, :], v, incl.unsqueeze(1).to_broadcast([128, nb, LCH]), ALU.mult, r=[bk, ctx.C], w=[RK])
                k.tt(P_[:], N[:], ident.unsqueeze(1).to_broadcast([128, 8, 128]), ALU.add, r=[N, ctx.C], w=[P_], eng="pool")
                for lvl in range(5):
                    last = lvl == 4
                    N2 = None if last else Ns.next()
                    NT2 = NTs.next()
                    if not last:
                        for (bk, v, b0, nb) in pairs_mm(NT, N):
                            evac(N2[:, b0:b0 + nb, :], v, [bk], [N2])
                    for (bk, v, b0, nb) in pairs_mm(N, NT):
                        evac(NT2[:, b0:b0 + nb, :], v, [bk], [NT2])
                    P2 = Ps.next()
                    for (bk, v, b0, nb) in pairs_mm(NT2, P_):
                        k.tt(P2[:, b0:b0 + nb, :], v, P_[:, b0:b0 + nb, :], ALU.add, r=[bk, P_], w=[P2])
                    N, NT, P_ = N2, NT2, P2
                X = Xs.next()
                for (bk, v, b0, nb) in pairs_mm(AT, Mst, lhs2=AK, rhs2=Vb):
                    evac(X[:, b0:b0 + nb, :], v, [bk], [X])
                U = Us.next()
                for (bk, v, b0, nb) in pairs_mm(P_, X):
                    evac(U[:, b0:b0 + nb, :], v, [bk], [U])
                yb = ctx.bank()
                for p in range(8):
                    o = yb[:, p * LCH:(p + 1) * LCH]
                    k.mm(o, Mst[:, p, :], RT[:, p, :], True, False, r=[Mst, RT], w=[yb])
                    k.mm(o, U[:, p, :], RB[:, p, :], False, False, r=[U, RB], w=[yb])
                    k.mm(o, Vb[:, p, :], RK[:, p, :], False, True, r=[Vb, RK], w=[yb])
                yv = yb[:].rearrange("p (a b) -> p a b", b=LCH)
                if d == 1:
                    ys = ysb.next()
                    k.copy(ys[:], yv, r=[yb], w=[ys])
                    k.dma("sp", S_yb.rearrange("(p q) t -> q p t", q=128)[:, :, t0:t0 + LCH], ys[:], r=[ys], w=[tok_yb])
                if c != order[-1]:
                    Bt, Kt = Bts.next(), Kts.next()
                    for (src, dst) in ((BT, Bt), (KT, Kt)):
                        for b0 in (0, 4):
                            bk = ctx.bank()
                            for p in range(b0, b0 + 4):
                                k.tr(bk[:, (p - b0) * 128:(p - b0 + 1) * 128], src[:, p, :], ident, r=[src, ctx.C], w=[bk])
                            evac(dst[:, b0:b0 + 4, :], bk[:].rearrange("p (a b) -> p a b", b=128), [bk], [dst])
                    Mn = Ms.next()
                    tm = tmpM.next()
                    for (bk, v, b0, nb) in pairs_mm(Bt, U, lhs2=Kt, rhs2=Vb):
                        k.tt(tm[:, b0:b0 + nb, :], v, Mst[:, b0:b0 + nb, :], ALU.add, r=[bk, Mst], w=[tm])
                        k.tt(Mn[:, b0:b0 + nb, :], tm[:, b0:b0 + nb, :], wt[:, b0:b0 + nb].unsqueeze(2).to_broadcast([128, nb, 128]), ALU.mult,
                             r=[tm, wt], w=[Mn], eng="pool")
                    Mst = Mn
                if d == 0:
                    ybt = ybl.next()
                    k.dma("sp", ybt[:], S_yb.rearrange("(p q) t -> q p t", q=128)[:, :, t0:t0 + LCH], r=[tok_yb], w=[ybt])
                    bon = o512[0].next()
                    gt = o512[1].next()
                    k.dma("sp", bon[:], S_bonus.rearrange("(p q) t -> q p t", q=128)[:, :, t0:t0 + LCH], r=[tok_s1], w=[bon])
                    k.dma("sp", gt[:], S_g.rearrange("(p q) t -> q p t", q=128)[:, :, t0:t0 + LCH], r=[tok_s1], w=[gt])
                    ysum, ysq, t3 = o512[2].next(), o512[3].next(), o512[4].next()
                    k.tt(ysum[:], yv, ybt[:], ALU.add, r=[yb, ybt], w=[ysum])
                    k.act(ysq[:], ysum[:], AF.Square, r=[ysum], w=[ysq])
                    f2 = lambda t: t[:].rearrange("p a b -> p (a b)")
                    mb, qb = ctx.bank(), ctx.bank()
                    k.mm(mb[:], ctx.cst("BLKM"), f2(ysum), r=[ctx.C, ysum], w=[mb])
                    k.mm(qb[:], ctx.cst("BLKM"), f2(ysq), r=[ctx.C, ysq], w=[qb])
                    k.act(f2(ysq), mb[:], AF.Square, r=[mb], w=[ysq])
                    k.tt(f2(ysq), qb[:], f2(ysq), ALU.subtract, r=[qb, ysq], w=[ysq])
                    k.ts(f2(ysq), f2(ysq), 64e-5, None, ALU.add, r=[ysq], w=[ysq], eng="pool")
                    k.act(f2(ysq), f2(ysq), AF.Ln, r=[ysq], w=[ysq])
                    k.act(f2(ysq), f2(ysq), AF.Exp, r=[ysq], w=[ysq], scale=-0.5)
                    k.tt(f2(t3), f2(ysum), mb[:], ALU.subtract, r=[ysum, mb], w=[t3])
                    k.tt(t3[:], t3[:], ysq[:], ALU.mult, r=[t3, ysq], w=[t3])
                    k.tt(t3[:], t3[:], lnw[:].unsqueeze(2).to_broadcast([128, 8, LCH]), ALU.mult, r=[t3, lnw], w=[t3], eng="pool")
                    k.tt(t3[:], t3[:], lnb[:].unsqueeze(2).to_broadcast([128, 8, LCH]), ALU.add, r=[t3, lnb], w=[t3], eng="pool")
                    k.tt(t3[:], t3[:], bon[:], ALU.add, r=[t3, bon], w=[t3])
                    zT = zTs.next()
                    k.tt(zT[:], t3[:], gt[:], ALU.mult, r=[t3, gt], w=[zT])
                    xr = xrs.next()
                    k.dma("sp", xr[:], x_in[t0:t0 + LCH, :], r=[tok_in], w=[xr])
                    xo = xos.next()
                    for half in range(2):
                        po = ctx.bank()
                        for cc in range(8):
                            k.mm(po[0:LCH, :], zT[:, cc, :], Wo[:, cc, half * 512:(half + 1) * 512], cc == 0, cc == 7, r=[zT, Wo], w=[po])
                        k.tt(xo[:, half * 512:(half + 1) * 512], po[0:LCH, :], xr[:, half * 512:(half + 1) * 512], ALU.add, r=[po, xr], w=[xo])
                    outs.append(k.dma("sp", x_out[t0:t0 + LCH, :], xo[:], r=[xo], w=[tok_out]))
    ctx.P.release(mark)
    return outs


PARAM_SHAPES = None


def build_program(T, nseq, shapes):
    import os
    nc = bass.Bass("TRN2", target_bir_lowering=False)
    n = T * nseq
    carr, _ = build_consts()

    def din(name, shape):
        return nc.dram_tensor(name, list(shape), F32, kind="ExternalInput").ap()

    def dint(name, shape, dt=F32):
        return nc.dram_tensor(name, list(shape), dt, kind=os.environ.get("KSCR", "Internal")).ap()

    class _Sl:
        def __init__(self, lst):
            self.lst = lst

        def __getitem__(self, i):
            return self.lst[i]
    A = {}
    for name, shp in shapes.items():
        if name in ("x", "mem", "norm_final"):
            A[name] = din(name, shp)
        else:
            A[name] = _Sl([din(f"{name}_{i}", shp[1:]) for i in range(shp[0])])
    cst = din("consts", carr.shape)
    out = nc.dram_tensor("out", [n, D], F32, kind="ExternalOutput").ap()
    xa, xb = dint("xa", (n, D)), dint("xb", (n, D))
    S_fm, S_tok = dint("S_fm", (FM_ROWS, n)), dint("S_tok", (n, TOKW))
    nch = n // 128
    S_cb, S_sb = dint("S_cb", (nch, 128, 516)), dint("S_sb", (nch, 128, 256))
    S1, S_wt = dint("S1", (8, D, n), BF16), dint("S_wt", (2, D, n // LCH))
    S_vtok, S_bonus, S_g, S_yb = (dint("S_vtok", (n, D), BF16), dint("S_bonus", (n // 256, 128, 8, 256), BF16),
                                    dint("S_g", (n // 256, 128, 8, 256), BF16), dint("S_yb", (2, n // 256, 128, 8, 256), BF16))
    P = Prog(nc)
    ctx = Ctx(nc, P, T, nseq, cst)
    tx = Buf("x")
    ta, tb_ = Buf("xa"), Buf("xb")
    import os
    nph = int(os.environ.get("KPH", "99"))

    def finish(outs_):
        P.emit(final_waits=outs_)
        P.close()
        return nc, carr
    tfm, ttok = Buf("fm"), Buf("tok")
    phase_a0(ctx, A["x"], A["ev_w_in"][0], A["norm_mix"][0], S_fm, S_tok, tx, tfm, ttok)
    if nph <= 0:
        return finish(list(P.dma_last.values()))
    prm0 = dict(conv=A["ev_conv_qk"][0], igb=A["ev_m_ig_bias"][0], fgb=A["ev_m_fg_bias"][0], mnorm=A["ev_m_norm"][0],
                w2=A["ev_g_decay_w2"][0], db=A["ev_g_decay_b"][0], gnorm=A["ev_g_norm"][0], wout=A["ev_w_out"][0])
    o_ = phase_b0(ctx, A["x"], out if nph <= 1 else xa, S_fm, S_tok, S_cb, S_sb, prm0, tx, tfm, ttok, ta)
    if nph <= 1:
        return finish(o_)
    mem2 = A["mem"]
    o_ = phase_xattn(ctx, xa, out if nph <= 2 else xb, mem2, A["xa_wq"][0], A["xa_wkv"][0], A["xa_wo"][0], A["norm_xattn"][0], A["norm_mem"][0], ta, tb_)
    if nph <= 2:
        return finish(o_)
    o_ = phase_ffn(ctx, xb, out if nph <= 3 else xa, A["ffn_w_gate"][0], A["ffn_w_up"][0], A["ffn_w_down"][0], A["norm_ffn"][0], tb_, ta)
    if nph <= 3:
        return finish(o_)
    prm1 = dict(gain=A["norm_mix"][1], w_rkv=A["od_w_rkv"][0], w0=A["od_w0"][0], w1=A["od_w1"][0], w2=A["od_w2"][0], a0=A["od_a0"][0],
                a1=A["od_a1"][0], a2=A["od_a2"][0], g1=A["od_g1"][0], g2=A["od_g2"][0], k_k=A["od_k_k"][0], k_a=A["od_k_a"][0],
                r_k=A["od_r_k"][0], mu=A["od_mu"][0], ln_w=A["od_ln_w"][0], ln_b=A["od_ln_b"][0], w_o=A["od_w_o"][0])
    ts1 = Buf("s1")
    phase_a1(ctx, xa, prm1, S1, S_wt, S_vtok, S_bonus, S_g, ta, ts1)
    ty = Buf("y")
    phase_b1v2(ctx, prm1, S1, S_wt, S_vtok, S_yb, ts1, ty)
    o_ = phase_c1(ctx, xa, out if nph <= 4 else xb, prm1, S_yb, S_bonus, S_g, ta, ts1, ty, tb_)
    if nph <= 4:
        return finish(o_)
    phase_xattn(ctx, xb, xa, mem2, A["xa_wq"][1], A["xa_wkv"][1], A["xa_wo"][1], A["norm_xattn"][1], A["norm_mem"][1], tb_, ta)
    tout = Buf("out")
    outs = phase_ffn(ctx, xa, out, A["ffn_w_gate"][1], A["ffn_w_up"][1], A["ffn_w_down"][1], A["norm_ffn"][1], ta, tout,
                     final_gain_ap=A["norm_final"])
    P.emit(final_waits=outs)
    P.close()
    return nc, carr


_CACHE = {}


def kernel(**inputs):
    import os
    ncores = int(os.environ.get("KNC", "8"))
    x = np.asarray(inputs["x"], np.float32)
    B, T, _ = x.shape
    nseq = B // ncores
    shapes = {}
    per_core = []
    for name, v in inputs.items():
        v = np.ascontiguousarray(np.asarray(v, np.float32))
        if name == "x":
            shapes[name] = (nseq * T, D)
        elif name == "mem":
            shapes[name] = (nseq * NMEM, D)
        else:
            shapes[name] = v.shape
    key = (T, nseq)
    if key not in _CACHE:
        _CACHE[key] = build_program(T, nseq, shapes)
    nc, carr = _CACHE[key]
    in_maps = []
    for c in range(ncores):
        m = {"consts": carr}
        for name, v in inputs.items():
            v = np.ascontiguousarray(np.asarray(v, np.float32))
            if name == "x":
                m[name] = np.ascontiguousarray(v[c * nseq:(c + 1) * nseq].reshape(nseq * T, D))
            elif name == "mem":
                m[name] = np.ascontiguousarray(v[c * nseq:(c + 1) * nseq].reshape(nseq * NMEM, D))
            elif name == "norm_final":
                m[name] = v
            else:
                for i in range(v.shape[0]):
                    m[f"{name}_{i}"] = np.ascontiguousarray(v[i])
        in_maps.append(m)
    res = run_bass_kernel_spmd(nc, in_maps, core_ids=list(range(ncores)))
    outs = [np.asarray(r["out"], np.float32).reshape(nseq, T, D) for r in res.results]
    return np.concatenate(outs, axis=0)


def phase_b1v2(ctx, prm, S1, S_wt, S_vtok, S_y, tok_s1, tok_y, nchains=2):
    k = ctx.k
    n, T = ctx.n, ctx.T
    NCH = T // LCH
    mark = ctx.P.mark()
    ident = ctx.cst("ident")
    flip = [0]

    def evac(dst_ap, src_ap, r, w):
        flip[0] = (flip[0] + 1) % 4
        k.copy(dst_ap, src_ap, r=r, w=w, eng="dve" if flip[0] == 0 else "act")

    def pairs_mm(lhs_tile, rhs_tile, width=128, lhs2=None, rhs2=None):
        per_bank = 512 // width
        res = []
        for b0 in range(0, 8, per_bank):
            bk = ctx.bank()
            for p in range(b0, b0 + per_bank):
                o = bk[:, (p - b0) * width:(p - b0 + 1) * width]
                k.mm(o, lhs_tile[:, p, :], rhs_tile[:, p, :], True, lhs2 is None, r=[lhs_tile, rhs_tile], w=[bk])
                if lhs2 is not None:
                    k.mm(o, lhs2[:, p, :], rhs2[:, p, :], False, True, r=[lhs2, rhs2], w=[bk])
            res.append((bk, bk[:].rearrange("p (a b) -> p a b", b=width), b0, per_bank))
        return res

    def bd_src(ap2d, t0):
        v = ap2d.rearrange("(p e k) t -> e k p t", e=2, k=64)
        return [v[e][:, :, t0:t0 + LCH] for e in range(2)]

    class Work:
        def __init__(self):
            big = lambda: k.tile([128, 8, 128], BF16)
            big32 = lambda: k.tile([128, 8, 128], F32)
            self.AT, self.BT, self.KT, self.Vb = big(), big(), big(), big()
            for t in (self.AT, self.BT, self.KT, self.Vb):
                k.memset(t[:], 0.0, w=[t])
            self.RT = k.tile([128, 8, LCH], BF16)
            self.wt = k.tile([128, 8, NCH], F32)
            self.St = [k.tile([128, 32, 256], BF16), k.tile([128, 32, 256], BF16)]
            self.Vt = k.tile([128, D], BF16)
            self.N = [big(), big()]
            self.NT = [big(), big()]
            self.P = [big(), big()]
            self.AK, self.X, self.U, self.Bt, self.Kt = big(), big(), big(), big(), big()
            self.M = [big32(), big32()]
            self.Mb = [big(), big()]
            self.RB = k.tile([128, 8, LCH], BF16)
            self.RK = k.tile([128, 8, LCH], BF16)
            self.ys = k.tile([128, 8, LCH], BF16)

    works = [Work() for _ in range(nchains)]

    def chain(W, s, d):
        base = d * 4
        mi = 0
        Mst = W.M[mi]
        Mb = W.Mb[mi]
        k.memset(Mst[:], 0.0, w=[Mst])
        k.memset(Mb[:], 0.0, w=[Mb])
        strict = ctx.cst("SF" if d == 0 else "SB")
        strictT = ctx.cst("SB" if d == 0 else "SF")
        incl = ctx.cst("IF" if d == 0 else "IB")[:, 0:LCH]
        order = list(range(NCH)) if d == 0 else list(range(NCH - 1, -1, -1))
        S1flat = S1[base:base + 4].rearrange("a r t -> (a r) t")
        cg0 = (s * T) // LCH
        k.dma("sp", W.wt[:], S_wt[d].rearrange("(p q) c -> q p c", q=128)[:, :, cg0:cg0 + NCH], r=[tok_s1], w=[W.wt],
              allow_slow_non_contiguous=True)
        groups = []
        for c in order:
            if not groups or groups[-1] != c // 4:
                groups.append(c // 4)

        def load_group(gi):
            g = groups[gi]
            St = W.St[gi % 2]
            tg = s * T + g * 256
            k.dma("sp", St[:], S1flat[:, tg:tg + 256].rearrange("(ap q) t -> q ap t", q=128), r=[tok_s1], w=[St])
        load_group(0)
        for c in order:
            t0 = s * T + c * LCH
            gi = groups.index(c // 4)
            if c // 4 != (order[order.index(c) - 1] // 4 if order.index(c) > 0 else -1) and gi + 1 < len(groups):
                load_group(gi + 1)
            St = W.St[gi % 2]
            off = (c % 4) * LCH
            AT, BT, KT, Vb, RT = W.AT, W.BT, W.KT, W.Vb, W.RT
            wt = W.wt
            for e in range(2):
                rows = slice(e * 64, (e + 1) * 64)
                k.dma("sp", W.Vt[rows, :], S_vtok[t0:t0 + LCH, :], r=[tok_s1], w=[W.Vt])
            for ai, dstt in ((0, AT), (1, BT), (2, KT)):
                for e in range(2):
                    rows = slice(e * 64, (e + 1) * 64)
                    k.copy(dstt[rows, :, rows], St[rows, ai * 8:(ai + 1) * 8, off:off + LCH], r=[St], w=[dstt], eng="pool" if e == 0 else "act")
            k.copy(RT[:], St[:, 24:32, off:off + LCH], r=[St], w=[RT], eng="pool")
            for e in range(2):
                rows = slice(e * 64, (e + 1) * 64)
                k.copy(Vb[rows, :, rows], W.Vt[rows, :].rearrange("t (p e v) -> t p e v", e=2, v=64)[:, :, e, :], r=[W.Vt], w=[Vb],
                       eng="pool" if e == 0 else "act")
            yield
            ni = 0
            N, NT, P_ = W.N[0], W.NT[0], W.P[0]
            AK, RB, RK = W.AK, W.RB, W.RK
            for (bk, v, b0, nb) in pairs_mm(BT, AT):
                k.tt(N[:, b0:b0 + nb, :], v, strict.unsqueeze(1).to_broadcast([128, nb, 128]), ALU.mult, r=[bk, ctx.C], w=[N])
            for (bk, v, b0, nb) in pairs_mm(AT, BT):
                k.tt(NT[:, b0:b0 + nb, :], v, strictT.unsqueeze(1).to_broadcast([128, nb, 128]), ALU.mult, r=[bk, ctx.C], w=[NT])
            k.tt(P_[:], N[:], ident.unsqueeze(1).to_broadcast([128, 8, 128]), ALU.add, r=[N, ctx.C], w=[P_], eng="pool")
            yield
            for (bk, v, b0, nb) in pairs_mm(KT, AT):
                k.tt(AK[:, b0:b0 + nb, :], v, strict.unsqueeze(1).to_broadcast([128, nb, 128]), ALU.mult, r=[bk, ctx.C], w=[AK])
            for (bk, v, b0, nb) in pairs_mm(BT, RT, width=LCH):
                k.tt(RB[:, b0:b0 + nb, :], v, incl.unsqueeze(1).to_broadcast([128, nb, LCH]), ALU.mult, r=[bk, ctx.C], w=[RB])
            for (bk, v, b0, nb) in pairs_mm(KT, RT, width=LCH):
                k.tt(RK[:, b0:b0 + nb, :], v, incl.unsqueeze(1).to_broadcast([128, nb, LCH]), ALU.mult, r=[bk, ctx.C], w=[RK])
            yield
            last_c = c == order[-1]
            if not last_c:
                for (src, dst) in ((BT, W.Bt), (KT, W.Kt)):
                    bk = ctx.bank()
                    pb = bk.t[:].bitcast(BF16)
                    for p in range(8):
                        k.tr(pb[:, p * 128:(p + 1) * 128], src[:, p, :], ctx.identb[:], r=[src, ctx.identb], w=[bk])
                    evac(dst[:], pb.rearrange("p (a b) -> p a b", b=128), [bk], [dst])
                yield
            for lvl in range(5):
                last = lvl == 4
                N2 = None if last else W.N[1 - ni]
                NT2 = W.NT[1 - ni]
                if not last:
                    for (bk, v, b0, nb) in pairs_mm(NT, N):
                        evac(N2[:, b0:b0 + nb, :], v, [bk], [N2])
                for (bk, v, b0, nb) in pairs_mm(N, NT):
                    evac(NT2[:, b0:b0 + nb, :], v, [bk], [NT2])
                yield
                P2 = W.P[1 - ni]
                for (bk, v, b0, nb) in pairs_mm(NT2, P_):
                    k.tt(P2[:, b0:b0 + nb, :], v, P_[:, b0:b0 + nb, :], ALU.add, r=[bk, P_], w=[P2])
                N, NT, P_ = N2, NT2, P2
                ni = 1 - ni
                yield
            X, U = W.X, W.U
            for (bk, v, b0, nb) in pairs_mm(AT, Mb, lhs2=AK, rhs2=Vb):
                evac(X[:, b0:b0 + nb, :], v, [bk], [X])
            yield
            for (bk, v, b0, nb) in pairs_mm(P_, X):
                evac(U[:, b0:b0 + nb, :], v, [bk], [U])
            yield
            yb = ctx.bank()
            for p in range(8):
                o = yb[:, p * LCH:(p + 1) * LCH]
                k.mm(o, Mb[:, p, :], RT[:, p, :], True, False, r=[Mb, RT], w=[yb])
                k.mm(o, U[:, p, :], RB[:, p, :], False, False, r=[U, RB], w=[yb])
                k.mm(o, Vb[:, p, :], RK[:, p, :], False, True, r=[Vb, RK], w=[yb])
            evac(W.ys[:], yb[:].rearrange("p (a b) -> p a b", b=LCH), [yb], [W.ys])
            k.dma("sp", S_y[d, t0 // 256, :, :, t0 % 256:t0 % 256 + LCH], W.ys[:], r=[W.ys], w=[tok_y])
            if not last_c:
                Mn = W.M[1 - mi]
                for (bk, v, b0, nb) in pairs_mm(W.Bt, U, lhs2=W.Kt, rhs2=Vb):
                    k.tt(Mn[:, b0:b0 + nb, :], v, Mst[:, b0:b0 + nb, :], ALU.add, r=[bk, Mst], w=[Mn])
                    k.tt(Mn[:, b0:b0 + nb, :], Mn[:, b0:b0 + nb, :], wt[:, b0:b0 + nb, c:c + 1].to_broadcast([128, nb, 128]), ALU.mult,
                         r=[Mn, wt], w=[Mn], eng="pool")
                Mb = W.Mb[1 - mi]
                k.copy(Mb[:], Mn[:], r=[Mn], w=[Mb], eng="act")
                Mst = Mn
                mi = 1 - mi
            yield

    jobs = [(s, d) for s in range(ctx.nseq) for d in (0, 1)]
    for g0 in range(0, len(jobs), nchains):
        gens = [chain(works[i], *jobs[g0 + i]) for i in range(min(nchains, len(jobs) - g0))]
        alive = list(gens)
        while alive:
            nxt = []
            for g in alive:
                try:
                    next(g)
                    nxt.append(g)
                except StopIteration:
                    pass
            alive = nxt
    ctx.P.release(mark)


def phase_c1(ctx, x_in, x_out, prm, S_y, S_bonus, S_g, tok_in, tok_s1, tok_y, tok_out):
    k = ctx.k
    n = ctx.n
    TT = 256
    mark = ctx.P.mark()
    Wo = load_w(ctx, prm["w_o"])
    lnw = k.tile([128, 8], F32)
    lnb = k.tile([128, 8], F32)
    k.dma("sp", lnw[:], prm["ln_w"].rearrange("(c p) -> p c", p=128), w=[lnw], allow_slow_non_contiguous=True)
    k.dma("sp", lnb[:], prm["ln_b"].rearrange("(c p) -> p c", p=128), w=[lnb], allow_slow_non_contiguous=True)
    rings = [k.ring(2, [128, 8, TT], BF16) for _ in range(4)] + [k.ring(2, [128, 8, TT], F32) for _ in range(3)]
    zTs = k.ring(2, [128, 8, TT], BF16)
    xrs = k.ring(2, [128, D], F32)
    xos = k.ring(2, [128, D], F32)
    outs = []
    fm = lambda ap2d, t0: ap2d.rearrange("(p q) t -> q p t", q=128)[:, :, t0:t0 + TT]
    f2 = lambda t: t[:].rearrange("p a b -> p (a b)")
    def step_gen(st):
        t0 = st * TT
        yf, yb, bon, gt, ysum, ysq, t3 = [r.next() for r in rings]
        k.dma("sp", yf[:], S_y[0, st], r=[tok_y], w=[yf])
        k.dma("sp", yb[:], S_y[1, st], r=[tok_y], w=[yb])
        k.dma("sp", bon[:], S_bonus[st], r=[tok_s1], w=[bon])
        k.dma("sp", gt[:], S_g[st], r=[tok_s1], w=[gt])
        yield
        k.tt(ysum[:], yf[:], yb[:], ALU.add, r=[yf, yb], w=[ysum])
        yield
        k.act(ysq[:], ysum[:], AF.Square, r=[ysum], w=[ysq])
        yield
        NH = (8 * TT) // 512
        for hf in range(NH):
            cs = slice(hf * 512, (hf + 1) * 512)
            mbk, qbk = ctx.bank(), ctx.bank()
            k.mm(mbk[:], ctx.cst("BLKM"), f2(ysum)[:, cs], r=[ctx.C, ysum], w=[mbk])
            k.mm(qbk[:], ctx.cst("BLKM"), f2(ysq)[:, cs], r=[ctx.C, ysq], w=[qbk])
            k.act(f2(ysq)[:, cs], mbk[:], AF.Square, r=[mbk], w=[ysq])
            k.tt(f2(ysq)[:, cs], qbk[:], f2(ysq)[:, cs], ALU.subtract, r=[qbk, ysq], w=[ysq])
            k.tt(f2(t3)[:, cs], f2(ysum)[:, cs], mbk[:], ALU.subtract, r=[ysum, mbk], w=[t3])
            yield
        k.ts(f2(ysq), f2(ysq), 64e-5, None, ALU.add, r=[ysq], w=[ysq], eng="pool")
        yield
        k.act(f2(ysq), f2(ysq), AF.Ln, r=[ysq], w=[ysq])
        yield
        k.act(f2(ysq), f2(ysq), AF.Exp, r=[ysq], w=[ysq], scale=-0.5)
        yield
        k.tt(t3[:], t3[:], ysq[:], ALU.mult, r=[t3, ysq], w=[t3])
        yield
        k.tt(t3[:], t3[:], lnw[:].unsqueeze(2).to_broadcast([128, 8, TT]), ALU.mult, r=[t3, lnw], w=[t3], eng="pool")
        yield
        k.tt(t3[:], t3[:], lnb[:].unsqueeze(2).to_broadcast([128, 8, TT]), ALU.add, r=[t3, lnb], w=[t3])
        yield
        k.tt(t3[:], t3[:], bon[:], ALU.add, r=[t3, bon], w=[t3])
        yield
        zT = zTs.next()
        k.tt(zT[:], t3[:], gt[:], ALU.mult, r=[t3, gt], w=[zT])
        yield
        for sub in range(TT // 128):
            ts0 = t0 + sub * 128
            xr = xrs.next()
            k.dma("sp", xr[:], x_in[ts0:ts0 + 128, :], r=[tok_in], w=[xr])
            xo = xos.next()
            for half in range(2):
                po = ctx.bank()
                for cc in range(8):
                    k.mm(po[:], zT[:, cc, sub * 128:(sub + 1) * 128], Wo[:, cc, half * 512:(half + 1) * 512], cc == 0, cc == 7, r=[zT, Wo], w=[po])
                k.tt(xo[:, half * 512:(half + 1) * 512], po[:], xr[:, half * 512:(half + 1) * 512], ALU.add, r=[po, xr], w=[xo])
            outs.append(k.dma("sp", x_out[ts0:ts0 + 128, :], xo[:], r=[xo], w=[tok_out]))
            yield

    nsteps = n // TT
    for s0 in range(0, nsteps, 2):
        alive = [step_gen(s0 + i) for i in range(min(2, nsteps - s0))]
        while alive:
            nxt = []
            for g_ in alive:
                try:
                    next(g_)
                    nxt.append(g_)
                except StopIteration:
                    pass
            alive = nxt
    ctx.P.release(mark)
    return outs
```

```python
import numpy as np
import concourse.bass as bass
import concourse.mybir as mybir
from concourse.bass_utils import run_bass_kernel_spmd

F32 = mybir.dt.float32
BF16 = mybir.dt.bfloat16
AF = mybir.ActivationFunctionType
ALU = mybir.AluOpType
AX = mybir.AxisListType

ENGS = ("pe", "act", "dve", "pool", "sp")
SEM_WRAP = 30000


class Buf:
    __slots__ = ("name", "ap", "w", "r")

    def __init__(self, name, ap=None):
        self.name = name
        self.ap = ap
        self.w = None
        self.r = {}


class Ins:
    __slots__ = ("eng", "fn", "deps", "sig", "sigval", "dma", "dsem", "dval", "prev_dma", "idx")

    def __init__(self, eng, fn, deps, dma=False):
        self.eng = eng
        self.fn = fn
        self.deps = deps
        self.sig = False
        self.sigval = None
        self.dma = dma
        self.dsem = None
        self.dval = None
        self.prev_dma = None


class Ring:
    def __init__(self, P, n, shape, dt, psum=False):
        self.slots = []
        for _ in range(n):
            t = P.ps(shape, dt) if psum else P.sb(shape, dt)
            self.slots.append((t, Buf("ring")))
        self.i = 0

    def next(self):
        s = self.slots[self.i % len(self.slots)]
        self.i += 1
        return s


class Prog:
    def __init__(self, nc, n_dma_sems=16, same_engine_sync=True):
        self.nc = nc
        self.streams = {e: [] for e in ENGS}
        self.n_dma_sems = n_dma_sems
        self.same_engine_sync = same_engine_sync
        self.dma_rr = {e: 0 for e in ENGS}
        self.dma_last = {}
        self.stack = []
        self.nbuf = 0
        self.extra = {e: [] for e in ENGS}

    def mark(self):
        return len(self.stack)

    def release(self, mark):
        deps = []
        for e in ENGS:
            for ins in reversed(self.streams[e]):
                if not ins.dma:
                    deps.append(ins)
                    break
        deps.extend(self.dma_last.values())
        for e in ENGS:
            self.extra[e] = list(deps)
        while len(self.stack) > mark:
            self.stack.pop().__exit__(None, None, None)

    def sb(self, shape, dt, name=None):
        self.nbuf += 1
        g = self.nc.sbuf_tensor(name or f"sb{self.nbuf}", list(shape), dt)
        t = g.__enter__()
        self.stack.append(g)
        return t

    def ps(self, shape, dt=F32, name=None):
        self.nbuf += 1
        g = self.nc.psum_tensor(name or f"ps{self.nbuf}", list(shape), dt)
        t = g.__enter__()
        self.stack.append(g)
        return t

    def buf(self, name="b"):
        return Buf(name)

    def ring(self, n, shape, dt, psum=False):
        return Ring(self, n, shape, dt, psum)

    def _deps(self, eng, reads, writes):
        deps = []
        for b in reads:
            if b.w is not None:
                deps.append(b.w)
        for b in writes:
            if b.w is not None:
                deps.append(b.w)
            deps.extend(b.r.values())
        if self.extra[eng]:
            deps.extend(self.extra[eng])
            self.extra[eng] = []
        return deps

    def op(self, eng, fn, reads=(), writes=()):
        deps = self._deps(eng, reads, writes)
        ins = Ins(eng, fn, deps)
        for b in writes:
            b.w = ins
            b.r = {}
        for b in reads:
            b.r[eng] = ins
        self.streams[eng].append(ins)
        return ins

    def dma(self, eng, out, in_, reads=(), writes=(), **kw):
        deps = self._deps(eng, reads, writes)

        def fn(e, out=out, in_=in_, kw=kw):
            return e.dma_start(out=out, in_=in_, **kw)

        ins = Ins(eng, fn, deps, dma=True)
        slot = (eng, self.dma_rr[eng] % self.n_dma_sems)
        self.dma_rr[eng] += 1
        ins.dsem = slot
        prev = self.dma_last.get(slot)
        ins.prev_dma = prev
        ins.dval = (prev.dval if prev is not None else 0) + 16
        self.dma_last[slot] = ins
        for b in writes:
            b.w = ins
            b.r = {}
        for b in reads:
            b.r[("dma", id(ins))] = ins
        self.streams[eng].append(ins)
        return ins

    def emit(self, final_waits=()):
        nc = self.nc
        for e in ENGS:
            for ins in self.streams[e]:
                for d in ins.deps:
                    if d.dma:
                        continue
                    if d.eng == "pe" and ins.eng == "pe":
                        continue
                    if d.eng == ins.eng and not self.same_engine_sync:
                        continue
                    d.sig = True
        for d in final_waits:
            if not d.dma:
                d.sig = True
        nsig = {}
        for e in ENGS:
            n = 0
            for ins in self.streams[e]:
                if ins.sig:
                    ins.sigval = (n // SEM_WRAP, n % SEM_WRAP + 1)
                    n += 1
            nsig[e] = n
        sems = {}
        guards = []

        def getsem(key):
            if key not in sems:
                g = nc.semaphore("s_" + "_".join(str(k) for k in key))
                sems[key] = g.__enter__()
                guards.append(g)
            return sems[key]

        for e in ENGS:
            for k in range((nsig[e] + SEM_WRAP - 1) // SEM_WRAP):
                getsem(("c", e, k))
        for slot in self.dma_last:
            getsem(("d",) + slot)

        engobj = {"pe": "tensor", "act": "scalar", "dve": "vector", "pool": "gpsimd", "sp": "sync"}
        streams = self.streams

        def run(e, eng):
            waited = {}
            for ins in streams[e]:
                need = {}
                for d in ins.deps:
                    if d.dma:
                        key = ("d",) + d.dsem
                        val = d.dval
                    else:
                        if d.eng == "pe" and e == "pe":
                            continue
                        if d.eng == e and not self.same_engine_sync:
                            continue
                        key = ("c", d.eng, d.sigval[0])
                        val = d.sigval[1]
                    if need.get(key, 0) < val:
                        need[key] = val
                if ins.dma and ins.prev_dma is not None:
                    key = ("d",) + ins.dsem
                    if need.get(key, 0) < ins.prev_dma.dval:
                        need[key] = ins.prev_dma.dval
                for key, val in need.items():
                    if waited.get(key, 0) < val:
                        eng.wait_ge(sems[key], val)
                        waited[key] = val
                bi = ins.fn(eng)
                if ins.dma:
                    bi.then_inc(sems[("d",) + ins.dsem], 16)
                elif ins.sig:
                    bi.then_inc(sems[("c", e, ins.sigval[0])], 1)
            if e == "sp":
                for d in final_waits:
                    if d.dma:
                        eng.wait_ge(sems[("d",) + d.dsem], d.dval)
                    else:
                        eng.wait_ge(sems[("c", d.eng, d.sigval[0])], d.sigval[1])

        with nc.Block() as block:
            @block.tensor
            def _(eng):
                run("pe", eng)

            @block.scalar
            def _(eng):
                run("act", eng)

            @block.vector
            def _(eng):
                run("dve", eng)

            @block.gpsimd
            def _(eng):
                run("pool", eng)

            @block.sync
            def _(eng):
                run("sp", eng)
        for g in reversed(guards):
            g.__exit__(None, None, None)

    def close(self):
        for g in reversed(self.stack):
            g.__exit__(None, None, None)
        self.stack = []


class Tile:
    __slots__ = ("t", "b")

    def __init__(self, t):
        self.t = t
        self.b = Buf("t")

    def __getitem__(self, k):
        return self.t[k]


def _tok(xs):
    out = []
    for x in xs:
        if x is None:
            continue
        out.append(x.b if isinstance(x, Tile) else x)
    return out


class K:
    def __init__(self, P):
        self.P = P

    def tile(self, shape, dt, psum=False):
        return Tile(self.P.ps(shape, dt) if psum else self.P.sb(shape, dt))

    def ring(self, n, shape, dt, psum=False):
        return TRing([self.tile(shape, dt, psum) for _ in range(n)])

    def dma(self, q, out, in_, r=(), w=(), **kw):
        return self.P.dma(q, out, in_, reads=_tok(r), writes=_tok(w), **kw)

    def act(self, out, in_, func, r=(), w=(), eng="act", **kw):
        return self.P.op(eng, lambda e: e.activation(out=out, in_=in_, func=func, **kw), _tok(r), _tok(w))

    def ts(self, out, in0, s1, s2, op0, op1=None, r=(), w=(), eng="dve", **kw):
        if op1 is None:
            return self.P.op(eng, lambda e: e.tensor_scalar(out=out, in0=in0, scalar1=s1, scalar2=None, op0=op0, **kw), _tok(r), _tok(w))
        return self.P.op(eng, lambda e: e.tensor_scalar(out=out, in0=in0, scalar1=s1, scalar2=s2, op0=op0, op1=op1, **kw), _tok(r), _tok(w))

    def tt(self, out, in0, in1, op, r=(), w=(), eng="dve"):
        return self.P.op(eng, lambda e: e.tensor_tensor(out=out, in0=in0, in1=in1, op=op), _tok(r), _tok(w))

    def stt(self, out, in0, scalar, in1, op0, op1, r=(), w=()):
        return self.P.op("dve", lambda e: e.scalar_tensor_tensor(out=out, in0=in0, scalar=scalar, in1=in1, op0=op0, op1=op1), _tok(r), _tok(w))

    def copy(self, out, in_, r=(), w=(), eng="dve"):
        if eng == "act":
            return self.P.op("act", lambda e: e.copy(out=out, in_=in_), _tok(r), _tok(w))
        return self.P.op(eng, lambda e: e.tensor_copy(out=out, in_=in_), _tok(r), _tok(w))

    def recip(self, out, in_, r=(), w=()):
        return self.P.op("dve", lambda e: e.reciprocal(out=out, in_=in_), _tok(r), _tok(w))

    def memset(self, out, val, w=(), eng="pool"):
        return self.P.op(eng, lambda e: e.memset(out, val), (), _tok(w))

    def reduce(self, out, in_, op, r=(), w=()):
        return self.P.op("dve", lambda e: e.tensor_reduce(out=out, in_=in_, axis=AX.X, op=op), _tok(r), _tok(w))

    def mm(self, out, lhsT, rhs, start=True, stop=True, r=(), w=()):
        return self.P.op("pe", lambda e: e.matmul(out, lhsT=lhsT, rhs=rhs, start=start, stop=stop), _tok(r), _tok(w))

    def tr(self, out, in_, ident, r=(), w=()):
        return self.P.op("pe", lambda e: e.transpose(out=out, in_=in_, identity=ident), _tok(r), _tok(w))


class TRing:
    def __init__(self, tiles):
        self.tiles = tiles
        self.i = 0

    def next(self):
        t = self.tiles[self.i % len(self.tiles)]
        self.i += 1
        return t


D = 1024
NMEM = 256
DFF = 2816
EPS = 1e-6


def build_consts():
    i = np.arange(128)
    s, t = i[:, None], i[None, :]
    c = {}
    f = lambda m: np.asarray(m, np.float32)
    c["ident"] = f(s == t)
    c["MU"] = f(s <= t)
    c["ML"] = f(s >= t)
    c["ONES"] = np.ones((128, 128), np.float32)
    c["NU"] = -f(s <= t)
    c["NL"] = -f(s >= t)
    c["NONES"] = -np.ones((128, 128), np.float32)
    c["BLK"] = f((s // 64) == (t // 64))
    c["BLKM"] = f((s // 64) == (t // 64)) / 64.0
    s6, t6 = s % 64, t % 64
    c["SF"] = f(s6 < t6)
    c["SB"] = f(s6 > t6)
    c["IF"] = f(s6 <= t6)
    c["IB"] = f(s6 >= t6)
    c["UN"] = -f(s <= t) / 16.0
    c["LN"] = -f(s >= t) / 16.0
    c["UC"] = -f(s > t) / 16.0
    c["LC"] = -f(s < t) / 16.0
    names = list(c)
    arr = np.concatenate([np.asarray(c[k], np.float32) for k in names], axis=1)
    offs = {k: j * 128 for j, k in enumerate(names)}
    return arr, offs


class Ctx:
    def __init__(self, nc, P, T, nseq, consts_ap):
        self.nc = nc
        self.P = P
        self.k = K(P)
        self.T = T
        self.nseq = nseq
        self.n = T * nseq
        k = self.k
        arr, offs = build_consts()
        self.coffs = offs
        self.C = k.tile([128, arr.shape[1]], F32)
        k.dma("sp", self.C[:], consts_ap, w=[self.C])
        self.identb = k.tile([128, 128], BF16)
        k.copy(self.identb[:], self.cst("ident"), r=[self.C], w=[self.identb])
        self.banks = k.ring(8, [128, 512], F32, psum=True)
        self.dram_tok = {}

    def cst(self, name):
        o = self.coffs[name]
        return self.C[:, o:o + 128]

    def bank(self):
        return self.banks.next()

    def dtok(self, key):
        if key not in self.dram_tok:
            self.dram_tok[key] = Buf(str(key))
        return self.dram_tok[key]


def load_gain_fm(ctx, g_ap, nchunk=8):
    k = ctx.k
    g = k.tile([128, nchunk], F32)
    k.dma("sp", g[:], g_ap.rearrange("(c p) -> p c", p=128), w=[g], allow_slow_non_contiguous=True)
    return g


def load_w(ctx, w_ap, gain_fm=None, dst=None, col0=0):
    k = ctx.k
    Kd, F = w_ap.shape
    nch = Kd // 128
    if dst is None:
        dst = k.tile([128, nch, F], BF16)
    for c in range(nch):
        k.dma("pool", dst[:, c, col0:col0 + F], w_ap[c * 128:(c + 1) * 128, :], w=[dst])
    if gain_fm is not None:
        for c in range(nch):
            k.ts(dst[:, c, col0:col0 + F], dst[:, c, col0:col0 + F], gain_fm[:, c:c + 1], None, ALU.mult,
                 r=[dst, gain_fm], w=[dst])
    return dst


class NormT:
    def __init__(self, ctx, with_xn=True, nxn=3, njunk=2):
        k = ctx.k
        self.ctx = ctx
        self.junk = k.ring(njunk, [128, D], BF16)
        self.ss = k.ring(4, [128, 1], F32)
        self.rs = k.ring(4, [128, 1], F32)
        if with_xn:
            self.xn = k.ring(nxn, [128, D], BF16)
        self.flip = 0

    def rstd(self, xt_ap, xt_tile, width=D, rows=128):
        k = self.ctx.k
        junk = self.junk.next()
        ss = self.ss.next()
        rs = self.rs.next()
        R = slice(0, rows)
        k.act(junk[R, 0:width], xt_ap, AF.Square, r=[xt_tile], w=[junk, ss], accum_out=ss[R, :])
        k.ts(rs[R, :], ss[R, :], 1.0 / width, EPS, ALU.mult, ALU.add, r=[ss], w=[rs])
        k.act(rs[R, :], rs[R, :], AF.Ln, r=[rs], w=[rs])
        k.act(rs[R, :], rs[R, :], AF.Exp, r=[rs], w=[rs], scale=-0.5)
        return rs

    def norm(self, xt):
        k = self.ctx.k
        rs = self.rstd(xt[:], xt)
        xn = self.xn.next()
        k.ts(xn[:], xt[:], rs[:], None, ALU.mult, r=[xt, rs], w=[xn])
        return xn

    def to_fm(self, xn, dst_ap, dst_tok, nchunk=8):
        ctx = self.ctx
        k = ctx.k
        bank = ctx.bank()
        pb = bank.t[:].bitcast(BF16)
        for c in range(nchunk):
            k.tr(pb[:, c * 128:(c + 1) * 128], xn[:, c * 128:(c + 1) * 128], ctx.identb[:], r=[xn, ctx.identb], w=[bank])
        src = pb[:, 0:nchunk * 128].rearrange("p (c t) -> p c t", c=nchunk)
        self.flip ^= 1
        k.copy(dst_ap, src, r=[bank], w=[dst_tok], eng="act" if self.flip else "dve")


def phase_ffn(ctx, x_in, x_out, wg, wu, wd, gain_ap, tok_in, tok_out, final_gain_ap=None):
    k = ctx.k
    n = ctx.n
    TB = 512
    NF = DFF // 128
    mark = ctx.P.mark()
    g_fm = load_gain_fm(ctx, gain_ap)
    Wgu = k.tile([128, 8, 2 * DFF], BF16)
    load_w(ctx, wg, None, dst=Wgu, col0=0)
    load_w(ctx, wu, None, dst=Wgu, col0=DFF)
    for c in range(8):
        k.ts(Wgu[:, c, :], Wgu[:, c, :], g_fm[:, c:c + 1], None, ALU.mult, r=[Wgu, g_fm], w=[Wgu])
    Wd = load_w(ctx, wd)
    nt = NormT(ctx, nxn=2, njunk=1)
    xts = k.ring(2, [128, D], F32)
    hTs = k.ring(1, [128, 8, TB], BF16)
    actT = k.tile([128, NF, TB], BF16)
    act_parts = [Buf("a") for _ in range(NF)]
    sgs = k.ring(2, [128, TB], F32)
    xos = k.ring(2, [128, D], F32)
    if final_gain_ap is not None:
        gbc = k.tile([128, D], F32)
        k.dma("sp", gbc[:], final_gain_ap.partition_broadcast(128), w=[gbc])
    outs = []
    for blk in range(n // TB):
        hT = hTs.next()
        for j in range(TB // 128):
            t0 = blk * TB + j * 128
            xt = xts.next()
            k.dma("sp", xt[:], x_in[t0:t0 + 128, :], r=[tok_in], w=[xt])
            xn = nt.norm(xt)
            nt.to_fm(xn, hT[:, :, j * 128:(j + 1) * 128], hT)
        for f in range(NF):
            pg = ctx.bank()
            pu = ctx.bank()
            for c in range(8):
                k.mm(pg[:, 0:TB], Wgu[:, c, f * 128:(f + 1) * 128], hT[:, c, :], c == 0, c == 7, r=[Wgu, hT], w=[pg])
            for c in range(8):
                k.mm(pu[:, 0:TB], Wgu[:, c, DFF + f * 128:DFF + (f + 1) * 128], hT[:, c, :], c == 0, c == 7, r=[Wgu, hT], w=[pu])
            sg = sgs.next()
            k.act(sg[:], pg[:, 0:TB], AF.Silu, r=[pg], w=[sg])
            k.tt(actT[:, f, :], sg[:], pu[:, 0:TB], ALU.mult, r=[sg, pu], w=[act_parts[f]])
        for j in range(TB // 128):
            t0 = blk * TB + j * 128
            xo = xos.next()
            k.dma("sp", xo[:], x_in[t0:t0 + 128, :], r=[tok_in], w=[xo])
            for half in range(2):
                po = ctx.bank()
                for f in range(NF):
                    k.mm(po[:], actT[:, f, j * 128:(j + 1) * 128], Wd[:, f, half * 512:(half + 1) * 512], f == 0, f == NF - 1,
                         r=[act_parts[f], Wd], w=[po])
                k.tt(xo[:, half * 512:(half + 1) * 512], po[:], xo[:, half * 512:(half + 1) * 512], ALU.add, r=[po, xo], w=[xo])
            if final_gain_ap is not None:
                rs = nt.rstd(xo[:], xo)
                k.stt(xo[:], xo[:], rs[:], gbc[:], ALU.mult, ALU.mult, r=[xo, rs, gbc], w=[xo])
            outs.append(k.dma("sp", x_out[t0:t0 + 128, :], xo[:], r=[xo], w=[tok_out]))
    ctx.P.release(mark)
    return outs


def phase_xattn(ctx, x_in, x_out, mem, wq, wkv, wo, g_x_ap, g_mem_ap, tok_in, tok_out):
    k = ctx.k
    n, T = ctx.n, ctx.T
    TB = 512
    mark = ctx.P.mark()
    gx = load_gain_fm(ctx, g_x_ap)
    gm = load_gain_fm(ctx, g_mem_ap)
    Wkv = load_w(ctx, wkv, gm)
    Wq = load_w(ctx, wq, gx)
    Wo = load_w(ctx, wo)
    nt = NormT(ctx)
    xts = k.ring(3, [128, D], F32)
    KT = k.tile([128, ctx.nseq, 8, NMEM], BF16)
    V = k.tile([128, ctx.nseq, 2, D], BF16)
    memT = k.tile([128, 8, NMEM], BF16)
    flip = 0
    for s in range(ctx.nseq):
        for j in range(2):
            xt = xts.next()
            k.dma("sp", xt[:], mem[s * NMEM + j * 128:s * NMEM + (j + 1) * 128, :], w=[xt])
            xn = nt.norm(xt)
            nt.to_fm(xn, memT[:, :, j * 128:(j + 1) * 128], memT)
        for f in range(8):
            b = ctx.bank()
            for c in range(8):
                k.mm(b[:, 0:NMEM], Wkv[:, c, f * 128:(f + 1) * 128], memT[:, c, :], c == 0, c == 7, r=[Wkv, memT], w=[b])
            flip ^= 1
            k.copy(KT[:, s, f, :], b[:, 0:NMEM], r=[b], w=[KT], eng="act" if flip else "dve")
        for j in range(2):
            for half in range(2):
                b = ctx.bank()
                for c in range(8):
                    k.mm(b[:], memT[:, c, j * 128:(j + 1) * 128], Wkv[:, c, D + half * 512:D + (half + 1) * 512], c == 0, c == 7,
                         r=[Wkv, memT], w=[b])
                flip ^= 1
                k.copy(V[:, s, j, half * 512:(half + 1) * 512], b[:], r=[b], w=[V], eng="act" if flip else "dve")
    hTs = k.ring(2, [128, 8, TB], BF16)
    qT = k.tile([128, 8, TB], BF16)
    qparts = [Buf("q") for _ in range(8)]
    pT = k.tile([128, 8, TB], BF16)
    pparts = [Buf("p") for _ in range(TB // 128)]
    oT = k.tile([128, 8, TB], BF16)
    oparts = [Buf("o") for _ in range(8)]
    mxs = k.ring(2, [128, 4], F32)
    nmxs = k.ring(2, [128, 4], F32)
    rsums = k.ring(2, [128, 4], F32)
    rinvs = k.ring(2, [128, 4], F32)
    ps_ = k.ring(2, [128, 4, NMEM], BF16)
    pns = k.ring(2, [128, 4, NMEM], BF16)
    xrs = k.ring(2, [128, D], F32)
    xos = k.ring(2, [128, D], F32)
    outs = []
    for blk in range(n // TB):
        s = (blk * TB) // T
        hT = hTs.next()
        for j in range(TB // 128):
            t0 = blk * TB + j * 128
            xt = xts.next()
            k.dma("sp", xt[:], x_in[t0:t0 + 128, :], r=[tok_in], w=[xt])
            xn = nt.norm(xt)
            nt.to_fm(xn, hT[:, :, j * 128:(j + 1) * 128], hT)
        for f in range(8):
            b = ctx.bank()
            for c in range(8):
                k.mm(b[:], Wq[:, c, f * 128:(f + 1) * 128], hT[:, c, :], c == 0, c == 7, r=[Wq, hT], w=[b])
            flip ^= 1
            k.copy(qT[:, f, :], b[:], r=[b], w=[qparts[f]], eng="act" if flip else "dve")
        for j in range(TB // 128):
            cols = slice(j * 128, (j + 1) * 128)
            b2 = [ctx.bank(), ctx.bank()]
            for h in range(4):
                b = b2[h // 2]
                for e in range(2):
                    k.mm(b[:, (h % 2) * 256:(h % 2 + 1) * 256], qT[:, 2 * h + e, cols], KT[:, s, 2 * h + e, :], e == 0, e == 1,
                         r=[qparts[2 * h + e], KT], w=[b])
            mx = mxs.next()
            for i in range(2):
                k.reduce(mx[:, 2 * i:2 * i + 2], b2[i][:].rearrange("p (h m) -> p h m", h=2), ALU.max, r=[b2[i]], w=[mx])
            nmx = nmxs.next()
            k.ts(nmx[:], mx[:], -1.0 / 16.0, None, ALU.mult, r=[mx], w=[nmx])
            p = ps_.next()
            rsum = rsums.next()
            for h in range(4):
                k.act(p[:, h, :], b2[h // 2][:, (h % 2) * 256:(h % 2 + 1) * 256], AF.Exp, r=[b2[h // 2], nmx], w=[p, rsum],
                      bias=nmx[:, h:h + 1], scale=1.0 / 16.0, accum_out=rsum[:, h:h + 1])
            rinv = rinvs.next()
            k.recip(rinv[:], rsum[:], r=[rsum], w=[rinv])
            pn = pns.next()
            k.tt(pn[:], p[:], rinv[:].unsqueeze(2).to_broadcast([128, 4, NMEM]), ALU.mult, r=[p, rinv], w=[pn])
            bank = ctx.bank()
            pb = bank.t[:].bitcast(BF16)
            for h in range(4):
                for e in range(2):
                    i = 2 * h + e
                    k.tr(pb[:, i * 128:(i + 1) * 128], pn[:, h, e * 128:(e + 1) * 128], ctx.identb[:], r=[pn, ctx.identb], w=[bank])
            flip ^= 1
            k.copy(pT[:, :, cols], pb.rearrange("p (c t) -> p c t", c=8), r=[bank], w=[pparts[j]], eng="act" if flip else "dve")
        for f in range(8):
            h = f // 2
            b = ctx.bank()
            for jm in range(2):
                k.mm(b[:], V[:, s, jm, f * 128:(f + 1) * 128], pT[:, 2 * h + jm, :], jm == 0, jm == 1, r=[V] + pparts, w=[b])
            flip ^= 1
            k.copy(oT[:, f, :], b[:], r=[b], w=[oparts[f]], eng="act" if flip else "dve")
        for j in range(TB // 128):
            t0 = blk * TB + j * 128
            cols = slice(j * 128, (j + 1) * 128)
            xr = xrs.next()
            k.dma("sp", xr[:], x_in[t0:t0 + 128, :], r=[tok_in], w=[xr])
            xo = xos.next()
            for half in range(2):
                po = ctx.bank()
                for c in range(8):
                    k.mm(po[:], oT[:, c, cols], Wo[:, c, half * 512:(half + 1) * 512], c == 0, c == 7, r=[oparts[c], Wo], w=[po])
                k.tt(xo[:, half * 512:(half + 1) * 512], po[:], xr[:, half * 512:(half + 1) * 512], ALU.add, r=[po, xr], w=[xo])
            outs.append(k.dma("sp", x_out[t0:t0 + 128, :], xo[:], r=[xo], w=[tok_out]))
    ctx.P.release(mark)
    return outs


FM_ROWS = 1568
TOKW = 2320


def phase_a0(ctx, x_in, w_in, gain_ap, S_fm, S_tok, tok_in, tok_fm, tok_tok):
    k = ctx.k
    n = ctx.n
    TB = 512
    mark = ctx.P.mark()
    g_fm = load_gain_fm(ctx, gain_ap)
    Win = load_w(ctx, w_in, g_fm)
    nt = NormT(ctx)
    xts = k.ring(3, [128, D], F32)
    hTs = k.ring(2, [128, 8, TB], BF16)
    fos = k.ring(3, [128, TB], F32)
    tks = k.ring(2, [128, TOKW], F32)
    fm_cols = [(j * 128, 128) for j in range(8)] + [(2064 + j * 128, 128) for j in range(4)] + [(3600, 32)]
    tok_groups = [(1024, 512, 0), (1536, 512, 512), (2048, 16, 1024), (2320, 256, 1040), (2576, 512, 1296), (3088, 512, 1808)]
    flip = 0
    for blk in range(n // TB):
        hT = hTs.next()
        for j in range(TB // 128):
            t0 = blk * TB + j * 128
            xt = xts.next()
            k.dma("sp", xt[:], x_in[t0:t0 + 128, :], r=[tok_in], w=[xt])
            xn = nt.norm(xt)
            nt.to_fm(xn, hT[:, :, j * 128:(j + 1) * 128], hT)
        for i, (c0, m) in enumerate(fm_cols):
            b = ctx.bank()
            for c in range(8):
                k.mm(b[0:m, :], Win[:, c, c0:c0 + m], hT[:, c, :], c == 0, c == 7, r=[Win, hT], w=[b])
            fo = fos.next()
            flip ^= 1
            k.copy(fo[0:m, :], b[0:m, :], r=[b], w=[fo], eng="act" if flip else "dve")
            r0 = i * 128
            k.dma("sp", S_fm[r0:r0 + m, blk * TB:(blk + 1) * TB], fo[0:m, :], r=[fo], w=[tok_fm])
        for j in range(TB // 128):
            t0 = blk * TB + j * 128
            tk = tks.next()
            for (c0, w, o0) in tok_groups:
                b = ctx.bank()
                for c in range(8):
                    k.mm(b[:, 0:w], hT[:, c, j * 128:(j + 1) * 128], Win[:, c, c0:c0 + w], c == 0, c == 7, r=[Win, hT], w=[b])
                flip ^= 1
                k.copy(tk[:, o0:o0 + w], b[:, 0:w], r=[b], w=[tk], eng="act" if flip else "dve")
            k.dma("sp", S_tok[t0:t0 + 128, :], tk[:], r=[tk], w=[tok_tok])
    ctx.P.release(mark)


class MixL0:
    def __init__(self, ctx, S_fm, S_tok, S_cb, S_sb, conv_ap, igb_ap, fgb_ap, mnorm_ap, w2_ap, db_ap, gnorm_ap, wout_ap,
                 tok_fm, tok_tok):
        self.ctx = ctx
        k = self.k = ctx.k
        self.S_fm, self.S_tok, self.S_cb, self.S_sb = S_fm, S_tok, S_cb, S_sb
        self.tok_fm, self.tok_tok = tok_fm, tok_tok
        self.tok_cb = Buf("cb")
        self.cw = k.tile([128, 3, 8], F32)
        for j in range(3):
            k.dma("sp", self.cw[:, j, :], conv_ap[j].rearrange("(c p) -> p c", p=128), w=[self.cw], allow_slow_non_contiguous=True)
        k.ts(self.cw[:], self.cw[:], 0.5, None, ALU.mult, r=[self.cw], w=[self.cw])
        self.gb = k.tile([128, 16], F32)
        k.dma("sp", self.gb[:, 0:8], igb_ap.rearrange("a b -> (a b)").partition_broadcast(128), w=[self.gb])
        k.dma("sp", self.gb[:, 8:16], fgb_ap.rearrange("a b -> (a b)").partition_broadcast(128), w=[self.gb])
        self.mnorm = k.tile([128, 512], F32)
        k.dma("sp", self.mnorm[:], mnorm_ap.partition_broadcast(128), w=[self.mnorm])
        self.gnorm = k.tile([128, 512], F32)
        k.dma("sp", self.gnorm[:], gnorm_ap.partition_broadcast(128), w=[self.gnorm])
        k.ts(self.mnorm[:], self.mnorm[:], 0.5, None, ALU.mult, r=[self.mnorm], w=[self.mnorm])
        k.ts(self.gnorm[:], self.gnorm[:], 0.5, None, ALU.mult, r=[self.gnorm], w=[self.gnorm])
        self.dbias = k.tile([128, 512], F32)
        k.dma("sp", self.dbias[:], db_ap.rearrange("a b -> (a b)").partition_broadcast(128), w=[self.dbias])
        self.w2p = k.tile([32, 2, 256], F32)
        k.memset(self.w2p[:], 0.0, w=[self.w2p])
        k.dma("sp", self.w2p[0:16, 0, :], w2_ap[0], w=[self.w2p])
        k.dma("sp", self.w2p[16:32, 1, :], w2_ap[1], w=[self.w2p])
        self.Wout = load_w(ctx, wout_ap)
        r = k.ring
        self.Xs = r(2, [128, 8, 130], F32)
        self.z1s = r(2, [128, 8, 128], F32)
        self.z2s = r(2, [128, 8, 128], F32)
        self.QKs = r(2, [128, 8, 128], BF16)
        self.TKs = r(2, [128, TOKW], F32)
        self.vps = r(2, [128, 4, 129], BF16)
        for t in self.vps.tiles:
            k.memset(t[:, :, 128:129], 1.0, w=[t])
        self.g8 = [r(2, [128, 8], F32) for _ in range(8)]
        self.glrs = r(2, [32, 128], F32)
        self.gqks = r(2, [128, 4, 128], F32)
        self.w512 = [r(2, [128, 512], F32) for _ in range(8)]
        self.khats = r(2, [128, 2, 256], BF16)
        self.ktz = r(2, [128, 8, 128], BF16)
        self.qtz = r(2, [128, 8, 128], BF16)
        self.thBs = r(2, [128, 512], F32)
        self.qts = r(2, [128, 512], BF16)
        self.kts = r(2, [128, 512], BF16)
        self.gvbs = r(2, [128, 512], BF16)
        self.Cfb = r(2, [128, 4, 129], BF16)
        self.Cbb = r(2, [128, 4, 129], BF16)
        self.Sfb = r(2, [128, 2, 128], BF16)
        self.Sbb = r(2, [128, 2, 128], BF16)
        for rr_ in (self.ktz, self.qtz):
            for t in rr_.tiles:
                k.memset(t[:], 0.0, w=[t])
        self.kToks = r(2, [128, 4, 128], BF16)
        self.CF = r(2, [128, 4, 129], F32)
        self.CB = r(3, [128, 4, 129], F32)
        self.SF = r(2, [128, 2, 128], F32)
        self.SB = r(3, [128, 2, 128], F32)
        self.pFB = r(2, [128, 8, 128], BF16)
        self.pA = r(2, [128, 8, 128], BF16)
        self.hms = r(2, [128, 4, 128], F32)
        self.small = [r(2, [128, 8], F32) for _ in range(8)]
        self.junks = r(2, [128, 128], F32)
        self.merged = r(2, [128, D], BF16)
        self.mTs = r(2, [128, 8, 128], BF16)
        self.xos = r(2, [128, D], F32)
        self.nt = NormT(ctx, with_xn=False)

    def prep(self, s, c, d_state, light=False):
        ctx, k = self.ctx, self.k
        T = ctx.T
        nch = T // 128
        t0 = s * T + c * 128
        o = {}
        X = self.Xs.next()
        lo = 1 if c == 0 else 0
        hi = 129 if c == nch - 1 else 130
        if c == 0:
            k.memset(X[:, :, 0:1], 0.0, w=[X])
        if c == nch - 1:
            k.memset(X[:, :, 129:130], 0.0, w=[X])
        k.dma("sp", X[:, :, lo:hi], self.S_fm[0:1024, t0 - 1 + lo:t0 - 1 + hi].rearrange("(c p) t -> p c t", p=128),
              r=[self.tok_fm], w=[X])
        yield
        z1 = self.z1s.next()
        z2 = self.z2s.next()
        cs_ = slice(4, 8) if light else slice(0, 8)
        nc_ = 4 if light else 8
        cwb = lambda j: self.cw[:, j, cs_].unsqueeze(2).to_broadcast([128, nc_, 128])
        k.tt(z1[:, cs_, :], X[:, cs_, 0:128], cwb(0), ALU.mult, r=[X, self.cw], w=[z1])
        yield
        k.tt(z2[:, cs_, :], X[:, cs_, 1:129], cwb(1), ALU.mult, r=[X, self.cw], w=[z2])
        yield
        k.tt(z1[:, cs_, :], z1[:, cs_, :], z2[:, cs_, :], ALU.add, r=[z1, z2], w=[z1])
        yield
        k.tt(z2[:, cs_, :], X[:, cs_, 2:130], cwb(2), ALU.mult, r=[X, self.cw], w=[z2])
        yield
        k.tt(z1[:, cs_, :], z1[:, cs_, :], z2[:, cs_, :], ALU.add, r=[z1, z2], w=[z1])
        yield
        k.act(z2[:, cs_, :], z1[:, cs_, :], AF.Tanh, r=[z1], w=[z2])
        yield
        QK = self.QKs.next()
        k.stt(QK[:, cs_, :], z2[:, cs_, :], 1.0, z1[:, cs_, :], ALU.add, ALU.mult, r=[z1, z2], w=[QK])
        yield
        o["QK"] = QK
        TK = self.TKs.next()
        k.dma("sp", TK[:], self.S_tok[t0:t0 + 128, :], r=[self.tok_tok], w=[TK])
        o["TK"] = TK
        vp = self.vps.next()
        k.copy(vp[:, :, 0:128], TK[:, 0:512].rearrange("p (h d) -> p h d", h=4), r=[TK], w=[vp], eng="pool")
        o["vp"] = vp
        if not light:
            thA = self.w512[7].next()
            thB = self.thBs.next()
            k.act(thA[:], TK[:, 512:1024], AF.Tanh, r=[TK], w=[thA], scale=0.5)
            yield
            k.act(thB[:], TK[:, 1808:2320], AF.Tanh, r=[TK], w=[thB], scale=0.5)
            yield
            o["thA"], o["thB"] = thA, thB
        gvb = self.gvbs.next()
        k.copy(gvb[:], TK[:, 1296:1808], r=[TK], w=[gvb], eng="act")
        yield
        o["gvb"] = gvb
        g = [rr.next() for rr in self.g8]
        ig, zf, l1f, t1, sw, qe, eg, kw = g
        k.tt(ig[:], TK[:, 1024:1032], self.gb[:, 0:8], ALU.add, r=[TK, self.gb], w=[ig])
        k.tt(zf[:], TK[:, 1032:1040], self.gb[:, 8:16], ALU.add, r=[TK, self.gb], w=[zf])
        yield
        k.act(zf[:], zf[:], AF.Exp, r=[zf], w=[zf], scale=-1.0)
        yield
        k.act(l1f[:], zf[:], AF.Ln, r=[zf], w=[l1f], bias=1.0)
        yield
        Gb = ctx.bank()
        k.mm(Gb[:, 0:4], ctx.cst("NU"), l1f[:, 0:4], r=[ctx.C, l1f], w=[Gb])
        k.mm(Gb[:, 4:8], ctx.cst("NL"), l1f[:, 4:8], r=[ctx.C, l1f], w=[Gb])
        k.mm(Gb[:, 8:16], ctx.cst("NONES"), l1f[:, 0:8], r=[ctx.C, l1f], w=[Gb])
        k.tt(t1[:], ig[:], Gb[:, 0:8], ALU.subtract, r=[ig, Gb], w=[t1])
        k.act(qe[:], Gb[:, 0:8], AF.Exp, r=[Gb], w=[qe])
        k.act(eg[:], Gb[:, 8:16], AF.Exp, r=[Gb], w=[eg])
        yield
        k.act(sw[:], t1[:], AF.Exp, r=[t1], w=[sw])
        yield
        k.ts(qe[:], qe[:], 128.0 ** -0.5, None, ALU.mult, r=[qe], w=[qe])
        yield
        k.tt(kw[:], sw[:], eg[:], ALU.mult, r=[sw, eg], w=[kw])
        yield
        o.update(sw=sw, qe=qe, eg=eg, kw=kw)
        kTb = ctx.bank()
        kTpb = kTb.t[:].bitcast(BF16)
        for h in range(4):
            k.tr(kTpb[:, h * 128:(h + 1) * 128], QK[:, 4 + h, :], ctx.identb[:], r=[QK, ctx.identb], w=[kTb])
        kTok = self.kToks.next()
        k.tt(kTok[:], kTpb[:, 0:512].rearrange("p (h t) -> p h t", h=4),
             kw[:, d_state * 4:(d_state + 1) * 4].unsqueeze(2).to_broadcast([128, 4, 128]), ALU.mult, r=[kTb, kw], w=[kTok])
        yield
        o["kTok"] = kTok
        glr = self.glrs.next()
        k.dma("sp", glr[:], self.S_fm[1536:1568, t0:t0 + 128], r=[self.tok_fm], w=[glr])
        if not light:
            gqk = self.gqks.next()
            k.dma("sp", gqk[:], self.S_fm[1024:1536, t0:t0 + 128].rearrange("(c p) t -> p c t", p=128), r=[self.tok_fm], w=[gqk])
        w = [rr.next() for rr in self.w512[0:7]]
        zb, l1, eT, emT, _q, _k, egmb = w
        qtT, ktT = (None, None) if light else (self.qts.next(), self.kts.next())
        zbk = ctx.bank()
        for d in range(2):
            k.mm(zbk[:, d * 256:(d + 1) * 256], glr[:], self.w2p[:, d, :], r=[glr, self.w2p], w=[zbk])
        k.tt(zb[:], zbk[:], self.dbias[:], ALU.add, r=[zbk, self.dbias], w=[zb])
        yield
        k.act(zb[:], zb[:], AF.Exp, r=[zb], w=[zb], scale=-1.0)
        yield
        k.act(l1[:], zb[:], AF.Ln, r=[zb], w=[l1], bias=1.0)
        yield
        bTb = ctx.bank()
        dirs = (1,) if light else (0, 1)
        for d in dirs:
            for j in range(2):
                i = d * 2 + j
                k.mm(bTb[:, i * 128:(i + 1) * 128], l1[:, d * 256 + j * 128:d * 256 + (j + 1) * 128],
                     ctx.cst("UN" if d == 0 else "LN"), r=[l1, ctx.C], w=[bTb])
        if light:
            k.act(eT[:, 256:512], bTb[:, 256:512], AF.Exp, r=[bTb], w=[eT])
        else:
            k.act(eT[:], bTb[:], AF.Exp, r=[bTb], w=[eT])
            k.act(emT[:], bTb[:], AF.Exp, r=[bTb], w=[emT], scale=-1.0)
        yield
        v4 = lambda t: t[:].rearrange("p (a b) -> p a b", a=4)
        for d in (() if light else (0, 1)):
            k.stt(v4(qtT)[:, d * 2:(d + 1) * 2, :], gqk[:, 0:2, :], 0.125, v4(eT)[:, d * 2:(d + 1) * 2, :], ALU.mult, ALU.mult,
                  r=[gqk, eT], w=[qtT])
            yield
            k.tt(v4(ktT)[:, d * 2:(d + 1) * 2, :], gqk[:, 2:4, :], v4(emT)[:, d * 2:(d + 1) * 2, :], ALU.mult, r=[gqk, emT], w=[ktT])
            yield
        gmb = ctx.bank()
        if not light:
            k.mm(gmb[:, 0:256], ctx.cst("UC"), l1[:, 0:256], r=[ctx.C, l1], w=[gmb])
        k.mm(gmb[:, 256:512], ctx.cst("LC"), l1[:, 256:512], r=[ctx.C, l1], w=[gmb])
        if light:
            k.act(egmb[:, 256:512], gmb[:, 256:512], AF.Exp, r=[gmb], w=[egmb])
        else:
            k.act(egmb[:], gmb[:], AF.Exp, r=[gmb], w=[egmb])
        yield
        khat = self.khats.next()
        for d in dirs:
            k.tt(khat[:, d, :], TK[:, 1040:1296], egmb[:, d * 256:(d + 1) * 256], ALU.mult, r=[TK, egmb], w=[khat])
            yield
        o.update(eT=eT, qtT=qtT, ktT=ktT, khat=khat)
        return o

    def state_update(self, o, d, Cold, Sold, Cring, Sring):
        ctx, k = self.ctx, self.k
        TK, vp = o["TK"], o["vp"]
        kTok = o["kTok"]
        Cn = Cring.next()
        for p in range(2):
            b = ctx.bank()
            bv = b[:, 0:258].rearrange("p (h e) -> p h e", h=2)
            for hh in range(2):
                h = 2 * p + hh
                k.mm(bv[:, hh, :], kTok[:, h, :], vp[:, h, :], r=[kTok, vp], w=[b])
            for hh in range(2):
                h = 2 * p + hh
                k.stt(Cn[:, h, :], Cold[:, h, :], o["eg"][:, d * 4 + h:d * 4 + h + 1], bv[:, hh, :], ALU.mult, ALU.add,
                      r=[Cold, o["eg"], b], w=[Cn])
            yield
        Sn = Sring.next()
        eT4 = o["eT"][:].rearrange("p (a b) -> p a b", a=4)
        col = 127 if d == 0 else 0
        for j in range(2):
            b = ctx.bank()
            k.mm(b[:, 0:256], o["khat"][:, d, j * 128:(j + 1) * 128], o["gvb"][:, j * 256:(j + 1) * 256], r=[o["khat"], o["gvb"]], w=[b])
            for e in range(2):
                rows = slice(e * 64, (e + 1) * 64)
                k.stt(Sn[rows, j, :], Sold[rows, j, :], eT4[rows, d * 2 + j, col:col + 1], b[rows, e * 128:(e + 1) * 128],
                      ALU.mult, ALU.add, r=[Sold, o["eT"], b], w=[Sn])
            yield
        return Cn, Sn

    def pass1(self, s):
        ctx, k = self.ctx, self.k
        nch = ctx.T // 128
        Cb = self.CB.next()
        Sb = self.SB.next()
        k.memset(Cb[:], 0.0, w=[Cb])
        k.memset(Sb[:], 0.0, w=[Sb])
        for c in range(nch - 1, -1, -1):
            idx = s * nch + c
            k.dma("sp", self.S_cb[idx], Cb[:].rearrange("p h e -> p (h e)"), r=[Cb], w=[self.tok_cb])
            k.dma("sp", self.S_sb[idx], Sb[:].rearrange("p j v -> p (j v)"), r=[Sb], w=[self.tok_cb])
            yield
            if c == 0:
                break
            o = yield from self.prep(s, c, 1, light=True)
            Cb, Sb = yield from self.state_update(o, 1, Cb, Sb, self.CB, self.SB)

    def pass2(self, s, x_in, x_out, tok_in, tok_out, outs):
        ctx, k = self.ctx, self.k
        T = ctx.T
        nch = T // 128
        Cf = self.CF.next()
        Sf = self.SF.next()
        k.memset(Cf[:], 0.0, w=[Cf])
        k.memset(Sf[:], 0.0, w=[Sf])
        Cfb, Sfb = self.Cfb.next(), self.Sfb.next()
        k.memset(Cfb[:], 0.0, w=[Cfb])
        k.memset(Sfb[:], 0.0, w=[Sfb])
        MU, ML = ctx.cst("MU"), ctx.cst("ML")
        import os
        kp2 = int(os.environ.get("KP2", "9"))
        for c in range(nch):
            t0 = s * T + c * 128
            idx = s * nch + c
            o = yield from self.prep(s, c, 0)
            QK, TK, vp = o["QK"], o["TK"], o["vp"]
            Cb = self.CB.next()
            Sb = self.SB.next()
            k.dma("sp", Cb[:].rearrange("p h e -> p (h e)"), self.S_cb[idx], r=[self.tok_cb], w=[Cb])
            k.dma("sp", Sb[:].rearrange("p j v -> p (j v)"), self.S_sb[idx], r=[self.tok_cb], w=[Sb])
            Cbb, Sbb = self.Cbb.next(), self.Sbb.next()
            k.copy(Cbb[:], Cb[:], r=[Cb], w=[Cbb], eng="act")
            k.copy(Sbb[:], Sb[:], r=[Sb], w=[Sbb], eng="pool")
            yield
            sb_ = ctx.bank()
            for h in range(4):
                k.mm(sb_[:, h * 128:(h + 1) * 128], QK[:, 4 + h, :], QK[:, h, :], r=[QK], w=[sb_])
            pFB = self.pFB.next()
            for d in range(2):
                for h in range(4):
                    k.stt(pFB[:, d * 4 + h, :], sb_[:, h * 128:(h + 1) * 128], o["sw"][:, d * 4 + h:d * 4 + h + 1], MU if d == 0 else ML,
                          ALU.mult, ALU.mult, r=[sb_, o["sw"], ctx.C], w=[pFB])
            if kp2 <= 1:
                continue
            yield
            sm = [rr.next() for rr in self.small]
            d1, nd, d2, rr_, ss, rs, ss2, rs2 = sm
            nb = {}
            for p in range(2):
                for d in range(2):
                    b = ctx.bank()
                    bv = b[:, 0:258].rearrange("p (h e) -> p h e", h=2)
                    Cst = Cfb if d == 0 else Cbb
                    for hh in range(2):
                        h = 2 * p + hh
                        k.mm(bv[:, hh, :], pFB[:, d * 4 + h, :], vp[:, h, :], True, False, r=[pFB, vp], w=[b])
                        k.mm(bv[:, hh, :], QK[:, h, :], Cst[:, h, :], False, True, r=[QK, Cst], w=[b])
                    nb[(p, d)] = (b, bv)
                    k.tt(d1[:, d * 4 + 2 * p:d * 4 + 2 * p + 2], bv[:, :, 128], o["qe"][:, d * 4 + 2 * p:d * 4 + 2 * p + 2], ALU.mult,
                         r=[b, o["qe"]], w=[d1])
            k.ts(nd[:], d1[:], -1.0, None, ALU.mult, r=[d1], w=[nd])
            k.tt(d2[:], d1[:], nd[:], ALU.max, r=[d1, nd], w=[d2])
            k.ts(d2[:], d2[:], 1.0, None, ALU.max, r=[d2], w=[d2])
            k.recip(d2[:], d2[:], r=[d2], w=[d2])
            k.tt(rr_[:], d2[:], o["qe"][:], ALU.mult, r=[d2, o["qe"]], w=[rr_])
            hm = self.hms.next()
            for h in range(4):
                p, hh = h // 2, h % 2
                bF, bvF = nb[(p, 0)]
                bB, bvB = nb[(p, 1)]
                k.ts(hm[:, h, :], bvF[:, hh, 0:128], rr_[:, h:h + 1], None, ALU.mult, r=[bF, rr_], w=[hm])
                k.stt(hm[:, h, :], bvB[:, hh, 0:128], rr_[:, 4 + h:5 + h], hm[:, h, :], ALU.mult, ALU.add, r=[bB, rr_, hm], w=[hm])
            yield
            for h in range(4):
                junk = self.junks.next()
                k.act(junk[:], hm[:, h, :], AF.Square, r=[hm], w=[junk, ss], accum_out=ss[:, h:h + 1])
            yield
            k.ts(rs[:, 0:4], ss[:, 0:4], 1.0 / 128, EPS, ALU.mult, ALU.add, r=[ss], w=[rs])
            yield
            k.act(rs[:, 0:4], rs[:, 0:4], AF.Ln, r=[rs], w=[rs])
            yield
            k.act(rs[:, 0:4], rs[:, 0:4], AF.Exp, r=[rs], w=[rs], scale=-0.5)
            yield
            wA = o["thA"]
            k.stt(wA[:], wA[:], 1.0, self.mnorm[:], ALU.add, ALU.mult, r=[wA, self.mnorm], w=[wA])
            yield
            mg = self.merged.next()
            for h in range(4):
                k.stt(mg[:, h * 128:(h + 1) * 128], hm[:, h, :], rs[:, h:h + 1], wA[:, h * 128:(h + 1) * 128], ALU.mult, ALU.mult,
                      r=[hm, rs, wA], w=[mg])
            if kp2 <= 2:
                continue
            yield
            qt4 = o["qtT"][:].rearrange("p (a b) -> p a b", a=4)
            kt4 = o["ktT"][:].rearrange("p (a b) -> p a b", a=4)
            pA = self.pA.next()
            ktz, qtz = self.ktz.next(), self.qtz.next()
            for d in range(2):
                for h in range(4):
                    j, e = h // 2, h % 2
                    rows = slice(e * 64, (e + 1) * 64)
                    k.copy(ktz[rows, d * 4 + h, :], kt4[rows, d * 2 + j, :], r=[o["ktT"]], w=[ktz], eng="pool")
                    k.copy(qtz[rows, d * 4 + h, :], qt4[rows, d * 2 + j, :], r=[o["qtT"]], w=[qtz])
            yield
            for d in range(2):
                b = ctx.bank()
                for h in range(4):
                    j, e = h // 2, h % 2
                    k.mm(b[:, h * 128:(h + 1) * 128], ktz[:, d * 4 + h, :], qt4[:, d * 2 + j, :], r=[ktz, o["qtT"]], w=[b])
                k.tt(pA[:, d * 4:(d + 1) * 4, :], b[:].rearrange("p (h t) -> p h t", h=4),
                     (MU if d == 0 else ML).unsqueeze(1).to_broadcast([128, 4, 128]), ALU.mult, r=[b, ctx.C], w=[pA])
                yield
            if kp2 <= 3:
                continue
            ob = ctx.bank()
            for h in range(4):
                j, e = h // 2, h % 2
                rows = slice(e * 64, (e + 1) * 64)
                gv = o["gvb"][:, h * 128:(h + 1) * 128]
                dst = ob[:, h * 128:(h + 1) * 128]
                k.mm(dst, pA[:, h, :], gv, True, False, r=[pA, o["gvb"]], w=[ob])
                k.mm(dst, pA[:, 4 + h, :], gv, False, False, r=[pA, o["gvb"]], w=[ob])
                k.mm(dst, qtz[:, h, :], Sfb[:, j, :], False, False, r=[qtz, Sfb], w=[ob])
                k.mm(dst, qtz[:, 4 + h, :], Sbb[:, j, :], False, True, r=[qtz, Sbb], w=[ob])
            for h in range(4):
                junk = self.junks.next()
                k.act(junk[:], ob[:, h * 128:(h + 1) * 128], AF.Square, r=[ob], w=[junk, ss2], accum_out=ss2[:, h:h + 1])
            k.ts(rs2[:, 0:4], ss2[:, 0:4], 1.0 / 128, EPS, ALU.mult, ALU.add, r=[ss2], w=[rs2])
            k.act(rs2[:, 0:4], rs2[:, 0:4], AF.Ln, r=[rs2], w=[rs2])
            k.act(rs2[:, 0:4], rs2[:, 0:4], AF.Exp, r=[rs2], w=[rs2], scale=-0.5)
            wB = o["thB"]
            k.stt(wB[:], wB[:], 1.0, TK[:, 1808:2320], ALU.add, ALU.mult, r=[wB, TK], w=[wB])
            k.tt(wB[:], wB[:], self.gnorm[:], ALU.mult, r=[wB, self.gnorm], w=[wB], eng="pool")
            for h in range(4):
                k.stt(mg[:, 512 + h * 128:512 + (h + 1) * 128], ob[:, h * 128:(h + 1) * 128], rs2[:, h:h + 1],
                      wB[:, h * 128:(h + 1) * 128], ALU.mult, ALU.mult, r=[ob, rs2, wB], w=[mg])
            if kp2 <= 4:
                continue
            yield
            if c < nch - 1:
                Cf, Sf = yield from self.state_update(o, 0, Cf, Sf, self.CF, self.SF)
                Cfb, Sfb = self.Cfb.next(), self.Sfb.next()
                k.copy(Cfb[:], Cf[:], r=[Cf], w=[Cfb], eng="act")
                k.copy(Sfb[:], Sf[:], r=[Sf], w=[Sfb], eng="pool")
                yield
            mT = self.mTs.next()
            self.nt.to_fm(mg, mT[:], mT)
            yield
            xo = self.xos.next()
            k.dma("sp", xo[:], x_in[t0:t0 + 128, :], r=[tok_in], w=[xo])
            for half in range(2):
                po = ctx.bank()
                for cc in range(8):
                    k.mm(po[:], mT[:, cc, :], self.Wout[:, cc, half * 512:(half + 1) * 512], cc == 0, cc == 7, r=[mT, self.Wout], w=[po])
                k.tt(xo[:, half * 512:(half + 1) * 512], po[:], xo[:, half * 512:(half + 1) * 512], ALU.add, r=[po, xo], w=[xo])
                yield
            outs.append(k.dma("sp", x_out[t0:t0 + 128, :], xo[:], r=[xo], w=[tok_out]))
            yield


def phase_b0(ctx, x_in, x_out, S_fm, S_tok, S_cb, S_sb, prm, tok_in, tok_fm, tok_tok, tok_out):
    mark = ctx.P.mark()
    mx = MixL0(ctx, S_fm, S_tok, S_cb, S_sb, prm["conv"], prm["igb"], prm["fgb"], prm["mnorm"], prm["w2"], prm["db"],
               prm["gnorm"], prm["wout"], tok_fm, tok_tok)
    outs = []
    import os
    kb0 = int(os.environ.get("KB0", "9"))
    if kb0 == 0:
        return list(ctx.P.dma_last.values())
    def run_all(gens):
        alive = list(gens)
        while alive:
            nxt = []
            for g_ in alive:
                try:
                    next(g_)
                    nxt.append(g_)
                except StopIteration:
                    pass
            alive = nxt
    for s0 in range(0, ctx.nseq, 2):
        ss_ = list(range(s0, min(s0 + 2, ctx.nseq)))
        run_all([mx.pass1(s) for s in ss_])
        run_all([mx.pass2(s, x_in, x_out, tok_in, tok_out, outs) for s in ss_])
    ctx.P.release(mark)
    return outs


C0 = float(np.exp(-0.5))
LCH = 64


def phase_a1(ctx, x_in, prm, S1, S_wt, S_vtok, S_bonus, S_g, tok_in, tok_s1):
    k = ctx.k
    n, T = ctx.n, ctx.T
    TB = 256
    NQ = TB // LCH
    mark = ctx.P.mark()
    g_fm = load_gain_fm(ctx, prm["gain"])
    Wr = load_w(ctx, prm["w_rkv"][0], g_fm)
    Wk = load_w(ctx, prm["w_rkv"][1], g_fm)
    Wv = load_w(ctx, prm["w_rkv"][2], g_fm)
    W1 = k.tile([128, 8, 352], BF16)
    load_w(ctx, prm["w1"][0], None, dst=W1, col0=0)
    load_w(ctx, prm["w1"][1], None, dst=W1, col0=64)
    load_w(ctx, prm["a1"], None, dst=W1, col0=128)
    load_w(ctx, prm["g1"], None, dst=W1, col0=192)
    for c in range(8):
        k.ts(W1[:, c, :], W1[:, c, :], g_fm[:, c:c + 1], None, ALU.mult, r=[W1, g_fm], w=[W1])
    w2t = k.tile([64, 3, D], BF16)
    k.dma("pool", w2t[:, 0, :], prm["w2"][0], w=[w2t])
    k.dma("pool", w2t[:, 1, :], prm["w2"][1], w=[w2t])
    k.dma("pool", w2t[:, 2, :], prm["a2"], w=[w2t])
    g2t = k.tile([128, 2, D], BF16)
    k.dma("pool", g2t[:, 0, :], prm["g2"][0:128, :], w=[g2t])
    k.dma("pool", g2t[0:32, 1, :], prm["g2"][128:160, :], w=[g2t])
    pc = k.tile([128, 7, 8], F32)
    srcs = [prm["w0"][0], prm["w0"][1], prm["a0"], prm["k_k"], prm["k_a"], prm["k_a"], prm["r_k"].rearrange("h d -> (h d)")]
    for i, sap in enumerate(srcs):
        k.dma("sp", pc[:, i, :], sap.rearrange("(c p) -> p c", p=128), w=[pc], allow_slow_non_contiguous=True)
    k.ts(pc[:, 5, :], pc[:, 5, :], -1.0, 1.0, ALU.mult, ALU.add, r=[pc], w=[pc])
    mu = k.tile([128, 6, 8], F32)
    for i in range(6):
        k.dma("sp", mu[:, i, :], prm["mu"][i].rearrange("(c p) -> p c", p=128), w=[mu], allow_slow_non_contiguous=True)
    nt = NormT(ctx, with_xn=False)
    xts = k.ring(2, [128, D], F32)
    xnf = k.ring(1, [128, D], F32)
    xh = k.ring(1, [2, D], F32)
    hTs = k.ring(1, [128, 8, TB + 2], F32)
    hhs = k.ring(1, [128, 8, TB], F32)
    mix_sets = [[k.tile([128, 8, TB], BF16) for _ in range(6)] for _ in range(2)]
    lows = k.ring(2, [128, 5, TB], BF16)
    NWAY = 2
    fsets = [[k.tile([128, TB], F32) for _ in range(13)] for _ in range(NWAY)]
    vts = k.ring(2, [128, D], BF16)
    ob4s = k.ring(5, [128, 4, TB], BF16)
    bg16 = k.ring(4, [128, TB], BF16)
    wtall = k.ring(2, [128, 2, 8, NQ], F32)
    ident = ctx.cst("ident")
    flipb = [0]

    def prologue(blk):
        mixes = mix_sets[blk % 2]
        b0 = blk * TB
        tpos = b0 % T
        hT = hTs.next()
        xhh = xh.next()
        k.memset(xhh[:], 0.0, w=[xhh])
        if tpos > 0:
            k.dma("sp", xhh[0:1, :], x_in[b0 - 1:b0, :], r=[tok_in], w=[xhh])
        if tpos + TB < T:
            k.dma("sp", xhh[1:2, :], x_in[b0 + TB:b0 + TB + 1, :], r=[tok_in], w=[xhh])
        rs = nt.rstd(xhh[:], xhh, rows=2)
        xn2 = xhh
        k.ts(xn2[:], xhh[:], rs[0:2, :], None, ALU.mult, r=[xhh, rs], w=[xn2])
        bk = ctx.bank()
        for c in range(8):
            k.tr(bk[:, c * 2:(c + 1) * 2], xn2[0:2, c * 128:(c + 1) * 128], ident[0:2, 0:2], r=[xn2, ctx.C], w=[bk])
        bkv = bk[:, 0:16].rearrange("p (c e) -> p c e", e=2)
        k.copy(hT[:, :, 0], bkv[:, :, 0], r=[bk], w=[hT])
        k.copy(hT[:, :, TB + 1], bkv[:, :, 1], r=[bk], w=[hT])
        yield
        for j in range(TB // 128):
            t0 = b0 + j * 128
            xt = xts.next()
            k.dma("sp", xt[:], x_in[t0:t0 + 128, :], r=[tok_in], w=[xt])
            rs = nt.rstd(xt[:], xt)
            xn = xnf.next()
            k.ts(xn[:], xt[:], rs[:], None, ALU.mult, r=[xt, rs], w=[xn])
            yield
            for half in range(2):
                bk = ctx.bank()
                for c4 in range(4):
                    c = half * 4 + c4
                    k.tr(bk[:, c4 * 128:(c4 + 1) * 128], xn[:, c * 128:(c + 1) * 128], ident, r=[xn, ctx.C], w=[bk])
                flipb[0] ^= 1
                k.copy(hT[:, half * 4:(half + 1) * 4, 1 + j * 128:1 + (j + 1) * 128], bk[:].rearrange("p (c t) -> p c t", c=4),
                       r=[bk], w=[hT], eng="act" if flipb[0] else "dve")
                yield
        hh = hhs.next()
        k.tt(hh[:], hT[:, :, 0:TB], hT[:, :, 2:TB + 2], ALU.add, r=[hT], w=[hh], eng="pool")
        yield
        k.stt(hh[:], hh[:], 0.5, hT[:, :, 1:TB + 1], ALU.mult, ALU.subtract, r=[hh, hT], w=[hh])
        yield
        for i in range(6):
            for c in range(8):
                k.stt(mixes[i][:, c, :], hh[:, c, :], mu[:, i, c:c + 1], hT[:, c, 1:TB + 1], ALU.mult, ALU.add, r=[hh, mu, hT], w=[mixes[i]])
                yield
        xr, xw, xk, xv, xa, xg = mixes
        low = lows.next()
        specs = [(xw, 0, 64, 0, AF.Tanh), (xw, 64, 64, 1, AF.Tanh), (xa, 128, 64, 2, AF.Copy), (xg, 192, 128, 3, AF.Sigmoid), (xg, 320, 32, 4, AF.Sigmoid)]
        for (src, c0, m, slot, fn) in specs:
            bk = ctx.bank()
            for c in range(8):
                k.mm(bk[0:m, 0:TB], W1[:, c, c0:c0 + m], src[:, c, :], c == 0, c == 7, r=[W1, src], w=[bk])
            k.act(low[0:m, slot, :], bk[0:m, 0:TB], fn, r=[bk], w=[low])
            yield
        for j in range(TB // 128):
            t0 = b0 + j * 128
            vt = vts.next()
            for half in range(2):
                bk = ctx.bank()
                for c in range(8):
                    k.mm(bk[:], xv[:, c, j * 128:(j + 1) * 128], Wv[:, c, half * 512:(half + 1) * 512], c == 0, c == 7, r=[xv, Wv], w=[bk])
                flipb[0] ^= 1
                k.copy(vt[:, half * 512:(half + 1) * 512], bk[:], r=[bk], w=[vt], eng="act" if flipb[0] else "dve")
                yield
            k.dma("sp", S_vtok[t0:t0 + 128, :], vt[:], r=[vt], w=[tok_s1])
            yield
        wta = wtall.next()
        wta_parts = [Buf("wt") for _ in range(16)]
        return (xr, xk, xv, low, b0, blk, wta, wta_parts)

    def fc_chain(fc, F, B):
        xr, xk, xv, low, b0, blk, wta, wta_parts = B
        fs = slice(fc * 128, (fc + 1) * 128)
        r_, k_, v_, a_, kk, k2, t1, t2, sg, G, cI, cE, W = F
        col = lambda i: pc[:, i, fc:fc + 1]

        def proj(Wt, src):
            bk = ctx.bank()
            for c in range(8):
                k.mm(bk[:, 0:TB], Wt[:, c, fs], src[:, c, :], c == 0, c == 7, r=[Wt, src], w=[bk])
            return bk
        bk = proj(Wr, xr)
        k.copy(r_[:], bk[:, 0:TB], r=[bk], w=[r_], eng="act")
        yield
        bk = proj(Wk, xk)
        k.copy(k_[:], bk[:, 0:TB], r=[bk], w=[k_])
        yield
        bk = proj(Wv, xv)
        k.copy(v_[:], bk[:, 0:TB], r=[bk], w=[v_], eng="act")
        yield
        bk = ctx.bank()
        k.mm(bk[:, 0:TB], w2t[:, 2, fs], low[0:64, 2, :], r=[w2t, low], w=[bk])
        k.act(a_[:], bk[:, 0:TB], AF.Sigmoid, r=[bk, pc], w=[a_], bias=col(2))
        yield
        bk = ctx.bank()
        k.mm(bk[:, 0:TB], g2t[:, 0, fs], low[:, 3, :], True, False, r=[g2t, low], w=[bk])
        k.mm(bk[:, 0:TB], g2t[0:32, 1, fs], low[0:32, 4, :], False, True, r=[g2t, low], w=[bk])
        gb16 = bg16.next()
        k.copy(gb16[:], bk[:, 0:TB], r=[bk], w=[gb16])
        k.dma("sp", S_g[blk, :, fc, :], gb16[:], r=[gb16], w=[tok_s1])
        yield
        k.ts(kk[:], k_[:], col(3), None, ALU.mult, r=[k_, pc], w=[kk])
        yield
        k.act(t2[:], kk[:], AF.Square, r=[kk], w=[t2])
        yield
        bk = ctx.bank()
        k.mm(bk[:, 0:TB], ctx.cst("BLK"), t2[:], r=[ctx.C, t2], w=[bk])
        k.ts(t2[:], bk[:, 0:TB], 1e-24, None, ALU.max, r=[bk], w=[t2])
        yield
        k.act(t2[:], t2[:], AF.Ln, r=[t2], w=[t2])
        yield
        k.act(t2[:], t2[:], AF.Exp, r=[t2], w=[t2], scale=-0.5)
        yield
        k.tt(kk[:], kk[:], t2[:], ALU.mult, r=[kk, t2], w=[kk])
        k.ts(k2[:], a_[:], col(4), col(5), ALU.mult, ALU.add, r=[a_, pc], w=[k2])
        yield
        k.tt(k2[:], k2[:], k_[:], ALU.mult, r=[k2, k_], w=[k2])
        yield
        k.stt(t2[:], r_[:], col(6), k2[:], ALU.mult, ALU.mult, r=[r_, pc, k2], w=[t2])
        yield
        bk = ctx.bank()
        k.mm(bk[:, 0:TB], ctx.cst("BLK"), t2[:], r=[ctx.C, t2], w=[bk])
        bb16 = bg16.next()
        k.tt(bb16[:], bk[:, 0:TB], v_[:], ALU.mult, r=[bk, v_], w=[bb16])
        k.dma("sp", S_bonus[blk, :, fc, :], bb16[:], r=[bb16], w=[tok_s1])
        k.tt(a_[:], a_[:], kk[:], ALU.mult, r=[a_, kk], w=[a_], eng="pool")
        yield
        for d in range(2):
            bk = ctx.bank()
            k.mm(bk[:, 0:TB], w2t[:, d, fs], low[0:64, d, :], r=[w2t, low], w=[bk])
            k.act(sg[:], bk[:, 0:TB], AF.Sigmoid, r=[bk, pc], w=[sg], bias=col(d))
            yield
            ctx.P.op("dve", lambda e, G=G, sg=sg: e.tensor_tensor_scan(out=G[:], data0=sg[:], data1=sg[:], initial=0.0, op0=ALU.add, op1=ALU.bypass),
                     _tok([sg]), _tok([G]))
            yield
            G3 = G[:].rearrange("p (q t) -> p q t", t=LCH)
            c3 = cI[:].rearrange("p (q t) -> p q t", t=LCH)
            e3 = cE[:].rearrange("p (q t) -> p q t", t=LCH)
            k.copy(c3[:, 0, :], G3[:, 0, :], r=[G], w=[cI], eng="pool")
            k.tt(c3[:, 1:NQ, :], G3[:, 1:NQ, :], G3[:, 0:NQ - 1, LCH - 1:LCH].to_broadcast([128, NQ - 1, LCH]), ALU.subtract, r=[G], w=[cI])
            yield
            tot = c3[:, :, LCH - 1:LCH]
            k.act(wta[:, d, fc, :], c3[:, :, LCH - 1], AF.Exp, r=[cI], w=[wta_parts[d * 8 + fc]], scale=-C0)
            if d == 0:
                k.tt(cE[:], cI[:], sg[:], ALU.subtract, r=[cI, sg], w=[cE], eng="pool")
                inc, exc = cI, cE
                yield
            else:
                k.tt(e3, tot.to_broadcast([128, NQ, LCH]), c3, ALU.subtract, r=[cI], w=[cE])
                yield
                k.tt(G[:], cE[:], sg[:], ALU.add, r=[cE, sg], w=[G], eng="pool")
                inc, exc = G, cE
                yield
            base = d * 4
            k.act(W[:], inc[:], AF.Exp, r=[inc], w=[W], scale=-C0)
            yield
            ob = ob4s.next()
            k.tt(ob[:, 3, :], r_[:], W[:], ALU.mult, r=[r_, W], w=[ob])
            yield
            k.act(W[:], inc[:], AF.Exp, r=[inc], w=[W], scale=C0)
            yield
            k.tt(ob[:, 2, :], k2[:], W[:], ALU.mult, r=[k2, W], w=[ob])
            k.tt(ob[:, 1, :], a_[:], W[:], ALU.mult, r=[a_, W], w=[ob], eng="pool")
            yield
            k.act(W[:], exc[:], AF.Exp, r=[exc], w=[W], scale=-C0)
            yield
            k.stt(ob[:, 0, :], kk[:], -1.0, W[:], ALU.mult, ALU.mult, r=[kk, W], w=[ob])
            k.dma("sp", S1[base:base + 4, fs, b0:b0 + TB].rearrange("a q t -> q a t"), ob[:], r=[ob], w=[tok_s1])
            yield

    def step(g_):
        try:
            next(g_)
            return True, None
        except StopIteration as e_:
            return False, e_.value

    nblk = n // TB
    pro = prologue(0)
    while True:
        ok, val = step(pro)
        if not ok:
            Bcur = val
            break
    for blk in range(nblk):
        pro = prologue(blk + 1) if blk + 1 < nblk else None
        Bnext = None
        for g0 in range(0, 8, NWAY):
            alive = [fc_chain(g0 + i_, fsets[i_], Bcur) for i_ in range(NWAY)]
            while alive:
                if pro is not None:
                    ok, val = step(pro)
                    if not ok:
                        Bnext, pro = val, None
                alive = [g_ for g_ in alive if step(g_)[0]]
        while pro is not None:
            ok, val = step(pro)
            if not ok:
                Bnext, pro = val, None
        wta, wta_parts = Bcur[6], Bcur[7]
        for d in range(2):
            k.dma("sp", S_wt[d].rearrange("(p q) c -> q p c", q=128)[:, :, blk * NQ:(blk + 1) * NQ], wta[:, d, :, :],
                  r=wta_parts[d * 8:(d + 1) * 8], w=[tok_s1], allow_slow_non_contiguous=True)
        Bcur = Bnext
    ctx.P.release(mark)


def phase_b1(ctx, x_in, x_out, prm, S1, S_wt, S_vtok, S_bonus, S_g, S_yb, tok_in, tok_s1, tok_out):
    k = ctx.k
    n, T = ctx.n, ctx.T
    NCH = T // LCH
    mark = ctx.P.mark()
    Wo = load_w(ctx, prm["w_o"])
    lnw = k.tile([128, 8], F32)
    lnb = k.tile([128, 8], F32)
    k.dma("sp", lnw[:], prm["ln_w"].rearrange("(c p) -> p c", p=128), w=[lnw], allow_slow_non_contiguous=True)
    k.dma("sp", lnb[:], prm["ln_b"].rearrange("(c p) -> p c", p=128), w=[lnb], allow_slow_non_contiguous=True)
    tok_yb = Buf("yb")

    def bdring(nslots):
        rr = k.ring(nslots, [128, 8, 128], F32)
        for t in rr.tiles:
            k.memset(t[:], 0.0, w=[t])
        return rr
    ATs, BTs, KTs, Vbs = bdring(2), bdring(2), bdring(2), bdring(2)
    RTs = k.ring(2, [128, 8, LCH], F32)
    wts = k.ring(2, [128, 8], F32)
    big = lambda nslots: k.ring(nslots, [128, 8, 128], F32)
    Ns, NTs, Ps = big(2), big(2), big(2)
    AKs, Xs, Us, Bts, Kts = big(1), big(1), big(1), big(1), big(1)
    Ms = big(2)
    tmpM = big(1)
    RBs = k.ring(1, [128, 8, LCH], F32)
    RKs = k.ring(1, [128, 8, LCH], F32)
    ysb = k.ring(2, [128, 8, LCH], F32)
    ybl = k.ring(2, [128, 8, LCH], F32)
    o512 = [k.ring(2, [128, 8, LCH], F32) for _ in range(5)]
    zTs = k.ring(2, [128, 8, LCH], BF16)
    xrs = k.ring(2, [LCH, D], F32)
    xos = k.ring(2, [LCH, D], F32)
    ident = ctx.cst("ident")
    outs = []
    flip = [0]

    def bd_src(ap2d, t0):
        v = ap2d.rearrange("(p e k) t -> e k p t", e=2, k=64)
        return [v[e][:, :, t0:t0 + LCH] for e in range(2)]

    def evac(dst_ap, src_ap, r, w):
        flip[0] ^= 1
        k.copy(dst_ap, src_ap, r=r, w=w, eng="act" if flip[0] else "dve")

    def pairs_mm(lhs_tile, rhs_tile, width=128, lhs2=None, rhs2=None):
        per_bank = 512 // width
        res = []
        for b0 in range(0, 8, per_bank):
            bk = ctx.bank()
            for p in range(b0, b0 + per_bank):
                o = bk[:, (p - b0) * width:(p - b0 + 1) * width]
                k.mm(o, lhs_tile[:, p, :], rhs_tile[:, p, :], True, lhs2 is None, r=[lhs_tile, rhs_tile], w=[bk])
                if lhs2 is not None:
                    k.mm(o, lhs2[:, p, :], rhs2[:, p, :], False, True, r=[lhs2, rhs2], w=[bk])
            res.append((bk, bk[:].rearrange("p (a b) -> p a b", b=width), b0, per_bank))
        return res

    for s in range(ctx.nseq):
        for d in (1, 0):
            base = d * 4
            Mst = Ms.next()
            k.memset(Mst[:], 0.0, w=[Mst])
            strict = ctx.cst("SF" if d == 0 else "SB")
            strictT = ctx.cst("SB" if d == 0 else "SF")
            incl = ctx.cst("IF" if d == 0 else "IB")[:, 0:LCH]
            order = range(NCH) if d == 0 else range(NCH - 1, -1, -1)
            for c in order:
                t0 = s * T + c * LCH
                cg = t0 // LCH
                AT, BT, KT, Vb = ATs.next(), BTs.next(), KTs.next(), Vbs.next()
                for e in range(2):
                    rows = slice(e * 64, (e + 1) * 64)
                    k.dma("sp", AT[rows, :, rows], bd_src(S1[base + 0], t0)[e], r=[tok_s1], w=[AT])
                    k.dma("sp", BT[rows, :, rows], bd_src(S1[base + 1], t0)[e], r=[tok_s1], w=[BT])
                    k.dma("sp", KT[rows, :, rows], bd_src(S1[base + 2], t0)[e], r=[tok_s1], w=[KT])
                    k.dma("sp", Vb[rows, :, rows], S_vtok[t0:t0 + LCH, :].rearrange("t (p e v) -> e t p v", e=2, v=64)[e],
                          r=[tok_s1], w=[Vb])
                RT = RTs.next()
                k.dma("sp", RT[:], S1[base + 3].rearrange("(p q) t -> q p t", q=128)[:, :, t0:t0 + LCH], r=[tok_s1], w=[RT])
                wt = wts.next()
                k.dma("sp", wt[:], S_wt[d].rearrange("(p q) c -> q p c", q=128)[:, :, cg], r=[tok_s1], w=[wt], allow_slow_non_contiguous=True)
                N, NT, P_ = Ns.next(), NTs.next(), Ps.next()
                AK = AKs.next()
                for (bk, v, b0, nb) in pairs_mm(BT, AT):
                    k.tt(N[:, b0:b0 + nb, :], v, strict.unsqueeze(1).to_broadcast([128, nb, 128]), ALU.mult, r=[bk, ctx.C], w=[N])
                for (bk, v, b0, nb) in pairs_mm(AT, BT):
                    k.tt(NT[:, b0:b0 + nb, :], v, strictT.unsqueeze(1).to_broadcast([128, nb, 128]), ALU.mult, r=[bk, ctx.C], w=[NT])
                for (bk, v, b0, nb) in pairs_mm(KT, AT):
                    k.tt(AK[:, b0:b0 + nb, :], v, strict.unsqueeze(1).to_broadcast([128, nb, 128]), ALU.mult, r=[bk, ctx.C], w=[AK])
                RB, RK = RBs.next(), RKs.next()
                for (bk, v, b0, nb) in pairs_mm(BT, RT, width=LCH):
                    k.tt(RB[:, b0:b0 + nb, :], v, incl.unsqueeze(1).to_broadcast([128, nb, LCH]), ALU.mult, r=[bk, ctx.C], w=[RB])
                for (bk, v, b0, nb) in pairs_mm(KT, RT, width=LCH):
                    k.tt(RK[:, b0:b0 + nb, :], v, incl.unsqueeze(1).to_broadcast([128, nb, LCH]), ALU.mult, r=[bk, ctx.C], w=[RK])
                k.tt(P_[:], N[:], ident.unsqueeze(1).to_broadcast([128, 8, 128]), ALU.add, r=[N, ctx.C], w=[P_], eng="pool")
                for lvl in range(5):
                    last = lvl == 4
                    N2 = None if last else Ns.next()
                    NT2 = NTs.next()
                    if not last:
                        for (bk, v, b0, nb) in pairs_mm(NT, N):
                            evac(N2[:, b0:b0 + nb, :], v, [bk], [N2])
                    for (bk, v, b0, nb) in pairs_mm(N, NT):
                        evac(NT2[:, b0:b0 + nb, :], v, [bk], [NT2])
                    P2 = Ps.next()
                    for (bk, v, b0, nb) in pairs_mm(NT2, P_):
                        k.tt(P2[:, b0:b0 + nb, :], v, P_[:, b0:b0 + nb, :], ALU.add, r=[bk, P_], w=[P2])
                    N, NT, P_ = N2, NT2, P2
                X = Xs.next()
                for (bk, v, b0, nb) in pairs_mm(AT, Mst, lhs2=AK, rhs2=Vb):
                    evac(X[:, b0:b0 + nb, :], v, [bk], [X])
                U = Us.next()
                for (bk, v, b0, nb) in pairs_mm(P_, X):
                    evac(U[:, b0:b0 + nb, :], v, [bk], [U])
                yb = ctx.bank()
                for p in range(8):
                    o = yb[:, p * LCH:(p + 1) * LCH]
                    k.mm(o, Mst[:, p, :], RT[:, p, :], True, False, r=[Mst, RT], w=[yb])
                    k.mm(o, U[:, p, :], RB[:, p, :], False, False, r=[U, RB], w=[yb])
                    k.mm(o, Vb[:, p, :], RK[:, p, :], False, True, r=[Vb, RK], w=[yb])
                yv = yb[:].rearrange("p (a b) -> p a b", b=LCH)
                if d == 1:
                    ys = ysb.next()
                    k.copy(ys[:], yv, r=[yb], w=[ys])
                    k.dma("sp", S_yb.rearrange("(p q) t -> q p t", q=128)[:, :, t0:t0 + LCH], ys[:], r=[ys], w=[tok_yb])
                if c != order[-1]:
                    Bt, Kt = Bts.next(), Kts.next()
                    for (src, dst) in ((BT, Bt), (KT, Kt)):
                        for b0 in (0, 4):
                            bk = ctx.bank()
                            for p in range(b0, b0 + 4):
                                k.tr(bk[:, (p - b0) * 128:(p - b0 + 1) * 128], src[:, p, :], ident, r=[src, ctx.C], w=[bk])
                            evac(dst[:, b0:b0 + 4, :], bk[:].rearrange("p (a b) -> p a b", b=128), [bk], [dst])
                    Mn = Ms.next()
                    tm = tmpM.next()
                    for (bk, v, b0, nb) in pairs_mm(Bt, U, lhs2=Kt, rhs2=Vb):
                        k.tt(tm[:, b0:b0 + nb, :], v, Mst[:, b0:b0 + nb, :], ALU.add, r=[bk, Mst], w=[tm])
                        k.tt(Mn[:, b0:b0 + nb, :], tm[:, b0:b0 + nb, :], wt[:, b0:b0 + nb].unsqueeze(2).to_broadcast([128, nb, 128]), ALU.mult,
                             r=[tm, wt], w=[Mn], eng="pool")
                    Mst = Mn
                if d == 0:
                    ybt = ybl.next()
                    k.dma("sp", ybt[:], S_yb.rearrange("(p q) t -> q p t", q=128)[:, :, t0:t0 + LCH], r=[tok_yb], w=[ybt])
                    bon = o512[0].next()
                    gt = o512[1].next()
                    k.dma("sp", bon[:], S_bonus.rearrange("(p q) t -> q p t", q=128)[:, :, t0:t0 + LCH], r=[tok_s1], w=[bon])
                    k.dma("sp", gt[:], S_g.rearrange("(p q) t -> q p t", q=128)[:, :, t0:t0 + LCH], r=[tok_s1], w=[gt])
                    ysum, ysq, t3 = o512[2].next(), o512[3].next(), o512[4].next()
                    k.tt(ysum[:], yv, ybt[:], ALU.add, r=[yb, ybt], w=[ysum])
                    k.act(ysq[:], ysum[:], AF.Square, r=[ysum], w=[ysq])
                    f2 = lambda t: t[:].rearrange("p a b -> p (a b)")
                    mb, qb = ctx.bank(), ctx.bank()
                    k.mm(mb[:], ctx.cst("BLKM"), f2(ysum), r=[ctx.C, ysum], w=[mb])
                    k.mm(qb[:], ctx.cst("BLKM"), f2(ysq), r=[ctx.C, ysq], w=[qb])
                    k.act(f2(ysq), mb[:], AF.Square, r=[mb], w=[ysq])
                    k.tt(f2(ysq), qb[:], f2(ysq), ALU.subtract, r=[qb, ysq], w=[ysq])
                    k.ts(f2(ysq), f2(ysq), 64e-5, None, ALU.add, r=[ysq], w=[ysq], eng="pool")
                    k.act(f2(ysq), f2(ysq), AF.Ln, r=[ysq], w=[ysq])
                    k.act(f2(ysq), f2(ysq), AF.Exp, r=[ysq], w=[ysq], scale=-0.5)
                    k.tt(f2(t3), f2(ysum), mb[:], ALU.subtract, r=[ysum, mb], w=[t3])
                    k.tt(t3[:], t3[:], ysq[:], ALU.mult, r=[t3, ysq], w=[t3])
                    k.tt(t3[:], t3[:], lnw[:].unsqueeze(2).to_broadcast([128, 8, LCH]), ALU.mult, r=[t3, lnw], w=[t3], eng="pool")
                    k.tt(t3[:], t3[:], lnb[:].unsqueeze(2).to_broadcast([128, 8, LCH]), ALU.add, r=[t3, lnb], w=[t3], eng="pool")
                    k.tt(t3[:], t3[:], bon[:], ALU.add, r=[t3, bon], w=[t3])
                    zT = zTs.next()
                    k.tt(zT[:], t3[:], gt[:], ALU.mult, r=[t3, gt], w=[zT])
                    xr = xrs.next()
                    k.dma("sp", xr[:], x_in[t0:t0 + LCH, :], r=[tok_in], w=[xr])
                    xo = xos.next()
                    for half in range(2):
                        po = ctx.bank()
                        for cc in range(8):
                            k.mm(po[0:LCH, :], zT[:, cc, :], Wo[:, cc, half * 512:(half + 1) * 512], cc == 0, cc == 7, r=[zT, Wo], w=[po])
                        k.tt(xo[:, half * 512:(half + 1) * 512], po[0:LCH, :], xr[:, half * 512:(half + 1) * 512], ALU.add, r=[po, xr], w=[xo])
                    outs.append(k.dma("sp", x_out[t0:t0 + LCH, :], xo[:], r=[xo], w=[tok_out]))
    ctx.P.release(mark)
    return outs


PARAM_SHAPES = None


def build_program(T, nseq, shapes):
    import os
    nc = bass.Bass("TRN2", target_bir_lowering=False)
    n = T * nseq
    carr, _ = build_consts()

    def din(name, shape):
        return nc.dram_tensor(name, list(shape), F32, kind="ExternalInput").ap()

    def dint(name, shape, dt=F32):
        return nc.dram_tensor(name, list(shape), dt, kind=os.environ.get("KSCR", "Internal")).ap()

    class _Sl:
        def __init__(self, lst):
            self.lst = lst

        def __getitem__(self, i):
            return self.lst[i]
    A = {}
    for name, shp in shapes.items():
        if name in ("x", "mem", "norm_final"):
            A[name] = din(name, shp)
        else:
            A[name] = _Sl([din(f"{name}_{i}", shp[1:]) for i in range(shp[0])])
    cst = din("consts", carr.shape)
    out = nc.dram_tensor("out", [n, D], F32, kind="ExternalOutput").ap()
    xa, xb = dint("xa", (n, D)), dint("xb", (n, D))
    S_fm, S_tok = dint("S_fm", (FM_ROWS, n)), dint("S_tok", (n, TOKW))
    nch = n // 128
    S_cb, S_sb = dint("S_cb", (nch, 128, 516)), dint("S_sb", (nch, 128, 256))
    S1, S_wt = dint("S1", (8, D, n), BF16), dint("S_wt", (2, D, n // LCH))
    S_vtok, S_bonus, S_g, S_yb = (dint("S_vtok", (n, D), BF16), dint("S_bonus", (n // 256, 128, 8, 256), BF16),
                                    dint("S_g", (n // 256, 128, 8, 256), BF16), dint("S_yb", (2, n // 256, 128, 8, 256), BF16))
    P = Prog(nc)
    ctx = Ctx(nc, P, T, nseq, cst)
    tx = Buf("x")
    ta, tb_ = Buf("xa"), Buf("xb")
    import os
    nph = int(os.environ.get("KPH", "99"))

    def finish(outs_):
        P.emit(final_waits=outs_)
        P.close()
        return nc, carr
    tfm, ttok = Buf("fm"), Buf("tok")
    phase_a0(ctx, A["x"], A["ev_w_in"][0], A["norm_mix"][0], S_fm, S_tok, tx, tfm, ttok)
    if nph <= 0:
        return finish(list(P.dma_last.values()))
    prm0 = dict(conv=A["ev_conv_qk"][0], igb=A["ev_m_ig_bias"][0], fgb=A["ev_m_fg_bias"][0], mnorm=A["ev_m_norm"][0],
                w2=A["ev_g_decay_w2"][0], db=A["ev_g_decay_b"][0], gnorm=A["ev_g_norm"][0], wout=A["ev_w_out"][0])
    o_ = phase_b0(ctx, A["x"], out if nph <= 1 else xa, S_fm, S_tok, S_cb, S_sb, prm0, tx, tfm, ttok, ta)
    if nph <= 1:
        return finish(o_)
    mem2 = A["mem"]
    o_ = phase_xattn(ctx, xa, out if nph <= 2 else xb, mem2, A["xa_wq"][0], A["xa_wkv"][0], A["xa_wo"][0], A["norm_xattn"][0], A["norm_mem"][0], ta, tb_)
    if nph <= 2:
        return finish(o_)
    o_ = phase_ffn(ctx, xb, out if nph <= 3 else xa, A["ffn_w_gate"][0], A["ffn_w_up"][0], A["ffn_w_down"][0], A["norm_ffn"][0], tb_, ta)
    if nph <= 3:
        return finish(o_)
    prm1 = dict(gain=A["norm_mix"][1], w_rkv=A["od_w_rkv"][0], w0=A["od_w0"][0], w1=A["od_w1"][0], w2=A["od_w2"][0], a0=A["od_a0"][0],
                a1=A["od_a1"][0], a2=A["od_a2"][0], g1=A["od_g1"][0], g2=A["od_g2"][0], k_k=A["od_k_k"][0], k_a=A["od_k_a"][0],
                r_k=A["od_r_k"][0], mu=A["od_mu"][0], ln_w=A["od_ln_w"][0], ln_b=A["od_ln_b"][0], w_o=A["od_w_o"][0])
    ts1 = Buf("s1")
    phase_a1(ctx, xa, prm1, S1, S_wt, S_vtok, S_bonus, S_g, ta, ts1)
    ty = Buf("y")
    phase_b1v2(ctx, prm1, S1, S_wt, S_vtok, S_yb, ts1, ty)
    o_ = phase_c1(ctx, xa, out if nph <= 4 else xb, prm1, S_yb, S_bonus, S_g, ta, ts1, ty, tb_)
    if nph <= 4:
        return finish(o_)
    phase_xattn(ctx, xb, xa, mem2, A["xa_wq"][1], A["xa_wkv"][1], A["xa_wo"][1], A["norm_xattn"][1], A["norm_mem"][1], tb_, ta)
    tout = Buf("out")
    outs = phase_ffn(ctx, xa, out, A["ffn_w_gate"][1], A["ffn_w_up"][1], A["ffn_w_down"][1], A["norm_ffn"][1], ta, tout,
                     final_gain_ap=A["norm_final"])
    P.emit(final_waits=outs)
    P.close()
    return nc, carr


_CACHE = {}


def kernel(**inputs):
    import os
    ncores = int(os.environ.get("KNC", "8"))
    x = np.asarray(inputs["x"], np.float32)
    B, T, _ = x.shape
    nseq = B // ncores
    shapes = {}
    per_core = []
    for name, v in inputs.items():
        v = np.ascontiguousarray(np.asarray(v, np.float32))
        if name == "x":
            shapes[name] = (nseq * T, D)
        elif name == "mem":
            shapes[name] = (nseq * NMEM, D)
        else:
            shapes[name] = v.shape
    key = (T, nseq)
    if key not in _CACHE:
        _CACHE[key] = build_program(T, nseq, shapes)
    nc, carr = _CACHE[key]
    in_maps = []
    for c in range(ncores):
        m = {"consts": carr}
        for name, v in inputs.items():
            v = np.ascontiguousarray(np.asarray(v, np.float32))
            if name == "x":
                m[name] = np.ascontiguousarray(v[c * nseq:(c + 1) * nseq].reshape(nseq * T, D))
            elif name == "mem":
                m[name] = np.ascontiguousarray(v[c * nseq:(c + 1) * nseq].reshape(nseq * NMEM, D))
            elif name == "norm_final":
                m[name] = v
            else:
                for i in range(v.shape[0]):
                    m[f"{name}_{i}"] = np.ascontiguousarray(v[i])
        in_maps.append(m)
    res = run_bass_kernel_spmd(nc, in_maps, core_ids=list(range(ncores)))
    outs = [np.asarray(r["out"], np.float32).reshape(nseq, T, D) for r in res.results]
    return np.concatenate(outs, axis=0)


def phase_b1v2(ctx, prm, S1, S_wt, S_vtok, S_y, tok_s1, tok_y, nchains=2):
    k = ctx.k
    n, T = ctx.n, ctx.T
    NCH = T // LCH
    mark = ctx.P.mark()
    ident = ctx.cst("ident")
    flip = [0]

    def evac(dst_ap, src_ap, r, w):
        flip[0] = (flip[0] + 1) % 4
        k.copy(dst_ap, src_ap, r=r, w=w, eng="dve" if flip[0] == 0 else "act")

    def pairs_mm(lhs_tile, rhs_tile, width=128, lhs2=None, rhs2=None):
        per_bank = 512 // width
        res = []
        for b0 in range(0, 8, per_bank):
            bk = ctx.bank()
            for p in range(b0, b0 + per_bank):
                o = bk[:, (p - b0) * width:(p - b0 + 1) * width]
                k.mm(o, lhs_tile[:, p, :], rhs_tile[:, p, :], True, lhs2 is None, r=[lhs_tile, rhs_tile], w=[bk])
                if lhs2 is not None:
                    k.mm(o, lhs2[:, p, :], rhs2[:, p, :], False, True, r=[lhs2, rhs2], w=[bk])
            res.append((bk, bk[:].rearrange("p (a b) -> p a b", b=width), b0, per_bank))
        return res

    def bd_src(ap2d, t0):
        v = ap2d.rearrange("(p e k) t -> e k p t", e=2, k=64)
        return [v[e][:, :, t0:t0 + LCH] for e in range(2)]

    class Work:
        def __init__(self):
            big = lambda: k.tile([128, 8, 128], BF16)
            big32 = lambda: k.tile([128, 8, 128], F32)
            self.AT, self.BT, self.KT, self.Vb = big(), big(), big(), big()
            for t in (self.AT, self.BT, self.KT, self.Vb):
                k.memset(t[:], 0.0, w=[t])
            self.RT = k.tile([128, 8, LCH], BF16)
            self.wt = k.tile([128, 8, NCH], F32)
            self.St = [k.tile([128, 32, 256], BF16), k.tile([128, 32, 256], BF16)]
            self.Vt = k.tile([128, D], BF16)
            self.N = [big(), big()]
            self.NT = [big(), big()]
            self.P = [big(), big()]
            self.AK, self.X, self.U, self.Bt, self.Kt = big(), big(), big(), big(), big()
            self.M = [big32(), big32()]
            self.Mb = [big(), big()]
            self.RB = k.tile([128, 8, LCH], BF16)
            self.RK = k.tile([128, 8, LCH], BF16)
            self.ys = k.tile([128, 8, LCH], BF16)

    works = [Work() for _ in range(nchains)]

    def chain(W, s, d):
        base = d * 4
        mi = 0
        Mst = W.M[mi]
        Mb = W.Mb[mi]
        k.memset(Mst[:], 0.0, w=[Mst])
        k.memset(Mb[:], 0.0, w=[Mb])
        strict = ctx.cst("SF" if d == 0 else "SB")
        strictT = ctx.cst("SB" if d == 0 else "SF")
        incl = ctx.cst("IF" if d == 0 else "IB")[:, 0:LCH]
        order = list(range(NCH)) if d == 0 else list(range(NCH - 1, -1, -1))
        S1flat = S1[base:base + 4].rearrange("a r t -> (a r) t")
        cg0 = (s * T) // LCH
        k.dma("sp", W.wt[:], S_wt[d].rearrange("(p q) c -> q p c", q=128)[:, :, cg0:cg0 + NCH], r=[tok_s1], w=[W.wt],
              allow_slow_non_contiguous=True)
        groups = []
        for c in order:
            if not groups or groups[-1] != c // 4:
                groups.append(c // 4)

        def load_group(gi):
            g = groups[gi]
            St = W.St[gi % 2]
            tg = s * T + g * 256
            k.dma("sp", St[:], S1flat[:, tg:tg + 256].rearrange("(ap q) t -> q ap t", q=128), r=[tok_s1], w=[St])
        load_group(0)
        for c in order:
            t0 = s * T + c * LCH
            gi = groups.index(c // 4)
            if c // 4 != (order[order.index(c) - 1] // 4 if order.index(c) > 0 else -1) and gi + 1 < len(groups):
                load_group(gi + 1)
            St = W.St[gi % 2]
            off = (c % 4) * LCH
            AT, BT, KT, Vb, RT = W.AT, W.BT, W.KT, W.Vb, W.RT
            wt = W.wt
            for e in range(2):
                rows = slice(e * 64, (e + 1) * 64)
                k.dma("sp", W.Vt[rows, :], S_vtok[t0:t0 + LCH, :], r=[tok_s1], w=[W.Vt])
            for ai, dstt in ((0, AT), (1, BT), (2, KT)):
                for e in range(2):
                    rows = slice(e * 64, (e + 1) * 64)
                    k.copy(dstt[rows, :, rows], St[rows, ai * 8:(ai + 1) * 8, off:off + LCH], r=[St], w=[dstt], eng="pool" if e == 0 else "act")
            k.copy(RT[:], St[:, 24:32, off:off + LCH], r=[St], w=[RT], eng="pool")
            for e in range(2):
                rows = slice(e * 64, (e + 1) * 64)
                k.copy(Vb[rows, :, rows], W.Vt[rows, :].rearrange("t (p e v) -> t p e v", e=2, v=64)[:, :, e, :], r=[W.Vt], w=[Vb],
                       eng="pool" if e == 0 else "act")
            yield
            ni = 0
            N, NT, P_ = W.N[0], W.NT[0], W.P[0]
            AK, RB, RK = W.AK, W.RB, W.RK
            for (bk, v, b0, nb) in pairs_mm(BT, AT):
                k.tt(N[:, b0:b0 + nb, :], v, strict.unsqueeze(1).to_broadcast([128, nb, 128]), ALU.mult, r=[bk, ctx.C], w=[N])
            for (bk, v, b0, nb) in pairs_mm(AT, BT):
                k.tt(NT[:, b0:b0 + nb, :], v, strictT.unsqueeze(1).to_broadcast([128, nb, 128]), ALU.mult, r=[bk, ctx.C], w=[NT])
            k.tt(P_[:], N[:], ident.unsqueeze(1).to_broadcast([128, 8, 128]), ALU.add, r=[N, ctx.C], w=[P_], eng="pool")
            yield
            for (bk, v, b0, nb) in pairs_mm(KT, AT):
                k.tt(AK[:, b0:b0 + nb, :], v, strict.unsqueeze(1).to_broadcast([128, nb, 128]), ALU.mult, r=[bk, ctx.C], w=[AK])
            for (bk, v, b0, nb) in pairs_mm(BT, RT, width=LCH):
                k.tt(RB[:, b0:b0 + nb, :], v, incl.unsqueeze(1).to_broadcast([128, nb, LCH]), ALU.mult, r=[bk, ctx.C], w=[RB])
            for (bk, v, b0, nb) in pairs_mm(KT, RT, width=LCH):
                k.tt(RK[:, b0:b0 + nb, :], v, incl.unsqueeze(1).to_broadcast([128, nb, LCH]), ALU.mult, r=[bk, ctx.C], w=[RK])
            yield
            last_c = c == order[-1]
            if not last_c:
                for (src, dst) in ((BT, W.Bt), (KT, W.Kt)):
                    bk = ctx.bank()
                    pb = bk.t[:].bitcast(BF16)
                    for p in range(8):
                        k.tr(pb[:, p * 128:(p + 1) * 128], src[:, p, :], ctx.identb[:], r=[src, ctx.identb], w=[bk])
                    evac(dst[:], pb.rearrange("p (a b) -> p a b", b=128), [bk], [dst])
                yield
            for lvl in range(5):
                last = lvl == 4
                N2 = None if last else W.N[1 - ni]
                NT2 = W.NT[1 - ni]
                if not last:
                    for (bk, v, b0, nb) in pairs_mm(NT, N):
                        evac(N2[:, b0:b0 + nb, :], v, [bk], [N2])
                for (bk, v, b0, nb) in pairs_mm(N, NT):
                    evac(NT2[:, b0:b0 + nb, :], v, [bk], [NT2])
                yield
                P2 = W.P[1 - ni]
                for (bk, v, b0, nb) in pairs_mm(NT2, P_):
                    k.tt(P2[:, b0:b0 + nb, :], v, P_[:, b0:b0 + nb, :], ALU.add, r=[bk, P_], w=[P2])
                N, NT, P_ = N2, NT2, P2
                ni = 1 - ni
                yield
            X, U = W.X, W.U
            for (bk, v, b0, nb) in pairs_mm(AT, Mb, lhs2=AK, rhs2=Vb):
                evac(X[:, b0:b0 + nb, :], v, [bk], [X])
            yield
            for (bk, v, b0, nb) in pairs_mm(P_, X):
                evac(U[:, b0:b0 + nb, :], v, [bk], [U])
            yield
            yb = ctx.bank()
            for p in range(8):
                o = yb[:, p * LCH:(p + 1) * LCH]
                k.mm(o, Mb[:, p, :], RT[:, p, :], True, False, r=[Mb, RT], w=[yb])
                k.mm(o, U[:, p, :], RB[:, p, :], False, False, r=[U, RB], w=[yb])
                k.mm(o, Vb[:, p, :], RK[:, p, :], False, True, r=[Vb, RK], w=[yb])
            evac(W.ys[:], yb[:].rearrange("p (a b) -> p a b", b=LCH), [yb], [W.ys])
            k.dma("sp", S_y[d, t0 // 256, :, :, t0 % 256:t0 % 256 + LCH], W.ys[:], r=[W.ys], w=[tok_y])
            if not last_c:
                Mn = W.M[1 - mi]
                for (bk, v, b0, nb) in pairs_mm(W.Bt, U, lhs2=W.Kt, rhs2=Vb):
                    k.tt(Mn[:, b0:b0 + nb, :], v, Mst[:, b0:b0 + nb, :], ALU.add, r=[bk, Mst], w=[Mn])
                    k.tt(Mn[:, b0:b0 + nb, :], Mn[:, b0:b0 + nb, :], wt[:, b0:b0 + nb, c:c + 1].to_broadcast([128, nb, 128]), ALU.mult,
                         r=[Mn, wt], w=[Mn], eng="pool")
                Mb = W.Mb[1 - mi]
                k.copy(Mb[:], Mn[:], r=[Mn], w=[Mb], eng="act")
                Mst = Mn
                mi = 1 - mi
            yield

    jobs = [(s, d) for s in range(ctx.nseq) for d in (0, 1)]
    for g0 in range(0, len(jobs), nchains):
        gens = [chain(works[i], *jobs[g0 + i]) for i in range(min(nchains, len(jobs) - g0))]
        alive = list(gens)
        while alive:
            nxt = []
            for g in alive:
                try:
                    next(g)
                    nxt.append(g)
                except StopIteration:
                    pass
            alive = nxt
    ctx.P.release(mark)


def phase_c1(ctx, x_in, x_out, prm, S_y, S_bonus, S_g, tok_in, tok_s1, tok_y, tok_out):
    k = ctx.k
    n = ctx.n
    TT = 256
    mark = ctx.P.mark()
    Wo = load_w(ctx, prm["w_o"])
    lnw = k.tile([128, 8], F32)
    lnb = k.tile([128, 8], F32)
    k.dma("sp", lnw[:], prm["ln_w"].rearrange("(c p) -> p c", p=128), w=[lnw], allow_slow_non_contiguous=True)
    k.dma("sp", lnb[:], prm["ln_b"].rearrange("(c p) -> p c", p=128), w=[lnb], allow_slow_non_contiguous=True)
    rings = [k.ring(2, [128, 8, TT], BF16) for _ in range(4)] + [k.ring(2, [128, 8, TT], F32) for _ in range(3)]
    zTs = k.ring(2, [128, 8, TT], BF16)
    xrs = k.ring(2, [128, D], F32)
    xos = k.ring(2, [128, D], F32)
    outs = []
    fm = lambda ap2d, t0: ap2d.rearrange("(p q) t -> q p t", q=128)[:, :, t0:t0 + TT]
    f2 = lambda t: t[:].rearrange("p a b -> p (a b)")
    def step_gen(st):
        t0 = st * TT
        yf, yb, bon, gt, ysum, ysq, t3 = [r.next() for r in rings]
        k.dma("sp", yf[:], S_y[0, st], r=[tok_y], w=[yf])
        k.dma("sp", yb[:], S_y[1, st], r=[tok_y], w=[yb])
        k.dma("sp", bon[:], S_bonus[st], r=[tok_s1], w=[bon])
        k.dma("sp", gt[:], S_g[st], r=[tok_s1], w=[gt])
        yield
        k.tt(ysum[:], yf[:], yb[:], ALU.add, r=[yf, yb], w=[ysum])
        yield
        k.act(ysq[:], ysum[:], AF.Square, r=[ysum], w=[ysq])
        yield
        NH = (8 * TT) // 512
        for hf in range(NH):
            cs = slice(hf * 512, (hf + 1) * 512)
            mbk, qbk = ctx.bank(), ctx.bank()
            k.mm(mbk[:], ctx.cst("BLKM"), f2(ysum)[:, cs], r=[ctx.C, ysum], w=[mbk])
            k.mm(qbk[:], ctx.cst("BLKM"), f2(ysq)[:, cs], r=[ctx.C, ysq], w=[qbk])
            k.act(f2(ysq)[:, cs], mbk[:], AF.Square, r=[mbk], w=[ysq])
            k.tt(f2(ysq)[:, cs], qbk[:], f2(ysq)[:, cs], ALU.subtract, r=[qbk, ysq], w=[ysq])
            k.tt(f2(t3)[:, cs], f2(ysum)[:, cs], mbk[:], ALU.subtract, r=[ysum, mbk], w=[t3])
            yield
        k.ts(f2(ysq), f2(ysq), 64e-5, None, ALU.add, r=[ysq], w=[ysq], eng="pool")
        yield
        k.act(f2(ysq), f2(ysq), AF.Ln, r=[ysq], w=[ysq])
        yield
        k.act(f2(ysq), f2(ysq), AF.Exp, r=[ysq], w=[ysq], scale=-0.5)
        yield
        k.tt(t3[:], t3[:], ysq[:], ALU.mult, r=[t3, ysq], w=[t3])
        yield
        k.tt(t3[:], t3[:], lnw[:].unsqueeze(2).to_broadcast([128, 8, TT]), ALU.mult, r=[t3, lnw], w=[t3], eng="pool")
        yield
        k.tt(t3[:], t3[:], lnb[:].unsqueeze(2).to_broadcast([128, 8, TT]), ALU.add, r=[t3, lnb], w=[t3])
        yield
        k.tt(t3[:], t3[:], bon[:], ALU.add, r=[t3, bon], w=[t3])
        yield
        zT = zTs.next()
        k.tt(zT[:], t3[:], gt[:], ALU.mult, r=[t3, gt], w=[zT])
        yield
        for sub in range(TT // 128):
            ts0 = t0 + sub * 128
            xr = xrs.next()
            k.dma("sp", xr[:], x_in[ts0:ts0 + 128, :], r=[tok_in], w=[xr])
            xo = xos.next()
            for half in range(2):
                po = ctx.bank()
                for cc in range(8):
                    k.mm(po[:], zT[:, cc, sub * 128:(sub + 1) * 128], Wo[:, cc, half * 512:(half + 1) * 512], cc == 0, cc == 7, r=[zT, Wo], w=[po])
                k.tt(xo[:, half * 512:(half + 1) * 512], po[:], xr[:, half * 512:(half + 1) * 512], ALU.add, r=[po, xr], w=[xo])
            outs.append(k.dma("sp", x_out[ts0:ts0 + 128, :], xo[:], r=[xo], w=[tok_out]))
            yield

    nsteps = n // TT
    for s0 in range(0, nsteps, 2):
        alive = [step_gen(s0 + i) for i in range(min(2, nsteps - s0))]
        while alive:
            nxt = []
            for g_ in alive:
                try:
                    next(g_)
                    nxt.append(g_)
                except StopIteration:
                    pass
            alive = nxt
    ctx.P.release(mark)
    return outs
```

```python
import numpy as np
import concourse.bass as bass
import concourse.mybir as mybir
from concourse.bass_utils import run_bass_kernel_spmd

F32 = mybir.dt.float32
BF16 = mybir.dt.bfloat16
AF = mybir.ActivationFunctionType
ALU = mybir.AluOpType
AX = mybir.AxisListType

ENGS = ("pe", "act", "dve", "pool", "sp")
SEM_WRAP = 30000


class Buf:
    __slots__ = ("name", "ap", "w", "r")

    def __init__(self, name, ap=None):
        self.name = name
        self.ap = ap
        self.w = None
        self.r = {}


class Ins:
    __slots__ = ("eng", "fn", "deps", "sig", "sigval", "dma", "dsem", "dval", "prev_dma", "idx")

    def __init__(self, eng, fn, deps, dma=False):
        self.eng = eng
        self.fn = fn
        self.deps = deps
        self.sig = False
        self.sigval = None
        self.dma = dma
        self.dsem = None
        self.dval = None
        self.prev_dma = None


class Ring:
    def __init__(self, P, n, shape, dt, psum=False):
        self.slots = []
        for _ in range(n):
            t = P.ps(shape, dt) if psum else P.sb(shape, dt)
            self.slots.append((t, Buf("ring")))
        self.i = 0

    def next(self):
        s = self.slots[self.i % len(self.slots)]
        self.i += 1
        return s


class Prog:
    def __init__(self, nc, n_dma_sems=32, same_engine_sync=True):
        self.nc = nc
        self.streams = {e: [] for e in ENGS}
        self.n_dma_sems = n_dma_sems
        self.same_engine_sync = same_engine_sync
        self.dma_rr = {e: 0 for e in ENGS}
        self.dma_last = {}
        self.stack = []
        self.nbuf = 0
        self.extra = {e: [] for e in ENGS}

    def mark(self):
        return len(self.stack)

    def release(self, mark):
        deps = []
        for e in ENGS:
            for ins in reversed(self.streams[e]):
                if not ins.dma:
                    deps.append(ins)
                    break
        deps.extend(self.dma_last.values())
        for e in ENGS:
            self.extra[e] = list(deps)
        while len(self.stack) > mark:
            self.stack.pop().__exit__(None, None, None)

    def sb(self, shape, dt, name=None):
        self.nbuf += 1
        g = self.nc.sbuf_tensor(name or f"sb{self.nbuf}", list(shape), dt)
        t = g.__enter__()
        self.stack.append(g)
        return t

    def ps(self, shape, dt=F32, name=None):
        self.nbuf += 1
        g = self.nc.psum_tensor(name or f"ps{self.nbuf}", list(shape), dt)
        t = g.__enter__()
        self.stack.append(g)
        return t

    def buf(self, name="b"):
        return Buf(name)

    def ring(self, n, shape, dt, psum=False):
        return Ring(self, n, shape, dt, psum)

    def _deps(self, eng, reads, writes):
        deps = []
        for b in reads:
            if b.w is not None:
                deps.append(b.w)
        for b in writes:
            if b.w is not None:
                deps.append(b.w)
            deps.extend(b.r.values())
        if self.extra[eng]:
            deps.extend(self.extra[eng])
            self.extra[eng] = []
        return deps

    def op(self, eng, fn, reads=(), writes=()):
        deps = self._deps(eng, reads, writes)
        ins = Ins(eng, fn, deps)
        for b in writes:
            b.w = ins
            b.r = {}
        for b in reads:
            b.r[eng] = ins
        self.streams[eng].append(ins)
        return ins

    def dma(self, eng, out, in_, reads=(), writes=(), **kw):
        deps = self._deps(eng, reads, writes)

        def fn(e, out=out, in_=in_, kw=kw):
            return e.dma_start(out=out, in_=in_, **kw)

        ins = Ins(eng, fn, deps, dma=True)
        slot = (eng, self.dma_rr[eng] % self.n_dma_sems)
        self.dma_rr[eng] += 1
        ins.dsem = slot
        prev = self.dma_last.get(slot)
        ins.prev_dma = prev
        ins.dval = (prev.dval if prev is not None else 0) + 16
        self.dma_last[slot] = ins
        for b in writes:
            b.w = ins
            b.r = {}
        for b in reads:
            b.r[("dma", id(ins))] = ins
        self.streams[eng].append(ins)
        return ins

    def emit(self, final_waits=()):
        nc = self.nc
        for e in ENGS:
            for ins in self.streams[e]:
                for d in ins.deps:
                    if d.dma:
                        continue
                    if d.eng == "pe" and ins.eng == "pe":
                        continue
                    if d.eng == ins.eng and not self.same_engine_sync:
                        continue
                    d.sig = True
        for d in final_waits:
            if not d.dma:
                d.sig = True
        nsig = {}
        for e in ENGS:
            n = 0
            for ins in self.streams[e]:
                if ins.sig:
                    ins.sigval = (n // SEM_WRAP, n % SEM_WRAP + 1)
                    n += 1
            nsig[e] = n
        sems = {}
        guards = []

        def getsem(key):
            if key not in sems:
                g = nc.semaphore("s_" + "_".join(str(k) for k in key))
                sems[key] = g.__enter__()
                guards.append(g)
            return sems[key]

        for e in ENGS:
            for k in range((nsig[e] + SEM_WRAP - 1) // SEM_WRAP):
                getsem(("c", e, k))
        for slot in self.dma_last:
            getsem(("d",) + slot)

        engobj = {"pe": "tensor", "act": "scalar", "dve": "vector", "pool": "gpsimd", "sp": "sync"}
        streams = self.streams

        def run(e, eng):
            waited = {}
            for ins in streams[e]:
                need = {}
                for d in ins.deps:
                    if d.dma:
                        key = ("d",) + d.dsem
                        val = d.dval
                    else:
                        if d.eng == "pe" and e == "pe":
                            continue
                        if d.eng == e and not self.same_engine_sync:
                            continue
                        key = ("c", d.eng, d.sigval[0])
                        val = d.sigval[1]
                    if need.get(key, 0) < val:
                        need[key] = val
                if ins.dma and ins.prev_dma is not None:
                    key = ("d",) + ins.dsem
                    if need.get(key, 0) < ins.prev_dma.dval:
                        need[key] = ins.prev_dma.dval
                for key, val in need.items():
                    if waited.get(key, 0) < val:
                        eng.wait_ge(sems[key], val)
                        waited[key] = val
                bi = ins.fn(eng)
                if ins.dma:
                    bi.then_inc(sems[("d",) + ins.dsem], 16)
                elif ins.sig:
                    bi.then_inc(sems[("c", e, ins.sigval[0])], 1)
            if e == "sp":
                for d in final_waits:
                    if d.dma:
                        eng.wait_ge(sems[("d",) + d.dsem], d.dval)
                    else:
                        eng.wait_ge(sems[("c", d.eng, d.sigval[0])], d.sigval[1])

        with nc.Block() as block:
            @block.tensor
            def _(eng):
                run("pe", eng)

            @block.scalar
            def _(eng):
                run("act", eng)

            @block.vector
            def _(eng):
                run("dve", eng)

            @block.gpsimd
            def _(eng):
                run("pool", eng)

            @block.sync
            def _(eng):
                run("sp", eng)
        for g in reversed(guards):
            g.__exit__(None, None, None)

    def close(self):
        for g in reversed(self.stack):
            g.__exit__(None, None, None)
        self.stack = []


class Tile:
    __slots__ = ("t", "b")

    def __init__(self, t):
        self.t = t
        self.b = Buf("t")

    def __getitem__(self, k):
        return self.t[k]


def _tok(xs):
    out = []
    for x in xs:
        if x is None:
            continue
        out.append(x.b if isinstance(x, Tile) else x)
    return out


class K:
    def __init__(self, P):
        self.P = P

    def tile(self, shape, dt, psum=False):
        return Tile(self.P.ps(shape, dt) if psum else self.P.sb(shape, dt))

    def ring(self, n, shape, dt, psum=False):
        return TRing([self.tile(shape, dt, psum) for _ in range(n)])

    def dma(self, q, out, in_, r=(), w=(), **kw):
        return self.P.dma(q, out, in_, reads=_tok(r), writes=_tok(w), **kw)

    def act(self, out, in_, func, r=(), w=(), eng="act", **kw):
        return self.P.op(eng, lambda e: e.activation(out=out, in_=in_, func=func, **kw), _tok(r), _tok(w))

    def ts(self, out, in0, s1, s2, op0, op1=None, r=(), w=(), eng="dve", **kw):
        if op1 is None:
            return self.P.op(eng, lambda e: e.tensor_scalar(out=out, in0=in0, scalar1=s1, scalar2=None, op0=op0, **kw), _tok(r), _tok(w))
        return self.P.op(eng, lambda e: e.tensor_scalar(out=out, in0=in0, scalar1=s1, scalar2=s2, op0=op0, op1=op1, **kw), _tok(r), _tok(w))

    def tt(self, out, in0, in1, op, r=(), w=(), eng="dve"):
        return self.P.op(eng, lambda e: e.tensor_tensor(out=out, in0=in0, in1=in1, op=op), _tok(r), _tok(w))

    def stt(self, out, in0, scalar, in1, op0, op1, r=(), w=()):
        return self.P.op("dve", lambda e: e.scalar_tensor_tensor(out=out, in0=in0, scalar=scalar, in1=in1, op0=op0, op1=op1), _tok(r), _tok(w))

    def copy(self, out, in_, r=(), w=(), eng="dve"):
        if eng == "act":
            return self.P.op("act", lambda e: e.copy(out=out, in_=in_), _tok(r), _tok(w))
        return self.P.op(eng, lambda e: e.tensor_copy(out=out, in_=in_), _tok(r), _tok(w))

    def recip(self, out, in_, r=(), w=()):
        return self.P.op("dve", lambda e: e.reciprocal(out=out, in_=in_), _tok(r), _tok(w))

    def memset(self, out, val, w=(), eng="pool"):
        return self.P.op(eng, lambda e: e.memset(out, val), (), _tok(w))

    def reduce(self, out, in_, op, r=(), w=()):
        return self.P.op("dve", lambda e: e.tensor_reduce(out=out, in_=in_, axis=AX.X, op=op), _tok(r), _tok(w))

    def mm(self, out, lhsT, rhs, start=True, stop=True, r=(), w=()):
        return self.P.op("pe", lambda e: e.matmul(out, lhsT=lhsT, rhs=rhs, start=start, stop=stop), _tok(r), _tok(w))

    def tr(self, out, in_, ident, r=(), w=()):
        return self.P.op("pe", lambda e: e.transpose(out=out, in_=in_, identity=ident), _tok(r), _tok(w))


class TRing:
    def __init__(self, tiles):
        self.tiles = tiles
        self.i = 0

    def next(self):
        t = self.tiles[self.i % len(self.tiles)]
        self.i += 1
        return t


D = 1024
NMEM = 256
DFF = 2816
EPS = 1e-6


def build_consts():
    i = np.arange(128)
    s, t = i[:, None], i[None, :]
    c = {}
    f = lambda m: np.asarray(m, np.float32)
    c["ident"] = f(s == t)
    c["MU"] = f(s <= t)
    c["ML"] = f(s >= t)
    c["ONES"] = np.ones((128, 128), np.float32)
    c["NU"] = -f(s <= t)
    c["NL"] = -f(s >= t)
    c["NONES"] = -np.ones((128, 128), np.float32)
    c["BLK"] = f((s // 64) == (t // 64))
    c["BLKM"] = f((s // 64) == (t // 64)) / 64.0
    s6, t6 = s % 64, t % 64
    c["SF"] = f(s6 < t6)
    c["SB"] = f(s6 > t6)
    c["IF"] = f(s6 <= t6)
    c["IB"] = f(s6 >= t6)
    c["UN"] = -f(s <= t) / 16.0
    c["LN"] = -f(s >= t) / 16.0
    c["UC"] = -f(s > t) / 16.0
    c["LC"] = -f(s < t) / 16.0
    names = list(c)
    arr = np.concatenate([np.asarray(c[k], np.float32) for k in names], axis=1)
    offs = {k: j * 128 for j, k in enumerate(names)}
    return arr, offs


class Ctx:
    def __init__(self, nc, P, T, nseq, consts_ap):
        self.nc = nc
        self.P = P
        self.k = K(P)
        self.T = T
        self.nseq = nseq
        self.n = T * nseq
        k = self.k
        arr, offs = build_consts()
        self.coffs = offs
        self.C = k.tile([128, arr.shape[1]], F32)
        k.dma("sp", self.C[:], consts_ap, w=[self.C])
        self.identb = k.tile([128, 128], BF16)
        k.copy(self.identb[:], self.cst("ident"), r=[self.C], w=[self.identb])
        self.banks = k.ring(8, [128, 512], F32, psum=True)
        self.dram_tok = {}

    def cst(self, name):
        o = self.coffs[name]
        return self.C[:, o:o + 128]

    def bank(self):
        return self.banks.next()

    def dtok(self, key):
        if key not in self.dram_tok:
            self.dram_tok[key] = Buf(str(key))
        return self.dram_tok[key]


def load_gain_fm(ctx, g_ap, nchunk=8):
    k = ctx.k
    g = k.tile([128, nchunk], F32)
    k.dma("sp", g[:], g_ap.rearrange("(c p) -> p c", p=128), w=[g], allow_slow_non_contiguous=True)
    return g


def load_w(ctx, w_ap, gain_fm=None, dst=None, col0=0):
    k = ctx.k
    Kd, F = w_ap.shape
    nch = Kd // 128
    if dst is None:
        dst = k.tile([128, nch, F], BF16)
    for c in range(nch):
        k.dma("pool", dst[:, c, col0:col0 + F], w_ap[c * 128:(c + 1) * 128, :], w=[dst])
    if gain_fm is not None:
        for c in range(nch):
            k.ts(dst[:, c, col0:col0 + F], dst[:, c, col0:col0 + F], gain_fm[:, c:c + 1], None, ALU.mult,
                 r=[dst, gain_fm], w=[dst])
    return dst


class NormT:
    def __init__(self, ctx, with_xn=True, nxn=3, njunk=2):
        k = ctx.k
        self.ctx = ctx
        self.junk = k.ring(njunk, [128, D], BF16)
        self.ss = k.ring(4, [128, 1], F32)
        self.rs = k.ring(4, [128, 1], F32)
        if with_xn:
            self.xn = k.ring(nxn, [128, D], BF16)
        self.flip = 0

    def rstd(self, xt_ap, xt_tile, width=D, rows=128):
        k = self.ctx.k
        junk = self.junk.next()
        ss = self.ss.next()
        rs = self.rs.next()
        R = slice(0, rows)
        k.act(junk[R, 0:width], xt_ap, AF.Square, r=[xt_tile], w=[junk, ss], accum_out=ss[R, :])
        k.ts(rs[R, :], ss[R, :], 1.0 / width, EPS, ALU.mult, ALU.add, r=[ss], w=[rs])
        k.act(rs[R, :], rs[R, :], AF.Ln, r=[rs], w=[rs])
        k.act(rs[R, :], rs[R, :], AF.Exp, r=[rs], w=[rs], scale=-0.5)
        return rs

    def norm(self, xt):
        k = self.ctx.k
        rs = self.rstd(xt[:], xt)
        xn = self.xn.next()
        k.ts(xn[:], xt[:], rs[:], None, ALU.mult, r=[xt, rs], w=[xn])
        return xn

    def to_fm(self, xn, dst_ap, dst_tok, nchunk=8):
        ctx = self.ctx
        k = ctx.k
        bank = ctx.bank()
        pb = bank.t[:].bitcast(BF16)
        for c in range(nchunk):
            k.tr(pb[:, c * 128:(c + 1) * 128], xn[:, c * 128:(c + 1) * 128], ctx.identb[:], r=[xn, ctx.identb], w=[bank])
        src = pb[:, 0:nchunk * 128].rearrange("p (c t) -> p c t", c=nchunk)
        self.flip ^= 1
        k.copy(dst_ap, src, r=[bank], w=[dst_tok], eng="act" if self.flip else "dve")


def phase_ffn(ctx, x_in, x_out, wg, wu, wd, gain_ap, tok_in, tok_out, final_gain_ap=None):
    k = ctx.k
    n = ctx.n
    TB = 512
    NF = DFF // 128
    mark = ctx.P.mark()
    g_fm = load_gain_fm(ctx, gain_ap)
    Wgu = k.tile([128, 8, 2 * DFF], BF16)
    load_w(ctx, wg, None, dst=Wgu, col0=0)
    load_w(ctx, wu, None, dst=Wgu, col0=DFF)
    for c in range(8):
        k.ts(Wgu[:, c, :], Wgu[:, c, :], g_fm[:, c:c + 1], None, ALU.mult, r=[Wgu, g_fm], w=[Wgu])
    Wd = load_w(ctx, wd)
    nt = NormT(ctx, nxn=2, njunk=1)
    xts = k.ring(2, [128, D], F32)
    hTs = k.ring(1, [128, 8, TB], BF16)
    actT = k.tile([128, NF, TB], BF16)
    act_parts = [Buf("a") for _ in range(NF)]
    sgs = k.ring(2, [128, TB], F32)
    xos = k.ring(2, [128, D], F32)
    if final_gain_ap is not None:
        gbc = k.tile([128, D], F32)
        k.dma("sp", gbc[:], final_gain_ap.partition_broadcast(128), w=[gbc])
    outs = []
    for blk in range(n // TB):
        hT = hTs.next()
        for j in range(TB // 128):
            t0 = blk * TB + j * 128
            xt = xts.next()
            k.dma("sp", xt[:], x_in[t0:t0 + 128, :], r=[tok_in], w=[xt])
            xn = nt.norm(xt)
            nt.to_fm(xn, hT[:, :, j * 128:(j + 1) * 128], hT)
        for f in range(NF):
            pg = ctx.bank()
            pu = ctx.bank()
            for c in range(8):
                k.mm(pg[:, 0:TB], Wgu[:, c, f * 128:(f + 1) * 128], hT[:, c, :], c == 0, c == 7, r=[Wgu, hT], w=[pg])
            for c in range(8):
                k.mm(pu[:, 0:TB], Wgu[:, c, DFF + f * 128:DFF + (f + 1) * 128], hT[:, c, :], c == 0, c == 7, r=[Wgu, hT], w=[pu])
            sg = sgs.next()
            k.act(sg[:], pg[:, 0:TB], AF.Silu, r=[pg], w=[sg])
            k.tt(actT[:, f, :], sg[:], pu[:, 0:TB], ALU.mult, r=[sg, pu], w=[act_parts[f]])
        for j in range(TB // 128):
            t0 = blk * TB + j * 128
            xo = xos.next()
            k.dma("sp", xo[:], x_in[t0:t0 + 128, :], r=[tok_in], w=[xo])
            for half in range(2):
                po = ctx.bank()
                for f in range(NF):
                    k.mm(po[:], actT[:, f, j * 128:(j + 1) * 128], Wd[:, f, half * 512:(half + 1) * 512], f == 0, f == NF - 1,
                         r=[act_parts[f], Wd], w=[po])
                k.tt(xo[:, half * 512:(half + 1) * 512], po[:], xo[:, half * 512:(half + 1) * 512], ALU.add, r=[po, xo], w=[xo])
            if final_gain_ap is not None:
                rs = nt.rstd(xo[:], xo)
                k.stt(xo[:], xo[:], rs[:], gbc[:], ALU.mult, ALU.mult, r=[xo, rs, gbc], w=[xo])
            outs.append(k.dma("sp", x_out[t0:t0 + 128, :], xo[:], r=[xo], w=[tok_out]))
    ctx.P.release(mark)
    return outs


def phase_xattn(ctx, x_in, x_out, mem, wq, wkv, wo, g_x_ap, g_mem_ap, tok_in, tok_out):
    k = ctx.k
    n, T = ctx.n, ctx.T
    TB = 512
    mark = ctx.P.mark()
    gx = load_gain_fm(ctx, g_x_ap)
    gm = load_gain_fm(ctx, g_mem_ap)
    Wkv = load_w(ctx, wkv, gm)
    Wq = load_w(ctx, wq, gx)
    Wo = load_w(ctx, wo)
    nt = NormT(ctx)
    xts = k.ring(3, [128, D], F32)
    KT = k.tile([128, ctx.nseq, 8, NMEM], BF16)
    V = k.tile([128, ctx.nseq, 2, D], BF16)
    memT = k.tile([128, 8, NMEM], BF16)
    flip = 0
    for s in range(ctx.nseq):
        for j in range(2):
            xt = xts.next()
            k.dma("sp", xt[:], mem[s * NMEM + j * 128:s * NMEM + (j + 1) * 128, :], w=[xt])
            xn = nt.norm(xt)
            nt.to_fm(xn, memT[:, :, j * 128:(j + 1) * 128], memT)
        for f in range(8):
            b = ctx.bank()
            for c in range(8):
                k.mm(b[:, 0:NMEM], Wkv[:, c, f * 128:(f + 1) * 128], memT[:, c, :], c == 0, c == 7, r=[Wkv, memT], w=[b])
            flip ^= 1
            k.copy(KT[:, s, f, :], b[:, 0:NMEM], r=[b], w=[KT], eng="act" if flip else "dve")
        for j in range(2):
            for half in range(2):
                b = ctx.bank()
                for c in range(8):
                    k.mm(b[:], memT[:, c, j * 128:(j + 1) * 128], Wkv[:, c, D + half * 512:D + (half + 1) * 512], c == 0, c == 7,
                         r=[Wkv, memT], w=[b])
                flip ^= 1
                k.copy(V[:, s, j, half * 512:(half + 1) * 512], b[:], r=[b], w=[V], eng="act" if flip else "dve")
    hTs = k.ring(2, [128, 8, TB], BF16)
    qT = k.tile([128, 8, TB], BF16)
    qparts = [Buf("q") for _ in range(8)]
    pT = k.tile([128, 8, TB], BF16)
    pparts = [Buf("p") for _ in range(TB // 128)]
    oT = k.tile([128, 8, TB], BF16)
    oparts = [Buf("o") for _ in range(8)]
    mxs = k.ring(2, [128, 4], F32)
    nmxs = k.ring(2, [128, 4], F32)
    rsums = k.ring(2, [128, 4], F32)
    rinvs = k.ring(2, [128, 4], F32)
    ps_ = k.ring(2, [128, 4, NMEM], BF16)
    pns = k.ring(2, [128, 4, NMEM], BF16)
    xrs = k.ring(2, [128, D], F32)
    xos = k.ring(2, [128, D], F32)
    outs = []
    for blk in range(n // TB):
        s = (blk * TB) // T
        hT = hTs.next()
        for j in range(TB // 128):
            t0 = blk * TB + j * 128
            xt = xts.next()
            k.dma("sp", xt[:], x_in[t0:t0 + 128, :], r=[tok_in], w=[xt])
            xn = nt.norm(xt)
            nt.to_fm(xn, hT[:, :, j * 128:(j + 1) * 128], hT)
        for f in range(8):
            b = ctx.bank()
            for c in range(8):
                k.mm(b[:], Wq[:, c, f * 128:(f + 1) * 128], hT[:, c, :], c == 0, c == 7, r=[Wq, hT], w=[b])
            flip ^= 1
            k.copy(qT[:, f, :], b[:], r=[b], w=[qparts[f]], eng="act" if flip else "dve")
        for j in range(TB // 128):
            cols = slice(j * 128, (j + 1) * 128)
            b2 = [ctx.bank(), ctx.bank()]
            for h in range(4):
                b = b2[h // 2]
                for e in range(2):
                    k.mm(b[:, (h % 2) * 256:(h % 2 + 1) * 256], qT[:, 2 * h + e, cols], KT[:, s, 2 * h + e, :], e == 0, e == 1,
                         r=[qparts[2 * h + e], KT], w=[b])
            mx = mxs.next()
            for i in range(2):
                k.reduce(mx[:, 2 * i:2 * i + 2], b2[i][:].rearrange("p (h m) -> p h m", h=2), ALU.max, r=[b2[i]], w=[mx])
            nmx = nmxs.next()
            k.ts(nmx[:], mx[:], -1.0 / 16.0, None, ALU.mult, r=[mx], w=[nmx])
            p = ps_.next()
            rsum = rsums.next()
            for h in range(4):
                k.act(p[:, h, :], b2[h // 2][:, (h % 2) * 256:(h % 2 + 1) * 256], AF.Exp, r=[b2[h // 2], nmx], w=[p, rsum],
                      bias=nmx[:, h:h + 1], scale=1.0 / 16.0, accum_out=rsum[:, h:h + 1])
            rinv = rinvs.next()
            k.recip(rinv[:], rsum[:], r=[rsum], w=[rinv])
            pn = pns.next()
            k.tt(pn[:], p[:], rinv[:].unsqueeze(2).to_broadcast([128, 4, NMEM]), ALU.mult, r=[p, rinv], w=[pn])
            bank = ctx.bank()
            pb = bank.t[:].bitcast(BF16)
            for h in range(4):
                for e in range(2):
                    i = 2 * h + e
                    k.tr(pb[:, i * 128:(i + 1) * 128], pn[:, h, e * 128:(e + 1) * 128], ctx.identb[:], r=[pn, ctx.identb], w=[bank])
            flip ^= 1
            k.copy(pT[:, :, cols], pb.rearrange("p (c t) -> p c t", c=8), r=[bank], w=[pparts[j]], eng="act" if flip else "dve")
        for f in range(8):
            h = f // 2
            b = ctx.bank()
            for jm in range(2):
                k.mm(b[:], V[:, s, jm, f * 128:(f + 1) * 128], pT[:, 2 * h + jm, :], jm == 0, jm == 1, r=[V] + pparts, w=[b])
            flip ^= 1
            k.copy(oT[:, f, :], b[:], r=[b], w=[oparts[f]], eng="act" if flip else "dve")
        for j in range(TB // 128):
            t0 = blk * TB + j * 128
            cols = slice(j * 128, (j + 1) * 128)
            xr = xrs.next()
            k.dma("sp", xr[:], x_in[t0:t0 + 128, :], r=[tok_in], w=[xr])
            xo = xos.next()
            for half in range(2):
                po = ctx.bank()
                for c in range(8):
                    k.mm(po[:], oT[:, c, cols], Wo[:, c, half * 512:(half + 1) * 512], c == 0, c == 7, r=[oparts[c], Wo], w=[po])
                k.tt(xo[:, half * 512:(half + 1) * 512], po[:], xr[:, half * 512:(half + 1) * 512], ALU.add, r=[po, xr], w=[xo])
            outs.append(k.dma("sp", x_out[t0:t0 + 128, :], xo[:], r=[xo], w=[tok_out]))
    ctx.P.release(mark)
    return outs


FM_ROWS = 1568
TOKW = 2320


def phase_a0(ctx, x_in, w_in, gain_ap, S_fm, S_tok, tok_in, tok_fm, tok_tok):
    k = ctx.k
    n = ctx.n
    TB = 512
    mark = ctx.P.mark()
    g_fm = load_gain_fm(ctx, gain_ap)
    Win = load_w(ctx, w_in, g_fm)
    nt = NormT(ctx)
    xts = k.ring(3, [128, D], F32)
    hTs = k.ring(2, [128, 8, TB], BF16)
    fos = k.ring(3, [128, TB], F32)
    tks = k.ring(2, [128, TOKW], F32)
    fm_cols = [(j * 128, 128) for j in range(8)] + [(2064 + j * 128, 128) for j in range(4)] + [(3600, 32)]
    tok_groups = [(1024, 512, 0), (1536, 512, 512), (2048, 16, 1024), (2320, 256, 1040), (2576, 512, 1296), (3088, 512, 1808)]
    flip = 0
    for blk in range(n // TB):
        hT = hTs.next()
        for j in range(TB // 128):
            t0 = blk * TB + j * 128
            xt = xts.next()
            k.dma("sp", xt[:], x_in[t0:t0 + 128, :], r=[tok_in], w=[xt])
            xn = nt.norm(xt)
            nt.to_fm(xn, hT[:, :, j * 128:(j + 1) * 128], hT)
        for i, (c0, m) in enumerate(fm_cols):
            b = ctx.bank()
            for c in range(8):
                k.mm(b[0:m, :], Win[:, c, c0:c0 + m], hT[:, c, :], c == 0, c == 7, r=[Win, hT], w=[b])
            fo = fos.next()
            flip ^= 1
            k.copy(fo[0:m, :], b[0:m, :], r=[b], w=[fo], eng="act" if flip else "dve")
            r0 = i * 128
            k.dma("sp", S_fm[r0:r0 + m, blk * TB:(blk + 1) * TB], fo[0:m, :], r=[fo], w=[tok_fm])
        for j in range(TB // 128):
            t0 = blk * TB + j * 128
            tk = tks.next()
            for (c0, w, o0) in tok_groups:
                b = ctx.bank()
                for c in range(8):
                    k.mm(b[:, 0:w], hT[:, c, j * 128:(j + 1) * 128], Win[:, c, c0:c0 + w], c == 0, c == 7, r=[Win, hT], w=[b])
                flip ^= 1
                k.copy(tk[:, o0:o0 + w], b[:, 0:w], r=[b], w=[tk], eng="act" if flip else "dve")
            k.dma("sp", S_tok[t0:t0 + 128, :], tk[:], r=[tk], w=[tok_tok])
    ctx.P.release(mark)


class MixL0:
    def __init__(self, ctx, S_fm, S_tok, S_cb, S_sb, conv_ap, igb_ap, fgb_ap, mnorm_ap, w2_ap, db_ap, gnorm_ap, wout_ap,
                 tok_fm, tok_tok):
        self.ctx = ctx
        k = self.k = ctx.k
        self.S_fm, self.S_tok, self.S_cb, self.S_sb = S_fm, S_tok, S_cb, S_sb
        self.tok_fm, self.tok_tok = tok_fm, tok_tok
        self.tok_cb = Buf("cb")
        self.cw = k.tile([128, 3, 8], F32)
        for j in range(3):
            k.dma("sp", self.cw[:, j, :], conv_ap[j].rearrange("(c p) -> p c", p=128), w=[self.cw], allow_slow_non_contiguous=True)
        k.ts(self.cw[:], self.cw[:], 0.5, None, ALU.mult, r=[self.cw], w=[self.cw])
        self.gb = k.tile([128, 16], F32)
        k.dma("sp", self.gb[:, 0:8], igb_ap.rearrange("a b -> (a b)").partition_broadcast(128), w=[self.gb])
        k.dma("sp", self.gb[:, 8:16], fgb_ap.rearrange("a b -> (a b)").partition_broadcast(128), w=[self.gb])
        self.mnorm = k.tile([128, 512], F32)
        k.dma("sp", self.mnorm[:], mnorm_ap.partition_broadcast(128), w=[self.mnorm])
        self.gnorm = k.tile([128, 512], F32)
        k.dma("sp", self.gnorm[:], gnorm_ap.partition_broadcast(128), w=[self.gnorm])
        k.ts(self.mnorm[:], self.mnorm[:], 0.5, None, ALU.mult, r=[self.mnorm], w=[self.mnorm])
        k.ts(self.gnorm[:], self.gnorm[:], 0.5, None, ALU.mult, r=[self.gnorm], w=[self.gnorm])
        self.dbias = k.tile([128, 512], F32)
        k.dma("sp", self.dbias[:], db_ap.rearrange("a b -> (a b)").partition_broadcast(128), w=[self.dbias])
        self.w2p = k.tile([32, 2, 256], F32)
        k.memset(self.w2p[:], 0.0, w=[self.w2p])
        k.dma("sp", self.w2p[0:16, 0, :], w2_ap[0], w=[self.w2p])
        k.dma("sp", self.w2p[16:32, 1, :], w2_ap[1], w=[self.w2p])
        self.Wout = load_w(ctx, wout_ap)
        r = k.ring
        self.Xs = r(2, [128, 8, 130], F32)
        self.z1s = r(2, [128, 8, 128], F32)
        self.z2s = r(2, [128, 8, 128], F32)
        self.QKs = r(2, [128, 8, 128], BF16)
        self.TKs = r(2, [128, TOKW], F32)
        self.vps = r(2, [128, 4, 129], BF16)
        for t in self.vps.tiles:
            k.memset(t[:, :, 128:129], 1.0, w=[t])
        self.g8 = [r(2, [128, 8], F32) for _ in range(8)]
        self.glrs = r(2, [32, 128], F32)
        self.gqks = r(2, [128, 4, 128], F32)
        self.w512 = [r(2, [128, 512], F32) for _ in range(8)]
        self.khats = r(2, [128, 2, 256], BF16)
        self.ktz = r(2, [128, 8, 128], BF16)
        self.qtz = r(2, [128, 8, 128], BF16)
        self.thBs = r(2, [128, 512], F32)
        self.qts = r(2, [128, 512], BF16)
        self.kts = r(2, [128, 512], BF16)
        self.gvbs = r(2, [128, 512], BF16)
        self.Cfb = r(2, [128, 4, 129], BF16)
        self.Cbb = r(2, [128, 4, 129], BF16)
        self.Sfb = r(2, [128, 2, 128], BF16)
        self.Sbb = r(2, [128, 2, 128], BF16)
        for rr_ in (self.ktz, self.qtz):
            for t in rr_.tiles:
                k.memset(t[:], 0.0, w=[t])
        self.kToks = r(2, [128, 4, 128], BF16)
        self.CF = r(2, [128, 4, 129], F32)
        self.CB = r(3, [128, 4, 129], F32)
        self.SF = r(2, [128, 2, 128], F32)
        self.SB = r(3, [128, 2, 128], F32)
        self.pFB = r(2, [128, 8, 128], BF16)
        self.pA = r(2, [128, 8, 128], BF16)
        self.hms = r(2, [128, 4, 128], F32)
        self.small = [r(2, [128, 8], F32) for _ in range(8)]
        self.junks = r(2, [128, 128], F32)
        self.merged = r(2, [128, D], BF16)
        self.mTs = r(2, [128, 8, 128], BF16)
        self.xos = r(2, [128, D], F32)
        self.nt = NormT(ctx, with_xn=False)

    def prep(self, s, c, d_state, light=False):
        ctx, k = self.ctx, self.k
        T = ctx.T
        nch = T // 128
        t0 = s * T + c * 128
        o = {}
        X = self.Xs.next()
        lo = 1 if c == 0 else 0
        hi = 129 if c == nch - 1 else 130
        if c == 0:
            k.memset(X[:, :, 0:1], 0.0, w=[X])
        if c == nch - 1:
            k.memset(X[:, :, 129:130], 0.0, w=[X])
        k.dma("sp", X[:, :, lo:hi], self.S_fm[0:1024, t0 - 1 + lo:t0 - 1 + hi].rearrange("(c p) t -> p c t", p=128),
              r=[self.tok_fm], w=[X])
        yield
        z1 = self.z1s.next()
        z2 = self.z2s.next()
        cs_ = slice(4, 8) if light else slice(0, 8)
        nc_ = 4 if light else 8
        cwb = lambda j: self.cw[:, j, cs_].unsqueeze(2).to_broadcast([128, nc_, 128])
        k.tt(z1[:, cs_, :], X[:, cs_, 0:128], cwb(0), ALU.mult, r=[X, self.cw], w=[z1])
        yield
        k.tt(z2[:, cs_, :], X[:, cs_, 1:129], cwb(1), ALU.mult, r=[X, self.cw], w=[z2])
        yield
        k.tt(z1[:, cs_, :], z1[:, cs_, :], z2[:, cs_, :], ALU.add, r=[z1, z2], w=[z1])
        yield
        k.tt(z2[:, cs_, :], X[:, cs_, 2:130], cwb(2), ALU.mult, r=[X, self.cw], w=[z2])
        yield
        k.tt(z1[:, cs_, :], z1[:, cs_, :], z2[:, cs_, :], ALU.add, r=[z1, z2], w=[z1])
        yield
        k.act(z2[:, cs_, :], z1[:, cs_, :], AF.Tanh, r=[z1], w=[z2])
        yield
        QK = self.QKs.next()
        k.stt(QK[:, cs_, :], z2[:, cs_, :], 1.0, z1[:, cs_, :], ALU.add, ALU.mult, r=[z1, z2], w=[QK])
        yield
        o["QK"] = QK
        TK = self.TKs.next()
        k.dma("sp", TK[:], self.S_tok[t0:t0 + 128, :], r=[self.tok_tok], w=[TK])
        o["TK"] = TK
        vp = self.vps.next()
        k.copy(vp[:, :, 0:128], TK[:, 0:512].rearrange("p (h d) -> p h d", h=4), r=[TK], w=[vp], eng="pool")
        o["vp"] = vp
        if not light:
            thA = self.w512[7].next()
            thB = self.thBs.next()
            k.act(thA[:], TK[:, 512:1024], AF.Tanh, r=[TK], w=[thA], scale=0.5)
            yield
            k.act(thB[:], TK[:, 1808:2320], AF.Tanh, r=[TK], w=[thB], scale=0.5)
            yield
            o["thA"], o["thB"] = thA, thB
        gvb = self.gvbs.next()
        k.copy(gvb[:], TK[:, 1296:1808], r=[TK], w=[gvb], eng="act")
        yield
        o["gvb"] = gvb
        g = [rr.next() for rr in self.g8]
        ig, zf, l1f, t1, sw, qe, eg, kw = g
        k.tt(ig[:], TK[:, 1024:1032], self.gb[:, 0:8], ALU.add, r=[TK, self.gb], w=[ig])
        k.tt(zf[:], TK[:, 1032:1040], self.gb[:, 8:16], ALU.add, r=[TK, self.gb], w=[zf])
        yield
        k.act(zf[:], zf[:], AF.Exp, r=[zf], w=[zf], scale=-1.0)
        yield
        k.act(l1f[:], zf[:], AF.Ln, r=[zf], w=[l1f], bias=1.0)
        yield
        Gb = ctx.bank()
        k.mm(Gb[:, 0:4], ctx.cst("NU"), l1f[:, 0:4], r=[ctx.C, l1f], w=[Gb])
        k.mm(Gb[:, 4:8], ctx.cst("NL"), l1f[:, 4:8], r=[ctx.C, l1f], w=[Gb])
        k.mm(Gb[:, 8:16], ctx.cst("NONES"), l1f[:, 0:8], r=[ctx.C, l1f], w=[Gb])
        k.tt(t1[:], ig[:], Gb[:, 0:8], ALU.subtract, r=[ig, Gb], w=[t1])
        k.act(qe[:], Gb[:, 0:8], AF.Exp, r=[Gb], w=[qe])
        k.act(eg[:], Gb[:, 8:16], AF.Exp, r=[Gb], w=[eg])
        yield
        k.act(sw[:], t1[:], AF.Exp, r=[t1], w=[sw])
        yield
        k.ts(qe[:], qe[:], 128.0 ** -0.5, None, ALU.mult, r=[qe], w=[qe])
        yield
        k.tt(kw[:], sw[:], eg[:], ALU.mult, r=[sw, eg], w=[kw])
        yield
        o.update(sw=sw, qe=qe, eg=eg, kw=kw)
        kTb = ctx.bank()
        kTpb = kTb.t[:].bitcast(BF16)
        for h in range(4):
            k.tr(kTpb[:, h * 128:(h + 1) * 128], QK[:, 4 + h, :], ctx.identb[:], r=[QK, ctx.identb], w=[kTb])
        kTok = self.kToks.next()
        k.tt(kTok[:], kTpb[:, 0:512].rearrange("p (h t) -> p h t", h=4),
             kw[:, d_state * 4:(d_state + 1) * 4].unsqueeze(2).to_broadcast([128, 4, 128]), ALU.mult, r=[kTb, kw], w=[kTok])
        yield
        o["kTok"] = kTok
        glr = self.glrs.next()
        k.dma("sp", glr[:], self.S_fm[1536:1568, t0:t0 + 128], r=[self.tok_fm], w=[glr])
        if not light:
            gqk = self.gqks.next()
            k.dma("sp", gqk[:], self.S_fm[1024:1536, t0:t0 + 128].rearrange("(c p) t -> p c t", p=128), r=[self.tok_fm], w=[gqk])
        w = [rr.next() for rr in self.w512[0:7]]
        zb, l1, eT, emT, _q, _k, egmb = w
        qtT, ktT = (None, None) if light else (self.qts.next(), self.kts.next())
        zbk = ctx.bank()
        for d in range(2):
            k.mm(zbk[:, d * 256:(d + 1) * 256], glr[:], self.w2p[:, d, :], r=[glr, self.w2p], w=[zbk])
        k.tt(zb[:], zbk[:], self.dbias[:], ALU.add, r=[zbk, self.dbias], w=[zb])
        yield
        k.act(zb[:], zb[:], AF.Exp, r=[zb], w=[zb], scale=-1.0)
        yield
        k.act(l1[:], zb[:], AF.Ln, r=[zb], w=[l1], bias=1.0)
        yield
        bTb = ctx.bank()
        dirs = (1,) if light else (0, 1)
        for d in dirs:
            for j in range(2):
                i = d * 2 + j
                k.mm(bTb[:, i * 128:(i + 1) * 128], l1[:, d * 256 + j * 128:d * 256 + (j + 1) * 128],
                     ctx.cst("UN" if d == 0 else "LN"), r=[l1, ctx.C], w=[bTb])
        if light:
            k.act(eT[:, 256:512], bTb[:, 256:512], AF.Exp, r=[bTb], w=[eT])
        else:
            k.act(eT[:], bTb[:], AF.Exp, r=[bTb], w=[eT])
            k.act(emT[:], bTb[:], AF.Exp, r=[bTb], w=[emT], scale=-1.0)
        yield
        v4 = lambda t: t[:].rearrange("p (a b) -> p a b", a=4)
        for d in (() if light else (0, 1)):
            k.stt(v4(qtT)[:, d * 2:(d + 1) * 2, :], gqk[:, 0:2, :], 0.125, v4(eT)[:, d * 2:(d + 1) * 2, :], ALU.mult, ALU.mult,
                  r=[gqk, eT], w=[qtT])
            yield
            k.tt(v4(ktT)[:, d * 2:(d + 1) * 2, :], gqk[:, 2:4, :], v4(emT)[:, d * 2:(d + 1) * 2, :], ALU.mult, r=[gqk, emT], w=[ktT])
            yield
        gmb = ctx.bank()
        if not light:
            k.mm(gmb[:, 0:256], ctx.cst("UC"), l1[:, 0:256], r=[ctx.C, l1], w=[gmb])
        k.mm(gmb[:, 256:512], ctx.cst("LC"), l1[:, 256:512], r=[ctx.C, l1], w=[gmb])
        if light:
            k.act(egmb[:, 256:512], gmb[:, 256:512], AF.Exp, r=[gmb], w=[egmb])
        else:
            k.act(egmb[:], gmb[:], AF.Exp, r=[gmb], w=[egmb])
        yield
        khat = self.khats.next()
        for d in dirs:
            k.tt(khat[:, d, :], TK[:, 1040:1296], egmb[:, d * 256:(d + 1) * 256], ALU.mult, r=[TK, egmb], w=[khat])
            yield
        o.update(eT=eT, qtT=qtT, ktT=ktT, khat=khat)
        return o

    def state_update(self, o, d, Cold, Sold, Cring, Sring):
        ctx, k = self.ctx, self.k
        TK, vp = o["TK"], o["vp"]
        kTok = o["kTok"]
        Cn = Cring.next()
        for p in range(2):
            b = ctx.bank()
            bv = b[:, 0:258].rearrange("p (h e) -> p h e", h=2)
            for hh in range(2):
                h = 2 * p + hh
                k.mm(bv[:, hh, :], kTok[:, h, :], vp[:, h, :], r=[kTok, vp], w=[b])
            for hh in range(2):
                h = 2 * p + hh
                k.stt(Cn[:, h, :], Cold[:, h, :], o["eg"][:, d * 4 + h:d * 4 + h + 1], bv[:, hh, :], ALU.mult, ALU.add,
                      r=[Cold, o["eg"], b], w=[Cn])
            yield
        Sn = Sring.next()
        eT4 = o["eT"][:].rearrange("p (a b) -> p a b", a=4)
        col = 127 if d == 0 else 0
        for j in range(2):
            b = ctx.bank()
            k.mm(b[:, 0:256], o["khat"][:, d, j * 128:(j + 1) * 128], o["gvb"][:, j * 256:(j + 1) * 256], r=[o["khat"], o["gvb"]], w=[b])
            for e in range(2):
                rows = slice(e * 64, (e + 1) * 64)
                k.stt(Sn[rows, j, :], Sold[rows, j, :], eT4[rows, d * 2 + j, col:col + 1], b[rows, e * 128:(e + 1) * 128],
                      ALU.mult, ALU.add, r=[Sold, o["eT"], b], w=[Sn])
            yield
        return Cn, Sn

    def pass1(self, s):
        ctx, k = self.ctx, self.k
        nch = ctx.T // 128
        Cb = self.CB.next()
        Sb = self.SB.next()
        k.memset(Cb[:], 0.0, w=[Cb])
        k.memset(Sb[:], 0.0, w=[Sb])
        for c in range(nch - 1, -1, -1):
            idx = s * nch + c
            k.dma("sp", self.S_cb[idx], Cb[:].rearrange("p h e -> p (h e)"), r=[Cb], w=[self.tok_cb])
            k.dma("sp", self.S_sb[idx], Sb[:].rearrange("p j v -> p (j v)"), r=[Sb], w=[self.tok_cb])
            yield
            if c == 0:
                break
            o = yield from self.prep(s, c, 1, light=True)
            Cb, Sb = yield from self.state_update(o, 1, Cb, Sb, self.CB, self.SB)

    def pass2(self, s, x_in, x_out, tok_in, tok_out, outs):
        ctx, k = self.ctx, self.k
        T = ctx.T
        nch = T // 128
        Cf = self.CF.next()
        Sf = self.SF.next()
        k.memset(Cf[:], 0.0, w=[Cf])
        k.memset(Sf[:], 0.0, w=[Sf])
        Cfb, Sfb = self.Cfb.next(), self.Sfb.next()
        k.memset(Cfb[:], 0.0, w=[Cfb])
        k.memset(Sfb[:], 0.0, w=[Sfb])
        MU, ML = ctx.cst("MU"), ctx.cst("ML")
        import os
        kp2 = int(os.environ.get("KP2", "9"))
        for c in range(nch):
            t0 = s * T + c * 128
            idx = s * nch + c
            o = yield from self.prep(s, c, 0)
            QK, TK, vp = o["QK"], o["TK"], o["vp"]
            Cb = self.CB.next()
            Sb = self.SB.next()
            k.dma("sp", Cb[:].rearrange("p h e -> p (h e)"), self.S_cb[idx], r=[self.tok_cb], w=[Cb])
            k.dma("sp", Sb[:].rearrange("p j v -> p (j v)"), self.S_sb[idx], r=[self.tok_cb], w=[Sb])
            Cbb, Sbb = self.Cbb.next(), self.Sbb.next()
            k.copy(Cbb[:], Cb[:], r=[Cb], w=[Cbb], eng="act")
            k.copy(Sbb[:], Sb[:], r=[Sb], w=[Sbb], eng="pool")
            yield
            sb_ = ctx.bank()
            for h in range(4):
                k.mm(sb_[:, h * 128:(h + 1) * 128], QK[:, 4 + h, :], QK[:, h, :], r=[QK], w=[sb_])
            pFB = self.pFB.next()
            for d in range(2):
                for h in range(4):
                    k.stt(pFB[:, d * 4 + h, :], sb_[:, h * 128:(h + 1) * 128], o["sw"][:, d * 4 + h:d * 4 + h + 1], MU if d == 0 else ML,
                          ALU.mult, ALU.mult, r=[sb_, o["sw"], ctx.C], w=[pFB])
            if kp2 <= 1:
                continue
            yield
            sm = [rr.next() for rr in self.small]
            d1, nd, d2, rr_, ss, rs, ss2, rs2 = sm
            nb = {}
            for p in range(2):
                for d in range(2):
                    b = ctx.bank()
                    bv = b[:, 0:258].rearrange("p (h e) -> p h e", h=2)
                    Cst = Cfb if d == 0 else Cbb
                    for hh in range(2):
                        h = 2 * p + hh
                        k.mm(bv[:, hh, :], pFB[:, d * 4 + h, :], vp[:, h, :], True, False, r=[pFB, vp], w=[b])
                        k.mm(bv[:, hh, :], QK[:, h, :], Cst[:, h, :], False, True, r=[QK, Cst], w=[b])
                    nb[(p, d)] = (b, bv)
                    k.tt(d1[:, d * 4 + 2 * p:d * 4 + 2 * p + 2], bv[:, :, 128], o["qe"][:, d * 4 + 2 * p:d * 4 + 2 * p + 2], ALU.mult,
                         r=[b, o["qe"]], w=[d1])
            k.ts(nd[:], d1[:], -1.0, None, ALU.mult, r=[d1], w=[nd])
            k.tt(d2[:], d1[:], nd[:], ALU.max, r=[d1, nd], w=[d2])
            k.ts(d2[:], d2[:], 1.0, None, ALU.max, r=[d2], w=[d2])
            k.recip(d2[:], d2[:], r=[d2], w=[d2])
            k.tt(rr_[:], d2[:], o["qe"][:], ALU.mult, r=[d2, o["qe"]], w=[rr_])
            hm = self.hms.next()
            for h in range(4):
                p, hh = h // 2, h % 2
                bF, bvF = nb[(p, 0)]
                bB, bvB = nb[(p, 1)]
                k.ts(hm[:, h, :], bvF[:, hh, 0:128], rr_[:, h:h + 1], None, ALU.mult, r=[bF, rr_], w=[hm])
                k.stt(hm[:, h, :], bvB[:, hh, 0:128], rr_[:, 4 + h:5 + h], hm[:, h, :], ALU.mult, ALU.add, r=[bB, rr_, hm], w=[hm])
            yield
            for h in range(4):
                junk = self.junks.next()
                k.act(junk[:], hm[:, h, :], AF.Square, r=[hm], w=[junk, ss], accum_out=ss[:, h:h + 1])
            yield
            k.ts(rs[:, 0:4], ss[:, 0:4], 1.0 / 128, EPS, ALU.mult, ALU.add, r=[ss], w=[rs])
            yield
            k.act(rs[:, 0:4], rs[:, 0:4], AF.Ln, r=[rs], w=[rs])
            yield
            k.act(rs[:, 0:4], rs[:, 0:4], AF.Exp, r=[rs], w=[rs], scale=-0.5)
            yield
            wA = o["thA"]
            k.stt(wA[:], wA[:], 1.0, self.mnorm[:], ALU.add, ALU.mult, r=[wA, self.mnorm], w=[wA])
            yield
            mg = self.merged.next()
            for h in range(4):
                k.stt(mg[:, h * 128:(h + 1) * 128], hm[:, h, :], rs[:, h:h + 1], wA[:, h * 128:(h + 1) * 128], ALU.mult, ALU.mult,
                      r=[hm, rs, wA], w=[mg])
            if kp2 <= 2:
                continue
            yield
            qt4 = o["qtT"][:].rearrange("p (a b) -> p a b", a=4)
            kt4 = o["ktT"][:].rearrange("p (a b) -> p a b", a=4)
            pA = self.pA.next()
            ktz, qtz = self.ktz.next(), self.qtz.next()
            for d in range(2):
                for h in range(4):
                    j, e = h // 2, h % 2
                    rows = slice(e * 64, (e + 1) * 64)
                    k.copy(ktz[rows, d * 4 + h, :], kt4[rows, d * 2 + j, :], r=[o["ktT"]], w=[ktz], eng="pool")
                    k.copy(qtz[rows, d * 4 + h, :], qt4[rows, d * 2 + j, :], r=[o["qtT"]], w=[qtz])
            yield
            for d in range(2):
                b = ctx.bank()
                for h in range(4):
                    j, e = h // 2, h % 2
                    k.mm(b[:, h * 128:(h + 1) * 128], ktz[:, d * 4 + h, :], qt4[:, d * 2 + j, :], r=[ktz, o["qtT"]], w=[b])
                k.tt(pA[:, d * 4:(d + 1) * 4, :], b[:].rearrange("p (h t) -> p h t", h=4),
                     (MU if d == 0 else ML).unsqueeze(1).to_broadcast([128, 4, 128]), ALU.mult, r=[b, ctx.C], w=[pA])
                yield
            if kp2 <= 3:
                continue
            ob = ctx.bank()
            for h in range(4):
                j, e = h // 2, h % 2
                rows = slice(e * 64, (e + 1) * 64)
                gv = o["gvb"][:, h * 128:(h + 1) * 128]
                dst = ob[:, h * 128:(h + 1) * 128]
                k.mm(dst, pA[:, h, :], gv, True, False, r=[pA, o["gvb"]], w=[ob])
                k.mm(dst, pA[:, 4 + h, :], gv, False, False, r=[pA, o["gvb"]], w=[ob])
                k.mm(dst, qtz[:, h, :], Sfb[:, j, :], False, False, r=[qtz, Sfb], w=[ob])
                k.mm(dst, qtz[:, 4 + h, :], Sbb[:, j, :], False, True, r=[qtz, Sbb], w=[ob])
            for h in range(4):
                junk = self.junks.next()
                k.act(junk[:], ob[:, h * 128:(h + 1) * 128], AF.Square, r=[ob], w=[junk, ss2], accum_out=ss2[:, h:h + 1])
            k.ts(rs2[:, 0:4], ss2[:, 0:4], 1.0 / 128, EPS, ALU.mult, ALU.add, r=[ss2], w=[rs2])
            k.act(rs2[:, 0:4], rs2[:, 0:4], AF.Ln, r=[rs2], w=[rs2])
            k.act(rs2[:, 0:4], rs2[:, 0:4], AF.Exp, r=[rs2], w=[rs2], scale=-0.5)
            wB = o["thB"]
            k.stt(wB[:], wB[:], 1.0, TK[:, 1808:2320], ALU.add, ALU.mult, r=[wB, TK], w=[wB])
            k.tt(wB[:], wB[:], self.gnorm[:], ALU.mult, r=[wB, self.gnorm], w=[wB], eng="pool")
            for h in range(4):
                k.stt(mg[:, 512 + h * 128:512 + (h + 1) * 128], ob[:, h * 128:(h + 1) * 128], rs2[:, h:h + 1],
                      wB[:, h * 128:(h + 1) * 128], ALU.mult, ALU.mult, r=[ob, rs2, wB], w=[mg])
            if kp2 <= 4:
                continue
            yield
            if c < nch - 1:
                Cf, Sf = yield from self.state_update(o, 0, Cf, Sf, self.CF, self.SF)
                Cfb, Sfb = self.Cfb.next(), self.Sfb.next()
                k.copy(Cfb[:], Cf[:], r=[Cf], w=[Cfb], eng="act")
                k.copy(Sfb[:], Sf[:], r=[Sf], w=[Sfb], eng="pool")
                yield
            mT = self.mTs.next()
            self.nt.to_fm(mg, mT[:], mT)
            yield
            xo = self.xos.next()
            k.dma("sp", xo[:], x_in[t0:t0 + 128, :], r=[tok_in], w=[xo])
            for half in range(2):
                po = ctx.bank()
                for cc in range(8):
                    k.mm(po[:], mT[:, cc, :], self.Wout[:, cc, half * 512:(half + 1) * 512], cc == 0, cc == 7, r=[mT, self.Wout], w=[po])
                k.tt(xo[:, half * 512:(half + 1) * 512], po[:], xo[:, half * 512:(half + 1) * 512], ALU.add, r=[po, xo], w=[xo])
                yield
            outs.append(k.dma("sp", x_out[t0:t0 + 128, :], xo[:], r=[xo], w=[tok_out]))
            yield


def phase_b0(ctx, x_in, x_out, S_fm, S_tok, S_cb, S_sb, prm, tok_in, tok_fm, tok_tok, tok_out):
    mark = ctx.P.mark()
    mx = MixL0(ctx, S_fm, S_tok, S_cb, S_sb, prm["conv"], prm["igb"], prm["fgb"], prm["mnorm"], prm["w2"], prm["db"],
               prm["gnorm"], prm["wout"], tok_fm, tok_tok)
    outs = []
    import os
    kb0 = int(os.environ.get("KB0", "9"))
    if kb0 == 0:
        return list(ctx.P.dma_last.values())
    def run_all(gens):
        alive = list(gens)
        while alive:
            nxt = []
            for g_ in alive:
                try:
                    next(g_)
                    nxt.append(g_)
                except StopIteration:
                    pass
            alive = nxt
    for s0 in range(0, ctx.nseq, 2):
        ss_ = list(range(s0, min(s0 + 2, ctx.nseq)))
        run_all([mx.pass1(s) for s in ss_])
        run_all([mx.pass2(s, x_in, x_out, tok_in, tok_out, outs) for s in ss_])
    ctx.P.release(mark)
    return outs


C0 = float(np.exp(-0.5))
LCH = 64


def phase_a1(ctx, x_in, prm, S1, S_wt, S_vtok, S_bonus, S_g, tok_in, tok_s1):
    k = ctx.k
    n, T = ctx.n, ctx.T
    TB = 256
    NQ = TB // LCH
    mark = ctx.P.mark()
    g_fm = load_gain_fm(ctx, prm["gain"])
    Wr = load_w(ctx, prm["w_rkv"][0], g_fm)
    Wk = load_w(ctx, prm["w_rkv"][1], g_fm)
    Wv = load_w(ctx, prm["w_rkv"][2], g_fm)
    W1 = k.tile([128, 8, 352], BF16)
    load_w(ctx, prm["w1"][0], None, dst=W1, col0=0)
    load_w(ctx, prm["w1"][1], None, dst=W1, col0=64)
    load_w(ctx, prm["a1"], None, dst=W1, col0=128)
    load_w(ctx, prm["g1"], None, dst=W1, col0=192)
    for c in range(8):
        k.ts(W1[:, c, :], W1[:, c, :], g_fm[:, c:c + 1], None, ALU.mult, r=[W1, g_fm], w=[W1])
    w2t = k.tile([64, 3, D], BF16)
    k.dma("pool", w2t[:, 0, :], prm["w2"][0], w=[w2t])
    k.dma("pool", w2t[:, 1, :], prm["w2"][1], w=[w2t])
    k.dma("pool", w2t[:, 2, :], prm["a2"], w=[w2t])
    g2t = k.tile([128, 2, D], BF16)
    k.dma("pool", g2t[:, 0, :], prm["g2"][0:128, :], w=[g2t])
    k.dma("pool", g2t[0:32, 1, :], prm["g2"][128:160, :], w=[g2t])
    pc = k.tile([128, 7, 8], F32)
    srcs = [prm["w0"][0], prm["w0"][1], prm["a0"], prm["k_k"], prm["k_a"], prm["k_a"], prm["r_k"].rearrange("h d -> (h d)")]
    for i, sap in enumerate(srcs):
        k.dma("sp", pc[:, i, :], sap.rearrange("(c p) -> p c", p=128), w=[pc], allow_slow_non_contiguous=True)
    k.ts(pc[:, 5, :], pc[:, 5, :], -1.0, 1.0, ALU.mult, ALU.add, r=[pc], w=[pc])
    mu = k.tile([128, 6, 8], F32)
    for i in range(6):
        k.dma("sp", mu[:, i, :], prm["mu"][i].rearrange("(c p) -> p c", p=128), w=[mu], allow_slow_non_contiguous=True)
    nt = NormT(ctx, with_xn=False)
    xts = k.ring(2, [128, D], F32)
    xnf = k.ring(1, [128, D], F32)
    xh = k.ring(1, [2, D], F32)
    hTs = k.ring(1, [128, 8, TB + 2], F32)
    hhs = k.ring(1, [128, 8, TB], F32)
    mix_sets = [[k.tile([128, 8, TB], BF16) for _ in range(6)] for _ in range(2)]
    lows = k.ring(2, [128, 5, TB], BF16)
    NWAY = 2
    fsets = [[k.tile([128, TB], F32) for _ in range(13)] for _ in range(NWAY)]
    vts = k.ring(2, [128, D], BF16)
    ob4s = k.ring(5, [128, 4, TB], BF16)
    bg16 = k.ring(4, [128, TB], BF16)
    wtall = k.ring(2, [128, 2, 8, NQ], F32)
    ident = ctx.cst("ident")
    flipb = [0]

    def prologue(blk):
        mixes = mix_sets[blk % 2]
        b0 = blk * TB
        tpos = b0 % T
        hT = hTs.next()
        xhh = xh.next()
        k.memset(xhh[:], 0.0, w=[xhh])
        if tpos > 0:
            k.dma("sp", xhh[0:1, :], x_in[b0 - 1:b0, :], r=[tok_in], w=[xhh])
        if tpos + TB < T:
            k.dma("sp", xhh[1:2, :], x_in[b0 + TB:b0 + TB + 1, :], r=[tok_in], w=[xhh])
        rs = nt.rstd(xhh[:], xhh, rows=2)
        xn2 = xhh
        k.ts(xn2[:], xhh[:], rs[0:2, :], None, ALU.mult, r=[xhh, rs], w=[xn2])
        bk = ctx.bank()
        for c in range(8):
            k.tr(bk[:, c * 2:(c + 1) * 2], xn2[0:2, c * 128:(c + 1) * 128], ident[0:2, 0:2], r=[xn2, ctx.C], w=[bk])
        bkv = bk[:, 0:16].rearrange("p (c e) -> p c e", e=2)
        k.copy(hT[:, :, 0], bkv[:, :, 0], r=[bk], w=[hT])
        k.copy(hT[:, :, TB + 1], bkv[:, :, 1], r=[bk], w=[hT])
        yield
        for j in range(TB // 128):
            t0 = b0 + j * 128
            xt = xts.next()
            k.dma("sp", xt[:], x_in[t0:t0 + 128, :], r=[tok_in], w=[xt])
            rs = nt.rstd(xt[:], xt)
            xn = xnf.next()
            k.ts(xn[:], xt[:], rs[:], None, ALU.mult, r=[xt, rs], w=[xn])
            yield
            for half in range(2):
                bk = ctx.bank()
                for c4 in range(4):
                    c = half * 4 + c4
                    k.tr(bk[:, c4 * 128:(c4 + 1) * 128], xn[:, c * 128:(c + 1) * 128], ident, r=[xn, ctx.C], w=[bk])
                flipb[0] ^= 1
                k.copy(hT[:, half * 4:(half + 1) * 4, 1 + j * 128:1 + (j + 1) * 128], bk[:].rearrange("p (c t) -> p c t", c=4),
                       r=[bk], w=[hT], eng="act" if flipb[0] else "dve")
                yield
        hh = hhs.next()
        k.tt(hh[:], hT[:, :, 0:TB], hT[:, :, 2:TB + 2], ALU.add, r=[hT], w=[hh], eng="pool")
        yield
        k.stt(hh[:], hh[:], 0.5, hT[:, :, 1:TB + 1], ALU.mult, ALU.subtract, r=[hh, hT], w=[hh])
        yield
        for i in range(6):
            for c in range(8):
                k.stt(mixes[i][:, c, :], hh[:, c, :], mu[:, i, c:c + 1], hT[:, c, 1:TB + 1], ALU.mult, ALU.add, r=[hh, mu, hT], w=[mixes[i]])
                yield
        xr, xw, xk, xv, xa, xg = mixes
        low = lows.next()
        specs = [(xw, 0, 64, 0, AF.Tanh), (xw, 64, 64, 1, AF.Tanh), (xa, 128, 64, 2, AF.Copy), (xg, 192, 128, 3, AF.Sigmoid), (xg, 320, 32, 4, AF.Sigmoid)]
        for (src, c0, m, slot, fn) in specs:
            bk = ctx.bank()
            for c in range(8):
                k.mm(bk[0:m, 0:TB], W1[:, c, c0:c0 + m], src[:, c, :], c == 0, c == 7, r=[W1, src], w=[bk])
            k.act(low[0:m, slot, :], bk[0:m, 0:TB], fn, r=[bk], w=[low])
            yield
        for j in range(TB // 128):
            t0 = b0 + j * 128
            vt = vts.next()
            for half in range(2):
                bk = ctx.bank()
                for c in range(8):
                    k.mm(bk[:], xv[:, c, j * 128:(j + 1) * 128], Wv[:, c, half * 512:(half + 1) * 512], c == 0, c == 7, r=[xv, Wv], w=[bk])
                flipb[0] ^= 1
                k.copy(vt[:, half * 512:(half + 1) * 512], bk[:], r=[bk], w=[vt], eng="act" if flipb[0] else "dve")
                yield
            k.dma("sp", S_vtok[t0:t0 + 128, :], vt[:], r=[vt], w=[tok_s1])
            yield
        wta = wtall.next()
        wta_parts = [Buf("wt") for _ in range(16)]
        return (xr, xk, xv, low, b0, blk, wta, wta_parts)

    def fc_chain(fc, F, B):
        xr, xk, xv, low, b0, blk, wta, wta_parts = B
        fs = slice(fc * 128, (fc + 1) * 128)
        r_, k_, v_, a_, kk, k2, t1, t2, sg, G, cI, cE, W = F
        col = lambda i: pc[:, i, fc:fc + 1]

        def proj(Wt, src):
            bk = ctx.bank()
            for c in range(8):
                k.mm(bk[:, 0:TB], Wt[:, c, fs], src[:, c, :], c == 0, c == 7, r=[Wt, src], w=[bk])
            return bk
        bk = proj(Wr, xr)
        k.copy(r_[:], bk[:, 0:TB], r=[bk], w=[r_], eng="act")
        yield
        bk = proj(Wk, xk)
        k.copy(k_[:], bk[:, 0:TB], r=[bk], w=[k_])
        yield
        bk = proj(Wv, xv)
        k.copy(v_[:], bk[:, 0:TB], r=[bk], w=[v_], eng="act")
        yield
        bk = ctx.bank()
        k.mm(bk[:, 0:TB], w2t[:, 2, fs], low[0:64, 2, :], r=[w2t, low], w=[bk])
        k.act(a_[:], bk[:, 0:TB], AF.Sigmoid, r=[bk, pc], w=[a_], bias=col(2))
        yield
        bk = ctx.bank()
        k.mm(bk[:, 0:TB], g2t[:, 0, fs], low[:, 3, :], True, False, r=[g2t, low], w=[bk])
        k.mm(bk[:, 0:TB], g2t[0:32, 1, fs], low[0:32, 4, :], False, True, r=[g2t, low], w=[bk])
        gb16 = bg16.next()
        k.copy(gb16[:], bk[:, 0:TB], r=[bk], w=[gb16])
        k.dma("sp", S_g[blk, :, fc, :], gb16[:], r=[gb16], w=[tok_s1])
        yield
        k.ts(kk[:], k_[:], col(3), None, ALU.mult, r=[k_, pc], w=[kk])
        yield
        k.act(t2[:], kk[:], AF.Square, r=[kk], w=[t2])
        yield
        bk = ctx.bank()
        k.mm(bk[:, 0:TB], ctx.cst("BLK"), t2[:], r=[ctx.C, t2], w=[bk])
        k.ts(t2[:], bk[:, 0:TB], 1e-24, None, ALU.max, r=[bk], w=[t2])
        yield
        k.act(t2[:], t2[:], AF.Ln, r=[t2], w=[t2])
        yield
        k.act(t2[:], t2[:], AF.Exp, r=[t2], w=[t2], scale=-0.5)
        yield
        k.tt(kk[:], kk[:], t2[:], ALU.mult, r=[kk, t2], w=[kk])
        k.ts(k2[:], a_[:], col(4), col(5), ALU.mult, ALU.add, r=[a_, pc], w=[k2])
        yield
        k.tt(k2[:], k2[:], k_[:], ALU.mult, r=[k2, k_], w=[k2])
        yield
        k.stt(t2[:], r_[:], col(6), k2[:], ALU.mult, ALU.mult, r=[r_, pc, k2], w=[t2])
        yield
        bk = ctx.bank()
        k.mm(bk[:, 0:TB], ctx.cst("BLK"), t2[:], r=[ctx.C, t2], w=[bk])
        bb16 = bg16.next()
        k.tt(bb16[:], bk[:, 0:TB], v_[:], ALU.mult, r=[bk, v_], w=[bb16])
        k.dma("sp", S_bonus[blk, :, fc, :], bb16[:], r=[bb16], w=[tok_s1])
        k.tt(a_[:], a_[:], kk[:], ALU.mult, r=[a_, kk], w=[a_], eng="pool")
        yield
        for d in range(2):
            bk = ctx.bank()
            k.mm(bk[:, 0:TB], w2t[:, d, fs], low[0:64, d, :], r=[w2t, low], w=[bk])
            k.act(sg[:], bk[:, 0:TB], AF.Sigmoid, r=[bk, pc], w=[sg], bias=col(d))
            yield
            ctx.P.op("dve", lambda e, G=G, sg=sg: e.tensor_tensor_scan(out=G[:], data0=sg[:], data1=sg[:], initial=0.0, op0=ALU.add, op1=ALU.bypass),
                     _tok([sg]), _tok([G]))
            yield
            G3 = G[:].rearrange("p (q t) -> p q t", t=LCH)
            c3 = cI[:].rearrange("p (q t) -> p q t", t=LCH)
            e3 = cE[:].rearrange("p (q t) -> p q t", t=LCH)
            k.copy(c3[:, 0, :], G3[:, 0, :], r=[G], w=[cI], eng="pool")
            k.tt(c3[:, 1:NQ, :], G3[:, 1:NQ, :], G3[:, 0:NQ - 1, LCH - 1:LCH].to_broadcast([128, NQ - 1, LCH]), ALU.subtract, r=[G], w=[cI])
            yield
            tot = c3[:, :, LCH - 1:LCH]
            k.act(wta[:, d, fc, :], c3[:, :, LCH - 1], AF.Exp, r=[cI], w=[wta_parts[d * 8 + fc]], scale=-C0)
            if d == 0:
                k.tt(cE[:], cI[:], sg[:], ALU.subtract, r=[cI, sg], w=[cE], eng="pool")
                inc, exc = cI, cE
                yield
            else:
                k.tt(e3, tot.to_broadcast([128, NQ, LCH]), c3, ALU.subtract, r=[cI], w=[cE])
                yield
                k.tt(G[:], cE[:], sg[:], ALU.add, r=[cE, sg], w=[G], eng="pool")
                inc, exc = G, cE
                yield
            base = d * 4
            k.act(W[:], inc[:], AF.Exp, r=[inc], w=[W], scale=-C0)
            yield
            ob = ob4s.next()
            k.tt(ob[:, 3, :], r_[:], W[:], ALU.mult, r=[r_, W], w=[ob])
            yield
            k.act(W[:], inc[:], AF.Exp, r=[inc], w=[W], scale=C0)
            yield
            k.tt(ob[:, 2, :], k2[:], W[:], ALU.mult, r=[k2, W], w=[ob])
            k.tt(ob[:, 1, :], a_[:], W[:], ALU.mult, r=[a_, W], w=[ob], eng="pool")
            yield
            k.act(W[:], exc[:], AF.Exp, r=[exc], w=[W], scale=-C0)
            yield
            k.stt(ob[:, 0, :], kk[:], -1.0, W[:], ALU.mult, ALU.mult, r=[kk, W], w=[ob])
            k.dma("sp", S1[base:base + 4, fs, b0:b0 + TB].rearrange("a q t -> q a t"), ob[:], r=[ob], w=[tok_s1])
            yield

    def step(g_):
        try:
            next(g_)
            return True, None
        except StopIteration as e_:
            return False, e_.value

    nblk = n // TB
    pro = prologue(0)
    while True:
        ok, val = step(pro)
        if not ok:
            Bcur = val
            break
    for blk in range(nblk):
        pro = prologue(blk + 1) if blk + 1 < nblk else None
        Bnext = None
        for g0 in range(0, 8, NWAY):
            alive = [fc_chain(g0 + i_, fsets[i_], Bcur) for i_ in range(NWAY)]
            while alive:
                if pro is not None:
                    ok, val = step(pro)
                    if not ok:
                        Bnext, pro = val, None
                alive = [g_ for g_ in alive if step(g_)[0]]
        while pro is not None:
            ok, val = step(pro)
            if not ok:
                Bnext, pro = val, None
        wta, wta_parts = Bcur[6], Bcur[7]
        for d in range(2):
            k.dma("sp", S_wt[d].rearrange("(p q) c -> q p c", q=128)[:, :, blk * NQ:(blk + 1) * NQ], wta[:, d, :, :],
                  r=wta_parts[d * 8:(d + 1) * 8], w=[tok_s1], allow_slow_non_contiguous=True)
        Bcur = Bnext
    ctx.P.release(mark)


def phase_b1(ctx, x_in, x_out, prm, S1, S_wt, S_vtok, S_bonus, S_g, S_yb, tok_in, tok_s1, tok_out):
    k = ctx.k
    n, T = ctx.n, ctx.T
    NCH = T // LCH
    mark = ctx.P.mark()
    Wo = load_w(ctx, prm["w_o"])
    lnw = k.tile([128, 8], F32)
    lnb = k.tile([128, 8], F32)
    k.dma("sp", lnw[:], prm["ln_w"].rearrange("(c p) -> p c", p=128), w=[lnw], allow_slow_non_contiguous=True)
    k.dma("sp", lnb[:], prm["ln_b"].rearrange("(c p) -> p c", p=128), w=[lnb], allow_slow_non_contiguous=True)
    tok_yb = Buf("yb")

    def bdring(nslots):
        rr = k.ring(nslots, [128, 8, 128], F32)
        for t in rr.tiles:
            k.memset(t[:], 0.0, w=[t])
        return rr
    ATs, BTs, KTs, Vbs = bdring(2), bdring(2), bdring(2), bdring(2)
    RTs = k.ring(2, [128, 8, LCH], F32)
    wts = k.ring(2, [128, 8], F32)
    big = lambda nslots: k.ring(nslots, [128, 8, 128], F32)
    Ns, NTs, Ps = big(2), big(2), big(2)
    AKs, Xs, Us, Bts, Kts = big(1), big(1), big(1), big(1), big(1)
    Ms = big(2)
    tmpM = big(1)
    RBs = k.ring(1, [128, 8, LCH], F32)
    RKs = k.ring(1, [128, 8, LCH], F32)
    ysb = k.ring(2, [128, 8, LCH], F32)
    ybl = k.ring(2, [128, 8, LCH], F32)
    o512 = [k.ring(2, [128, 8, LCH], F32) for _ in range(5)]
    zTs = k.ring(2, [128, 8, LCH], BF16)
    xrs = k.ring(2, [LCH, D], F32)
    xos = k.ring(2, [LCH, D], F32)
    ident = ctx.cst("ident")
    outs = []
    flip = [0]

    def bd_src(ap2d, t0):
        v = ap2d.rearrange("(p e k) t -> e k p t", e=2, k=64)
        return [v[e][:, :, t0:t0 + LCH] for e in range(2)]

    def evac(dst_ap, src_ap, r, w):
        flip[0] ^= 1
        k.copy(dst_ap, src_ap, r=r, w=w, eng="act" if flip[0] else "dve")

    def pairs_mm(lhs_tile, rhs_tile, width=128, lhs2=None, rhs2=None):
        per_bank = 512 // width
        res = []
        for b0 in range(0, 8, per_bank):
            bk = ctx.bank()
            for p in range(b0, b0 + per_bank):
                o = bk[:, (p - b0) * width:(p - b0 + 1) * width]
                k.mm(o, lhs_tile[:, p, :], rhs_tile[:, p, :], True, lhs2 is None, r=[lhs_tile, rhs_tile], w=[bk])
                if lhs2 is not None:
                    k.mm(o, lhs2[:, p, :], rhs2[:, p, :], False, True, r=[lhs2, rhs2], w=[bk])
            res.append((bk, bk[:].rearrange("p (a b) -> p a b", b=width), b0, per_bank))
        return res

    for s in range(ctx.nseq):
        for d in (1, 0):
            base = d * 4
            Mst = Ms.next()
            k.memset(Mst[:], 0.0, w=[Mst])
            strict = ctx.cst("SF" if d == 0 else "SB")
            strictT = ctx.cst("SB" if d == 0 else "SF")
            incl = ctx.cst("IF" if d == 0 else "IB")[:, 0:LCH]
            order = range(NCH) if d == 0 else range(NCH - 1, -1, -1)
            for c in order:
                t0 = s * T + c * LCH
                cg = t0 // LCH
                AT, BT, KT, Vb = ATs.next(), BTs.next(), KTs.next(), Vbs.next()
                for e in range(2):
                    rows = slice(e * 64, (e + 1) * 64)
                    k.dma("sp", AT[rows, :, rows], bd_src(S1[base + 0], t0)[e], r=[tok_s1], w=[AT])
                    k.dma("sp", BT[rows, :, rows], bd_src(S1[base + 1], t0)[e], r=[tok_s1], w=[BT])
                    k.dma("sp", KT[rows, :, rows], bd_src(S1[base + 2], t0)[e], r=[tok_s1], w=[KT])
                    k.dma("sp", Vb[rows, :, rows], S_vtok[t0:t0 + LCH, :].rearrange("t (p e v) -> e t p v", e=2, v=64)[e],
                          r=[tok_s1], w=[Vb])
                RT = RTs.next()
                k.dma("sp", RT[:], S1[base + 3].rearrange("(p q) t -> q p t", q=128)[:, :, t0:t0 + LCH], r=[tok_s1], w=[RT])
                wt = wts.next()
                k.dma("sp", wt[:], S_wt[d].rearrange("(p q) c -> q p c", q=128)[:, :, cg], r=[tok_s1], w=[wt], allow_slow_non_contiguous=True)
                N, NT, P_ = Ns.next(), NTs.next(), Ps.next()
                AK = AKs.next()
                for (bk, v, b0, nb) in pairs_mm(BT, AT):
                    k.tt(N[:, b0:b0 + nb, :], v, strict.unsqueeze(1).to_broadcast([128, nb, 128]), ALU.mult, r=[bk, ctx.C], w=[N])
                for (bk, v, b0, nb) in pairs_mm(AT, BT):
                    k.tt(NT[:, b0:b0 + nb, :], v, strictT.unsqueeze(1).to_broadcast([128, nb, 128]), ALU.mult, r=[bk, ctx.C], w=[NT])
                for (bk, v, b0, nb) in pairs_mm(KT, AT):
                    k.tt(AK[:, b0:b0 + nb, :], v, strict.unsqueeze(1).to_broadcast([128, nb, 128]), ALU.mult, r=[bk, ctx.C], w=[AK])
                RB, RK = RBs.next(), RKs.next()
                for (bk, v, b0, nb) in pairs_mm(BT, RT, width=LCH):
                    k.tt(RB[:, b0:b0 + nb, :], v, incl.unsqueeze(1).to_broadcast([128, nb, LCH]), ALU.mult, r=[bk, ctx.C], w=[RB])
                for (bk, v, b0, nb) in pairs_mm(KT, RT, width=LCH):
                    k.tt(RK[:, b0:b0 + nb, :], v, incl.unsqueeze(1).to_broadcast([128, nb, LCH]), ALU.mult, r=[bk, ctx.C], w=[RK])
                k.tt(P_[:], N[:], ident.unsqueeze(1).to_broadcast([128, 8, 128]), ALU.add, r=[N, ctx.C], w=[P_], eng="pool")
                for lvl in range(5):
                    last = lvl == 4
                    N2 = None if last else Ns.next()
                    NT2 = NTs.next()
                    if not last:
                        for (bk, v, b0, nb) in pairs_mm(NT, N):
                            evac(N2[:, b0:b0 + nb, :], v, [bk], [N2])
                    for (bk, v, b0, nb) in pairs_mm(N, NT):
                        evac(NT2[:, b0:b0 + nb, :], v, [bk], [NT2])
                    P2 = Ps.next()
                    for (bk, v, b0, nb) in pairs_mm(NT2, P_):
                        k.tt(P2[:, b0:b0 + nb, :], v, P_[:, b0:b0 + nb, :], ALU.add, r=[bk, P_], w=[P2])
                    N, NT, P_ = N2, NT2, P2
                X = Xs.next()
                for (bk, v, b0, nb) in pairs_mm(AT, Mst, lhs2=AK, rhs2=Vb):
                    evac(X[:, b0:b0 + nb, :], v, [bk], [X])
                U = Us.next()
                for (bk, v, b0, nb) in pairs_mm(P_, X):
                    evac(U[:, b0:b0 + nb, :], v, [bk], [U])
                yb = ctx.bank()
                for p in range(8):
                    o = yb[:, p * LCH:(p + 1) * LCH]
                    k.mm(o, Mst[:, p, :], RT[:, p, :], True, False, r=[Mst, RT], w=[yb])
                    k.mm(o, U[:, p, :], RB[:, p, :], False, False, r=[U, RB], w=[yb])
                    k.mm(o, Vb[:, p, :], RK[:, p, :], False, True, r=[Vb, RK], w=[yb])
                yv = yb[:].rearrange("p (a b) -> p a b", b=LCH)
                if d == 1:
                    ys = ysb.next()
                    k.copy(ys[:], yv, r=[yb], w=[ys])
                    k.dma("sp", S_yb.rearrange("(p q) t -> q p t", q=128)[:, :, t0:t0 + LCH], ys[:], r=[ys], w=[tok_yb])
                if c != order[-1]:
                    Bt, Kt = Bts.next(), Kts.next()
                    for (src, dst) in ((BT, Bt), (KT, Kt)):
                        for b0 in (0, 4):
                            bk = ctx.bank()
                            for p in range(b0, b0 + 4):
                                k.tr(bk[:, (p - b0) * 128:(p - b0 + 1) * 128], src[:, p, :], ident, r=[src, ctx.C], w=[bk])
                            evac(dst[:, b0:b0 + 4, :], bk[:].rearrange("p (a b) -> p a b", b=128), [bk], [dst])
                    Mn = Ms.next()
                    tm = tmpM.next()
                    for (bk, v, b0, nb) in pairs_mm(Bt, U, lhs2=Kt, rhs2=Vb):
                        k.tt(tm[:, b0:b0 + nb, :], v, Mst[:, b0:b0 + nb, :], ALU.add, r=[bk, Mst], w=[tm])
                        k.tt(Mn[:, b0:b0 + nb, :], tm[:, b0:b0 + nb, :], wt[:, b0:b0 + nb].unsqueeze(2).to_broadcast([128, nb, 128]), ALU.mult,
                             r=[tm, wt], w=[Mn], eng="pool")
                    Mst = Mn
                if d == 0:
                    ybt = ybl.next()
                    k.dma("sp", ybt[:], S_yb.rearrange("(p q) t -> q p t", q=128)[:, :, t0:t0 + LCH], r=[tok_yb], w=[ybt])
                    bon = o512[0].next()
                    gt = o512[1].next()
                    k.dma("sp", bon[:], S_bonus.rearrange("(p q) t -> q p t", q=128)[:, :, t0:t0 + LCH], r=[tok_s1], w=[bon])
                    k.dma("sp", gt[:], S_g.rearrange("(p q) t -> q p t", q=128)[:, :, t0:t0 + LCH], r=[tok_s1], w=[gt])
                    ysum, ysq, t3 = o512[2].next(), o512[3].next(), o512[4].next()
                    k.tt(ysum[:], yv, ybt[:], ALU.add, r=[yb, ybt], w=[ysum])
                    k.act(ysq[:], ysum[:], AF.Square, r=[ysum], w=[ysq])
                    f2 = lambda t: t[:].rearrange("p a b -> p (a b)")
                    mb, qb = ctx.bank(), ctx.bank()
                    k.mm(mb[:], ctx.cst("BLKM"), f2(ysum), r=[ctx.C, ysum], w=[mb])
                    k.mm(qb[:], ctx.cst("BLKM"), f2(ysq), r=[ctx.C, ysq], w=[qb])
                    k.act(f2(ysq), mb[:], AF.Square, r=[mb], w=[ysq])
                    k.tt(f2(ysq), qb[:], f2(ysq), ALU.subtract, r=[qb, ysq], w=[ysq])
                    k.ts(f2(ysq), f2(ysq), 64e-5, None, ALU.add, r=[ysq], w=[ysq], eng="pool")
                    k.act(f2(ysq), f2(ysq), AF.Ln, r=[ysq], w=[ysq])
                    k.act(f2(ysq), f2(ysq), AF.Exp, r=[ysq], w=[ysq], scale=-0.5)
                    k.tt(f2(t3), f2(ysum), mb[:], ALU.subtract, r=[ysum, mb], w=[t3])
                    k.tt(t3[:], t3[:], ysq[:], ALU.mult, r=[t3, ysq], w=[t3])
                    k.tt(t3[:], t3[:], lnw[:].unsqueeze(2).to_broadcast([128, 8, LCH]), ALU.mult, r=[t3, lnw], w=[t3], eng="pool")
                    k.tt(t3[:], t3[:], lnb[:].unsqueeze(2).to_broadcast([128, 8, LCH]), ALU.add, r=[t3, lnb], w=[t3], eng="pool")
                    k.tt(t3[:], t3[:], bon[:], ALU.add, r=[t3, bon], w=[t3])
                    zT = zTs.next()
                    k.tt(zT[:], t3[:], gt[:], ALU.mult, r=[t3, gt], w=[zT])
                    xr = xrs.next()
                    k.dma("sp", xr[:], x_in[t0:t0 + LCH, :], r=[tok_in], w=[xr])
                    xo = xos.next()
                    for half in range(2):
                        po = ctx.bank()
                        for cc in range(8):
                            k.mm(po[0:LCH, :], zT[:, cc, :], Wo[:, cc, half * 512:(half + 1) * 512], cc == 0, cc == 7, r=[zT, Wo], w=[po])
                        k.tt(xo[:, half * 512:(half + 1) * 512], po[0:LCH, :], xr[:, half * 512:(half + 1) * 512], ALU.add, r=[po, xr], w=[xo])
                    outs.append(k.dma("sp", x_out[t0:t0 + LCH, :], xo[:], r=[xo], w=[tok_out]))
    ctx.P.release(mark)
    return outs


PARAM_SHAPES = None


def build_program(T, nseq, shapes):
    import os
    nc = bass.Bass("TRN2", target_bir_lowering=False)
    n = T * nseq
    carr, _ = build_consts()

    def din(name, shape):
        return nc.dram_tensor(name, list(shape), F32, kind="ExternalInput").ap()

    def dint(name, shape, dt=F32):
        return nc.dram_tensor(name, list(shape), dt, kind=os.environ.get("KSCR", "Internal")).ap()

    class _Sl:
        def __init__(self, lst):
            self.lst = lst

        def __getitem__(self, i):
            return self.lst[i]
    A = {}
    for name, shp in shapes.items():
        if name in ("x", "mem", "norm_final"):
            A[name] = din(name, shp)
        else:
            A[name] = _Sl([din(f"{name}_{i}", shp[1:]) for i in range(shp[0])])
    cst = din("consts", carr.shape)
    out = nc.dram_tensor("out", [n, D], F32, kind="ExternalOutput").ap()
    xa, xb = dint("xa", (n, D)), dint("xb", (n, D))
    S_fm, S_tok = dint("S_fm", (FM_ROWS, n)), dint("S_tok", (n, TOKW))
    nch = n // 128
    S_cb, S_sb = dint("S_cb", (nch, 128, 516)), dint("S_sb", (nch, 128, 256))
    S1, S_wt = dint("S1", (8, D, n), BF16), dint("S_wt", (2, D, n // LCH))
    S_vtok, S_bonus, S_g, S_yb = (dint("S_vtok", (n, D), BF16), dint("S_bonus", (n // 256, 128, 8, 256), BF16),
                                    dint("S_g", (n // 256, 128, 8, 256), BF16), dint("S_yb", (2, n // 256, 128, 8, 256), BF16))
    P = Prog(nc)
    ctx = Ctx(nc, P, T, nseq, cst)
    tx = Buf("x")
    ta, tb_ = Buf("xa"), Buf("xb")
    import os
    nph = int(os.environ.get("KPH", "99"))

    def finish(outs_):
        P.emit(final_waits=outs_)
        P.close()
        return nc, carr
    tfm, ttok = Buf("fm"), Buf("tok")
    phase_a0(ctx, A["x"], A["ev_w_in"][0], A["norm_mix"][0], S_fm, S_tok, tx, tfm, ttok)
    if nph <= 0:
        return finish(list(P.dma_last.values()))
    prm0 = dict(conv=A["ev_conv_qk"][0], igb=A["ev_m_ig_bias"][0], fgb=A["ev_m_fg_bias"][0], mnorm=A["ev_m_norm"][0],
                w2=A["ev_g_decay_w2"][0], db=A["ev_g_decay_b"][0], gnorm=A["ev_g_norm"][0], wout=A["ev_w_out"][0])
    o_ = phase_b0(ctx, A["x"], out if nph <= 1 else xa, S_fm, S_tok, S_cb, S_sb, prm0, tx, tfm, ttok, ta)
    if nph <= 1:
        return finish(o_)
    mem2 = A["mem"]
    o_ = phase_xattn(ctx, xa, out if nph <= 2 else xb, mem2, A["xa_wq"][0], A["xa_wkv"][0], A["xa_wo"][0], A["norm_xattn"][0], A["norm_mem"][0], ta, tb_)
    if nph <= 2:
        return finish(o_)
    o_ = phase_ffn(ctx, xb, out if nph <= 3 else xa, A["ffn_w_gate"][0], A["ffn_w_up"][0], A["ffn_w_down"][0], A["norm_ffn"][0], tb_, ta)
    if nph <= 3:
        return finish(o_)
    prm1 = dict(gain=A["norm_mix"][1], w_rkv=A["od_w_rkv"][0], w0=A["od_w0"][0], w1=A["od_w1"][0], w2=A["od_w2"][0], a0=A["od_a0"][0],
                a1=A["od_a1"][0], a2=A["od_a2"][0], g1=A["od_g1"][0], g2=A["od_g2"][0], k_k=A["od_k_k"][0], k_a=A["od_k_a"][0],
                r_k=A["od_r_k"][0], mu=A["od_mu"][0], ln_w=A["od_ln_w"][0], ln_b=A["od_ln_b"][0], w_o=A["od_w_o"][0])
    ts1 = Buf("s1")
    phase_a1(ctx, xa, prm1, S1, S_wt, S_vtok, S_bonus, S_g, ta, ts1)
    ty = Buf("y")
    phase_b1v2(ctx, prm1, S1, S_wt, S_vtok, S_yb, ts1, ty)
    o_ = phase_c1(ctx, xa, out if nph <= 4 else xb, prm1, S_yb, S_bonus, S_g, ta, ts1, ty, tb_)
    if nph <= 4:
        return finish(o_)
    phase_xattn(ctx, xb, xa, mem2, A["xa_wq"][1], A["xa_wkv"][1], A["xa_wo"][1], A["norm_xattn"][1], A["norm_mem"][1], tb_, ta)
    tout = Buf("out")
    outs = phase_ffn(ctx, xa, out, A["ffn_w_gate"][1], A["ffn_w_up"][1], A["ffn_w_down"][1], A["norm_ffn"][1], ta, tout,
                     final_gain_ap=A["norm_final"])
    P.emit(final_waits=outs)
    P.close()
    return nc, carr


_CACHE = {}


def kernel(**inputs):
    import os
    ncores = int(os.environ.get("KNC", "8"))
    x = np.asarray(inputs["x"], np.float32)
    B, T, _ = x.shape
    nseq = B // ncores
    shapes = {}
    per_core = []
    for name, v in inputs.items():
        v = np.ascontiguousarray(np.asarray(v, np.float32))
        if name == "x":
            shapes[name] = (nseq * T, D)
        elif name == "mem":
            shapes[name] = (nseq * NMEM, D)
        else:
            shapes[name] = v.shape
    key = (T, nseq)
    if key not in _CACHE:
        _CACHE[key] = build_program(T, nseq, shapes)
    nc, carr = _CACHE[key]
    in_maps = []
    for c in range(ncores):
        m = {"consts": carr}
        for name, v in inputs.items():
            v = np.ascontiguousarray(np.asarray(v, np.float32))
            if name == "x":
                m[name] = np.ascontiguousarray(v[c * nseq:(c + 1) * nseq].reshape(nseq * T, D))
            elif name == "mem":
                m[name] = np.ascontiguousarray(v[c * nseq:(c + 1) * nseq].reshape(nseq * NMEM, D))
            elif name == "norm_final":
                m[name] = v
            else:
                for i in range(v.shape[0]):
                    m[f"{name}_{i}"] = np.ascontiguousarray(v[i])
        in_maps.append(m)
    res = run_bass_kernel_spmd(nc, in_maps, core_ids=list(range(ncores)))
    outs = [np.asarray(r["out"], np.float32).reshape(nseq, T, D) for r in res.results]
    return np.concatenate(outs, axis=0)


def phase_b1v2(ctx, prm, S1, S_wt, S_vtok, S_y, tok_s1, tok_y, nchains=2):
    k = ctx.k
    n, T = ctx.n, ctx.T
    NCH = T // LCH
    mark = ctx.P.mark()
    ident = ctx.cst("ident")
    flip = [0]

    def evac(dst_ap, src_ap, r, w):
        flip[0] = (flip[0] + 1) % 4
        k.copy(dst_ap, src_ap, r=r, w=w, eng="dve" if flip[0] == 0 else "act")

    def pairs_mm(lhs_tile, rhs_tile, width=128, lhs2=None, rhs2=None):
        per_bank = 512 // width
        res = []
        for b0 in range(0, 8, per_bank):
            bk = ctx.bank()
            for p in range(b0, b0 + per_bank):
                o = bk[:, (p - b0) * width:(p - b0 + 1) * width]
                k.mm(o, lhs_tile[:, p, :], rhs_tile[:, p, :], True, lhs2 is None, r=[lhs_tile, rhs_tile], w=[bk])
                if lhs2 is not None:
                    k.mm(o, lhs2[:, p, :], rhs2[:, p, :], False, True, r=[lhs2, rhs2], w=[bk])
            res.append((bk, bk[:].rearrange("p (a b) -> p a b", b=width), b0, per_bank))
        return res

    def bd_src(ap2d, t0):
        v = ap2d.rearrange("(p e k) t -> e k p t", e=2, k=64)
        return [v[e][:, :, t0:t0 + LCH] for e in range(2)]

    class Work:
        def __init__(self):
            big = lambda: k.tile([128, 8, 128], BF16)
            big32 = lambda: k.tile([128, 8, 128], F32)
            self.AT, self.BT, self.KT, self.Vb = big(), big(), big(), big()
            for t in (self.AT, self.BT, self.KT, self.Vb):
                k.memset(t[:], 0.0, w=[t])
            self.RT = k.tile([128, 8, LCH], BF16)
            self.wt = k.tile([128, 8, NCH], F32)
            self.St = [k.tile([128, 32, 256], BF16), k.tile([128, 32, 256], BF16)]
            self.Vt = k.tile([128, D], BF16)
            self.N = [big(), big()]
            self.NT = [big(), big()]
            self.P = [big(), big()]
            self.AK, self.X, self.U, self.Bt, self.Kt = big(), big(), big(), big(), big()
            self.M = [big32(), big32()]
            self.Mb = [big(), big()]
            self.RB = k.tile([128, 8, LCH], BF16)
            self.RK = k.tile([128, 8, LCH], BF16)
            self.ys = k.tile([128, 8, LCH], BF16)

    works = [Work() for _ in range(nchains)]

    def chain(W, s, d):
        base = d * 4
        mi = 0
        Mst = W.M[mi]
        Mb = W.Mb[mi]
        k.memset(Mst[:], 0.0, w=[Mst])
        k.memset(Mb[:], 0.0, w=[Mb])
        strict = ctx.cst("SF" if d == 0 else "SB")
        strictT = ctx.cst("SB" if d == 0 else "SF")
        incl = ctx.cst("IF" if d == 0 else "IB")[:, 0:LCH]
        order = list(range(NCH)) if d == 0 else list(range(NCH - 1, -1, -1))
        S1flat = S1[base:base + 4].rearrange("a r t -> (a r) t")
        cg0 = (s * T) // LCH
        k.dma("sp", W.wt[:], S_wt[d].rearrange("(p q) c -> q p c", q=128)[:, :, cg0:cg0 + NCH], r=[tok_s1], w=[W.wt],
              allow_slow_non_contiguous=True)
        groups = []
        for c in order:
            if not groups or groups[-1] != c // 4:
                groups.append(c // 4)

        def load_group(gi):
            g = groups[gi]
            St = W.St[gi % 2]
            tg = s * T + g * 256
            k.dma("sp", St[:], S1flat[:, tg:tg + 256].rearrange("(ap q) t -> q ap t", q=128), r=[tok_s1], w=[St])
        load_group(0)
        for c in order:
            t0 = s * T + c * LCH
            gi = groups.index(c // 4)
            if c // 4 != (order[order.index(c) - 1] // 4 if order.index(c) > 0 else -1) and gi + 1 < len(groups):
                load_group(gi + 1)
            St = W.St[gi % 2]
            off = (c % 4) * LCH
            AT, BT, KT, Vb, RT = W.AT, W.BT, W.KT, W.Vb, W.RT
            wt = W.wt
            for e in range(2):
                rows = slice(e * 64, (e + 1) * 64)
                k.dma("sp", W.Vt[rows, :], S_vtok[t0:t0 + LCH, :], r=[tok_s1], w=[W.Vt])
            for ai, dstt in ((0, AT), (1, BT), (2, KT)):
                for e in range(2):
                    rows = slice(e * 64, (e + 1) * 64)
                    k.copy(dstt[rows, :, rows], St[rows, ai * 8:(ai + 1) * 8, off:off + LCH], r=[St], w=[dstt], eng="pool" if e == 0 else "act")
            k.copy(RT[:], St[:, 24:32, off:off + LCH], r=[St], w=[RT], eng="pool")
            for e in range(2):
                rows = slice(e * 64, (e + 1) * 64)
                k.copy(Vb[rows, :, rows], W.Vt[rows, :].rearrange("t (p e v) -> t p e v", e=2, v=64)[:, :, e, :], r=[W.Vt], w=[Vb],
                       eng="pool" if e == 0 else "act")
            yield
            ni = 0
            N, NT, P_ = W.N[0], W.NT[0], W.P[0]
            AK, RB, RK = W.AK, W.RB, W.RK
            for (bk, v, b0, nb) in pairs_mm(BT, AT):
                k.tt(N[:, b0:b0 + nb, :], v, strict.unsqueeze(1).to_broadcast([128, nb, 128]), ALU.mult, r=[bk, ctx.C], w=[N])
            for (bk, v, b0, nb) in pairs_mm(AT, BT):
                k.tt(NT[:, b0:b0 + nb, :], v, strictT.unsqueeze(1).to_broadcast([128, nb, 128]), ALU.mult, r=[bk, ctx.C], w=[NT])
            k.tt(P_[:], N[:], ident.unsqueeze(1).to_broadcast([128, 8, 128]), ALU.add, r=[N, ctx.C], w=[P_], eng="pool")
            yield
            for (bk, v, b0, nb) in pairs_mm(KT, AT):
                k.tt(AK[:, b0:b0 + nb, :], v, strict.unsqueeze(1).to_broadcast([128, nb, 128]), ALU.mult, r=[bk, ctx.C], w=[AK])
            for (bk, v, b0, nb) in pairs_mm(BT, RT, width=LCH):
                k.tt(RB[:, b0:b0 + nb, :], v, incl.unsqueeze(1).to_broadcast([128, nb, LCH]), ALU.mult, r=[bk, ctx.C], w=[RB])
            for (bk, v, b0, nb) in pairs_mm(KT, RT, width=LCH):
                k.tt(RK[:, b0:b0 + nb, :], v, incl.unsqueeze(1).to_broadcast([128, nb, LCH]), ALU.mult, r=[bk, ctx.C], w=[RK])
            yield
            last_c = c == order[-1]
            if not last_c:
                for (src, dst) in ((BT, W.Bt), (KT, W.Kt)):
                    bk = ctx.bank()
                    pb = bk.t[:].bitcast(BF16)
                    for p in range(8):
                        k.tr(pb[:, p * 128:(p + 1) * 128], src[:, p, :], ctx.identb[:], r=[src, ctx.identb], w=[bk])
                    evac(dst[:], pb.rearrange("p (a b) -> p a b", b=128), [bk], [dst])
                yield
            for lvl in range(5):
                last = lvl == 4
                N2 = None if last else W.N[1 - ni]
                NT2 = W.NT[1 - ni]
                if not last:
                    for (bk, v, b0, nb) in pairs_mm(NT, N):
                        evac(N2[:, b0:b0 + nb, :], v, [bk], [N2])
                for (bk, v, b0, nb) in pairs_mm(N, NT):
                    evac(NT2[:, b0:b0 + nb, :], v, [bk], [NT2])
                yield
                P2 = W.P[1 - ni]
                for (bk, v, b0, nb) in pairs_mm(NT2, P_):
                    k.tt(P2[:, b0:b0 + nb, :], v, P_[:, b0:b0 + nb, :], ALU.add, r=[bk, P_], w=[P2])
                N, NT, P_ = N2, NT2, P2
                ni = 1 - ni
                yield
            X, U = W.X, W.U
            for (bk, v, b0, nb) in pairs_mm(AT, Mb, lhs2=AK, rhs2=Vb):
                evac(X[:, b0:b0 + nb, :], v, [bk], [X])
            yield
            for (bk, v, b0, nb) in pairs_mm(P_, X):
                evac(U[:, b0:b0 + nb, :], v, [bk], [U])
            yield
            yb = ctx.bank()
            for p in range(8):
                o = yb[:, p * LCH:(p + 1) * LCH]
                k.mm(o, Mb[:, p, :], RT[:, p, :], True, False, r=[Mb, RT], w=[yb])
                k.mm(o, U[:, p, :], RB[:, p, :], False, False, r=[U, RB], w=[yb])
                k.mm(o, Vb[:, p, :], RK[:, p, :], False, True, r=[Vb, RK], w=[yb])
            evac(W.ys[:], yb[:].rearrange("p (a b) -> p a b", b=LCH), [yb], [W.ys])
            k.dma("sp", S_y[d, t0 // 256, :, :, t0 % 256:t0 % 256 + LCH], W.ys[:], r=[W.ys], w=[tok_y])
            if not last_c:
                Mn = W.M[1 - mi]
                for (bk, v, b0, nb) in pairs_mm(W.Bt, U, lhs2=W.Kt, rhs2=Vb):
                    k.tt(Mn[:, b0:b0 + nb, :], v, Mst[:, b0:b0 + nb, :], ALU.add, r=[bk, Mst], w=[Mn])
                    k.tt(Mn[:, b0:b0 + nb, :], Mn[:, b0:b0 + nb, :], wt[:, b0:b0 + nb, c:c + 1].to_broadcast([128, nb, 128]), ALU.mult,
                         r=[Mn, wt], w=[Mn], eng="pool")
                Mb = W.Mb[1 - mi]
                k.copy(Mb[:], Mn[:], r=[Mn], w=[Mb], eng="act")
                Mst = Mn
                mi = 1 - mi
            yield

    jobs = [(s, d) for s in range(ctx.nseq) for d in (0, 1)]
    for g0 in range(0, len(jobs), nchains):
        gens = [chain(works[i], *jobs[g0 + i]) for i in range(min(nchains, len(jobs) - g0))]
        alive = list(gens)
        while alive:
            nxt = []
            for g in alive:
                try:
                    next(g)
                    nxt.append(g)
                except StopIteration:
                    pass
            alive = nxt
    ctx.P.release(mark)


def phase_c1(ctx, x_in, x_out, prm, S_y, S_bonus, S_g, tok_in, tok_s1, tok_y, tok_out):
    k = ctx.k
    n = ctx.n
    TT = 256
    mark = ctx.P.mark()
    Wo = load_w(ctx, prm["w_o"])
    lnw = k.tile([128, 8], F32)
    lnb = k.tile([128, 8], F32)
    k.dma("sp", lnw[:], prm["ln_w"].rearrange("(c p) -> p c", p=128), w=[lnw], allow_slow_non_contiguous=True)
    k.dma("sp", lnb[:], prm["ln_b"].rearrange("(c p) -> p c", p=128), w=[lnb], allow_slow_non_contiguous=True)
    rings = [k.ring(2, [128, 8, TT], BF16) for _ in range(4)] + [k.ring(2, [128, 8, TT], F32) for _ in range(3)]
    zTs = k.ring(2, [128, 8, TT], BF16)
    xrs = k.ring(2, [128, D], F32)
    xos = k.ring(2, [128, D], F32)
    outs = []
    fm = lambda ap2d, t0: ap2d.rearrange("(p q) t -> q p t", q=128)[:, :, t0:t0 + TT]
    f2 = lambda t: t[:].rearrange("p a b -> p (a b)")
    def step_gen(st):
        t0 = st * TT
        yf, yb, bon, gt, ysum, ysq, t3 = [r.next() for r in rings]
        k.dma("sp", yf[:], S_y[0, st], r=[tok_y], w=[yf])
        k.dma("sp", yb[:], S_y[1, st], r=[tok_y], w=[yb])
        k.dma("sp", bon[:], S_bonus[st], r=[tok_s1], w=[bon])
        k.dma("sp", gt[:], S_g[st], r=[tok_s1], w=[gt])
        yield
        k.tt(ysum[:], yf[:], yb[:], ALU.add, r=[yf, yb], w=[ysum])
        yield
        k.act(ysq[:], ysum[:], AF.Square, r=[ysum], w=[ysq])
        yield
        NH = (8 * TT) // 512
        for hf in range(NH):
            cs = slice(hf * 512, (hf + 1) * 512)
            mbk, qbk = ctx.bank(), ctx.bank()
            k.mm(mbk[:], ctx.cst("BLKM"), f2(ysum)[:, cs], r=[ctx.C, ysum], w=[mbk])
            k.mm(qbk[:], ctx.cst("BLKM"), f2(ysq)[:, cs], r=[ctx.C, ysq], w=[qbk])
            k.act(f2(ysq)[:, cs], mbk[:], AF.Square, r=[mbk], w=[ysq])
            k.tt(f2(ysq)[:, cs], qbk[:], f2(ysq)[:, cs], ALU.subtract, r=[qbk, ysq], w=[ysq])
            k.tt(f2(t3)[:, cs], f2(ysum)[:, cs], mbk[:], ALU.subtract, r=[ysum, mbk], w=[t3])
            yield
        k.ts(f2(ysq), f2(ysq), 64e-5, None, ALU.add, r=[ysq], w=[ysq], eng="pool")
        yield
        k.act(f2(ysq), f2(ysq), AF.Ln, r=[ysq], w=[ysq])
        yield
        k.act(f2(ysq), f2(ysq), AF.Exp, r=[ysq], w=[ysq], scale=-0.5)
        yield
        k.tt(t3[:], t3[:], ysq[:], ALU.mult, r=[t3, ysq], w=[t3])
        yield
        k.tt(t3[:], t3[:], lnw[:].unsqueeze(2).to_broadcast([128, 8, TT]), ALU.mult, r=[t3, lnw], w=[t3], eng="pool")
        yield
        k.tt(t3[:], t3[:], lnb[:].unsqueeze(2).to_broadcast([128, 8, TT]), ALU.add, r=[t3, lnb], w=[t3])
        yield
        k.tt(t3[:], t3[:], bon[:], ALU.add, r=[t3, bon], w=[t3])
        yield
        zT = zTs.next()
        k.tt(zT[:], t3[:], gt[:], ALU.mult, r=[t3, gt], w=[zT])
        yield
        for sub in range(TT // 128):
            ts0 = t0 + sub * 128
            xr = xrs.next()
            k.dma("sp", xr[:], x_in[ts0:ts0 + 128, :], r=[tok_in], w=[xr])
            xo = xos.next()
            for half in range(2):
                po = ctx.bank()
                for cc in range(8):
                    k.mm(po[:], zT[:, cc, sub * 128:(sub + 1) * 128], Wo[:, cc, half * 512:(half + 1) * 512], cc == 0, cc == 7, r=[zT, Wo], w=[po])
                k.tt(xo[:, half * 512:(half + 1) * 512], po[:], xr[:, half * 512:(half + 1) * 512], ALU.add, r=[po, xr], w=[xo])
            outs.append(k.dma("sp", x_out[ts0:ts0 + 128, :], xo[:], r=[xo], w=[tok_out]))
            yield

    nsteps = n // TT
    for s0 in range(0, nsteps, 2):
        alive = [step_gen(s0 + i) for i in range(min(2, nsteps - s0))]
        while alive:
            nxt = []
            for g_ in alive:
                try:
                    next(g_)
                    nxt.append(g_)
                except StopIteration:
                    pass
            alive = nxt
    ctx.P.release(mark)
    return outs
```

```python
import numpy as np
import concourse.bass as bass
import concourse.mybir as mybir
from concourse.bass_utils import run_bass_kernel_spmd

F32 = mybir.dt.float32
BF16 = mybir.dt.bfloat16
AF = mybir.ActivationFunctionType
ALU = mybir.AluOpType
AX = mybir.AxisListType

ENGS = ("pe", "act", "dve", "pool", "sp")
SEM_WRAP = 30000


class Buf:
    __slots__ = ("name", "ap", "w", "r")

    def __init__(self, name, ap=None):
        self.name = name
        self.ap = ap
        self.w = None
        self.r = {}


class Ins:
    __slots__ = ("eng", "fn", "deps", "sig", "sigval", "dma", "dsem", "dval", "prev_dma", "idx")

    def __init__(self, eng, fn, deps, dma=False):
        self.eng = eng
        self.fn = fn
        self.deps = deps
        self.sig = False
        self.sigval = None
        self.dma = dma
        self.dsem = None
        self.dval = None
        self.prev_dma = None


class Ring:
    def __init__(self, P, n, shape, dt, psum=False):
        self.slots = []
        for _ in range(n):
            t = P.ps(shape, dt) if psum else P.sb(shape, dt)
            self.slots.append((t, Buf("ring")))
        self.i = 0

    def next(self):
        s = self.slots[self.i % len(self.slots)]
        self.i += 1
        return s


class Prog:
    def __init__(self, nc, n_dma_sems=16, same_engine_sync=True):
        self.nc = nc
        self.streams = {e: [] for e in ENGS}
        self.n_dma_sems = n_dma_sems
        self.same_engine_sync = same_engine_sync
        self.dma_rr = {e: 0 for e in ENGS}
        self.dma_last = {}
        self.stack = []
        self.nbuf = 0
        self.extra = {e: [] for e in ENGS}

    def mark(self):
        return len(self.stack)

    def release(self, mark):
        deps = []
        for e in ENGS:
            for ins in reversed(self.streams[e]):
                if not ins.dma:
                    deps.append(ins)
                    break
        deps.extend(self.dma_last.values())
        for e in ENGS:
            self.extra[e] = list(deps)
        while len(self.stack) > mark:
            self.stack.pop().__exit__(None, None, None)

    def sb(self, shape, dt, name=None):
        self.nbuf += 1
        g = self.nc.sbuf_tensor(name or f"sb{self.nbuf}", list(shape), dt)
        t = g.__enter__()
        self.stack.append(g)
        return t

    def ps(self, shape, dt=F32, name=None):
        self.nbuf += 1
        g = self.nc.psum_tensor(name or f"ps{self.nbuf}", list(shape), dt)
        t = g.__enter__()
        self.stack.append(g)
        return t

    def buf(self, name="b"):
        return Buf(name)

    def ring(self, n, shape, dt, psum=False):
        return Ring(self, n, shape, dt, psum)

    def _deps(self, eng, reads, writes):
        deps = []
        for b in reads:
            if b.w is not None:
                deps.append(b.w)
        for b in writes:
            if b.w is not None:
                deps.append(b.w)
            deps.extend(b.r.values())
        if self.extra[eng]:
            deps.extend(self.extra[eng])
            self.extra[eng] = []
        return deps

    def op(self, eng, fn, reads=(), writes=()):
        deps = self._deps(eng, reads, writes)
        ins = Ins(eng, fn, deps)
        for b in writes:
            b.w = ins
            b.r = {}
        for b in reads:
            b.r[eng] = ins
        self.streams[eng].append(ins)
        return ins

    def dma(self, eng, out, in_, reads=(), writes=(), **kw):
        deps = self._deps(eng, reads, writes)

        def fn(e, out=out, in_=in_, kw=kw):
            return e.dma_start(out=out, in_=in_, **kw)

        ins = Ins(eng, fn, deps, dma=True)
        slot = (eng, self.dma_rr[eng] % self.n_dma_sems)
        self.dma_rr[eng] += 1
        ins.dsem = slot
        prev = self.dma_last.get(slot)
        ins.prev_dma = prev
        ins.dval = (prev.dval if prev is not None else 0) + 16
        self.dma_last[slot] = ins
        for b in writes:
            b.w = ins
            b.r = {}
        for b in reads:
            b.r[("dma", id(ins))] = ins
        self.streams[eng].append(ins)
        return ins

    def emit(self, final_waits=()):
        nc = self.nc
        for e in ENGS:
            for ins in self.streams[e]:
                for d in ins.deps:
                    if d.dma:
                        continue
                    if d.eng == "pe" and ins.eng == "pe":
                        continue
                    if d.eng == ins.eng and not self.same_engine_sync:
                        continue
                    d.sig = True
        for d in final_waits:
            if not d.dma:
                d.sig = True
        nsig = {}
        for e in ENGS:
            n = 0
            for ins in self.streams[e]:
                if ins.sig:
                    ins.sigval = (n // SEM_WRAP, n % SEM_WRAP + 1)
                    n += 1
            nsig[e] = n
        sems = {}
        guards = []

        def getsem(key):
            if key not in sems:
                g = nc.semaphore("s_" + "_".join(str(k) for k in key))
                sems[key] = g.__enter__()
                guards.append(g)
            return sems[key]

        for e in ENGS:
            for k in range((nsig[e] + SEM_WRAP - 1) // SEM_WRAP):
                getsem(("c", e, k))
        for slot in self.dma_last:
            getsem(("d",) + slot)

        engobj = {"pe": "tensor", "act": "scalar", "dve": "vector", "pool": "gpsimd", "sp": "sync"}
        streams = self.streams

        def run(e, eng):
            waited = {}
            for ins in streams[e]:
                need = {}
                for d in ins.deps:
                    if d.dma:
                        key = ("d",) + d.dsem
                        val = d.dval
                    else:
                        if d.eng == "pe" and e == "pe":
                            continue
                        if d.eng == e and not self.same_engine_sync:
                            continue
                        key = ("c", d.eng, d.sigval[0])
                        val = d.sigval[1]
                    if need.get(key, 0) < val:
                        need[key] = val
                if ins.dma and ins.prev_dma is not None:
                    key = ("d",) + ins.dsem
                    if need.get(key, 0) < ins.prev_dma.dval:
                        need[key] = ins.prev_dma.dval
                for key, val in need.items():
                    if waited.get(key, 0) < val:
                        eng.wait_ge(sems[key], val)
                        waited[key] = val
                bi = ins.fn(eng)
                if ins.dma:
                    bi.then_inc(sems[("d",) + ins.dsem], 16)
                elif ins.sig:
                    bi.then_inc(sems[("c", e, ins.sigval[0])], 1)
            if e == "sp":
                for d in final_waits:
                    if d.dma:
                        eng.wait_ge(sems[("d",) + d.dsem], d.dval)
                    else:
                        eng.wait_ge(sems[("c", d.eng, d.sigval[0])], d.sigval[1])

        with nc.Block() as block:
            @block.tensor
            def _(eng):
                run("pe", eng)

            @block.scalar
            def _(eng):
                run("act", eng)

            @block.vector
            def _(eng):
                run("dve", eng)

            @block.gpsimd
            def _(eng):
                run("pool", eng)

            @block.sync
            def _(eng):
                run("sp", eng)
        for g in reversed(guards):
            g.__exit__(None, None, None)

    def close(self):
        for g in reversed(self.stack):
            g.__exit__(None, None, None)
        self.stack = []


class Tile:
    __slots__ = ("t", "b")

    def __init__(self, t):
        self.t = t
        self.b = Buf("t")

    def __getitem__(self, k):
        return self.t[k]


def _tok(xs):
    out = []
    for x in xs:
        if x is None:
            continue
        out.append(x.b if isinstance(x, Tile) else x)
    return out


class K:
    def __init__(self, P):
        self.P = P

    def tile(self, shape, dt, psum=False):
        return Tile(self.P.ps(shape, dt) if psum else self.P.sb(shape, dt))

    def ring(self, n, shape, dt, psum=False):
        return TRing([self.tile(shape, dt, psum) for _ in range(n)])

    def dma(self, q, out, in_, r=(), w=(), **kw):
        return self.P.dma(q, out, in_, reads=_tok(r), writes=_tok(w), **kw)

    def act(self, out, in_, func, r=(), w=(), eng="act", **kw):
        return self.P.op(eng, lambda e: e.activation(out=out, in_=in_, func=func, **kw), _tok(r), _tok(w))

    def ts(self, out, in0, s1, s2, op0, op1=None, r=(), w=(), eng="dve", **kw):
        if op1 is None:
            return self.P.op(eng, lambda e: e.tensor_scalar(out=out, in0=in0, scalar1=s1, scalar2=None, op0=op0, **kw), _tok(r), _tok(w))
        return self.P.op(eng, lambda e: e.tensor_scalar(out=out, in0=in0, scalar1=s1, scalar2=s2, op0=op0, op1=op1, **kw), _tok(r), _tok(w))

    def tt(self, out, in0, in1, op, r=(), w=(), eng="dve"):
        return self.P.op(eng, lambda e: e.tensor_tensor(out=out, in0=in0, in1=in1, op=op), _tok(r), _tok(w))

    def stt(self, out, in0, scalar, in1, op0, op1, r=(), w=()):
        return self.P.op("dve", lambda e: e.scalar_tensor_tensor(out=out, in0=in0, scalar=scalar, in1=in1, op0=op0, op1=op1), _tok(r), _tok(w))

    def copy(self, out, in_, r=(), w=(), eng="dve"):
        if eng == "act":
            return self.P.op("act", lambda e: e.copy(out=out, in_=in_), _tok(r), _tok(w))
        return self.P.op(eng, lambda e: e.tensor_copy(out=out, in_=in_), _tok(r), _tok(w))

    def recip(self, out, in_, r=(), w=()):
        return self.P.op("dve", lambda e: e.reciprocal(out=out, in_=in_), _tok(r), _tok(w))

    def memset(self, out, val, w=(), eng="pool"):
        return self.P.op(eng, lambda e: e.memset(out, val), (), _tok(w))

    def reduce(self, out, in_, op, r=(), w=()):
        return self.P.op("dve", lambda e: e.tensor_reduce(out=out, in_=in_, axis=AX.X, op=op), _tok(r), _tok(w))

    def mm(self, out, lhsT, rhs, start=True, stop=True, r=(), w=()):
        return self.P.op("pe", lambda e: e.matmul(out, lhsT=lhsT, rhs=rhs, start=start, stop=stop), _tok(r), _tok(w))

    def tr(self, out, in_, ident, r=(), w=()):
        return self.P.op("pe", lambda e: e.transpose(out=out, in_=in_, identity=ident), _tok(r), _tok(w))


class TRing:
    def __init__(self, tiles):
        self.tiles = tiles
        self.i = 0

    def next(self):
        t = self.tiles[self.i % len(self.tiles)]
        self.i += 1
        return t


D = 1024
NMEM = 256
DFF = 2816
EPS = 1e-6


def build_consts():
    i = np.arange(128)
    s, t = i[:, None], i[None, :]
    c = {}
    f = lambda m: np.asarray(m, np.float32)
    c["ident"] = f(s == t)
    c["MU"] = f(s <= t)
    c["ML"] = f(s >= t)
    c["ONES"] = np.ones((128, 128), np.float32)
    c["NU"] = -f(s <= t)
    c["NL"] = -f(s >= t)
    c["NONES"] = -np.ones((128, 128), np.float32)
    c["BLK"] = f((s // 64) == (t // 64))
    c["BLKM"] = f((s // 64) == (t // 64)) / 64.0
    s6, t6 = s % 64, t % 64
    c["SF"] = f(s6 < t6)
    c["SB"] = f(s6 > t6)
    c["IF"] = f(s6 <= t6)
    c["IB"] = f(s6 >= t6)
    c["UN"] = -f(s <= t) / 16.0
    c["LN"] = -f(s >= t) / 16.0
    c["UC"] = -f(s > t) / 16.0
    c["LC"] = -f(s < t) / 16.0
    names = list(c)
    arr = np.concatenate([np.asarray(c[k], np.float32) for k in names], axis=1)
    offs = {k: j * 128 for j, k in enumerate(names)}
    return arr, offs


class Ctx:
    def __init__(self, nc, P, T, nseq, consts_ap):
        self.nc = nc
        self.P = P
        self.k = K(P)
        self.T = T
        self.nseq = nseq
        self.n = T * nseq
        k = self.k
        arr, offs = build_consts()
        self.coffs = offs
        self.C = k.tile([128, arr.shape[1]], F32)
        k.dma("sp", self.C[:], consts_ap, w=[self.C])
        self.identb = k.tile([128, 128], BF16)
        k.copy(self.identb[:], self.cst("ident"), r=[self.C], w=[self.identb])
        self.banks = k.ring(8, [128, 512], F32, psum=True)
        self.dram_tok = {}

    def cst(self, name):
        o = self.coffs[name]
        return self.C[:, o:o + 128]

    def bank(self):
        return self.banks.next()

    def dtok(self, key):
        if key not in self.dram_tok:
            self.dram_tok[key] = Buf(str(key))
        return self.dram_tok[key]


def load_gain_fm(ctx, g_ap, nchunk=8):
    k = ctx.k
    g = k.tile([128, nchunk], F32)
    k.dma("sp", g[:], g_ap.rearrange("(c p) -> p c", p=128), w=[g], allow_slow_non_contiguous=True)
    return g


def load_w(ctx, w_ap, gain_fm=None, dst=None, col0=0):
    k = ctx.k
    Kd, F = w_ap.shape
    nch = Kd // 128
    if dst is None:
        dst = k.tile([128, nch, F], BF16)
    for c in range(nch):
        k.dma("pool", dst[:, c, col0:col0 + F], w_ap[c * 128:(c + 1) * 128, :], w=[dst])
    if gain_fm is not None:
        for c in range(nch):
            k.ts(dst[:, c, col0:col0 + F], dst[:, c, col0:col0 + F], gain_fm[:, c:c + 1], None, ALU.mult,
                 r=[dst, gain_fm], w=[dst])
    return dst


class NormT:
    def __init__(self, ctx, with_xn=True, nxn=3, njunk=2):
        k = ctx.k
        self.ctx = ctx
        self.junk = k.ring(njunk, [128, D], BF16)
        self.ss = k.ring(4, [128, 1], F32)
        self.rs = k.ring(4, [128, 1], F32)
        if with_xn:
            self.xn = k.ring(nxn, [128, D], BF16)
        self.flip = 0

    def rstd(self, xt_ap, xt_tile, width=D, rows=128):
        k = self.ctx.k
        junk = self.junk.next()
        ss = self.ss.next()
        rs = self.rs.next()
        R = slice(0, rows)
        k.act(junk[R, 0:width], xt_ap, AF.Square, r=[xt_tile], w=[junk, ss], accum_out=ss[R, :])
        k.act(rs[R, :], ss[R, :], AF.Ln, r=[ss], w=[rs], scale=1.0 / width, bias=EPS)
        k.act(rs[R, :], rs[R, :], AF.Exp, r=[rs], w=[rs], scale=-0.5)
        return rs

    def norm(self, xt):
        k = self.ctx.k
        rs = self.rstd(xt[:], xt)
        xn = self.xn.next()
        k.ts(xn[:], xt[:], rs[:], None, ALU.mult, r=[xt, rs], w=[xn])
        return xn

    def to_fm(self, xn, dst_ap, dst_tok, nchunk=8):
        ctx = self.ctx
        k = ctx.k
        bank = ctx.bank()
        pb = bank.t[:].bitcast(BF16)
        for c in range(nchunk):
            k.tr(pb[:, c * 128:(c + 1) * 128], xn[:, c * 128:(c + 1) * 128], ctx.identb[:], r=[xn, ctx.identb], w=[bank])
        src = pb[:, 0:nchunk * 128].rearrange("p (c t) -> p c t", c=nchunk)
        self.flip ^= 1
        k.copy(dst_ap, src, r=[bank], w=[dst_tok], eng="act" if self.flip else "dve")


def phase_ffn(ctx, x_in, x_out, wg, wu, wd, gain_ap, tok_in, tok_out, final_gain_ap=None):
    k = ctx.k
    n = ctx.n
    TB = 512
    NF = DFF // 128
    mark = ctx.P.mark()
    g_fm = load_gain_fm(ctx, gain_ap)
    Wgu = k.tile([128, 8, 2 * DFF], BF16)
    load_w(ctx, wg, None, dst=Wgu, col0=0)
    load_w(ctx, wu, None, dst=Wgu, col0=DFF)
    for c in range(8):
        k.ts(Wgu[:, c, :], Wgu[:, c, :], g_fm[:, c:c + 1], None, ALU.mult, r=[Wgu, g_fm], w=[Wgu])
    Wd = load_w(ctx, wd)
    nt = NormT(ctx, nxn=2, njunk=1)
    xts = k.ring(2, [128, D], F32)
    hTs = k.ring(1, [128, 8, TB], BF16)
    actT = k.tile([128, NF, TB], BF16)
    act_parts = [Buf("a") for _ in range(NF)]
    sgs = k.ring(2, [128, TB], F32)
    xos = k.ring(2, [128, D], F32)
    if final_gain_ap is not None:
        gbc = k.tile([128, D], F32)
        k.dma("sp", gbc[:], final_gain_ap.partition_broadcast(128), w=[gbc])
    outs = []
    for blk in range(n // TB):
        hT = hTs.next()
        for j in range(TB // 128):
            t0 = blk * TB + j * 128
            xt = xts.next()
            k.dma("sp", xt[:], x_in[t0:t0 + 128, :], r=[tok_in], w=[xt])
            xn = nt.norm(xt)
            nt.to_fm(xn, hT[:, :, j * 128:(j + 1) * 128], hT)
        for f in range(NF):
            pg = ctx.bank()
            pu = ctx.bank()
            for c in range(8):
                k.mm(pg[:, 0:TB], Wgu[:, c, f * 128:(f + 1) * 128], hT[:, c, :], c == 0, c == 7, r=[Wgu, hT], w=[pg])
            for c in range(8):
                k.mm(pu[:, 0:TB], Wgu[:, c, DFF + f * 128:DFF + (f + 1) * 128], hT[:, c, :], c == 0, c == 7, r=[Wgu, hT], w=[pu])
            sg = sgs.next()
            k.act(sg[:], pg[:, 0:TB], AF.Silu, r=[pg], w=[sg])
            k.tt(actT[:, f, :], sg[:], pu[:, 0:TB], ALU.mult, r=[sg, pu], w=[act_parts[f]])
        for j in range(TB // 128):
            t0 = blk * TB + j * 128
            xo = xos.next()
            k.dma("sp", xo[:], x_in[t0:t0 + 128, :], r=[tok_in], w=[xo])
            for half in range(2):
                po = ctx.bank()
                for f in range(NF):
                    k.mm(po[:], actT[:, f, j * 128:(j + 1) * 128], Wd[:, f, half * 512:(half + 1) * 512], f == 0, f == NF - 1,
                         r=[act_parts[f], Wd], w=[po])
                k.tt(xo[:, half * 512:(half + 1) * 512], po[:], xo[:, half * 512:(half + 1) * 512], ALU.add, r=[po, xo], w=[xo])
            if final_gain_ap is not None:
                rs = nt.rstd(xo[:], xo)
                k.stt(xo[:], xo[:], rs[:], gbc[:], ALU.mult, ALU.mult, r=[xo, rs, gbc], w=[xo])
            outs.append(k.dma("sp", x_out[t0:t0 + 128, :], xo[:], r=[xo], w=[tok_out]))
    ctx.P.release(mark)
    return outs


def phase_xattn(ctx, x_in, x_out, mem, wq, wkv, wo, g_x_ap, g_mem_ap, tok_in, tok_out):
    k = ctx.k
    n, T = ctx.n, ctx.T
    TB = 512
    mark = ctx.P.mark()
    gx = load_gain_fm(ctx, g_x_ap)
    gm = load_gain_fm(ctx, g_mem_ap)
    Wkv = load_w(ctx, wkv, gm)
    Wq = load_w(ctx, wq, gx)
    Wo = load_w(ctx, wo)
    nt = NormT(ctx)
    xts = k.ring(3, [128, D], F32)
    KT = k.tile([128, ctx.nseq, 8, NMEM], BF16)
    V = k.tile([128, ctx.nseq, 2, D], BF16)
    memT = k.tile([128, 8, NMEM], BF16)
    flip = 0
    for s in range(ctx.nseq):
        for j in range(2):
            xt = xts.next()
            k.dma("sp", xt[:], mem[s * NMEM + j * 128:s * NMEM + (j + 1) * 128, :], w=[xt])
            xn = nt.norm(xt)
            nt.to_fm(xn, memT[:, :, j * 128:(j + 1) * 128], memT)
        for f in range(8):
            b = ctx.bank()
            for c in range(8):
                k.mm(b[:, 0:NMEM], Wkv[:, c, f * 128:(f + 1) * 128], memT[:, c, :], c == 0, c == 7, r=[Wkv, memT], w=[b])
            flip ^= 1
            k.copy(KT[:, s, f, :], b[:, 0:NMEM], r=[b], w=[KT], eng="act" if flip else "dve")
        for j in range(2):
            for half in range(2):
                b = ctx.bank()
                for c in range(8):
                    k.mm(b[:], memT[:, c, j * 128:(j + 1) * 128], Wkv[:, c, D + half * 512:D + (half + 1) * 512], c == 0, c == 7,
                         r=[Wkv, memT], w=[b])
                flip ^= 1
                k.copy(V[:, s, j, half * 512:(half + 1) * 512], b[:], r=[b], w=[V], eng="act" if flip else "dve")
    hTs = k.ring(2, [128, 8, TB], BF16)
    qT = k.tile([128, 8, TB], BF16)
    qparts = [Buf("q") for _ in range(8)]
    pT = k.tile([128, 8, TB], BF16)
    pparts = [Buf("p") for _ in range(TB // 128)]
    oT = k.tile([128, 8, TB], BF16)
    oparts = [Buf("o") for _ in range(8)]
    mxs = k.ring(2, [128, 4], F32)
    nmxs = k.ring(2, [128, 4], F32)
    rsums = k.ring(2, [128, 4], F32)
    rinvs = k.ring(2, [128, 4], F32)
    ps_ = k.ring(2, [128, 4, NMEM], BF16)
    pns = k.ring(2, [128, 4, NMEM], BF16)
    xrs = k.ring(2, [128, D], F32)
    xos = k.ring(2, [128, D], F32)
    outs = []
    for blk in range(n // TB):
        s = (blk * TB) // T
        hT = hTs.next()
        for j in range(TB // 128):
            t0 = blk * TB + j * 128
            xt = xts.next()
            k.dma("sp", xt[:], x_in[t0:t0 + 128, :], r=[tok_in], w=[xt])
            xn = nt.norm(xt)
            nt.to_fm(xn, hT[:, :, j * 128:(j + 1) * 128], hT)
        for f in range(8):
            b = ctx.bank()
            for c in range(8):
                k.mm(b[:], Wq[:, c, f * 128:(f + 1) * 128], hT[:, c, :], c == 0, c == 7, r=[Wq, hT], w=[b])
            flip ^= 1
            k.copy(qT[:, f, :], b[:], r=[b], w=[qparts[f]], eng="act" if flip else "dve")
        for j in range(TB // 128):
            cols = slice(j * 128, (j + 1) * 128)
            b2 = [ctx.bank(), ctx.bank()]
            for h in range(4):
                b = b2[h // 2]
                for e in range(2):
                    k.mm(b[:, (h % 2) * 256:(h % 2 + 1) * 256], qT[:, 2 * h + e, cols], KT[:, s, 2 * h + e, :], e == 0, e == 1,
                         r=[qparts[2 * h + e], KT], w=[b])
            mx = mxs.next()
            for i in range(2):
                k.reduce(mx[:, 2 * i:2 * i + 2], b2[i][:].rearrange("p (h m) -> p h m", h=2), ALU.max, r=[b2[i]], w=[mx])
            nmx = nmxs.next()
            k.ts(nmx[:], mx[:], -1.0 / 16.0, None, ALU.mult, r=[mx], w=[nmx])
            p = ps_.next()
            rsum = rsums.next()
            for h in range(4):
                k.act(p[:, h, :], b2[h // 2][:, (h % 2) * 256:(h % 2 + 1) * 256], AF.Exp, r=[b2[h // 2], nmx], w=[p, rsum],
                      bias=nmx[:, h:h + 1], scale=1.0 / 16.0, accum_out=rsum[:, h:h + 1])
            rinv = rinvs.next()
            k.recip(rinv[:], rsum[:], r=[rsum], w=[rinv])
            pn = pns.next()
            k.tt(pn[:], p[:], rinv[:].unsqueeze(2).to_broadcast([128, 4, NMEM]), ALU.mult, r=[p, rinv], w=[pn])
            bank = ctx.bank()
            pb = bank.t[:].bitcast(BF16)
            for h in range(4):
                for e in range(2):
                    i = 2 * h + e
                    k.tr(pb[:, i * 128:(i + 1) * 128], pn[:, h, e * 128:(e + 1) * 128], ctx.identb[:], r=[pn, ctx.identb], w=[bank])
            flip ^= 1
            k.copy(pT[:, :, cols], pb.rearrange("p (c t) -> p c t", c=8), r=[bank], w=[pparts[j]], eng="act" if flip else "dve")
        for f in range(8):
            h = f // 2
            b = ctx.bank()
            for jm in range(2):
                k.mm(b[:], V[:, s, jm, f * 128:(f + 1) * 128], pT[:, 2 * h + jm, :], jm == 0, jm == 1, r=[V] + pparts, w=[b])
            flip ^= 1
            k.copy(oT[:, f, :], b[:], r=[b], w=[oparts[f]], eng="act" if flip else "dve")
        for j in range(TB // 128):
            t0 = blk * TB + j * 128
            cols = slice(j * 128, (j + 1) * 128)
            xr = xrs.next()
            k.dma("sp", xr[:], x_in[t0:t0 + 128, :], r=[tok_in], w=[xr])
            xo = xos.next()
            for half in range(2):
                po = ctx.bank()
                for c in range(8):
                    k.mm(po[:], oT[:, c, cols], Wo[:, c, half * 512:(half + 1) * 512], c == 0, c == 7, r=[oparts[c], Wo], w=[po])
                k.tt(xo[:, half * 512:(half + 1) * 512], po[:], xr[:, half * 512:(half + 1) * 512], ALU.add, r=[po, xr], w=[xo])
            outs.append(k.dma("sp", x_out[t0:t0 + 128, :], xo[:], r=[xo], w=[tok_out]))
    ctx.P.release(mark)
    return outs


FM_ROWS = 1568
TOKW = 2320


def phase_a0(ctx, x_in, w_in, gain_ap, S_fm, S_tok, tok_in, tok_fm, tok_tok):
    k = ctx.k
    n = ctx.n
    TB = 512
    mark = ctx.P.mark()
    g_fm = load_gain_fm(ctx, gain_ap)
    Win = load_w(ctx, w_in, g_fm)
    nt = NormT(ctx)
    xts = k.ring(3, [128, D], F32)
    hTs = k.ring(2, [128, 8, TB], BF16)
    fos = k.ring(3, [128, TB], F32)
    tks = k.ring(2, [128, TOKW], F32)
    fm_cols = [(j * 128, 128) for j in range(8)] + [(2064 + j * 128, 128) for j in range(4)] + [(3600, 32)]
    tok_groups = [(1024, 512, 0), (1536, 512, 512), (2048, 16, 1024), (2320, 256, 1040), (2576, 512, 1296), (3088, 512, 1808)]
    flip = 0
    for blk in range(n // TB):
        hT = hTs.next()
        for j in range(TB // 128):
            t0 = blk * TB + j * 128
            xt = xts.next()
            k.dma("sp", xt[:], x_in[t0:t0 + 128, :], r=[tok_in], w=[xt])
            xn = nt.norm(xt)
            nt.to_fm(xn, hT[:, :, j * 128:(j + 1) * 128], hT)
        for i, (c0, m) in enumerate(fm_cols):
            b = ctx.bank()
            for c in range(8):
                k.mm(b[0:m, :], Win[:, c, c0:c0 + m], hT[:, c, :], c == 0, c == 7, r=[Win, hT], w=[b])
            fo = fos.next()
            flip ^= 1
            k.copy(fo[0:m, :], b[0:m, :], r=[b], w=[fo], eng="act" if flip else "dve")
            r0 = i * 128
            k.dma("sp", S_fm[r0:r0 + m, blk * TB:(blk + 1) * TB], fo[0:m, :], r=[fo], w=[tok_fm])
        for j in range(TB // 128):
            t0 = blk * TB + j * 128
            tk = tks.next()
            for (c0, w, o0) in tok_groups:
                b = ctx.bank()
                for c in range(8):
                    k.mm(b[:, 0:w], hT[:, c, j * 128:(j + 1) * 128], Win[:, c, c0:c0 + w], c == 0, c == 7, r=[Win, hT], w=[b])
                flip ^= 1
                k.copy(tk[:, o0:o0 + w], b[:, 0:w], r=[b], w=[tk], eng="act" if flip else "dve")
            k.dma("sp", S_tok[t0:t0 + 128, :], tk[:], r=[tk], w=[tok_tok])
    ctx.P.release(mark)


class MixL0:
    def __init__(self, ctx, S_fm, S_tok, S_cb, S_sb, conv_ap, igb_ap, fgb_ap, mnorm_ap, w2_ap, db_ap, gnorm_ap, wout_ap,
                 tok_fm, tok_tok):
        self.ctx = ctx
        k = self.k = ctx.k
        self.S_fm, self.S_tok, self.S_cb, self.S_sb = S_fm, S_tok, S_cb, S_sb
        self.tok_fm, self.tok_tok = tok_fm, tok_tok
        self.tok_cb = Buf("cb")
        self.cw = k.tile([128, 3, 8], F32)
        for j in range(3):
            k.dma("sp", self.cw[:, j, :], conv_ap[j].rearrange("(c p) -> p c", p=128), w=[self.cw], allow_slow_non_contiguous=True)
        k.ts(self.cw[:], self.cw[:], 0.5, None, ALU.mult, r=[self.cw], w=[self.cw])
        self.gb = k.tile([128, 16], F32)
        k.dma("sp", self.gb[:, 0:8], igb_ap.rearrange("a b -> (a b)").partition_broadcast(128), w=[self.gb])
        k.dma("sp", self.gb[:, 8:16], fgb_ap.rearrange("a b -> (a b)").partition_broadcast(128), w=[self.gb])
        self.mnorm = k.tile([128, 512], F32)
        k.dma("sp", self.mnorm[:], mnorm_ap.partition_broadcast(128), w=[self.mnorm])
        self.gnorm = k.tile([128, 512], F32)
        k.dma("sp", self.gnorm[:], gnorm_ap.partition_broadcast(128), w=[self.gnorm])
        k.ts(self.mnorm[:], self.mnorm[:], 0.5, None, ALU.mult, r=[self.mnorm], w=[self.mnorm])
        k.ts(self.gnorm[:], self.gnorm[:], 0.5, None, ALU.mult, r=[self.gnorm], w=[self.gnorm])
        self.dbias = k.tile([128, 512], F32)
        k.dma("sp", self.dbias[:], db_ap.rearrange("a b -> (a b)").partition_broadcast(128), w=[self.dbias])
        self.w2p = k.tile([32, 2, 256], F32)
        k.memset(self.w2p[:], 0.0, w=[self.w2p])
        k.dma("sp", self.w2p[0:16, 0, :], w2_ap[0], w=[self.w2p])
        k.dma("sp", self.w2p[16:32, 1, :], w2_ap[1], w=[self.w2p])
        self.Wout = load_w(ctx, wout_ap)
        r = k.ring
        self.Xs = r(2, [128, 8, 130], F32)
        self.z1s = r(2, [128, 8, 128], F32)
        self.z2s = r(2, [128, 8, 128], F32)
        self.QKs = r(2, [128, 8, 128], BF16)
        self.TKs = r(2, [128, TOKW], F32)
        self.vps = r(2, [128, 4, 129], BF16)
        for t in self.vps.tiles:
            k.memset(t[:, :, 128:129], 1.0, w=[t])
        self.g8 = [r(2, [128, 8], F32) for _ in range(8)]
        self.glrs = r(2, [32, 128], F32)
        self.gqks = r(2, [128, 4, 128], F32)
        self.w512 = [r(2, [128, 512], F32) for _ in range(8)]
        self.khats = r(2, [128, 2, 256], BF16)
        self.ktz = r(2, [128, 8, 128], BF16)
        self.qtz = r(2, [128, 8, 128], BF16)
        self.thBs = r(2, [128, 512], F32)
        self.qts = r(2, [128, 512], BF16)
        self.kts = r(2, [128, 512], BF16)
        self.gvbs = r(2, [128, 512], BF16)
        self.Cfb = r(2, [128, 4, 129], BF16)
        self.Cbb = r(2, [128, 4, 129], BF16)
        self.Sfb = r(2, [128, 2, 128], BF16)
        self.Sbb = r(2, [128, 2, 128], BF16)
        for rr_ in (self.ktz, self.qtz):
            for t in rr_.tiles:
                k.memset(t[:], 0.0, w=[t])
        self.kToks = r(2, [128, 4, 128], BF16)
        self.CF = r(2, [128, 4, 129], F32)
        self.CB = r(3, [128, 4, 129], F32)
        self.SF = r(2, [128, 2, 128], F32)
        self.SB = r(3, [128, 2, 128], F32)
        self.pFB = r(2, [128, 8, 128], BF16)
        self.pA = r(2, [128, 8, 128], BF16)
        self.hms = r(2, [128, 4, 128], F32)
        self.small = [r(2, [128, 8], F32) for _ in range(8)]
        self.junks = r(2, [128, 128], F32)
        self.merged = r(2, [128, D], BF16)
        self.mTs = r(2, [128, 8, 128], BF16)
        self.xos = r(2, [128, D], F32)
        self.nt = NormT(ctx, with_xn=False)

    def prep(self, s, c, d_state, light=False):
        ctx, k = self.ctx, self.k
        T = ctx.T
        nch = T // 128
        t0 = s * T + c * 128
        o = {}
        X = self.Xs.next()
        lo = 1 if c == 0 else 0
        hi = 129 if c == nch - 1 else 130
        if c == 0:
            k.memset(X[:, :, 0:1], 0.0, w=[X])
        if c == nch - 1:
            k.memset(X[:, :, 129:130], 0.0, w=[X])
        k.dma("sp", X[:, :, lo:hi], self.S_fm[0:1024, t0 - 1 + lo:t0 - 1 + hi].rearrange("(c p) t -> p c t", p=128),
              r=[self.tok_fm], w=[X])
        yield
        z1 = self.z1s.next()
        z2 = self.z2s.next()
        cs_ = slice(4, 8) if light else slice(0, 8)
        nc_ = 4 if light else 8
        cwb = lambda j: self.cw[:, j, cs_].unsqueeze(2).to_broadcast([128, nc_, 128])
        k.tt(z1[:, cs_, :], X[:, cs_, 0:128], cwb(0), ALU.mult, r=[X, self.cw], w=[z1])
        yield
        k.tt(z2[:, cs_, :], X[:, cs_, 1:129], cwb(1), ALU.mult, r=[X, self.cw], w=[z2])
        yield
        k.tt(z1[:, cs_, :], z1[:, cs_, :], z2[:, cs_, :], ALU.add, r=[z1, z2], w=[z1])
        yield
        k.tt(z2[:, cs_, :], X[:, cs_, 2:130], cwb(2), ALU.mult, r=[X, self.cw], w=[z2])
        yield
        k.tt(z1[:, cs_, :], z1[:, cs_, :], z2[:, cs_, :], ALU.add, r=[z1, z2], w=[z1])
        yield
        k.act(z2[:, cs_, :], z1[:, cs_, :], AF.Tanh, r=[z1], w=[z2])
        yield
        QK = self.QKs.next()
        k.stt(QK[:, cs_, :], z2[:, cs_, :], 1.0, z1[:, cs_, :], ALU.add, ALU.mult, r=[z1, z2], w=[QK])
        yield
        o["QK"] = QK
        TK = self.TKs.next()
        k.dma("sp", TK[:], self.S_tok[t0:t0 + 128, :], r=[self.tok_tok], w=[TK])
        o["TK"] = TK
        vp = self.vps.next()
        k.copy(vp[:, :, 0:128], TK[:, 0:512].rearrange("p (h d) -> p h d", h=4), r=[TK], w=[vp], eng="pool")
        o["vp"] = vp
        if not light:
            thA = self.w512[7].next()
            thB = self.thBs.next()
            k.act(thA[:], TK[:, 512:1024], AF.Tanh, r=[TK], w=[thA], scale=0.5)
            yield
            k.act(thB[:], TK[:, 1808:2320], AF.Tanh, r=[TK], w=[thB], scale=0.5)
            yield
            o["thA"], o["thB"] = thA, thB
        gvb = self.gvbs.next()
        k.copy(gvb[:], TK[:, 1296:1808], r=[TK], w=[gvb], eng="act")
        yield
        o["gvb"] = gvb
        g = [rr.next() for rr in self.g8]
        ig, zf, l1f, t1, sw, qe, eg, kw = g
        k.tt(ig[:], TK[:, 1024:1032], self.gb[:, 0:8], ALU.add, r=[TK, self.gb], w=[ig])
        k.tt(zf[:], TK[:, 1032:1040], self.gb[:, 8:16], ALU.add, r=[TK, self.gb], w=[zf])
        yield
        k.act(zf[:], zf[:], AF.Exp, r=[zf], w=[zf], scale=-1.0)
        yield
        k.act(l1f[:], zf[:], AF.Ln, r=[zf], w=[l1f], bias=1.0)
        yield
        Gb = ctx.bank()
        k.mm(Gb[:, 0:4], ctx.cst("NU"), l1f[:, 0:4], r=[ctx.C, l1f], w=[Gb])
        k.mm(Gb[:, 4:8], ctx.cst("NL"), l1f[:, 4:8], r=[ctx.C, l1f], w=[Gb])
        k.mm(Gb[:, 8:16], ctx.cst("NONES"), l1f[:, 0:8], r=[ctx.C, l1f], w=[Gb])
        k.tt(t1[:], ig[:], Gb[:, 0:8], ALU.subtract, r=[ig, Gb], w=[t1])
        k.act(qe[:], Gb[:, 0:8], AF.Exp, r=[Gb], w=[qe])
        k.act(eg[:], Gb[:, 8:16], AF.Exp, r=[Gb], w=[eg])
        yield
        k.act(sw[:], t1[:], AF.Exp, r=[t1], w=[sw])
        yield
        k.ts(qe[:], qe[:], 128.0 ** -0.5, None, ALU.mult, r=[qe], w=[qe])
        yield
        k.tt(kw[:], sw[:], eg[:], ALU.mult, r=[sw, eg], w=[kw])
        yield
        o.update(sw=sw, qe=qe, eg=eg, kw=kw)
        kTb = ctx.bank()
        kTpb = kTb.t[:].bitcast(BF16)
        for h in range(4):
            k.tr(kTpb[:, h * 128:(h + 1) * 128], QK[:, 4 + h, :], ctx.identb[:], r=[QK, ctx.identb], w=[kTb])
        kTok = self.kToks.next()
        k.tt(kTok[:], kTpb[:, 0:512].rearrange("p (h t) -> p h t", h=4),
             kw[:, d_state * 4:(d_state + 1) * 4].unsqueeze(2).to_broadcast([128, 4, 128]), ALU.mult, r=[kTb, kw], w=[kTok])
        yield
        o["kTok"] = kTok
        glr = self.glrs.next()
        k.dma("sp", glr[:], self.S_fm[1536:1568, t0:t0 + 128], r=[self.tok_fm], w=[glr])
        if not light:
            gqk = self.gqks.next()
            k.dma("sp", gqk[:], self.S_fm[1024:1536, t0:t0 + 128].rearrange("(c p) t -> p c t", p=128), r=[self.tok_fm], w=[gqk])
        w = [rr.next() for rr in self.w512[0:7]]
        zb, l1, eT, emT, _q, _k, egmb = w
        qtT, ktT = (None, None) if light else (self.qts.next(), self.kts.next())
        zbk = ctx.bank()
        for d in range(2):
            k.mm(zbk[:, d * 256:(d + 1) * 256], glr[:], self.w2p[:, d, :], r=[glr, self.w2p], w=[zbk])
        k.tt(zb[:], zbk[:], self.dbias[:], ALU.add, r=[zbk, self.dbias], w=[zb])
        yield
        k.act(zb[:], zb[:], AF.Exp, r=[zb], w=[zb], scale=-1.0)
        yield
        k.act(l1[:], zb[:], AF.Ln, r=[zb], w=[l1], bias=1.0)
        yield
        bTb = ctx.bank()
        dirs = (1,) if light else (0, 1)
        for d in dirs:
            for j in range(2):
                i = d * 2 + j
                k.mm(bTb[:, i * 128:(i + 1) * 128], l1[:, d * 256 + j * 128:d * 256 + (j + 1) * 128],
                     ctx.cst("UN" if d == 0 else "LN"), r=[l1, ctx.C], w=[bTb])
        if light:
            k.act(eT[:, 256:512], bTb[:, 256:512], AF.Exp, r=[bTb], w=[eT])
        else:
            k.act(eT[:], bTb[:], AF.Exp, r=[bTb], w=[eT])
            k.act(emT[:], bTb[:], AF.Exp, r=[bTb], w=[emT], scale=-1.0)
        yield
        v4 = lambda t: t[:].rearrange("p (a b) -> p a b", a=4)
        for d in (() if light else (0, 1)):
            k.stt(v4(qtT)[:, d * 2:(d + 1) * 2, :], gqk[:, 0:2, :], 0.125, v4(eT)[:, d * 2:(d + 1) * 2, :], ALU.mult, ALU.mult,
                  r=[gqk, eT], w=[qtT])
            yield
            k.tt(v4(ktT)[:, d * 2:(d + 1) * 2, :], gqk[:, 2:4, :], v4(emT)[:, d * 2:(d + 1) * 2, :], ALU.mult, r=[gqk, emT], w=[ktT])
            yield
        gmb = ctx.bank()
        if not light:
            k.mm(gmb[:, 0:256], ctx.cst("UC"), l1[:, 0:256], r=[ctx.C, l1], w=[gmb])
        k.mm(gmb[:, 256:512], ctx.cst("LC"), l1[:, 256:512], r=[ctx.C, l1], w=[gmb])
        if light:
            k.act(egmb[:, 256:512], gmb[:, 256:512], AF.Exp, r=[gmb], w=[egmb])
        else:
            k.act(egmb[:], gmb[:], AF.Exp, r=[gmb], w=[egmb])
        yield
        khat = self.khats.next()
        for d in dirs:
            k.tt(khat[:, d, :], TK[:, 1040:1296], egmb[:, d * 256:(d + 1) * 256], ALU.mult, r=[TK, egmb], w=[khat])
            yield
        o.update(eT=eT, qtT=qtT, ktT=ktT, khat=khat)
        return o

    def state_update(self, o, d, Cold, Sold, Cring, Sring):
        ctx, k = self.ctx, self.k
        TK, vp = o["TK"], o["vp"]
        kTok = o["kTok"]
        Cn = Cring.next()
        for p in range(2):
            b = ctx.bank()
            bv = b[:, 0:258].rearrange("p (h e) -> p h e", h=2)
            for hh in range(2):
                h = 2 * p + hh
                k.mm(bv[:, hh, :], kTok[:, h, :], vp[:, h, :], r=[kTok, vp], w=[b])
            for hh in range(2):
                h = 2 * p + hh
                k.stt(Cn[:, h, :], Cold[:, h, :], o["eg"][:, d * 4 + h:d * 4 + h + 1], bv[:, hh, :], ALU.mult, ALU.add,
                      r=[Cold, o["eg"], b], w=[Cn])
            yield
        Sn = Sring.next()
        eT4 = o["eT"][:].rearrange("p (a b) -> p a b", a=4)
        col = 127 if d == 0 else 0
        for j in range(2):
            b = ctx.bank()
            k.mm(b[:, 0:256], o["khat"][:, d, j * 128:(j + 1) * 128], o["gvb"][:, j * 256:(j + 1) * 256], r=[o["khat"], o["gvb"]], w=[b])
            for e in range(2):
                rows = slice(e * 64, (e + 1) * 64)
                k.stt(Sn[rows, j, :], Sold[rows, j, :], eT4[rows, d * 2 + j, col:col + 1], b[rows, e * 128:(e + 1) * 128],
                      ALU.mult, ALU.add, r=[Sold, o["eT"], b], w=[Sn])
            yield
        return Cn, Sn

    def pass1(self, s):
        ctx, k = self.ctx, self.k
        nch = ctx.T // 128
        Cb = self.CB.next()
        Sb = self.SB.next()
        k.memset(Cb[:], 0.0, w=[Cb])
        k.memset(Sb[:], 0.0, w=[Sb])
        for c in range(nch - 1, -1, -1):
            idx = s * nch + c
            k.dma("sp", self.S_cb[idx], Cb[:].rearrange("p h e -> p (h e)"), r=[Cb], w=[self.tok_cb])
            k.dma("sp", self.S_sb[idx], Sb[:].rearrange("p j v -> p (j v)"), r=[Sb], w=[self.tok_cb])
            yield
            if c == 0:
                break
            o = yield from self.prep(s, c, 1, light=True)
            Cb, Sb = yield from self.state_update(o, 1, Cb, Sb, self.CB, self.SB)

    def pass2(self, s, x_in, x_out, tok_in, tok_out, outs):
        ctx, k = self.ctx, self.k
        T = ctx.T
        nch = T // 128
        Cf = self.CF.next()
        Sf = self.SF.next()
        k.memset(Cf[:], 0.0, w=[Cf])
        k.memset(Sf[:], 0.0, w=[Sf])
        Cfb, Sfb = self.Cfb.next(), self.Sfb.next()
        k.memset(Cfb[:], 0.0, w=[Cfb])
        k.memset(Sfb[:], 0.0, w=[Sfb])
        MU, ML = ctx.cst("MU"), ctx.cst("ML")
        import os
        kp2 = int(os.environ.get("KP2", "9"))
        for c in range(nch):
            t0 = s * T + c * 128
            idx = s * nch + c
            o = yield from self.prep(s, c, 0)
            QK, TK, vp = o["QK"], o["TK"], o["vp"]
            Cb = self.CB.next()
            Sb = self.SB.next()
            k.dma("sp", Cb[:].rearrange("p h e -> p (h e)"), self.S_cb[idx], r=[self.tok_cb], w=[Cb])
            k.dma("sp", Sb[:].rearrange("p j v -> p (j v)"), self.S_sb[idx], r=[self.tok_cb], w=[Sb])
            Cbb, Sbb = self.Cbb.next(), self.Sbb.next()
            k.copy(Cbb[:], Cb[:], r=[Cb], w=[Cbb], eng="act")
            k.copy(Sbb[:], Sb[:], r=[Sb], w=[Sbb], eng="pool")
            yield
            sb_ = ctx.bank()
            for h in range(4):
                k.mm(sb_[:, h * 128:(h + 1) * 128], QK[:, 4 + h, :], QK[:, h, :], r=[QK], w=[sb_])
            pFB = self.pFB.next()
            for d in range(2):
                for h in range(4):
                    k.stt(pFB[:, d * 4 + h, :], sb_[:, h * 128:(h + 1) * 128], o["sw"][:, d * 4 + h:d * 4 + h + 1], MU if d == 0 else ML,
                          ALU.mult, ALU.mult, r=[sb_, o["sw"], ctx.C], w=[pFB])
            if kp2 <= 1:
                continue
            yield
            sm = [rr.next() for rr in self.small]
            d1, nd, d2, rr_, ss, rs, ss2, rs2 = sm
            nb = {}
            for p in range(2):
                for d in range(2):
                    b = ctx.bank()
                    bv = b[:, 0:258].rearrange("p (h e) -> p h e", h=2)
                    Cst = Cfb if d == 0 else Cbb
                    for hh in range(2):
                        h = 2 * p + hh
                        k.mm(bv[:, hh, :], pFB[:, d * 4 + h, :], vp[:, h, :], True, False, r=[pFB, vp], w=[b])
                        k.mm(bv[:, hh, :], QK[:, h, :], Cst[:, h, :], False, True, r=[QK, Cst], w=[b])
                    nb[(p, d)] = (b, bv)
                    k.tt(d1[:, d * 4 + 2 * p:d * 4 + 2 * p + 2], bv[:, :, 128], o["qe"][:, d * 4 + 2 * p:d * 4 + 2 * p + 2], ALU.mult,
                         r=[b, o["qe"]], w=[d1])
            k.ts(nd[:], d1[:], -1.0, None, ALU.mult, r=[d1], w=[nd])
            k.tt(d2[:], d1[:], nd[:], ALU.max, r=[d1, nd], w=[d2])
            k.ts(d2[:], d2[:], 1.0, None, ALU.max, r=[d2], w=[d2])
            k.recip(d2[:], d2[:], r=[d2], w=[d2])
            k.tt(rr_[:], d2[:], o["qe"][:], ALU.mult, r=[d2, o["qe"]], w=[rr_])
            hm = self.hms.next()
            for h in range(4):
                p, hh = h // 2, h % 2
                bF, bvF = nb[(p, 0)]
                bB, bvB = nb[(p, 1)]
                k.ts(hm[:, h, :], bvF[:, hh, 0:128], rr_[:, h:h + 1], None, ALU.mult, r=[bF, rr_], w=[hm])
                k.stt(hm[:, h, :], bvB[:, hh, 0:128], rr_[:, 4 + h:5 + h], hm[:, h, :], ALU.mult, ALU.add, r=[bB, rr_, hm], w=[hm])
            yield
            for h in range(4):
                junk = self.junks.next()
                k.act(junk[:], hm[:, h, :], AF.Square, r=[hm], w=[junk, ss], accum_out=ss[:, h:h + 1])
            yield
            k.act(rs[:, 0:4], ss[:, 0:4], AF.Ln, r=[ss], w=[rs], scale=1.0 / 128, bias=EPS)
            yield
            k.act(rs[:, 0:4], rs[:, 0:4], AF.Exp, r=[rs], w=[rs], scale=-0.5)
            yield
            wA = o["thA"]
            k.stt(wA[:], wA[:], 1.0, self.mnorm[:], ALU.add, ALU.mult, r=[wA, self.mnorm], w=[wA])
            yield
            mg = self.merged.next()
            for h in range(4):
                k.stt(mg[:, h * 128:(h + 1) * 128], hm[:, h, :], rs[:, h:h + 1], wA[:, h * 128:(h + 1) * 128], ALU.mult, ALU.mult,
                      r=[hm, rs, wA], w=[mg])
            if kp2 <= 2:
                continue
            yield
            qt4 = o["qtT"][:].rearrange("p (a b) -> p a b", a=4)
            kt4 = o["ktT"][:].rearrange("p (a b) -> p a b", a=4)
            pA = self.pA.next()
            ktz, qtz = self.ktz.next(), self.qtz.next()
            for d in range(2):
                for h in range(4):
                    j, e = h // 2, h % 2
                    rows = slice(e * 64, (e + 1) * 64)
                    k.copy(ktz[rows, d * 4 + h, :], kt4[rows, d * 2 + j, :], r=[o["ktT"]], w=[ktz], eng="pool")
                    k.copy(qtz[rows, d * 4 + h, :], qt4[rows, d * 2 + j, :], r=[o["qtT"]], w=[qtz])
            yield
            for d in range(2):
                b = ctx.bank()
                for h in range(4):
                    j, e = h // 2, h % 2
                    k.mm(b[:, h * 128:(h + 1) * 128], ktz[:, d * 4 + h, :], qt4[:, d * 2 + j, :], r=[ktz, o["qtT"]], w=[b])
                k.tt(pA[:, d * 4:(d + 1) * 4, :], b[:].rearrange("p (h t) -> p h t", h=4),
                     (MU if d == 0 else ML).unsqueeze(1).to_broadcast([128, 4, 128]), ALU.mult, r=[b, ctx.C], w=[pA])
                yield
            if kp2 <= 3:
                continue
            ob = ctx.bank()
            for h in range(4):
                j, e = h // 2, h % 2
                rows = slice(e * 64, (e + 1) * 64)
                gv = o["gvb"][:, h * 128:(h + 1) * 128]
                dst = ob[:, h * 128:(h + 1) * 128]
                k.mm(dst, pA[:, h, :], gv, True, False, r=[pA, o["gvb"]], w=[ob])
                k.mm(dst, pA[:, 4 + h, :], gv, False, False, r=[pA, o["gvb"]], w=[ob])
                k.mm(dst, qtz[:, h, :], Sfb[:, j, :], False, False, r=[qtz, Sfb], w=[ob])
                k.mm(dst, qtz[:, 4 + h, :], Sbb[:, j, :], False, True, r=[qtz, Sbb], w=[ob])
            for h in range(4):
                junk = self.junks.next()
                k.act(junk[:], ob[:, h * 128:(h + 1) * 128], AF.Square, r=[ob], w=[junk, ss2], accum_out=ss2[:, h:h + 1])
            k.act(rs2[:, 0:4], ss2[:, 0:4], AF.Ln, r=[ss2], w=[rs2], scale=1.0 / 128, bias=EPS)
            k.act(rs2[:, 0:4], rs2[:, 0:4], AF.Exp, r=[rs2], w=[rs2], scale=-0.5)
            wB = o["thB"]
            k.stt(wB[:], wB[:], 1.0, TK[:, 1808:2320], ALU.add, ALU.mult, r=[wB, TK], w=[wB])
            k.tt(wB[:], wB[:], self.gnorm[:], ALU.mult, r=[wB, self.gnorm], w=[wB], eng="pool")
            for h in range(4):
                k.stt(mg[:, 512 + h * 128:512 + (h + 1) * 128], ob[:, h * 128:(h + 1) * 128], rs2[:, h:h + 1],
                      wB[:, h * 128:(h + 1) * 128], ALU.mult, ALU.mult, r=[ob, rs2, wB], w=[mg])
            if kp2 <= 4:
                continue
            yield
            if c < nch - 1:
                Cf, Sf = yield from self.state_update(o, 0, Cf, Sf, self.CF, self.SF)
                Cfb, Sfb = self.Cfb.next(), self.Sfb.next()
                k.copy(Cfb[:], Cf[:], r=[Cf], w=[Cfb], eng="act")
                k.copy(Sfb[:], Sf[:], r=[Sf], w=[Sfb], eng="pool")
                yield
            mT = self.mTs.next()
            self.nt.to_fm(mg, mT[:], mT)
            yield
            xo = self.xos.next()
            k.dma("sp", xo[:], x_in[t0:t0 + 128, :], r=[tok_in], w=[xo])
            for half in range(2):
                po = ctx.bank()
                for cc in range(8):
                    k.mm(po[:], mT[:, cc, :], self.Wout[:, cc, half * 512:(half + 1) * 512], cc == 0, cc == 7, r=[mT, self.Wout], w=[po])
                k.tt(xo[:, half * 512:(half + 1) * 512], po[:], xo[:, half * 512:(half + 1) * 512], ALU.add, r=[po, xo], w=[xo])
                yield
            outs.append(k.dma("sp", x_out[t0:t0 + 128, :], xo[:], r=[xo], w=[tok_out]))
            yield


def phase_b0(ctx, x_in, x_out, S_fm, S_tok, S_cb, S_sb, prm, tok_in, tok_fm, tok_tok, tok_out):
    mark = ctx.P.mark()
    mx = MixL0(ctx, S_fm, S_tok, S_cb, S_sb, prm["conv"], prm["igb"], prm["fgb"], prm["mnorm"], prm["w2"], prm["db"],
               prm["gnorm"], prm["wout"], tok_fm, tok_tok)
    outs = []
    import os
    kb0 = int(os.environ.get("KB0", "9"))
    if kb0 == 0:
        return list(ctx.P.dma_last.values())
    def run_all(gens):
        alive = list(gens)
        while alive:
            nxt = []
            for g_ in alive:
                try:
                    next(g_)
                    nxt.append(g_)
                except StopIteration:
                    pass
            alive = nxt
    for s0 in range(0, ctx.nseq, 2):
        ss_ = list(range(s0, min(s0 + 2, ctx.nseq)))
        run_all([mx.pass1(s) for s in ss_])
        run_all([mx.pass2(s, x_in, x_out, tok_in, tok_out, outs) for s in ss_])
    ctx.P.release(mark)
    return outs


C0 = float(np.exp(-0.5))
LCH = 64


def phase_a1(ctx, x_in, prm, S1, S_wt, S_vtok, S_bonus, S_g, tok_in, tok_s1):
    k = ctx.k
    n, T = ctx.n, ctx.T
    TB = 256
    NQ = TB // LCH
    mark = ctx.P.mark()
    g_fm = load_gain_fm(ctx, prm["gain"])
    Wr = load_w(ctx, prm["w_rkv"][0], g_fm)
    Wk = load_w(ctx, prm["w_rkv"][1], g_fm)
    Wv = load_w(ctx, prm["w_rkv"][2], g_fm)
    W1 = k.tile([128, 8, 352], BF16)
    load_w(ctx, prm["w1"][0], None, dst=W1, col0=0)
    load_w(ctx, prm["w1"][1], None, dst=W1, col0=64)
    load_w(ctx, prm["a1"], None, dst=W1, col0=128)
    load_w(ctx, prm["g1"], None, dst=W1, col0=192)
    for c in range(8):
        k.ts(W1[:, c, :], W1[:, c, :], g_fm[:, c:c + 1], None, ALU.mult, r=[W1, g_fm], w=[W1])
    w2t = k.tile([64, 3, D], BF16)
    k.dma("pool", w2t[:, 0, :], prm["w2"][0], w=[w2t])
    k.dma("pool", w2t[:, 1, :], prm["w2"][1], w=[w2t])
    k.dma("pool", w2t[:, 2, :], prm["a2"], w=[w2t])
    g2t = k.tile([128, 2, D], BF16)
    k.dma("pool", g2t[:, 0, :], prm["g2"][0:128, :], w=[g2t])
    k.dma("pool", g2t[0:32, 1, :], prm["g2"][128:160, :], w=[g2t])
    pc = k.tile([128, 7, 8], F32)
    srcs = [prm["w0"][0], prm["w0"][1], prm["a0"], prm["k_k"], prm["k_a"], prm["k_a"], prm["r_k"].rearrange("h d -> (h d)")]
    for i, sap in enumerate(srcs):
        k.dma("sp", pc[:, i, :], sap.rearrange("(c p) -> p c", p=128), w=[pc], allow_slow_non_contiguous=True)
    k.ts(pc[:, 5, :], pc[:, 5, :], -1.0, 1.0, ALU.mult, ALU.add, r=[pc], w=[pc])
    mu = k.tile([128, 6, 8], F32)
    for i in range(6):
        k.dma("sp", mu[:, i, :], prm["mu"][i].rearrange("(c p) -> p c", p=128), w=[mu], allow_slow_non_contiguous=True)
    nt = NormT(ctx, with_xn=False)
    xts = k.ring(2, [128, D], F32)
    xnf = k.ring(1, [128, D], F32)
    xh = k.ring(1, [2, D], F32)
    hTs = k.ring(1, [128, 8, TB + 2], F32)
    hhs = k.ring(1, [128, 8, TB], F32)
    mix_sets = [[k.tile([128, 8, TB], BF16) for _ in range(6)] for _ in range(2)]
    lows = k.ring(2, [128, 5, TB], BF16)
    NWAY = 2
    fsets = [[k.tile([128, TB], F32) for _ in range(13)] for _ in range(NWAY)]
    vts = k.ring(2, [128, D], BF16)
    ob4s = k.ring(5, [128, 4, TB], BF16)
    bg16 = k.ring(4, [128, TB], BF16)
    wtall = k.ring(2, [128, 2, 8, NQ], F32)
    ident = ctx.cst("ident")
    flipb = [0]

    def prologue(blk):
        mixes = mix_sets[blk % 2]
        b0 = blk * TB
        tpos = b0 % T
        hT = hTs.next()
        xhh = xh.next()
        k.memset(xhh[:], 0.0, w=[xhh])
        if tpos > 0:
            k.dma("sp", xhh[0:1, :], x_in[b0 - 1:b0, :], r=[tok_in], w=[xhh])
        if tpos + TB < T:
            k.dma("sp", xhh[1:2, :], x_in[b0 + TB:b0 + TB + 1, :], r=[tok_in], w=[xhh])
        rs = nt.rstd(xhh[:], xhh, rows=2)
        xn2 = xhh
        k.ts(xn2[:], xhh[:], rs[0:2, :], None, ALU.mult, r=[xhh, rs], w=[xn2])
        bk = ctx.bank()
        for c in range(8):
            k.tr(bk[:, c * 2:(c + 1) * 2], xn2[0:2, c * 128:(c + 1) * 128], ident[0:2, 0:2], r=[xn2, ctx.C], w=[bk])
        bkv = bk[:, 0:16].rearrange("p (c e) -> p c e", e=2)
        k.copy(hT[:, :, 0], bkv[:, :, 0], r=[bk], w=[hT])
        k.copy(hT[:, :, TB + 1], bkv[:, :, 1], r=[bk], w=[hT])
        yield
        for j in range(TB // 128):
            t0 = b0 + j * 128
            xt = xts.next()
            k.dma("sp", xt[:], x_in[t0:t0 + 128, :], r=[tok_in], w=[xt])
            rs = nt.rstd(xt[:], xt)
            xn = xnf.next()
            k.ts(xn[:], xt[:], rs[:], None, ALU.mult, r=[xt, rs], w=[xn])
            yield
            for half in range(2):
                bk = ctx.bank()
                for c4 in range(4):
                    c = half * 4 + c4
                    k.tr(bk[:, c4 * 128:(c4 + 1) * 128], xn[:, c * 128:(c + 1) * 128], ident, r=[xn, ctx.C], w=[bk])
                flipb[0] ^= 1
                k.copy(hT[:, half * 4:(half + 1) * 4, 1 + j * 128:1 + (j + 1) * 128], bk[:].rearrange("p (c t) -> p c t", c=4),
                       r=[bk], w=[hT], eng="act" if flipb[0] else "dve")
                yield
        hh = hhs.next()
        k.tt(hh[:], hT[:, :, 0:TB], hT[:, :, 2:TB + 2], ALU.add, r=[hT], w=[hh], eng="pool")
        yield
        k.stt(hh[:], hh[:], 0.5, hT[:, :, 1:TB + 1], ALU.mult, ALU.subtract, r=[hh, hT], w=[hh])
        yield
        for i in range(6):
            for c in range(8):
                k.stt(mixes[i][:, c, :], hh[:, c, :], mu[:, i, c:c + 1], hT[:, c, 1:TB + 1], ALU.mult, ALU.add, r=[hh, mu, hT], w=[mixes[i]])
                yield
        xr, xw, xk, xv, xa, xg = mixes
        low = lows.next()
        specs = [(xw, 0, 64, 0, AF.Tanh), (xw, 64, 64, 1, AF.Tanh), (xa, 128, 64, 2, AF.Copy), (xg, 192, 128, 3, AF.Sigmoid), (xg, 320, 32, 4, AF.Sigmoid)]
        for (src, c0, m, slot, fn) in specs:
            bk = ctx.bank()
            for c in range(8):
                k.mm(bk[0:m, 0:TB], W1[:, c, c0:c0 + m], src[:, c, :], c == 0, c == 7, r=[W1, src], w=[bk])
            k.act(low[0:m, slot, :], bk[0:m, 0:TB], fn, r=[bk], w=[low])
            yield
        for j in range(TB // 128):
            t0 = b0 + j * 128
            vt = vts.next()
            for half in range(2):
                bk = ctx.bank()
                for c in range(8):
                    k.mm(bk[:], xv[:, c, j * 128:(j + 1) * 128], Wv[:, c, half * 512:(half + 1) * 512], c == 0, c == 7, r=[xv, Wv], w=[bk])
                flipb[0] ^= 1
                k.copy(vt[:, half * 512:(half + 1) * 512], bk[:], r=[bk], w=[vt], eng="act" if flipb[0] else "dve")
                yield
            k.dma("sp", S_vtok[t0:t0 + 128, :], vt[:], r=[vt], w=[tok_s1])
            yield
        wta = wtall.next()
        wta_parts = [Buf("wt") for _ in range(16)]
        return (xr, xk, xv, low, b0, blk, wta, wta_parts)

    def fc_chain(fc, F, B):
        xr, xk, xv, low, b0, blk, wta, wta_parts = B
        fs = slice(fc * 128, (fc + 1) * 128)
        r_, k_, v_, a_, kk, k2, t1, t2, sg, G, cI, cE, W = F
        col = lambda i: pc[:, i, fc:fc + 1]

        def proj(Wt, src):
            bk = ctx.bank()
            for c in range(8):
                k.mm(bk[:, 0:TB], Wt[:, c, fs], src[:, c, :], c == 0, c == 7, r=[Wt, src], w=[bk])
            return bk
        bk = proj(Wr, xr)
        k.copy(r_[:], bk[:, 0:TB], r=[bk], w=[r_], eng="act")
        yield
        bk = proj(Wk, xk)
        k.copy(k_[:], bk[:, 0:TB], r=[bk], w=[k_])
        yield
        bk = proj(Wv, xv)
        k.copy(v_[:], bk[:, 0:TB], r=[bk], w=[v_], eng="act")
        yield
        bk = ctx.bank()
        k.mm(bk[:, 0:TB], w2t[:, 2, fs], low[0:64, 2, :], r=[w2t, low], w=[bk])
        k.act(a_[:], bk[:, 0:TB], AF.Sigmoid, r=[bk, pc], w=[a_], bias=col(2))
        yield
        bk = ctx.bank()
        k.mm(bk[:, 0:TB], g2t[:, 0, fs], low[:, 3, :], True, False, r=[g2t, low], w=[bk])
        k.mm(bk[:, 0:TB], g2t[0:32, 1, fs], low[0:32, 4, :], False, True, r=[g2t, low], w=[bk])
        gb16 = bg16.next()
        k.copy(gb16[:], bk[:, 0:TB], r=[bk], w=[gb16])
        k.dma("sp", S_g[blk, :, fc, :], gb16[:], r=[gb16], w=[tok_s1])
        yield
        k.ts(kk[:], k_[:], col(3), None, ALU.mult, r=[k_, pc], w=[kk])
        yield
        k.act(t2[:], kk[:], AF.Square, r=[kk], w=[t2])
        yield
        bk = ctx.bank()
        k.mm(bk[:, 0:TB], ctx.cst("BLK"), t2[:], r=[ctx.C, t2], w=[bk])
        k.ts(t2[:], bk[:, 0:TB], 1e-24, None, ALU.max, r=[bk], w=[t2])
        yield
        k.act(t2[:], t2[:], AF.Ln, r=[t2], w=[t2])
        yield
        k.act(t2[:], t2[:], AF.Exp, r=[t2], w=[t2], scale=-0.5)
        yield
        k.tt(kk[:], kk[:], t2[:], ALU.mult, r=[kk, t2], w=[kk])
        k.ts(k2[:], a_[:], col(4), col(5), ALU.mult, ALU.add, r=[a_, pc], w=[k2])
        yield
        k.tt(k2[:], k2[:], k_[:], ALU.mult, r=[k2, k_], w=[k2])
        yield
        k.stt(t2[:], r_[:], col(6), k2[:], ALU.mult, ALU.mult, r=[r_, pc, k2], w=[t2])
        yield
        bk = ctx.bank()
        k.mm(bk[:, 0:TB], ctx.cst("BLK"), t2[:], r=[ctx.C, t2], w=[bk])
        bb16 = bg16.next()
        k.tt(bb16[:], bk[:, 0:TB], v_[:], ALU.mult, r=[bk, v_], w=[bb16])
        k.dma("sp", S_bonus[blk, :, fc, :], bb16[:], r=[bb16], w=[tok_s1])
        k.tt(a_[:], a_[:], kk[:], ALU.mult, r=[a_, kk], w=[a_], eng="pool")
        yield
        for d in range(2):
            bk = ctx.bank()
            k.mm(bk[:, 0:TB], w2t[:, d, fs], low[0:64, d, :], r=[w2t, low], w=[bk])
            k.act(sg[:], bk[:, 0:TB], AF.Sigmoid, r=[bk, pc], w=[sg], bias=col(d))
            yield
            ctx.P.op("dve", lambda e, G=G, sg=sg: e.tensor_tensor_scan(out=G[:], data0=sg[:], data1=sg[:], initial=0.0, op0=ALU.add, op1=ALU.bypass),
                     _tok([sg]), _tok([G]))
            yield
            G3 = G[:].rearrange("p (q t) -> p q t", t=LCH)
            c3 = cI[:].rearrange("p (q t) -> p q t", t=LCH)
            e3 = cE[:].rearrange("p (q t) -> p q t", t=LCH)
            k.copy(c3[:, 0, :], G3[:, 0, :], r=[G], w=[cI], eng="pool")
            k.tt(c3[:, 1:NQ, :], G3[:, 1:NQ, :], G3[:, 0:NQ - 1, LCH - 1:LCH].to_broadcast([128, NQ - 1, LCH]), ALU.subtract, r=[G], w=[cI])
            yield
            tot = c3[:, :, LCH - 1:LCH]
            k.act(wta[:, d, fc, :], c3[:, :, LCH - 1], AF.Exp, r=[cI], w=[wta_parts[d * 8 + fc]], scale=-C0)
            if d == 0:
                k.tt(cE[:], cI[:], sg[:], ALU.subtract, r=[cI, sg], w=[cE], eng="pool")
                inc, exc = cI, cE
                yield
            else:
                k.tt(e3, tot.to_broadcast([128, NQ, LCH]), c3, ALU.subtract, r=[cI], w=[cE])
                yield
                k.tt(G[:], cE[:], sg[:], ALU.add, r=[cE, sg], w=[G], eng="pool")
                inc, exc = G, cE
                yield
            base = d * 4
            k.act(W[:], inc[:], AF.Exp, r=[inc], w=[W], scale=-C0)
            yield
            ob = ob4s.next()
            k.tt(ob[:, 3, :], r_[:], W[:], ALU.mult, r=[r_, W], w=[ob])
            yield
            k.act(W[:], inc[:], AF.Exp, r=[inc], w=[W], scale=C0)
            yield
            k.tt(ob[:, 2, :], k2[:], W[:], ALU.mult, r=[k2, W], w=[ob])
            k.tt(ob[:, 1, :], a_[:], W[:], ALU.mult, r=[a_, W], w=[ob], eng="pool")
            yield
            k.act(W[:], exc[:], AF.Exp, r=[exc], w=[W], scale=-C0)
            yield
            k.stt(ob[:, 0, :], kk[:], -1.0, W[:], ALU.mult, ALU.mult, r=[kk, W], w=[ob])
            k.dma("sp", S1[base:base + 4, fs, b0:b0 + TB].rearrange("a q t -> q a t"), ob[:], r=[ob], w=[tok_s1])
            yield

    def step(g_):
        try:
            next(g_)
            return True, None
        except StopIteration as e_:
            return False, e_.value

    nblk = n // TB
    pro = prologue(0)
    while True:
        ok, val = step(pro)
        if not ok:
            Bcur = val
            break
    for blk in range(nblk):
        pro = prologue(blk + 1) if blk + 1 < nblk else None
        Bnext = None
        for g0 in range(0, 8, NWAY):
            alive = [fc_chain(g0 + i_, fsets[i_], Bcur) for i_ in range(NWAY)]
            while alive:
                if pro is not None:
                    ok, val = step(pro)
                    if not ok:
                        Bnext, pro = val, None
                alive = [g_ for g_ in alive if step(g_)[0]]
        while pro is not None:
            ok, val = step(pro)
            if not ok:
                Bnext, pro = val, None
        wta, wta_parts = Bcur[6], Bcur[7]
        for d in range(2):
            k.dma("sp", S_wt[d].rearrange("(p q) c -> q p c", q=128)[:, :, blk * NQ:(blk + 1) * NQ], wta[:, d, :, :],
                  r=wta_parts[d * 8:(d + 1) * 8], w=[tok_s1], allow_slow_non_contiguous=True)
        Bcur = Bnext
    ctx.P.release(mark)


def phase_b1(ctx, x_in, x_out, prm, S1, S_wt, S_vtok, S_bonus, S_g, S_yb, tok_in, tok_s1, tok_out):
    k = ctx.k
    n, T = ctx.n, ctx.T
    NCH = T // LCH
    mark = ctx.P.mark()
    Wo = load_w(ctx, prm["w_o"])
    lnw = k.tile([128, 8], F32)
    lnb = k.tile([128, 8], F32)
    k.dma("sp", lnw[:], prm["ln_w"].rearrange("(c p) -> p c", p=128), w=[lnw], allow_slow_non_contiguous=True)
    k.dma("sp", lnb[:], prm["ln_b"].rearrange("(c p) -> p c", p=128), w=[lnb], allow_slow_non_contiguous=True)
    tok_yb = Buf("yb")

    def bdring(nslots):
        rr = k.ring(nslots, [128, 8, 128], F32)
        for t in rr.tiles:
            k.memset(t[:], 0.0, w=[t])
        return rr
    ATs, BTs, KTs, Vbs = bdring(2), bdring(2), bdring(2), bdring(2)
    RTs = k.ring(2, [128, 8, LCH], F32)
    wts = k.ring(2, [128, 8], F32)
    big = lambda nslots: k.ring(nslots, [128, 8, 128], F32)
    Ns, NTs, Ps = big(2), big(2), big(2)
    AKs, Xs, Us, Bts, Kts = big(1), big(1), big(1), big(1), big(1)
    Ms = big(2)
    tmpM = big(1)
    RBs = k.ring(1, [128, 8, LCH], F32)
    RKs = k.ring(1, [128, 8, LCH], F32)
    ysb = k.ring(2, [128, 8, LCH], F32)
    ybl = k.ring(2, [128, 8, LCH], F32)
    o512 = [k.ring(2, [128, 8, LCH], F32) for _ in range(5)]
    zTs = k.ring(2, [128, 8, LCH], BF16)
    xrs = k.ring(2, [LCH, D], F32)
    xos = k.ring(2, [LCH, D], F32)
    ident = ctx.cst("ident")
    outs = []
    flip = [0]

    def bd_src(ap2d, t0):
        v = ap2d.rearrange("(p e k) t -> e k p t", e=2, k=64)
        return [v[e][:, :, t0:t0 + LCH] for e in range(2)]

    def evac(dst_ap, src_ap, r, w):
        flip[0] ^= 1
        k.copy(dst_ap, src_ap, r=r, w=w, eng="act" if flip[0] else "dve")

    def pairs_mm(lhs_tile, rhs_tile, width=128, lhs2=None, rhs2=None):
        per_bank = 512 // width
        res = []
        for b0 in range(0, 8, per_bank):
            bk = ctx.bank()
            for p in range(b0, b0 + per_bank):
                o = bk[:, (p - b0) * width:(p - b0 + 1) * width]
                k.mm(o, lhs_tile[:, p, :], rhs_tile[:, p, :], True, lhs2 is None, r=[lhs_tile, rhs_tile], w=[bk])
                if lhs2 is not None:
                    k.mm(o, lhs2[:, p, :], rhs2[:, p, :], False, True, r=[lhs2, rhs2], w=[bk])
            res.append((bk, bk[:].rearrange("p (a b) -> p a b", b=width), b0, per_bank))
        return res

    for s in range(ctx.nseq):
        for d in (1, 0):
            base = d * 4
            Mst = Ms.next()
            k.memset(Mst[:], 0.0, w=[Mst])
            strict = ctx.cst("SF" if d == 0 else "SB")
            strictT = ctx.cst("SB" if d == 0 else "SF")
            incl = ctx.cst("IF" if d == 0 else "IB")[:, 0:LCH]
            order = range(NCH) if d == 0 else range(NCH - 1, -1, -1)
            for c in order:
                t0 = s * T + c * LCH
                cg = t0 // LCH
                AT, BT, KT, Vb = ATs.next(), BTs.next(), KTs.next(), Vbs.next()
                for e in range(2):
                    rows = slice(e * 64, (e + 1) * 64)
                    k.dma("sp", AT[rows, :, rows], bd_src(S1[base + 0], t0)[e], r=[tok_s1], w=[AT])
                    k.dma("sp", BT[rows, :, rows], bd_src(S1[base + 1], t0)[e], r=[tok_s1], w=[BT])
                    k.dma("sp", KT[rows, :, rows], bd_src(S1[base + 2], t0)[e], r=[tok_s1], w=[KT])
                    k.dma("sp", Vb[rows, :, rows], S_vtok[t0:t0 + LCH, :].rearrange("t (p e v) -> e t p v", e=2, v=64)[e],
                          r=[tok_s1], w=[Vb])
                RT = RTs.next()
                k.dma("sp", RT[:], S1[base + 3].rearrange("(p q) t -> q p t", q=128)[:, :, t0:t0 + LCH], r=[tok_s1], w=[RT])
                wt = wts.next()
                k.dma("sp", wt[:], S_wt[d].rearrange("(p q) c -> q p c", q=128)[:, :, cg], r=[tok_s1], w=[wt], allow_slow_non_contiguous=True)
                N, NT, P_ = Ns.next(), NTs.next(), Ps.next()
                AK = AKs.next()
                for (bk, v, b0, nb) in pairs_mm(BT, AT):
                    k.tt(N[:, b0:b0 + nb, :], v, strict.unsqueeze(1).to_broadcast([128, nb, 128]), ALU.mult, r=[bk, ctx.C], w=[N])
                for (bk, v, b0, nb) in pairs_mm(AT, BT):
                    k.tt(NT[:, b0:b0 + nb, :], v, strictT.unsqueeze(1).to_broadcast([128, nb, 128]), ALU.mult, r=[bk, ctx.C], w=[NT])
                for (bk, v, b0, nb) in pairs_mm(KT, AT):
                    k.tt(AK[:, b0:b0 + nb, :], v, strict.unsqueeze(1).to_broadcast([128, nb, 128]), ALU.mult, r=[bk, ctx.C], w=[AK])
                RB, RK = RBs.next(), RKs.next()
                for (bk, v, b0, nb) in pairs_mm(BT, RT, width=LCH):
                    k.tt(RB[:, b0:b0 + nb, :], v, incl.unsqueeze(1).to_broadcast([128, nb, LCH]), ALU.mult, r=[bk, ctx.C], w=[RB])
                for (bk, v, b0, nb) in pairs_mm(KT, RT, width=LCH):
                    k.tt(RK[:, b0:b0 + nb, :], v, incl.unsqueeze(1).to_broadcast([128, nb, LCH]), ALU.mult, r=[bk, ctx.C], w=[RK])
                k.tt(P_[:], N[:], ident.unsqueeze(1).to_broadcast([128, 8, 128]), ALU.add, r=[N, ctx.C], w=[P_], eng="pool")
                for lvl in range(5):
                    last = lvl == 4
                    N2 = None if last else Ns.next()
                    NT2 = NTs.next()
                    if not last:
                        for (bk, v, b0, nb) in pairs_mm(NT, N):
                            evac(N2[:, b0:b0 + nb, :], v, [bk], [N2])
                    for (bk, v, b0, nb) in pairs_mm(N, NT):
                        evac(NT2[:, b0:b0 + nb, :], v, [bk], [NT2])
                    P2 = Ps.next()
                    for (bk, v, b0, nb) in pairs_mm(NT2, P_):
                        k.tt(P2[:, b0:b0 + nb, :], v, P_[:, b0:b0 + nb, :], ALU.add, r=[bk, P_], w=[P2])
                    N, NT, P_ = N2, NT2, P2
                X = Xs.next()
                for (bk, v, b0, nb) in pairs_mm(AT, Mst, lhs2=AK, rhs2=Vb):
                    evac(X[:, b0:b0 + nb, :], v, [bk], [X])
                U = Us.next()
                for (bk, v, b0, nb) in pairs_mm(P_, X):
                    evac(U[:, b0:b0 + nb, :], v, [bk], [U])
                yb = ctx.bank()
                for p in range(8):
                    o = yb[:, p * LCH:(p + 1) * LCH]
                    k.mm(o, Mst[:, p, :], RT[:, p, :], True, False, r=[Mst, RT], w=[yb])
                    k.mm(o, U[:, p, :], RB[:, p, :], False, False, r=[U, RB], w=[yb])
                    k.mm(o, Vb[:, p, :], RK[:, p, :], False, True, r=[Vb, RK], w=[yb])
                yv = yb[:].rearrange("p (a b) -> p a b", b=LCH)
                if d == 1:
                    ys = ysb.next()
                    k.copy(ys[:], yv, r=[yb], w=[ys])
                    k.dma("sp", S_yb.rearrange("(p q) t -> q p t", q=128)[:, :, t0:t0 + LCH], ys[:], r=[ys], w=[tok_yb])
                if c != order[-1]:
                    Bt, Kt = Bts.next(), Kts.next()
                    for (src, dst) in ((BT, Bt), (KT, Kt)):
                        for b0 in (0, 4):
                            bk = ctx.bank()
                            for p in range(b0, b0 + 4):
                                k.tr(bk[:, (p - b0) * 128:(p - b0 + 1) * 128], src[:, p, :], ident, r=[src, ctx.C], w=[bk])
                            evac(dst[:, b0:b0 + 4, :], bk[:].rearrange("p (a b) -> p a b", b=128), [bk], [dst])
                    Mn = Ms.next()
                    tm = tmpM.next()
                    for (bk, v, b0, nb) in pairs_mm(Bt, U, lhs2=Kt, rhs2=Vb):
                        k.tt(tm[:, b0:b0 + nb, :], v, Mst[:, b0:b0 + nb, :], ALU.add, r=[bk, Mst], w=[tm])
                        k.tt(Mn[:, b0:b0 + nb, :], tm[:, b0:b0 + nb, :], wt[:, b0:b0 + nb].unsqueeze(2).to_broadcast([128, nb, 128]), ALU.mult,
                             r=[tm, wt], w=[Mn], eng="pool")
                    Mst = Mn
                if d == 0:
                    ybt = ybl.next()
                    k.dma("sp", ybt[:], S_yb.rearrange("(p q) t -> q p t", q=128)[:, :, t0:t0 + LCH], r=[tok_yb], w=[ybt])
                    bon = o512[0].next()
                    gt = o512[1].next()
                    k.dma("sp", bon[:], S_bonus.rearrange("(p q) t -> q p t", q=128)[:, :, t0:t0 + LCH], r=[tok_s1], w=[bon])
                    k.dma("sp", gt[:], S_g.rearrange("(p q) t -> q p t", q=128)[:, :, t0:t0 + LCH], r=[tok_s1], w=[gt])
                    ysum, ysq, t3 = o512[2].next(), o512[3].next(), o512[4].next()
                    k.tt(ysum[:], yv, ybt[:], ALU.add, r=[yb, ybt], w=[ysum])
                    k.act(ysq[:], ysum[:], AF.Square, r=[ysum], w=[ysq])
                    f2 = lambda t: t[:].rearrange("p a b -> p (a b)")
                    mb, qb = ctx.bank(), ctx.bank()
                    k.mm(mb[:], ctx.cst("BLKM"), f2(ysum), r=[ctx.C, ysum], w=[mb])
                    k.mm(qb[:], ctx.cst("BLKM"), f2(ysq), r=[ctx.C, ysq], w=[qb])
                    k.act(f2(ysq), mb[:], AF.Square, r=[mb], w=[ysq])
                    k.tt(f2(ysq), qb[:], f2(ysq), ALU.subtract, r=[qb, ysq], w=[ysq])
                    k.ts(f2(ysq), f2(ysq), 64e-5, None, ALU.add, r=[ysq], w=[ysq], eng="pool")
                    k.act(f2(ysq), f2(ysq), AF.Ln, r=[ysq], w=[ysq])
                    k.act(f2(ysq), f2(ysq), AF.Exp, r=[ysq], w=[ysq], scale=-0.5)
                    k.tt(f2(t3), f2(ysum), mb[:], ALU.subtract, r=[ysum, mb], w=[t3])
                    k.tt(t3[:], t3[:], ysq[:], ALU.mult, r=[t3, ysq], w=[t3])
                    k.tt(t3[:], t3[:], lnw[:].unsqueeze(2).to_broadcast([128, 8, LCH]), ALU.mult, r=[t3, lnw], w=[t3], eng="pool")
                    k.tt(t3[:], t3[:], lnb[:].unsqueeze(2).to_broadcast([128, 8, LCH]), ALU.add, r=[t3, lnb], w=[t3], eng="pool")
                    k.tt(t3[:], t3[:], bon[:], ALU.add, r=[t3, bon], w=[t3])
                    zT = zTs.next()
                    k.tt(zT[:], t3[:], gt[:], ALU.mult, r=[t3, gt], w=[zT])
                    xr = xrs.next()
                    k.dma("sp", xr[:], x_in[t0:t0 + LCH, :], r=[tok_in], w=[xr])
                    xo = xos.next()
                    for half in range(2):
                        po = ctx.bank()
                        for cc in range(8):
                            k.mm(po[0:LCH, :], zT[:, cc, :], Wo[:, cc, half * 512:(half + 1) * 512], cc == 0, cc == 7, r=[zT, Wo], w=[po])
                        k.tt(xo[:, half * 512:(half + 1) * 512], po[0:LCH, :], xr[:, half * 512:(half + 1) * 512], ALU.add, r=[po, xr], w=[xo])
                    outs.append(k.dma("sp", x_out[t0:t0 + LCH, :], xo[:], r=[xo], w=[tok_out]))
    ctx.P.release(mark)
    return outs


PARAM_SHAPES = None


def build_program(T, nseq, shapes):
    import os
    nc = bass.Bass("TRN2", target_bir_lowering=False)
    n = T * nseq
    carr, _ = build_consts()

    def din(name, shape):
        return nc.dram_tensor(name, list(shape), F32, kind="ExternalInput").ap()

    def dint(name, shape, dt=F32):
        return nc.dram_tensor(name, list(shape), dt, kind=os.environ.get("KSCR", "Internal")).ap()

    class _Sl:
        def __init__(self, lst):
            self.lst = lst

        def __getitem__(self, i):
            return self.lst[i]
    A = {}
    for name, shp in shapes.items():
        if name in ("x", "mem", "norm_final"):
            A[name] = din(name, shp)
        else:
            A[name] = _Sl([din(f"{name}_{i}", shp[1:]) for i in range(shp[0])])
    cst = din("consts", carr.shape)
    out = nc.dram_tensor("out", [n, D], F32, kind="ExternalOutput").ap()
    xa, xb = dint("xa", (n, D)), dint("xb", (n, D))
    S_fm, S_tok = dint("S_fm", (FM_ROWS, n)), dint("S_tok", (n, TOKW))
    nch = n // 128
    S_cb, S_sb = dint("S_cb", (nch, 128, 516)), dint("S_sb", (nch, 128, 256))
    S1, S_wt = dint("S1", (8, D, n), BF16), dint("S_wt", (2, D, n // LCH))
    S_vtok, S_bonus, S_g, S_yb = (dint("S_vtok", (n, D), BF16), dint("S_bonus", (n // 256, 128, 8, 256), BF16),
                                    dint("S_g", (n // 256, 128, 8, 256), BF16), dint("S_yb", (2, n // 256, 128, 8, 256), BF16))
    P = Prog(nc)
    ctx = Ctx(nc, P, T, nseq, cst)
    tx = Buf("x")
    ta, tb_ = Buf("xa"), Buf("xb")
    import os
    nph = int(os.environ.get("KPH", "99"))

    def finish(outs_):
        P.emit(final_waits=outs_)
        P.close()
        return nc, carr
    tfm, ttok = Buf("fm"), Buf("tok")
    phase_a0(ctx, A["x"], A["ev_w_in"][0], A["norm_mix"][0], S_fm, S_tok, tx, tfm, ttok)
    if nph <= 0:
        return finish(list(P.dma_last.values()))
    prm0 = dict(conv=A["ev_conv_qk"][0], igb=A["ev_m_ig_bias"][0], fgb=A["ev_m_fg_bias"][0], mnorm=A["ev_m_norm"][0],
                w2=A["ev_g_decay_w2"][0], db=A["ev_g_decay_b"][0], gnorm=A["ev_g_norm"][0], wout=A["ev_w_out"][0])
    o_ = phase_b0(ctx, A["x"], out if nph <= 1 else xa, S_fm, S_tok, S_cb, S_sb, prm0, tx, tfm, ttok, ta)
    if nph <= 1:
        return finish(o_)
    mem2 = A["mem"]
    o_ = phase_xattn(ctx, xa, out if nph <= 2 else xb, mem2, A["xa_wq"][0], A["xa_wkv"][0], A["xa_wo"][0], A["norm_xattn"][0], A["norm_mem"][0], ta, tb_)
    if nph <= 2:
        return finish(o_)
    o_ = phase_ffn(ctx, xb, out if nph <= 3 else xa, A["ffn_w_gate"][0], A["ffn_w_up"][0], A["ffn_w_down"][0], A["norm_ffn"][0], tb_, ta)
    if nph <= 3:
        return finish(o_)
    prm1 = dict(gain=A["norm_mix"][1], w_rkv=A["od_w_rkv"][0], w0=A["od_w0"][0], w1=A["od_w1"][0], w2=A["od_w2"][0], a0=A["od_a0"][0],
                a1=A["od_a1"][0], a2=A["od_a2"][0], g1=A["od_g1"][0], g2=A["od_g2"][0], k_k=A["od_k_k"][0], k_a=A["od_k_a"][0],
                r_k=A["od_r_k"][0], mu=A["od_mu"][0], ln_w=A["od_ln_w"][0], ln_b=A["od_ln_b"][0], w_o=A["od_w_o"][0])
    ts1 = Buf("s1")
    phase_a1(ctx, xa, prm1, S1, S_wt, S_vtok, S_bonus, S_g, ta, ts1)
    ty = Buf("y")
    phase_b1v2(ctx, prm1, S1, S_wt, S_vtok, S_yb, ts1, ty)
    o_ = phase_c1(ctx, xa, out if nph <= 4 else xb, prm1, S_yb, S_bonus, S_g, ta, ts1, ty, tb_)
    if nph <= 4:
        return finish(o_)
    phase_xattn(ctx, xb, xa, mem2, A["xa_wq"][1], A["xa_wkv"][1], A["xa_wo"][1], A["norm_xattn"][1], A["norm_mem"][1], tb_, ta)
    tout = Buf("out")
    outs = phase_ffn(ctx, xa, out, A["ffn_w_gate"][1], A["ffn_w_up"][1], A["ffn_w_down"][1], A["norm_ffn"][1], ta, tout,
                     final_gain_ap=A["norm_final"])
    P.emit(final_waits=outs)
    P.close()
    return nc, carr


_CACHE = {}


def kernel(**inputs):
    import os
    ncores = int(os.environ.get("KNC", "8"))
    x = np.asarray(inputs["x"], np.float32)
    B, T, _ = x.shape
    nseq = B // ncores
    shapes = {}
    per_core = []
    for name, v in inputs.items():
        v = np.ascontiguousarray(np.asarray(v, np.float32))
        if name == "x":
            shapes[name] = (nseq * T, D)
        elif name == "mem":
            shapes[name] = (nseq * NMEM, D)
        else:
            shapes[name] = v.shape
    key = (T, nseq)
    if key not in _CACHE:
        _CACHE[key] = build_program(T, nseq, shapes)
    nc, carr = _CACHE[key]
    in_maps = []
    for c in range(ncores):
        m = {"consts": carr}
        for name, v in inputs.items():
            v = np.ascontiguousarray(np.asarray(v, np.float32))
            if name == "x":
                m[name] = np.ascontiguousarray(v[c * nseq:(c + 1) * nseq].reshape(nseq * T, D))
            elif name == "mem":
                m[name] = np.ascontiguousarray(v[c * nseq:(c + 1) * nseq].reshape(nseq * NMEM, D))
            elif name == "norm_final":
                m[name] = v
            else:
                for i in range(v.shape[0]):
                    m[f"{name}_{i}"] = np.ascontiguousarray(v[i])
        in_maps.append(m)
    res = run_bass_kernel_spmd(nc, in_maps, core_ids=list(range(ncores)))
    outs = [np.asarray(r["out"], np.float32).reshape(nseq, T, D) for r in res.results]
    return np.concatenate(outs, axis=0)


def phase_b1v2(ctx, prm, S1, S_wt, S_vtok, S_y, tok_s1, tok_y, nchains=2):
    k = ctx.k
    n, T = ctx.n, ctx.T
    NCH = T // LCH
    mark = ctx.P.mark()
    ident = ctx.cst("ident")
    flip = [0]

    def evac(dst_ap, src_ap, r, w):
        flip[0] = (flip[0] + 1) % 4
        k.copy(dst_ap, src_ap, r=r, w=w, eng="dve" if flip[0] == 0 else "act")

    def pairs_mm(lhs_tile, rhs_tile, width=128, lhs2=None, rhs2=None):
        per_bank = 512 // width
        res = []
        for b0 in range(0, 8, per_bank):
            bk = ctx.bank()
            for p in range(b0, b0 + per_bank):
                o = bk[:, (p - b0) * width:(p - b0 + 1) * width]
                k.mm(o, lhs_tile[:, p, :], rhs_tile[:, p, :], True, lhs2 is None, r=[lhs_tile, rhs_tile], w=[bk])
                if lhs2 is not None:
                    k.mm(o, lhs2[:, p, :], rhs2[:, p, :], False, True, r=[lhs2, rhs2], w=[bk])
            res.append((bk, bk[:].rearrange("p (a b) -> p a b", b=width), b0, per_bank))
        return res

    def bd_src(ap2d, t0):
        v = ap2d.rearrange("(p e k) t -> e k p t", e=2, k=64)
        return [v[e][:, :, t0:t0 + LCH] for e in range(2)]

    class Work:
        def __init__(self):
            big = lambda: k.tile([128, 8, 128], BF16)
            big32 = lambda: k.tile([128, 8, 128], F32)
            self.AT, self.BT, self.KT, self.Vb = big(), big(), big(), big()
            for t in (self.AT, self.BT, self.KT, self.Vb):
                k.memset(t[:], 0.0, w=[t])
            self.RT = k.tile([128, 8, LCH], BF16)
            self.wt = k.tile([128, 8, NCH], F32)
            self.St = [k.tile([128, 32, 256], BF16), k.tile([128, 32, 256], BF16)]
            self.Vt = k.tile([128, D], BF16)
            self.N = [big(), big()]
            self.NT = [big(), big()]
            self.P = [big(), big()]
            self.AK, self.X, self.U, self.Bt, self.Kt = big(), big(), big(), big(), big()
            self.M = [big32(), big32()]
            self.Mb = [big(), big()]
            self.RB = k.tile([128, 8, LCH], BF16)
            self.RK = k.tile([128, 8, LCH], BF16)
            self.ys = k.tile([128, 8, LCH], BF16)

    works = [Work() for _ in range(nchains)]

    def chain(W, s, d):
        base = d * 4
        mi = 0
        Mst = W.M[mi]
        Mb = W.Mb[mi]
        k.memset(Mst[:], 0.0, w=[Mst])
        k.memset(Mb[:], 0.0, w=[Mb])
        strict = ctx.cst("SF" if d == 0 else "SB")
        strictT = ctx.cst("SB" if d == 0 else "SF")
        incl = ctx.cst("IF" if d == 0 else "IB")[:, 0:LCH]
        order = list(range(NCH)) if d == 0 else list(range(NCH - 1, -1, -1))
        S1flat = S1[base:base + 4].rearrange("a r t -> (a r) t")
        cg0 = (s * T) // LCH
        k.dma("sp", W.wt[:], S_wt[d].rearrange("(p q) c -> q p c", q=128)[:, :, cg0:cg0 + NCH], r=[tok_s1], w=[W.wt],
              allow_slow_non_contiguous=True)
        groups = []
        for c in order:
            if not groups or groups[-1] != c // 4:
                groups.append(c // 4)

        def load_group(gi):
            g = groups[gi]
            St = W.St[gi % 2]
            tg = s * T + g * 256
            k.dma("sp", St[:], S1flat[:, tg:tg + 256].rearrange("(ap q) t -> q ap t", q=128), r=[tok_s1], w=[St])
        load_group(0)
        for c in order:
            t0 = s * T + c * LCH
            gi = groups.index(c // 4)
            if c // 4 != (order[order.index(c) - 1] // 4 if order.index(c) > 0 else -1) and gi + 1 < len(groups):
                load_group(gi + 1)
            St = W.St[gi % 2]
            off = (c % 4) * LCH
            AT, BT, KT, Vb, RT = W.AT, W.BT, W.KT, W.Vb, W.RT
            wt = W.wt
            for e in range(2):
                rows = slice(e * 64, (e + 1) * 64)
                k.dma("sp", W.Vt[rows, :], S_vtok[t0:t0 + LCH, :], r=[tok_s1], w=[W.Vt])
            for ai, dstt in ((0, AT), (1, BT), (2, KT)):
                for e in range(2):
                    rows = slice(e * 64, (e + 1) * 64)
                    k.copy(dstt[rows, :, rows], St[rows, ai * 8:(ai + 1) * 8, off:off + LCH], r=[St], w=[dstt], eng="pool" if e == 0 else "act")
            k.copy(RT[:], St[:, 24:32, off:off + LCH], r=[St], w=[RT], eng="pool")
            for e in range(2):
                rows = slice(e * 64, (e + 1) * 64)
                k.copy(Vb[rows, :, rows], W.Vt[rows, :].rearrange("t (p e v) -> t p e v", e=2, v=64)[:, :, e, :], r=[W.Vt], w=[Vb],
                       eng="pool" if e == 0 else "act")
            yield
            ni = 0
            N, NT, P_ = W.N[0], W.NT[0], W.P[0]
            AK, RB, RK = W.AK, W.RB, W.RK
            for (bk, v, b0, nb) in pairs_mm(BT, AT):
                k.tt(N[:, b0:b0 + nb, :], v, strict.unsqueeze(1).to_broadcast([128, nb, 128]), ALU.mult, r=[bk, ctx.C], w=[N])
            for (bk, v, b0, nb) in pairs_mm(AT, BT):
                k.tt(NT[:, b0:b0 + nb, :], v, strictT.unsqueeze(1).to_broadcast([128, nb, 128]), ALU.mult, r=[bk, ctx.C], w=[NT])
            k.tt(P_[:], N[:], ident.unsqueeze(1).to_broadcast([128, 8, 128]), ALU.add, r=[N, ctx.C], w=[P_], eng="pool")
            yield
            for (bk, v, b0, nb) in pairs_mm(KT, AT):
                k.tt(AK[:, b0:b0 + nb, :], v, strict.unsqueeze(1).to_broadcast([128, nb, 128]), ALU.mult, r=[bk, ctx.C], w=[AK])
            for (bk, v, b0, nb) in pairs_mm(BT, RT, width=LCH):
                k.tt(RB[:, b0:b0 + nb, :], v, incl.unsqueeze(1).to_broadcast([128, nb, LCH]), ALU.mult, r=[bk, ctx.C], w=[RB])
            for (bk, v, b0, nb) in pairs_mm(KT, RT, width=LCH):
                k.tt(RK[:, b0:b0 + nb, :], v, incl.unsqueeze(1).to_broadcast([128, nb, LCH]), ALU.mult, r=[bk, ctx.C], w=[RK])
            yield
            last_c = c == order[-1]
            if not last_c:
                for (src, dst) in ((BT, W.Bt), (KT, W.Kt)):
                    bk = ctx.bank()
                    pb = bk.t[:].bitcast(BF16)
                    for p in range(8):
                        k.tr(pb[:, p * 128:(p + 1) * 128], src[:, p, :], ctx.identb[:], r=[src, ctx.identb], w=[bk])
                    evac(dst[:], pb.rearrange("p (a b) -> p a b", b=128), [bk], [dst])
                yield
            for lvl in range(5):
                last = lvl == 4
                N2 = None if last else W.N[1 - ni]
                NT2 = W.NT[1 - ni]
                if not last:
                    for (bk, v, b0, nb) in pairs_mm(NT, N):
                        evac(N2[:, b0:b0 + nb, :], v, [bk], [N2])
                for (bk, v, b0, nb) in pairs_mm(N, NT):
                    evac(NT2[:, b0:b0 + nb, :], v, [bk], [NT2])
                yield
                P2 = W.P[1 - ni]
                for (bk, v, b0, nb) in pairs_mm(NT2, P_):
                    k.tt(P2[:, b0:b0 + nb, :], v, P_[:, b0:b0 + nb, :], ALU.add, r=[bk, P_], w=[P2])
                N, NT, P_ = N2, NT2, P2
                ni = 1 - ni
                yield
            X, U = W.X, W.U
            for (bk, v, b0, nb) in pairs_mm(AT, Mb, lhs2=AK, rhs2=Vb):
                evac(X[:, b0:b0 + nb, :], v, [bk], [X])
            yield
            for (bk, v, b0, nb) in pairs_mm(P_, X):
                evac(U[:, b0:b0 + nb, :], v, [bk], [U])
            yield
            yb = ctx.bank()
            for p in range(8):
                o = yb[:, p * LCH:(p + 1) * LCH]
                k.mm(o, Mb[:, p, :], RT[:, p, :], True, False, r=[Mb, RT], w=[yb])
                k.mm(o, U[:, p, :], RB[:, p, :], False, False, r=[U, RB], w=[yb])
                k.mm(o, Vb[:, p, :], RK[:, p, :], False, True, r=[Vb, RK], w=[yb])
            evac(W.ys[:], yb[:].rearrange("p (a b) -> p a b", b=LCH), [yb], [W.ys])
            k.dma("sp", S_y[d, t0 // 256, :, :, t0 % 256:t0 % 256 + LCH], W.ys[:], r=[W.ys], w=[tok_y])
            if not last_c:
                Mn = W.M[1 - mi]
                for (bk, v, b0, nb) in pairs_mm(W.Bt, U, lhs2=W.Kt, rhs2=Vb):
                    k.tt(Mn[:, b0:b0 + nb, :], v, Mst[:, b0:b0 + nb, :], ALU.add, r=[bk, Mst], w=[Mn])
                    k.tt(Mn[:, b0:b0 + nb, :], Mn[:, b0:b0 + nb, :], wt[:, b0:b0 + nb, c:c + 1].to_broadcast([128, nb, 128]), ALU.mult,
                         r=[Mn, wt], w=[Mn], eng="pool")
                Mb = W.Mb[1 - mi]
                k.copy(Mb[:], Mn[:], r=[Mn], w=[Mb], eng="act")
                Mst = Mn
                mi = 1 - mi
            yield

    jobs = [(s, d) for s in range(ctx.nseq) for d in (0, 1)]
    for g0 in range(0, len(jobs), nchains):
        gens = [chain(works[i], *jobs[g0 + i]) for i in range(min(nchains, len(jobs) - g0))]
        alive = list(gens)
        while alive:
            nxt = []
            for g in alive:
                try:
                    next(g)
                    nxt.append(g)
                except StopIteration:
                    pass
            alive = nxt
    ctx.P.release(mark)


def phase_c1(ctx, x_in, x_out, prm, S_y, S_bonus, S_g, tok_in, tok_s1, tok_y, tok_out):
    k = ctx.k
    n = ctx.n
    TT = 256
    mark = ctx.P.mark()
    Wo = load_w(ctx, prm["w_o"])
    lnw = k.tile([128, 8], F32)
    lnb = k.tile([128, 8], F32)
    k.dma("sp", lnw[:], prm["ln_w"].rearrange("(c p) -> p c", p=128), w=[lnw], allow_slow_non_contiguous=True)
    k.dma("sp", lnb[:], prm["ln_b"].rearrange("(c p) -> p c", p=128), w=[lnb], allow_slow_non_contiguous=True)
    rings = [k.ring(2, [128, 8, TT], BF16) for _ in range(4)] + [k.ring(2, [128, 8, TT], F32) for _ in range(3)]
    zTs = k.ring(2, [128, 8, TT], BF16)
    xrs = k.ring(2, [128, D], F32)
    xos = k.ring(2, [128, D], F32)
    outs = []
    fm = lambda ap2d, t0: ap2d.rearrange("(p q) t -> q p t", q=128)[:, :, t0:t0 + TT]
    f2 = lambda t: t[:].rearrange("p a b -> p (a b)")
    def step_gen(st):
        t0 = st * TT
        yf, yb, bon, gt, ysum, ysq, t3 = [r.next() for r in rings]
        k.dma("sp", yf[:], S_y[0, st], r=[tok_y], w=[yf])
        k.dma("sp", yb[:], S_y[1, st], r=[tok_y], w=[yb])
        k.dma("sp", bon[:], S_bonus[st], r=[tok_s1], w=[bon])
        k.dma("sp", gt[:], S_g[st], r=[tok_s1], w=[gt])
        yield
        k.tt(ysum[:], yf[:], yb[:], ALU.add, r=[yf, yb], w=[ysum])
        yield
        k.act(ysq[:], ysum[:], AF.Square, r=[ysum], w=[ysq])
        yield
        NH = (8 * TT) // 512
        for hf in range(NH):
            cs = slice(hf * 512, (hf + 1) * 512)
            mbk, qbk = ctx.bank(), ctx.bank()
            k.mm(mbk[:], ctx.cst("BLKM"), f2(ysum)[:, cs], r=[ctx.C, ysum], w=[mbk])
            k.mm(qbk[:], ctx.cst("BLKM"), f2(ysq)[:, cs], r=[ctx.C, ysq], w=[qbk])
            k.act(f2(ysq)[:, cs], mbk[:], AF.Square, r=[mbk], w=[ysq])
            k.tt(f2(ysq)[:, cs], qbk[:], f2(ysq)[:, cs], ALU.subtract, r=[qbk, ysq], w=[ysq])
            k.tt(f2(t3)[:, cs], f2(ysum)[:, cs], mbk[:], ALU.subtract, r=[ysum, mbk], w=[t3])
            yield
        k.act(f2(ysq), f2(ysq), AF.Ln, r=[ysq], w=[ysq], bias=64e-5)
        yield
        k.act(f2(ysq), f2(ysq), AF.Exp, r=[ysq], w=[ysq], scale=-0.5)
        yield
        k.tt(t3[:], t3[:], ysq[:], ALU.mult, r=[t3, ysq], w=[t3])
        yield
        k.tt(t3[:], t3[:], lnw[:].unsqueeze(2).to_broadcast([128, 8, TT]), ALU.mult, r=[t3, lnw], w=[t3], eng="pool")
        yield
        k.tt(t3[:], t3[:], lnb[:].unsqueeze(2).to_broadcast([128, 8, TT]), ALU.add, r=[t3, lnb], w=[t3])
        yield
        k.tt(t3[:], t3[:], bon[:], ALU.add, r=[t3, bon], w=[t3])
        yield
        zT = zTs.next()
        k.tt(zT[:], t3[:], gt[:], ALU.mult, r=[t3, gt], w=[zT])
        yield
        for sub in range(TT // 128):
            ts0 = t0 + sub * 128
            xr = xrs.next()
            k.dma("sp", xr[:], x_in[ts0:ts0 + 128, :], r=[tok_in], w=[xr])
            xo = xos.next()
            for half in range(2):
                po = ctx.bank()
                for cc in range(8):
                    k.mm(po[:], zT[:, cc, sub * 128:(sub + 1) * 128], Wo[:, cc, half * 512:(half + 1) * 512], cc == 0, cc == 7, r=[zT, Wo], w=[po])
                k.tt(xo[:, half * 512:(half + 1) * 512], po[:], xr[:, half * 512:(half + 1) * 512], ALU.add, r=[po, xr], w=[xo])
            outs.append(k.dma("sp", x_out[ts0:ts0 + 128, :], xo[:], r=[xo], w=[tok_out]))
            yield

    nsteps = n // TT
    for s0 in range(0, nsteps, 2):
        alive = [step_gen(s0 + i) for i in range(min(2, nsteps - s0))]
        while alive:
            nxt = []
            for g_ in alive:
                try:
                    next(g_)
                    nxt.append(g_)
                except StopIteration:
                    pass
            alive = nxt
    ctx.P.release(mark)
    return outs
```
